# Optimizing a Trainium2 kernel written in Bass

```python
import math
import jax, jax.numpy as jnp
from jax import lax
import numpy as np

D_MODEL = 1024
BATCH = 8
SEQ = 4096
DEPTH = 1

GRID_W = 64
CTX_LEN = 256
N_ADA = 6
HY_WIDTH = 512
HY_ORDER = 2
HY_EMB = 33
HY_BANDS = (HY_EMB - 1) // 2
HY_FFN = 64
HY_DECAY_TARGET = 1e-2
HY_FAST_PCT = 0.3
HY_SLOW_PCT = 1.5
AT_HEADS = 4
AT_D = 64
AT_W = AT_HEADS * 2 * AT_D
ROPE_AXIS = AT_D // 2
ROPE_BASE = 10000.0
Q_BLOCK = 128
D_FF = 2816
CONV_W = 3
LN_EPS = 1e-5
RMS_EPS = 1e-5
DN_ALPHA = (2.0 * DEPTH) ** 0.25
DN_BETA = (8.0 * DEPTH) ** -0.25
IN_COLS = 3 * HY_WIDTH + 3 * AT_W + 2 * D_MODEL

kernel_name = 'hybrid_hyena_diffattn_convffn_dit'


def layer_norm(x, g, b):
    xf = x.astype(jnp.float32)
    mu = jnp.mean(xf, -1, keepdims=True)
    var = jnp.mean(jnp.square(xf - mu), -1, keepdims=True)
    return ((xf - mu) * lax.rsqrt(var + LN_EPS)).astype(x.dtype) * g + b


def modulate(x, shift, scale):
    return x * (1.0 + scale) + shift


def dwconv3(u, w, b):
    up = jnp.pad(u, ((0, 0), (1, 1), (0, 0)))
    return up[:, :-2] * w[0] + up[:, 1:-1] * w[1] + up[:, 2:] * w[2] + b


def axial_rope_tables(L):
    rows = L // GRID_W
    row = jnp.repeat(jnp.arange(rows, dtype=jnp.float32), GRID_W)
    col = jnp.tile(jnp.arange(GRID_W, dtype=jnp.float32), rows)
    inv = ROPE_BASE ** (-jnp.arange(0, ROPE_AXIS, 2, dtype=jnp.float32) / ROPE_AXIS)
    ang_r = row[:, None] * inv[None]
    ang_c = col[:, None] * inv[None]
    return jnp.cos(ang_r), jnp.sin(ang_r), jnp.cos(ang_c), jnp.sin(ang_c)


def rotate_half(x, cos, sin):
    cos = cos[None, :, None, None, :].astype(x.dtype)
    sin = sin[None, :, None, None, :].astype(x.dtype)
    half = x.shape[-1] // 2
    x1, x2 = x[..., :half], x[..., half:]
    return jnp.concatenate([x1 * cos - x2 * sin, x2 * cos + x1 * sin], -1)


def axial_rope(x, cos_r, sin_r, cos_c, sin_c):
    return jnp.concatenate([rotate_half(x[..., :ROPE_AXIS], cos_r, sin_r),
                            rotate_half(x[..., ROPE_AXIS:], cos_c, sin_c)], -1)


def split_proj(z):
    B, L, _ = z.shape
    o1 = 3 * HY_WIDTH
    o2 = o1 + AT_W
    o3 = o2 + AT_W
    o4 = o3 + AT_W
    o5 = o4 + D_MODEL
    hy = z[..., :o1]
    q = z[..., o1:o2].reshape(B, L, AT_HEADS, 2, AT_D)
    k = z[..., o2:o3].reshape(B, L, AT_HEADS, 2, AT_D)
    v = z[..., o3:o4].reshape(B, L, AT_HEADS, 2 * AT_D)
    g_hy = z[..., o4:o5]
    g_at = z[..., o5:]
    return hy, q, k, v, g_hy, g_at


def hyena_filters(L, w1, b1, w2, b2, w3, b3, freq, w_out):
    t = jnp.linspace(0.0, 1.0, L, dtype=jnp.float32)[:, None]
    w = (2.0 * math.pi / L) * jnp.arange(L, dtype=jnp.float32)[:, None]
    f = jnp.linspace(1e-4, HY_BANDS - 1, HY_BANDS, dtype=jnp.float32)[None, :]
    z = jnp.concatenate([t, jnp.cos(f * w), -jnp.sin(f * w)], -1).astype(w1.dtype)
    hdn = jnp.sin(freq[0] * (z @ w1 + b1))
    hdn = jnp.sin(freq[1] * (hdn @ w2 + b2))
    hdn = jnp.sin(freq[2] * (hdn @ w3 + b3))
    h = (hdn @ w_out).reshape(L, HY_ORDER, 2, HY_WIDTH)
    min_decay = math.log(HY_DECAY_TARGET) / HY_SLOW_PCT
    max_decay = math.log(HY_DECAY_TARGET) / HY_FAST_PCT
    deltas = jnp.linspace(min_decay, max_decay, HY_WIDTH, dtype=jnp.float32)
    decay = jnp.exp(-t * jnp.abs(deltas)[None, :])
    h = h * decay[:, None, None, :].astype(h.dtype)
    fwd, bwd = h[:, :, 0], h[:, :, 1]
    return jnp.concatenate([fwd, jnp.zeros_like(fwd[:1]), bwd[:0:-1]], 0)


def long_conv(u, k, bias):
    L = u.shape[1]
    uf = jnp.fft.rfft(u.astype(jnp.float32), n=2 * L, axis=1)
    kf = jnp.fft.rfft(k.astype(jnp.float32), n=2 * L, axis=0)
    y = jnp.fft.irfft(uf * kf[None], n=2 * L, axis=1)[:, :L]
    return y.astype(u.dtype) + u * bias


def hyena_branch(z_hy, conv_w, conv_b, filt, bias):
    u = dwconv3(z_hy, conv_w, conv_b)
    v, x1, x2 = jnp.split(u, 3, -1)
    zz = x1 * long_conv(v, filt[:, 0], bias[0])
    return x2 * long_conv(zz, filt[:, 1], bias[1])


def diff_attn(q, k, v, lam):
    s = jnp.einsum('bqhcd,bkhcd->bhcqk', q, k).astype(jnp.float32) * (AT_D ** -0.5)
    p = jax.nn.softmax(s, axis=-1)
    a = p[:, :, 0] - lam * p[:, :, 1]
    return jnp.einsum('bhqk,bkhv->bqhv', a.astype(v.dtype), v)


def blocked_diff_attn(q, k, v, lam):
    B, L, H, C, d = q.shape
    nb = L // Q_BLOCK
    qb = jnp.moveaxis(q.reshape(B, nb, Q_BLOCK, H, C, d), 1, 0)
    ob = lax.map(lambda qi: diff_attn(qi, k, v, lam), qb)
    return jnp.moveaxis(ob, 0, 1).reshape(B, L, H, 2 * AT_D)


def merge_branches(y_hy, o_at, g_hy, g_at, subln_g, lam_init, w_hy_o, w_at_o, w_out):
    B, L = o_at.shape[:2]
    of = o_at.astype(jnp.float32)
    of = of * lax.rsqrt(jnp.mean(jnp.square(of), -1, keepdims=True) + RMS_EPS)
    o = (of.astype(o_at.dtype) * subln_g * (1.0 - lam_init)).reshape(B, L, AT_W)
    m = jax.nn.sigmoid(g_hy) * (y_hy @ w_hy_o) + jax.nn.sigmoid(g_at) * (o @ w_at_o)
    return m @ w_out


def conv_ffn(h, w_up, conv_w, conv_b, w_down):
    u = dwconv3(h @ w_up, conv_w, conv_b)
    a, g = jnp.split(u, 2, -1)
    return (jax.nn.silu(g) * a) @ w_down


def setup_inputs(seed: int = 0) -> dict:
    key = jax.random.key(seed)
    ks = jax.random.split(key, 40)
    D = D_MODEL
    L = DEPTH

    def nrm(i, shape, s):
        return s * jax.random.normal(ks[i], shape, jnp.float32)

    return {
        'x': nrm(0, (BATCH, SEQ, D), 1.0),
        'c': nrm(1, (BATCH, D), 1.0),
        'ctx': nrm(2, (BATCH, CTX_LEN, D), 1.0),
        'c_ctx': nrm(3, (D,), 1.0),
        'ln_in_g': 1.0 + nrm(4, (D,), 0.02),
        'ln_in_b': nrm(5, (D,), 0.02),
        'w_ada': nrm(6, (L, D, N_ADA * D), D ** -0.5),
        'b_ada': nrm(7, (L, N_ADA * D), 0.02),
        'w_in': nrm(8, (L, D, IN_COLS), D ** -0.5),
        'hy_conv_w': nrm(9, (L, CONV_W, 3 * HY_WIDTH), CONV_W ** -0.5),
        'hy_conv_b': nrm(10, (L, 3 * HY_WIDTH), 0.02),
        'hy_f_w1': nrm(11, (L, HY_EMB, HY_FFN), HY_EMB ** -0.5),
        'hy_f_b1': nrm(12, (L, HY_FFN), 0.02),
        'hy_f_w2': nrm(13, (L, HY_FFN, HY_FFN), HY_FFN ** -0.5),
        'hy_f_b2': nrm(14, (L, HY_FFN), 0.02),
        'hy_f_w3': nrm(15, (L, HY_FFN, HY_FFN), HY_FFN ** -0.5),
        'hy_f_b3': nrm(16, (L, HY_FFN), 0.02),
        'hy_f_freq': 1.0 + nrm(17, (L, 3, HY_FFN), 0.02),
        'hy_f_wout': nrm(18, (L, HY_FFN, HY_ORDER * 2 * HY_WIDTH), 0.1 * HY_FFN ** -0.5),
        'hy_bias': nrm(19, (L, HY_ORDER, HY_WIDTH), 1.0),
        'lam_q1': nrm(20, (L, AT_D), 0.1),
        'lam_k1': nrm(21, (L, AT_D), 0.1),
        'lam_q2': nrm(22, (L, AT_D), 0.1),
        'lam_k2': nrm(23, (L, AT_D), 0.1),
        'at_subln_g': 1.0 + nrm(24, (L, 2 * AT_D), 0.02),
        'w_hy_o': nrm(25, (L, HY_WIDTH, D), HY_WIDTH ** -0.5),
        'w_at_o': nrm(26, (L, AT_W, D), AT_W ** -0.5),
        'w_out': nrm(27, (L, D, D), DN_BETA * D ** -0.5),
        'ln1_g': 1.0 + nrm(28, (L, D), 0.02),
        'ln1_b': nrm(29, (L, D), 0.02),
        'ffn_w_up': nrm(30, (L, D, 2 * D_FF), D ** -0.5),
        'ffn_conv_w': nrm(31, (L, CONV_W, 2 * D_FF), CONV_W ** -0.5),
        'ffn_conv_b': nrm(32, (L, 2 * D_FF), 0.02),
        'ffn_w_down': nrm(33, (L, D_FF, D), DN_BETA * D_FF ** -0.5),
        'ln2_g': 1.0 + nrm(34, (L, D), 0.02),
        'ln2_b': nrm(35, (L, D), 0.02),
    }


def reference(x, c, ctx, c_ctx, ln_in_g, ln_in_b, w_ada, b_ada, w_in, hy_conv_w, hy_conv_b,
              hy_f_w1, hy_f_b1, hy_f_w2, hy_f_b2, hy_f_w3, hy_f_b3, hy_f_freq, hy_f_wout, hy_bias,
              lam_q1, lam_k1, lam_q2, lam_k2, at_subln_g, w_hy_o, w_at_o, w_out, ln1_g, ln1_b,
              ffn_w_up, ffn_conv_w, ffn_conv_b, ffn_w_down, ln2_g, ln2_b):
    seq_len = x.shape[1]
    ctx_len = ctx.shape[1]
    cos_r, sin_r, cos_c, sin_c = axial_rope_tables(seq_len)
    x = layer_norm(x, ln_in_g, ln_in_b)
    ctx_s = layer_norm(ctx, ln_in_g, ln_in_b)

    for l in range(DEPTH):
        last = l == DEPTH - 1
        lam_init = 0.8 - 0.6 * math.exp(-0.3 * l)
        mod_lat = jax.nn.silu(c) @ w_ada[l] + b_ada[l]
        mod_ctx = jax.nn.silu(c_ctx) @ w_ada[l] + b_ada[l]
        sh1, sc1, g1, sh2, sc2, g2 = jnp.split(mod_lat[:, None, :], N_ADA, -1)
        sh1c, sc1c, g1c, sh2c, sc2c, g2c = jnp.split(mod_ctx[None, None, :], N_ADA, -1)
        lam = (jnp.exp(jnp.sum(lam_q1[l] * lam_k1[l]).astype(jnp.float32))
               - jnp.exp(jnp.sum(lam_q2[l] * lam_k2[l]).astype(jnp.float32)) + lam_init)

        z_lat = modulate(x, sh1, sc1) @ w_in[l]
        z_ctx = modulate(ctx_s, sh1c, sc1c) @ w_in[l]
        hy_l, q_l, k_l, v_l, gh_l, ga_l = split_proj(z_lat)
        hy_c, q_c, k_c, v_c, gh_c, ga_c = split_proj(z_ctx)
        q_l = axial_rope(q_l, cos_r, sin_r, cos_c, sin_c)
        k_l = axial_rope(k_l, cos_r, sin_r, cos_c, sin_c)
        k_all = jnp.concatenate([k_c, k_l], 1)
        v_all = jnp.concatenate([v_c, v_l], 1)
        o_l = blocked_diff_attn(q_l, k_all, v_all, lam)
        filt_lat = hyena_filters(seq_len, hy_f_w1[l], hy_f_b1[l], hy_f_w2[l], hy_f_b2[l],
                                 hy_f_w3[l], hy_f_b3[l], hy_f_freq[l], hy_f_wout[l])
        y_hy_l = hyena_branch(hy_l, hy_conv_w[l], hy_conv_b[l], filt_lat, hy_bias[l])
        y_l = merge_branches(y_hy_l, o_l, gh_l, ga_l, at_subln_g[l], lam_init,
                             w_hy_o[l], w_at_o[l], w_out[l])

        if not last:
            o_c = diff_attn(q_c, k_c, v_c, lam)
            filt_ctx = hyena_filters(ctx_len, hy_f_w1[l], hy_f_b1[l], hy_f_w2[l], hy_f_b2[l],
                                     hy_f_w3[l], hy_f_b3[l], hy_f_freq[l], hy_f_wout[l])
            y_hy_c = hyena_branch(hy_c, hy_conv_w[l], hy_conv_b[l], filt_ctx, hy_bias[l])
            y_c = merge_branches(y_hy_c, o_c, gh_c, ga_c, at_subln_g[l], lam_init,
                                 w_hy_o[l], w_at_o[l], w_out[l])
            ctx_s = layer_norm(DN_ALPHA * ctx_s + g1c * y_c, ln1_g[l], ln1_b[l])
            f_c = conv_ffn(modulate(ctx_s, sh2c, sc2c), ffn_w_up[l], ffn_conv_w[l],
                           ffn_conv_b[l], ffn_w_down[l])
            ctx_s = layer_norm(DN_ALPHA * ctx_s + g2c * f_c, ln2_g[l], ln2_b[l])

        x = layer_norm(DN_ALPHA * x + g1 * y_l, ln1_g[l], ln1_b[l])
        f_l = conv_ffn(modulate(x, sh2, sc2), ffn_w_up[l], ffn_conv_w[l],
                       ffn_conv_b[l], ffn_w_down[l])
        x = layer_norm(DN_ALPHA * x + g2 * f_l, ln2_g[l], ln2_b[l])
    return x
```

```python
import math
import contextlib
import numpy as np
import concourse.bass as bass
import concourse.mybir as mybir
from concourse.bass_utils import run_bass_kernel_spmd

F32 = mybir.dt.float32
BF16 = mybir.dt.bfloat16
ALU = mybir.AluOpType
AF = mybir.ActivationFunctionType
AX = mybir.AxisListType

ENGS = ("pe", "act", "dve", "pool", "sp")

D = 1024
L = 4096
CTX = 256
NTT = 34
LK = CTX + L
HYW = 512
DFF = 2816
NFC = DFF // 128
ALPHA = 2.0 ** 0.25
LAM_INIT = 0.2
PI = math.pi
DEBUG = False


class Prog:
    NDMA = 6

    def __init__(self, nc):
        self.nc = nc
        self.ops = {e: [] for e in ENGS}
        self.cnt = {e: 0 for e in ENGS}
        self.waited = {e: {} for e in ENGS}
        self.last_w = {}
        self.reads = {}
        self.dma_uses = {}
        self.dma_rr = {e: 0 for e in ENGS}
        self.semkeys = set()
        self.sems = {}

    def _deps(self, eng, reads, writes):
        deps = []
        for k in reads:
            if k in self.last_w:
                deps.append(self.last_w[k])
        for k in writes:
            if k in self.last_w:
                deps.append(self.last_w[k])
            for ev in self.reads.get(k, ()):
                if ev[0] == eng:
                    continue
                deps.append(ev)
        best = {}
        for sk, v in deps:
            if eng == "pe" and sk == "pe":
                continue
            if v > best.get(sk, 0):
                best[sk] = v
        out = []
        w = self.waited[eng]
        for sk, v in best.items():
            if w.get(sk, 0) >= v:
                continue
            w[sk] = v
            out.append((sk, v))
        return out

    def _commit(self, ev, reads, writes):
        for k in reads:
            self.reads.setdefault(k, []).append(ev)
        for k in writes:
            self.last_w[k] = ev
            self.reads[k] = []

    def op(self, eng, fn, reads=(), writes=(), sig=True):
        waits = self._deps(eng, reads, writes)
        if sig:
            self.cnt[eng] += 1
            ev = (eng, self.cnt[eng])
            self.semkeys.add(eng)
            self.ops[eng].append(("op", fn, waits, eng))
        else:
            ev = (eng, self.cnt[eng] + 1)
            self.ops[eng].append(("op", fn, waits, None))
        self._commit(ev, reads, writes)
        return ev

    def dma(self, q, fn, reads=(), writes=()):
        slot = self.dma_rr[q] % self.NDMA
        self.dma_rr[q] += 1
        sk = ("dma", q, slot)
        self.semkeys.add(sk)
        uses = self.dma_uses.get(sk, 0)
        waits = self._deps(q, reads, writes)
        if uses > 0 and self.waited[q].get(sk, 0) < 16 * uses:
            self.waited[q][sk] = 16 * uses
            waits.append((sk, 16 * uses))
        uses += 1
        self.dma_uses[sk] = uses
        ev = (sk, 16 * uses)
        self.ops[q].append(("dma", fn, waits, sk))
        self._commit(ev, reads, writes)
        return ev

    def barrier(self):
        evs = []
        for e in ENGS:
            if self.cnt[e] > 0:
                evs.append((e, self.cnt[e]))
        for sk, uses in self.dma_uses.items():
            evs.append((sk, 16 * uses))
        for e in ENGS:
            waits = []
            for sk, v in evs:
                if sk == e:
                    continue
                if self.waited[e].get(sk, 0) >= v:
                    continue
                self.waited[e][sk] = v
                waits.append((sk, v))
            if waits:
                self.ops[e].append(("wait", None, waits, None))
        self.last_w = {}
        self.reads = {}

    def run(self):
        nc = self.nc
        with contextlib.ExitStack() as st:
            for sk in sorted(self.semkeys, key=str):
                name = sk if isinstance(sk, str) else "d_%s_%d" % (sk[1], sk[2])
                self.sems[sk] = st.enter_context(nc.semaphore("s_" + name))
            block = st.enter_context(nc.Block())

            def mk(ename):
                def body(e):
                    for kind, fn, waits, sk in self.ops[ename]:
                        for wk, wv in waits:
                            e.wait_ge(self.sems[wk], wv)
                        if kind == "wait":
                            continue
                        ins = fn(e)
                        if sk is not None:
                            ins.then_inc(self.sems[sk], 16 if kind == "dma" else 1)
                return body

            block.tensor(mk("pe"))
            block.scalar(mk("act"))
            block.vector(mk("dve"))
            block.gpsimd(mk("pool"))
            block.sync(mk("sp"))

    @staticmethod
    def _k(*aps):
        out = []
        for a in aps:
            if a is None or isinstance(a, (int, float)):
                continue
            out.append(a.name)
        return out

    def mm(self, out, lhsT, rhs, start=True, stop=True, sig=True):
        return self.op("pe", lambda e: e.matmul(out, lhsT, rhs, start=start, stop=stop),
                       reads=self._k(lhsT, rhs), writes=self._k(out), sig=sig)

    def tr(self, out, in_, ident, sig=True):
        return self.op("pe", lambda e: e.transpose(out, in_, ident),
                       reads=self._k(in_, ident), writes=self._k(out), sig=sig)

    def act(self, out, in_, func, bias=None, scale=None, accum_out=None):
        kw = {}
        if bias is not None:
            kw["bias"] = bias
        if scale is not None:
            kw["scale"] = scale
        if accum_out is not None:
            kw["accum_out"] = accum_out
        return self.op("act", lambda e: e.activation(out=out, in_=in_, func=func, **kw),
                       reads=self._k(in_, bias, scale), writes=self._k(out, accum_out))

    def tt(self, eng, out, in0, in1, op):
        return self.op(eng, lambda e: e.tensor_tensor(out=out, in0=in0, in1=in1, op=op),
                       reads=self._k(in0, in1), writes=self._k(out))

    def ts(self, eng, out, in0, s1, s2, op0, op1=None):
        if op1 is None:
            return self.op(eng, lambda e: e.tensor_scalar(out=out, in0=in0, scalar1=s1, scalar2=None, op0=op0),
                           reads=self._k(in0, s1), writes=self._k(out))
        return self.op(eng, lambda e: e.tensor_scalar(out=out, in0=in0, scalar1=s1, scalar2=s2, op0=op0, op1=op1),
                       reads=self._k(in0, s1, s2), writes=self._k(out))

    def stt(self, eng, out, in0, scalar, in1, op0, op1):
        return self.op(eng, lambda e: e.scalar_tensor_tensor(out=out, in0=in0, scalar=scalar, in1=in1, op0=op0, op1=op1),
                       reads=self._k(in0, scalar, in1), writes=self._k(out))

    def cp(self, eng, out, in_):
        if eng == "act":
            return self.act(out, in_, AF.Copy)
        return self.op(eng, lambda e: e.tensor_copy(out=out, in_=in_), reads=self._k(in_), writes=self._k(out))

    def memset(self, eng, ap, val):
        return self.op(eng, lambda e: e.memset(ap, val), writes=self._k(ap))

    def ld(self, q, out, in_):
        return self.dma(q, lambda e: e.dma_start(out=out, in_=in_), reads=self._k(in_), writes=self._k(out))


def build_program():
    nc = bass.Bass("TRN2", target_bir_lowering=False)
    P = Prog(nc)

    def din(name, shape, dt=F32):
        return nc.dram_tensor(name, list(shape), dt, kind="ExternalInput")

    def dscr(name, shape, dt, dbg=False):
        if DEBUG and dbg:
            return nc.dram_tensor(name, list(shape), dt, kind="ExternalOutput")
        return nc.dram_tensor(name, list(shape), dt)

    x_d = din("x", [L, D]); ctx_d = din("ctx", [CTX, D])
    cT_d = din("cT", [128, 8, 2])
    lng_d = din("ln_in_g", [D]); lnb_d = din("ln_in_b", [D])
    wada_d = din("w_ada", [D, 6 * D]); bada_d = din("b_ada", [6 * D]); badaT_d = din("b_adaT", [128, 48])
    win_d = din("w_in", [D, 5120]); wqkp_d = din("w_qkp", [D, 1024])
    cos_d = din("rope_cos", [128, L]); sin_d = din("rope_sin", [128, L])
    hcw_d = din("hy_cw", [128, 12, 3]); hcb_d = din("hy_cb", [128, 12])
    fw1_d = din("f_w1", [33, 64]); fw2_d = din("f_w2", [64, 64]); fw3_d = din("f_w3", [64, 64])
    fwo_d = din("f_wout", [64, 2048]); ffr_d = din("f_freqT", [64, 3]); fb_d = din("f_bT", [64, 3])
    zpf_d = din("zposT_f", [33, L]); zpr_d = din("zposT_r", [33, L])
    dcf_d = din("decayT_f", [HYW, L]); dcr_d = din("decayT_r", [HYW, L])
    hyb_d = din("hy_bias", [2, HYW])
    lam_d = din("lamv", [4, 64])
    sub_d = din("subln_g", [128, 1])
    who_d = din("w_hy_o", [HYW, D]); wao_d = din("w_at_o", [512, D]); wout_d = din("w_out", [D, D])
    l1g_d = din("ln1_g", [D]); l1b_d = din("ln1_b", [D]); l2g_d = din("ln2_g", [D]); l2b_d = din("ln2_b", [D])
    wup_d = din("ffn_w_up", [D, 2 * DFF]); fcw_d = din("ffn_cw", [128, 44, 3]); fcb_d = din("ffn_cb", [128, 44])
    wdn_d = din("ffn_w_down", [DFF, D])
    out_d = nc.dram_tensor("out", [L, D], F32, kind="ExternalOutput")

    xln_s = dscr("xln_s", [L, D], F32, True)
    qT_s = dscr("qT_s", [4, 128, L], BF16, True)
    kT_s = dscr("kT_s", [4, 128, LK], BF16, True)
    V_s = dscr("V_s", [NTT, 128, 512], BF16, True)
    hyu_s = dscr("hyu_s", [128, 1536 * 32], BF16, True)
    gat_s = dscr("gat_s", [16, 128, L], BF16, True)
    A_s = dscr("A_s", [2 * HYW, 8192], BF16, True)
    oT_s = dscr("oT_s", [4, 128, L], BF16, True)
    yhT_s = dscr("yhT_s", [4, 128, L], BF16, True)
    x1_s = dscr("x1_s", [L, D], F32, True)
    hT_s = dscr("hT_s", [NFC, 128, L], BF16, True)

    top = contextlib.ExitStack()

    def sbt(st, name, shape, dt):
        return st.enter_context(nc.sbuf_tensor("sb_" + name, list(shape), dt))

    def pst(st, name, shape, dt=F32):
        return st.enter_context(nc.psum_tensor("ps_" + name, list(shape), dt))

    with top:
        ident = sbt(top, "ident", [128, 128], BF16)
        jrev = sbt(top, "jrev", [128, 128], BF16)
        identf = sbt(top, "identf", [128, 128], F32)
        epsb = sbt(top, "epsb", [128, 1], F32)
        npib = sbt(top, "npib", [128, 1], F32)
        modT = sbt(top, "modT", [128, 48, 2], F32)
        onep = sbt(top, "onep", [128, 16, 2], F32)
        g1bc = sbt(top, "g1bc", [128, D], F32)
        g2bc = sbt(top, "g2bc", [128, D], F32)
        neglam = sbt(top, "neglam", [128, 1], F32)
        subg = sbt(top, "subg", [128, 1], F32)

        P.memset("pool", identf[:], 0.0)
        P.op("pool", lambda e: e.affine_select(out=identf[:], in_=identf[:], pattern=[[-1, 128]],
                                               compare_op=ALU.not_equal, fill=1.0, base=0, channel_multiplier=1),
             reads=[identf.name], writes=[identf.name])
        P.cp("dve", ident[:], identf[:])
        P.memset("pool", identf[:], 0.0)
        P.op("pool", lambda e: e.affine_select(out=identf[:], in_=identf[:], pattern=[[1, 128]],
                                               compare_op=ALU.not_equal, fill=1.0, base=-127, channel_multiplier=1),
             reads=[identf.name], writes=[identf.name])
        P.cp("dve", jrev[:], identf[:])
        P.memset("pool", epsb[:], 1e-5)
        P.memset("pool", npib[:], -PI)

        with contextlib.ExitStack() as st:
            cT = sbt(st, "cT", [128, 8, 2], F32)
            sc = sbt(st, "sc", [128, 8, 2], F32)
            scb = sbt(st, "scb", [128, 8, 128], F32)
            wa = [sbt(st, "wa%d" % i, [128, 8, 512], F32) for i in range(2)]
            badaT = sbt(st, "badaT", [128, 48], F32)
            bg = sbt(st, "bg", [128, D], F32)
            lamv = sbt(st, "lamv_t", [128, 4, 64], F32)
            lamp = sbt(st, "lamp", [128, 2, 64], F32)
            lams = sbt(st, "lams", [128, 2], F32)
            pm = pst(st, "pm", [128, 48, 2])
            pb = pst(st, "pb", [128, 512])
            P.ld("sp", cT[:], cT_d.ap())
            P.ld("sp", badaT[:], badaT_d.ap())
            P.ld("sp", subg[:], sub_d.ap())
            P.ld("sp", lamv[:].rearrange("p a b -> p (a b)"),
                 lam_d.ap().rearrange("a b -> (a b)").partition_broadcast(128))
            P.act(sc[:], cT[:], AF.Silu)
            P.cp("dve", scb[:], sc[:, :, 0:1].to_broadcast([128, 8, 128]))
            P.ts("dve", subg[:], subg[:], 1.0 - LAM_INIT, None, ALU.mult)
            P.tt("dve", lamp[:, 0, :], lamv[:, 0, :], lamv[:, 1, :], ALU.mult)
            P.tt("dve", lamp[:, 1, :], lamv[:, 2, :], lamv[:, 3, :], ALU.mult)
            P.op("dve", lambda e: e.reduce_sum(out=lams[:], in_=lamp[:], axis=AX.X), reads=[lamp.name], writes=[lams.name])
            P.act(lams[:], lams[:], AF.Exp)
            P.tt("dve", neglam[:], lams[:, 1:2], lams[:, 0:1], ALU.subtract)
            P.ts("dve", neglam[:], neglam[:], -LAM_INIT, None, ALU.add)
            for g in range(12):
                w = wa[g % 2]
                P.ld("sp" if g % 2 == 0 else "act", w[:],
                     wada_d.ap()[:, g * 512:(g + 1) * 512].rearrange("(kc p) c -> p kc c", p=128))
                for jj in range(4):
                    j = g * 4 + jj
                    for kc in range(8):
                        P.mm(pm[:, j, :], w[:, kc, jj * 128:(jj + 1) * 128], sc[:, kc, :],
                             start=(kc == 0), stop=(kc == 7), sig=(kc == 7))
                if g in (4, 5, 10, 11):
                    for kc in range(8):
                        P.mm(pb[:], scb[:, kc, :], w[:, kc, :], start=(kc == 0), stop=(kc == 7), sig=(kc == 7))
                    dst = g1bc if g in (4, 5) else g2bc
                    half = g % 2
                    P.ld("sp", bg[:, 0:512], bada_d.ap()[g * 512:(g + 1) * 512].partition_broadcast(128))
                    P.tt("dve", dst[:, half * 512:(half + 1) * 512], pb[:], bg[:, 0:512], ALU.add)
            P.tt("dve", modT[:], pm[:], badaT[:].unsqueeze(2).to_broadcast([128, 48, 2]), ALU.add)
            P.ts("dve", onep[:, 0:8, :], modT[:, 8:16, :], 1.0, None, ALU.add)
            P.ts("dve", onep[:, 8:16, :], modT[:, 32:40, :], 1.0, None, ALU.add)
            P.barrier()

        def layer_norm_tile(pfx, src, gbc, bbc, dst, stats, mv, rstd, tmp):
            for h in range(2):
                P.op("dve", (lambda e, h=h: e.bn_stats(out=stats[:, h, :], in_=src[:, h * 512:(h + 1) * 512])),
                     reads=[src.name], writes=[stats.name])
            P.op("dve", lambda e: e.bn_aggr(out=mv[:], in_=stats[:].rearrange("p a b -> p (a b)")),
                 reads=[stats.name], writes=[mv.name])
            P.act(rstd[:], mv[:, 1:2], AF.Sqrt, bias=epsb[:], scale=1.0)
            P.op("dve", lambda e: e.reciprocal(out=rstd[:], in_=rstd[:]), reads=[rstd.name], writes=[rstd.name])
            P.ts("dve", tmp[:], src[:], mv[:, 0:1], rstd[:], ALU.subtract, ALU.mult)
            P.tt("pool", tmp[:], tmp[:], gbc[:], ALU.mult)
            P.tt("pool", dst[:], tmp[:], bbc[:], ALU.add)

        with contextlib.ExitStack() as stA:
            xmT = sbt(stA, "xmT", [128, 8, L], BF16)
            xcT = sbt(stA, "xcT", [128, 8, CTX], BF16)
            with contextlib.ExitStack() as st:
                gbc = sbt(st, "gbc", [128, D], F32)
                bbc = sbt(st, "bbc", [128, D], F32)
                xt = [sbt(st, "xt%d" % i, [128, D], F32) for i in range(3)]
                xh = [sbt(st, "xh%d" % i, [128, D], F32) for i in range(3)]
                xl = [sbt(st, "xl%d" % i, [128, D], F32) for i in range(3)]
                xb = [sbt(st, "xb%d" % i, [128, D], BF16) for i in range(2)]
                stats = [sbt(st, "stats%d" % i, [128, 2, 6], F32) for i in range(3)]
                mv = [sbt(st, "mv%d" % i, [128, 2], F32) for i in range(3)]
                rstd = [sbt(st, "rstd%d" % i, [128, 1], F32) for i in range(3)]
                pT = [pst(st, "pT%d" % i, [128, 8, 128], BF16) for i in range(2)]
                P.ld("sp", gbc[:], lng_d.ap().partition_broadcast(128))
                P.ld("sp", bbc[:], lnb_d.ap().partition_broadcast(128))
                def phase2(t):
                    i = t % 3
                    is_ctx = t < 2
                    if not is_ctx:
                        P.ld("pool", xln_s.ap()[(t - 2) * 128:(t - 1) * 128, :], xl[i][:])
                    P.cp("act", xb[t % 2][:], xl[i][:])
                    for kc in range(8):
                        P.tr(pT[t % 2][:, kc, :], xb[t % 2][:, kc * 128:(kc + 1) * 128], ident[:], sig=(kc == 7))
                    w = 1 if is_ctx else 0
                    for kc in range(8):
                        dst = xcT[:, kc, t * 128:(t + 1) * 128] if is_ctx else xmT[:, kc, (t - 2) * 128:(t - 1) * 128]
                        P.act(dst, pT[t % 2][:, kc, :], AF.Identity, bias=modT[:, kc, w:w + 1], scale=onep[:, kc, w:w + 1])

                for t in range(NTT):
                    i = t % 3
                    is_ctx = t < 2
                    src = ctx_d.ap()[t * 128:(t + 1) * 128, :] if is_ctx else x_d.ap()[(t - 2) * 128:(t - 1) * 128, :]
                    P.ld("sp", xt[i][:], src)
                    layer_norm_tile("A", xt[i], gbc, bbc, xl[i], stats[i], mv[i], rstd[i], xh[i])
                    if t > 0:
                        phase2(t - 1)
                phase2(NTT - 1)
                P.barrier()

            wf = [sbt(stA, "wf%d" % i, [128, 8, 128], F32) for i in range(2)]
            wb = [sbt(stA, "wb%d" % i, [128, 8, 128], BF16) for i in range(4)]
            psB = [pst(stA, "psB%d" % i, [128, 512]) for i in range(4)]
            ctr = {"w": 0, "ps": 0}

            w_items = []
            for which_ in range(2):
                for h_ in range(4):
                    w_items.append((win_d, 1536 + which_ * 512 + h_ * 128))
                    w_items.append((wqkp_d, which_ * 512 + h_ * 128))
            for cc_ in range(12):
                w_items.append((win_d, cc_ * 128))
            for gc_ in range(16):
                w_items.append((win_d, 3072 + gc_ * 128))
            for c4_ in range(4):
                w_items.append((win_d, 2560 + c4_ * 128))
            ctr["issued"] = 0

            def _issue_w():
                i = ctr["issued"]
                if i >= len(w_items):
                    return
                ctr["issued"] += 1
                src_d, col0 = w_items[i]
                P.ld("sp", wf[i % 2][:], src_d.ap()[:, col0:col0 + 128].rearrange("(kc p) c -> p kc c", p=128))
                P.cp("pool", wb[i % 4][:], wf[i % 2][:])

            def load_w(src_d, col0):
                i = ctr["w"]
                ctr["w"] += 1
                assert w_items[i][1] == col0 and w_items[i][0] is src_d
                while ctr["issued"] <= min(i + 2, len(w_items) - 1):
                    _issue_w()
                return wb[i % 4]

            def proj_fm(wt, T):
                ps = psB[ctr["ps"] % 4]
                ctr["ps"] += 1
                for kc in range(8):
                    P.mm(ps[:], wt[:, kc, :], xmT[:, kc, T * 512:(T + 1) * 512], start=(kc == 0), stop=(kc == 7),
                         sig=(kc == 7))
                return ps

            with contextlib.ExitStack() as st:
                cosT = sbt(st, "cosT", [128, L], F32)
                sinT = sbt(st, "sinT", [128, L], F32)
                ra = [sbt(st, "ra%d" % i, [128, 512], F32) for i in range(2)]
                rb = [sbt(st, "rb%d" % i, [128, 512], F32) for i in range(2)]
                qrow = [sbt(st, "qrow%d" % i, [128, LK], BF16) for i in range(2)]
                P.ld("sp", cosT[:], cos_d.ap())
                P.ld("sp", sinT[:], sin_d.ap())
                n = 0
                for which in range(2):
                    for h in range(4):
                        col = 1536 + which * 512 + h * 128
                        w_main = load_w(win_d, col)
                        w_perm = load_w(wqkp_d, which * 512 + h * 128)
                        row = qrow[n % 2]
                        off = CTX if which == 1 else 0
                        if which == 1:
                            ps = psB[ctr["ps"] % 4]
                            ctr["ps"] += 1
                            for kc in range(8):
                                P.mm(ps[:, 0:CTX], w_main[:, kc, :], xcT[:, kc, :], start=(kc == 0), stop=(kc == 7),
                                     sig=(kc == 7))
                            P.cp("act", row[:, 0:CTX], ps[:, 0:CTX])
                        for T in range(8):
                            pa = proj_fm(w_main, T)
                            pb_ = proj_fm(w_perm, T)
                            P.tt("dve", ra[T % 2][:], pa[:], cosT[:, T * 512:(T + 1) * 512], ALU.mult)
                            P.tt("dve", rb[T % 2][:], pb_[:], sinT[:, T * 512:(T + 1) * 512], ALU.mult)
                            P.tt("pool", row[:, off + T * 512:off + (T + 1) * 512], ra[T % 2][:], rb[T % 2][:], ALU.add)
                        if which == 0:
                            P.ld("sp", qT_s.ap()[h], row[:, 0:L])
                        else:
                            P.ld("sp", kT_s.ap()[h], row[:, 0:LK])
                        n += 1
                P.barrier()

            with contextlib.ExitStack() as st:
                zrows = [sbt(st, "zrow%d" % i, [128, L + 2], F32) for i in range(2)]
                t1 = sbt(st, "t1", [128, L], F32)
                urows = [sbt(st, "urow%d" % i, [128, L], BF16) for i in range(2)]
                ucc = [sbt(st, "ucc%d" % i, [128, 128, 32], BF16) for i in range(2)]
                hcw = sbt(st, "hcw", [128, 12, 3], F32)
                hcb = sbt(st, "hcb", [128, 12], F32)
                pU = [pst(st, "pU%d" % i, [128, 4, 128], BF16) for i in range(2)]
                P.ld("sp", hcw[:], hcw_d.ap())
                P.ld("sp", hcb[:], hcb_d.ap())
                for zrow in zrows:
                    P.memset("pool", zrow[:, 0:1], 0.0)
                    P.memset("pool", zrow[:, L + 1:L + 2], 0.0)
                def b2_phase2(cc):
                    urow = urows[cc % 2]
                    u = ucc[cc % 2]
                    for jq in range(8):
                        pp = pU[jq % 2]
                        for jj in range(4):
                            j = jq * 4 + jj
                            P.tr(pp[:, jj, :], urow[:, j * 128:(j + 1) * 128], ident[:], sig=(jj == 3))
                        P.cp("act", u[:, :, jq * 4:(jq + 1) * 4], pp[:].rearrange("p j c -> p c j"))
                    P.ld("sp", hyu_s.ap()[:, cc * 4096:(cc + 1) * 4096], u[:].rearrange("p c j -> p (c j)"))

                for cc in range(12):
                    zrow = zrows[cc % 2]
                    urow = urows[cc % 2]
                    wt = load_w(win_d, cc * 128)
                    for T in range(8):
                        ps = proj_fm(wt, T)
                        P.cp("act", zrow[:, 1 + T * 512:1 + (T + 1) * 512], ps[:])
                    P.ts("pool", t1[:], zrow[:, 0:L], hcw[:, cc, 0:1], hcb[:, cc:cc + 1], ALU.mult, ALU.add)
                    P.stt("dve", t1[:], zrow[:, 1:L + 1], hcw[:, cc, 1:2], t1[:], ALU.mult, ALU.add)
                    P.stt("dve", urow[:], zrow[:, 2:L + 2], hcw[:, cc, 2:3], t1[:], ALU.mult, ALU.add)
                    if cc > 0:
                        b2_phase2(cc - 1)
                    if cc == 11:
                        b2_phase2(cc)
                P.barrier()

            with contextlib.ExitStack() as st:
                grow = [sbt(st, "grow%d" % i, [128, L], BF16) for i in range(2)]
                wv = sbt(st, "wv", [128, 8, 512], BF16)
                vrow = [sbt(st, "vrow%d" % i, [128, 512], BF16) for i in range(2)]
                for gc in range(16):
                    wt = load_w(win_d, 3072 + gc * 128)
                    g = grow[gc % 2]
                    for T in range(8):
                        ps = proj_fm(wt, T)
                        P.act(g[:, T * 512:(T + 1) * 512], ps[:], AF.Sigmoid)
                    P.ld("sp", gat_s.ap()[gc], g[:])
                for c4 in range(4):
                    wt = load_w(win_d, 2560 + c4 * 128)
                    P.cp("dve", wv[:, :, c4 * 128:(c4 + 1) * 128], wt[:])
                for t in range(NTT):
                    ps = psB[ctr["ps"] % 4]
                    ctr["ps"] += 1
                    for kc in range(8):
                        lhs = xcT[:, kc, t * 128:(t + 1) * 128] if t < 2 else xmT[:, kc, (t - 2) * 128:(t - 1) * 128]
                        P.mm(ps[:], lhs, wv[:, kc, :], start=(kc == 0), stop=(kc == 7), sig=(kc == 7))
                    v = vrow[t % 2]
                    P.cp("act", v[:], ps[:])
                    P.ld("sp", V_s.ap()[t], v[:])
                P.barrier()

        with contextlib.ExitStack() as st:
            fw1 = sbt(st, "fw1", [33, 64], F32); fw2 = sbt(st, "fw2", [64, 64], F32); fw3 = sbt(st, "fw3", [64, 64], F32)
            fwo = sbt(st, "fwo", [64, 2048], F32)
            ffr = sbt(st, "ffr", [64, 3], F32); fbt = sbt(st, "fbt", [64, 3], F32); ffb = sbt(st, "ffb", [64, 3], F32)
            zp = sbt(st, "zp", [33, L], F32)
            hA = sbt(st, "hA", [64, L], F32); hB = sbt(st, "hB", [64, L], F32)
            targ = [sbt(st, "targ%d" % i, [64, 2048], F32) for i in range(2)]
            targm = [sbt(st, "targm%d" % i, [64, 2048], F32) for i in range(2)]
            dct = [sbt(st, "dct%d" % i, [128, L], F32) for i in range(2)]
            arow = [sbt(st, "arow%d" % i, [128, L], BF16) for i in range(2)]
            pf = [pst(st, "pf%d" % i, [128, 2048]) for i in range(2)]
            P.ld("sp", fw1[:], fw1_d.ap()); P.ld("sp", fw2[:], fw2_d.ap()); P.ld("sp", fw3[:], fw3_d.ap())
            P.ld("act", fwo[:], fwo_d.ap()); P.ld("sp", ffr[:], ffr_d.ap()); P.ld("sp", fbt[:], fb_d.ap())
            P.tt("dve", ffb[:], ffr[:], fbt[:], ALU.mult)
            npf = 0
            nrow = 0
            for ev in range(2):
                P.ld("sp", zp[:], (zpf_d if ev == 0 else zpr_d).ap())
                srcs = [(zp, fw1, 33), (hA, fw2, 64), (hB, fw3, 64)]
                dsts = [hA, hB, hA]
                for li in range(3):
                    src, wt, kk = srcs[li]
                    dst = dsts[li]
                    for hf in range(2):
                        ps = pf[npf % 2]
                        ta = targ[npf % 2]
                        tm = targm[npf % 2]
                        npf += 1
                        for q4 in range(4):
                            col = hf * 2048 + q4 * 512
                            P.mm(ps[0:64, q4 * 512:(q4 + 1) * 512], wt[0:kk, :], src[0:kk, col:col + 512], sig=(q4 == 3))
                        P.ts("dve", ta[:], ps[0:64, :], ffr[:, li:li + 1], ffb[:, li:li + 1], ALU.mult, ALU.add)
                        P.ts("dve", tm[:], ta[:], PI, -2.0 * PI, ALU.is_gt, ALU.mult)
                        P.tt("dve", ta[:], ta[:], tm[:], ALU.add)
                        P.ts("dve", tm[:], ta[:], -PI, 2.0 * PI, ALU.is_lt, ALU.mult)
                        P.tt("dve", ta[:], ta[:], tm[:], ALU.add)
                        P.ts("dve", ta[:], ta[:], PI, -PI, ALU.min, ALU.max)
                        P.act(dst[:, hf * 2048:(hf + 1) * 2048], ta[:], AF.Sin)
                for o in range(2):
                    for cc in range(4):
                        dc = dct[nrow % 2]
                        ar = arow[nrow % 2]
                        nrow += 1
                        P.ld("sp" if nrow % 2 == 0 else "act", dc[:],
                             (dcf_d if ev == 0 else dcr_d).ap()[cc * 128:(cc + 1) * 128, :])
                        col = o * 1024 + ev * 512 + cc * 128
                        for hf in range(2):
                            ps = pf[npf % 2]
                            npf += 1
                            for q4 in range(4):
                                c_ = hf * 2048 + q4 * 512
                                P.mm(ps[:, q4 * 512:(q4 + 1) * 512], fwo[:, col:col + 128], hA[:, c_:c_ + 512], sig=(q4 == 3))
                            P.tt("dve", ar[:, hf * 2048:(hf + 1) * 2048], ps[:], dc[:, hf * 2048:(hf + 1) * 2048], ALU.mult)
                        rows = A_s.ap()[o * HYW + cc * 128:o * HYW + (cc + 1) * 128, :]
                        if ev == 0:
                            P.ld("sp", rows[:, 4095:8191], ar[:])
                        else:
                            P.ld("sp", rows[:, 0:4095], ar[:, 0:4095])
            P.barrier()

        G = 16
        S = 4
        KB = 128 // S
        TW = 8192 - KB
        NG = G // S
        with contextlib.ExitStack() as stY:
            yall = sbt(stY, "yall", [128, HYW, 32], BF16)
            with contextlib.ExitStack() as st:
                qh = sbt(st, "qh", [128, L], BF16)
                kh = sbt(st, "kh", [128, LK], BF16)
                Vh = sbt(st, "Vh", [128, NTT, 128], BF16)
                ones_f = sbt(st, "ones_f", [128, 128], F32)
                ones_b = sbt(st, "ones_b", [128, 128], BF16)
                NR = 4
                PT = [sbt(st, "PT%d" % i, [128, 512], BF16) for i in range(NR)]
                rc = [sbt(st, "rc%d" % i, [128, 512], F32) for i in range(2)]
                acc = [sbt(st, "acc%d" % i, [128, 512], F32) for i in range(2)]
                on = [sbt(st, "on%d" % i, [128, 512], F32) for i in range(4)]
                oc = [sbt(st, "oc%d" % i, [128, 512], F32) for i in range(2)]
                osq = [sbt(st, "osq%d" % i, [128, 512], F32) for i in range(2)]
                rst = [sbt(st, "rst%d" % i, [128, 512], F32) for i in range(2)]
                orow = [sbt(st, "orow%d" % i, [128, 512], BF16) for i in range(2)]
                pS = [pst(st, "pS%d" % i, [128, 512]) for i in range(2)]
                pO = [pst(st, "pO%d" % i, [128, 512]) for i in range(2)]
                pSm = [pst(st, "pSm%d" % i, [128, 512]) for i in range(2)]
                NT = 4
                tsk = [sbt(st, "tsk%d" % i, [128, TW], BF16) for i in range(NT)]
                rself = sbt(st, "rself", [128, KB], F32)
                fsel = sbt(st, "fsel", [128, S, S, 128], BF16)
                b0 = sbt(st, "b0", [128, HYW], F32); b1 = sbt(st, "b1", [128, HYW], F32)
                vg = [sbt(st, "vg%d" % i, [128, G, 32], BF16) for i in range(2)]
                x1g = [sbt(st, "x1g%d" % i, [128, G, 32], BF16) for i in range(2)]
                x2g = [sbt(st, "x2g%d" % i, [128, G, 32], BF16) for i in range(2)]
                vr = sbt(st, "vr", [128, NG, S, 32, S], BF16)
                vb = sbt(st, "vb", [128, G, 32], F32)
                tmp = sbt(st, "tmpE", [128, G, 32], F32)
                zz = sbt(st, "zz", [128, G, 32], BF16)
                zr = sbt(st, "zr", [128, NG, S, 32, S], BF16)
                zb = sbt(st, "zb", [128, G, 32], F32)
                pc = [pst(st, "pc%d" % o, [128, NG, 32, S]) for o in range(2)]
                pj = pc[1]

                P.memset("pool", ones_f[:], 1.0)
                P.memset("pool", ones_b[:], 1.0)
                P.ld("sp", b0[:], hyb_d.ap()[0].partition_broadcast(128))
                P.ld("sp", b1[:], hyb_d.ap()[1].partition_broadcast(128))
                hy3 = hyu_s.ap().rearrange("p (c j) -> p c j", j=32)
                P.memset("pool", fsel[:], 0.0)
                for hi_ in range(S):
                    P.memset("pool", rself[:], 0.0)
                    P.op("pool", (lambda e, b_=-(KB * hi_ + KB - 1): e.affine_select(
                        out=rself[:], in_=rself[:], pattern=[[1, KB]], compare_op=ALU.not_equal, fill=1.0,
                        base=b_, channel_multiplier=1)), reads=[rself.name], writes=[rself.name])
                    for sl_ in range(S):
                        P.cp("dve", fsel[:, hi_, sl_, KB * sl_:KB * sl_ + KB], rself[:])

                steps = [(h, Q, c, kb) for h in range(4) for Q in range(8) for c in range(2) for kb in range(NTT)]
                NS = len(steps)

                def gen_C():
                    pending = []
                    state = {"head": -1, "qk": -1}

                    def load_head(h):
                        P.ld("sp", qh[:], qT_s.ap()[h])
                        P.ld("sp", kh[:], kT_s.ap()[h])
                        for t0_ in (0, 17):
                            P.ld("sp", Vh[:, t0_:t0_ + 17, :],
                                 V_s.ap()[t0_:t0_ + 17, :, h * 128:(h + 1) * 128].rearrange("t p v -> p t v"))

                    def ensure_qk(m):
                        if m <= state["qk"] or m >= NS:
                            return
                        h, Q, c, kb = steps[m]
                        if h != state["head"]:
                            load_head(h)
                            state["head"] = h
                        P.mm(pS[m % 2][:], kh[64 * c:64 * c + 64, kb * 128:(kb + 1) * 128],
                             qh[64 * c:64 * c + 64, Q * 512:(Q + 1) * 512])
                        P.act(PT[m % NR][:], pS[m % 2][:], AF.Exp, scale=0.125)
                        state["qk"] = m

                    def fin1(h, Q, c, s_):
                        g = (h * 8 + Q) % 2
                        P.mm(pSm[s_][:], ones_f[:], acc[s_][:])
                        P.op("dve", (lambda e, a=rc[c][:], b=pSm[s_][:]: e.reciprocal(out=a, in_=b)),
                             reads=[pSm[s_].name], writes=[rc[c].name])
                        P.tt("dve", on[2 * g + c][:], pO[s_][:], rc[c][:], ALU.mult)
                        if c == 1:
                            P.stt("dve", oc[g][:], on[2 * g + 1][:], neglam[:], on[2 * g][:], ALU.mult, ALU.add)
                            P.tt("pool", osq[g][:], oc[g][:], oc[g][:], ALU.mult)

                    def fin2(h, Q, s_):
                        g = (h * 8 + Q) % 2
                        P.mm(pSm[s_][:], ones_f[:], osq[g][:])
                        P.act(rst[g][:], pSm[s_][:], AF.Sqrt, bias=epsb[:], scale=1.0 / 128.0)
                        P.op("dve", (lambda e, a=rst[g][:]: e.reciprocal(out=a, in_=a)),
                             reads=[rst[g].name], writes=[rst[g].name])
                        P.stt("dve", orow[g][:], oc[g][:], subg[:], rst[g][:], ALU.mult, ALU.mult)
                        P.ld("pool", oT_s.ap()[h][:, Q * 512:(Q + 1) * 512], orow[g][:])

                    for n in range(NS):
                        h, Q, c, kb = steps[n]
                        s_ = (n // NTT) % 2
                        ensure_qk(n)
                        if n + 1 < NS and steps[n + 1][0] == h:
                            ensure_qk(n + 1)
                        P.mm(pO[s_][:], Vh[:, kb, :], PT[n % NR][:], start=(kb == 0), stop=(kb == NTT - 1),
                             sig=(kb == NTT - 1))
                        if kb == 0:
                            P.cp("dve", acc[s_][:], PT[n % NR][:])
                        else:
                            P.tt("dve", acc[s_][:], acc[s_][:], PT[n % NR][:], ALU.add)
                        for item in list(pending):
                            if item[0] <= n:
                                item[1]()
                                pending.remove(item)
                        if kb == NTT - 1:
                            pending.append((n + 3, (lambda h=h, Q=Q, c=c, s_=s_: fin1(h, Q, c, s_))))
                            if c == 1:
                                pending.append((n + 10, (lambda h=h, Q=Q, s_=s_: fin2(h, Q, s_))))
                            if n + 1 < NS and steps[n + 1][0] != h:
                                for item in pending:
                                    item[1]()
                                pending = []
                        yield
                    for item in pending:
                        item[1]()

                elist = [0] + [e for e in range(-(32 * S - 1), 31 * S + 1) if e != 0]
                ectr = {"tsk": 0}

                def conv_grp(o, c, rhs_t, grp, pb):
                    tk = tsk[ectr["tsk"] % NT]
                    ectr["tsk"] += 1
                    keys = []
                    for sl in range(S):
                        q = "sp"
                        key = tk.name + "_s%d" % sl
                        keys.append(key)
                        P.dma(q, (lambda e, o_=tk[KB * sl:KB * sl + KB, :],
                                         i_=bass.AP(A_s, (o * HYW + c + sl) * 8192, [[1, KB], [1, TW]]):
                                     e.dma_start(out=o_, in_=i_)),
                              reads=[], writes=[key])
                    last = len(elist) - 1
                    for n_, e in enumerate(elist):
                        i_lo = max(0, -((-e) // S))
                        i_hi = min(31, (32 * S - 1 + e) // S)
                        nn = i_hi - i_lo + 1
                        hi = (-e) % S
                        j0 = i_lo - (e + hi) // S
                        assert nn > 0 and (e + hi) % S == 0 and 0 <= j0 and j0 + nn <= 32
                        ce = KB * e + 4096 - KB
                        assert 0 <= ce and ce + 128 <= TW
                        P.op("pe", (lambda en, o_=pb[:, grp, i_lo:i_hi + 1, :].rearrange("p i s -> p (i s)"),
                                           l_=tk[:, ce:ce + 128],
                                           r_=rhs_t[:, grp, hi, j0:j0 + nn, :].rearrange("p j s -> p (j s)"),
                                           s0=(n_ == 0), s1=(n_ == last): en.matmul(o_, l_, r_, start=s0, stop=s1)),
                             reads=keys + [rhs_t.name], writes=[pb.name], sig=(n_ == last))
                        if n_ % 28 == 27:
                            yield

                def reverse(dst, src):
                    sv = src[:].rearrange("p (g s) j -> p g s j", s=S)
                    pjf = pj[:].rearrange("p g i s -> p (g i s)")
                    for hi in range(S):
                        for sl in range(S):
                            P.mm(pjf[:, sl * NG * 32:(sl + 1) * NG * 32], fsel[:, hi, sl, :], sv[:, :, sl, :],
                                 sig=(sl == S - 1))
                        P.cp("act", dst[:, :, hi, :, :],
                             pjf.rearrange("p (s g j) -> p g j s", s=S, g=NG))

                def gen_E():
                    for g in range(HYW // G):
                        i = g % 2
                        c0 = g * G
                        P.ld("sp", vg[i][:], hy3[:, c0:c0 + G, :])
                        P.ld("sp", x1g[i][:], hy3[:, HYW + c0:HYW + c0 + G, :])
                        P.ld("sp", x2g[i][:], hy3[:, 2 * HYW + c0:2 * HYW + c0 + G, :])
                        reverse(vr, vg[i])
                        P.tt("pool", vb[:], vg[i][:], b0[:, c0:c0 + G].unsqueeze(2).to_broadcast([128, G, 32]), ALU.mult)
                        for grp in range(NG):
                            yield from conv_grp(0, c0 + S * grp, vr, grp, pc[0])
                        P.tt("dve", tmp[:].rearrange("p (q h) i -> p q h i", h=S), pc[0][:].rearrange("p q i h -> p q h i"),
                             vb[:].rearrange("p (q h) i -> p q h i", h=S), ALU.add)
                        P.tt("dve", zz[:], tmp[:], x1g[i][:], ALU.mult)
                        reverse(zr, zz)
                        P.tt("pool", zb[:], zz[:], b1[:, c0:c0 + G].unsqueeze(2).to_broadcast([128, G, 32]), ALU.mult)
                        for grp in range(NG):
                            yield from conv_grp(1, c0 + S * grp, zr, grp, pc[1])
                        P.tt("dve", tmp[:].rearrange("p (q h) i -> p q h i", h=S), pc[1][:].rearrange("p q i h -> p q h i"),
                             zb[:].rearrange("p (q h) i -> p q h i", h=S), ALU.add)
                        P.tt("dve", yall[:, c0:c0 + G, :], tmp[:], x2g[i][:], ALU.mult)
                        yield

                gC = gen_C()
                gE = gen_E()
                doneC = doneE = False
                while not (doneC and doneE):
                    if not doneE:
                        try:
                            next(gE)
                        except StopIteration:
                            doneE = True
                    if not doneC:
                        try:
                            next(gC)
                        except StopIteration:
                            doneC = True
                P.barrier()

            with contextlib.ExitStack() as st:
                yrow = [sbt(st, "yrow%d" % i, [128, 512], BF16) for i in range(2)]
                pY = [pst(st, "pY%d" % i, [128, 4, 128], BF16) for i in range(2)]
                n = 0
                for cc in range(4):
                    for iq in range(8):
                        for ii in range(4):
                            P.tr(pY[n % 2][:, ii, :], yall[:, cc * 128:(cc + 1) * 128, iq * 4 + ii], ident[:], sig=(ii == 3))
                        r = yrow[n % 2]
                        P.cp("act", r[:], pY[n % 2][:].rearrange("p a b -> p (a b)"))
                        P.ld("sp", yhT_s.ap()[cc][:, iq * 512:(iq + 1) * 512], r[:])
                        n += 1
                P.barrier()

        def stream_weight_bf16(wt, src_d, nk, ncol, wfs):
            n = 0
            for kc in range(nk):
                for c0 in range(0, ncol, 1024):
                    s_ = wfs[n % 2]
                    n += 1
                    P.ld("sp" if n % 2 == 0 else "act", s_[:], src_d.ap()[kc * 128:(kc + 1) * 128, c0:c0 + 1024])
                    P.cp("dve" if n % 2 == 0 else "act", wt[:, kc, c0:c0 + 1024], s_[:])
            return wt

        def residual_ln(pfx, st_tiles, ps_lo, ps_hi, gbc_mod, res_src_ap, lg, lb, dst):
            rs, r, stats, mv, rstd, tmp = st_tiles
            P.ld("sp", rs[:], res_src_ap)
            P.tt("dve", r[:, 0:512], ps_lo[:], gbc_mod[:, 0:512], ALU.mult)
            P.tt("dve", r[:, 512:1024], ps_hi[:], gbc_mod[:, 512:1024], ALU.mult)
            P.stt("dve", r[:], rs[:], ALPHA, r[:], ALU.mult, ALU.add)
            layer_norm_tile(pfx, r, lg, lb, dst, stats, mv, rstd, tmp)

        with contextlib.ExitStack() as stF:
            x1mT = sbt(stF, "x1mT", [128, 8, L], BF16)
            with contextlib.ExitStack() as st:
                who = sbt(st, "who", [128, 4, D], BF16)
                wao = sbt(st, "wao", [128, 4, D], BF16)
                wo = sbt(st, "wo", [128, 8, D], BF16)
                with contextlib.ExitStack() as stw:
                    wfs = [sbt(stw, "wfs%d" % i, [128, 1024], F32) for i in range(2)]
                    stream_weight_bf16(who, who_d, 4, D, wfs)
                    stream_weight_bf16(wao, wao_d, 4, D, wfs)
                    stream_weight_bf16(wo, wout_d, 8, D, wfs)
                    P.barrier()
                l1g = sbt(st, "l1g", [128, D], F32); l1b = sbt(st, "l1b", [128, D], F32)
                P.ld("sp", l1g[:], l1g_d.ap().partition_broadcast(128))
                P.ld("sp", l1b[:], l1b_d.ap().partition_broadcast(128))
                yh = [sbt(st, "yh%d" % i, [128, 4, 512], BF16) for i in range(2)]
                ot = [sbt(st, "ot%d" % i, [128, 4, 512], BF16) for i in range(2)]
                gt = [sbt(st, "gt%d" % i, [128, 16, 512], BF16) for i in range(2)]
                mT = sbt(st, "mT", [128, 8, 512], BF16)
                m1 = sbt(st, "m1", [128, 512], F32); m2 = sbt(st, "m2", [128, 512], F32)
                lnt = [(sbt(st, "rsF%d" % i, [128, D], F32), sbt(st, "rF%d" % i, [128, D], F32),
                        sbt(st, "statsF%d" % i, [128, 2, 6], F32), sbt(st, "mvF%d" % i, [128, 2], F32),
                        sbt(st, "rstdF%d" % i, [128, 1], F32), sbt(st, "tmpF%d" % i, [128, D], F32)) for i in range(2)]
                x1t = [sbt(st, "x1t0", [128, D], F32)] * 2
                x1bs = [sbt(st, "x1b%d" % i, [128, D], BF16) for i in range(2)]
                pA = [pst(st, "pA%d" % i, [128, 512]) for i in range(2)]
                pB2 = [pst(st, "pB2%d" % i, [128, 512]) for i in range(2)]
                pYl = [pst(st, "pYl%d" % i, [128, 512]) for i in range(2)]
                pT2 = pst(st, "pT2", [128, 8, 128], BF16)

                def load_T(T):
                    i = T % 2
                    sl = slice(T * 512, (T + 1) * 512)
                    P.ld("sp", yh[i][:], yhT_s.ap()[:, :, sl].rearrange("c p t -> p c t"))
                    P.ld("act", ot[i][:], oT_s.ap()[:, :, sl].rearrange("c p t -> p c t"))
                    P.ld("sp", gt[i][:], gat_s.ap()[:, :, sl].rearrange("g p t -> p g t"))

                def transposes(x1b, tok0):
                    for kc in range(8):
                        P.tr(pT2[:, kc, :], x1b[:, kc * 128:(kc + 1) * 128], ident[:], sig=(kc == 7))
                    for kc in range(8):
                        P.act(x1mT[:, kc, tok0:tok0 + 128], pT2[:, kc, :], AF.Identity,
                              bias=modT[:, 24 + kc, 0:1], scale=onep[:, 8 + kc, 0:1])

                nt = 0
                prev = None
                load_T(0)
                for T in range(8):
                    i = T % 2
                    if T + 1 < 8:
                        load_T(T + 1)
                    for fc in range(8):
                        a = pA[fc % 2]; b = pB2[fc % 2]
                        for cc in range(4):
                            P.mm(a[:], who[:, cc, fc * 128:(fc + 1) * 128], yh[i][:, cc, :], start=(cc == 0), stop=(cc == 3),
                                 sig=(cc == 3))
                        for cc in range(4):
                            P.mm(b[:], wao[:, cc, fc * 128:(fc + 1) * 128], ot[i][:, cc, :], start=(cc == 0), stop=(cc == 3),
                                 sig=(cc == 3))
                        P.tt("dve", m1[:], a[:], gt[i][:, fc, :], ALU.mult)
                        P.tt("dve", m2[:], b[:], gt[i][:, 8 + fc, :], ALU.mult)
                        P.tt("pool", mT[:, fc, :], m1[:], m2[:], ALU.add)
                    for tb in range(4):
                        tok0 = T * 512 + tb * 128
                        for hf in range(2):
                            for kc in range(8):
                                P.mm(pYl[hf][:], mT[:, kc, tb * 128:(tb + 1) * 128], wo[:, kc, hf * 512:(hf + 1) * 512],
                                     start=(kc == 0), stop=(kc == 7), sig=(kc == 7))
                        if prev is not None:
                            transposes(*prev)
                        xo = x1t[nt % 2]
                        x1b = x1bs[nt % 2]
                        lnt_ = lnt[nt % 2]
                        nt += 1
                        residual_ln("F", lnt_, pYl[0], pYl[1], g1bc,
                                    xln_s.ap()[tok0:tok0 + 128, :], l1g, l1b, xo)
                        P.ld("pool", x1_s.ap()[tok0:tok0 + 128, :], xo[:])
                        P.cp("act", x1b[:], xo[:])
                        prev = (x1b, tok0)
                transposes(*prev)
                P.barrier()

            with contextlib.ExitStack() as st:
                wuf = [sbt(st, "wuf%d" % i, [128, 8, 128], F32) for i in range(2)]
                wub = [sbt(st, "wub%d" % i, [128, 8, 128], BF16) for i in range(4)]
                g_items = []
                for fc_ in range(NFC):
                    for part_ in range(2):
                        g_items.append(part_ * NFC + fc_)
                gctr = {"issued": 0}

                def g_issue():
                    i = gctr["issued"]
                    if i >= len(g_items):
                        return
                    gctr["issued"] += 1
                    ch_ = g_items[i]
                    P.ld("sp", wuf[i % 2][:], wup_d.ap()[:, ch_ * 128:(ch_ + 1) * 128].rearrange("(kc p) c -> p kc c", p=128))
                    P.cp("pool", wub[i % 4][:], wuf[i % 2][:])
                zr2 = [sbt(st, "zr2_%d" % i, [128, L + 2], F32) for i in range(2)]
                tgs = [sbt(st, "tg%d" % i, [128, L], F32) for i in range(2)]
                tas = [sbt(st, "ta_%d" % i, [128, L], F32) for i in range(2)]
                hrow = [sbt(st, "hrow%d" % i, [128, L], BF16) for i in range(2)]
                fcw = sbt(st, "fcw", [128, 44, 3], F32); fcb = sbt(st, "fcb", [128, 44], F32)
                pG = [pst(st, "pG%d" % i, [128, 512]) for i in range(4)]
                P.ld("sp", fcw[:], fcw_d.ap()); P.ld("sp", fcb[:], fcb_d.ap())
                for i in range(2):
                    P.memset("pool", zr2[i][:, 0:1], 0.0)
                    P.memset("pool", zr2[i][:, L + 1:L + 2], 0.0)
                nw = 0
                npg = 0
                for fc in range(NFC):
                    tg = tgs[fc % 2]
                    ta_ = tas[fc % 2]
                    for part in range(2):
                        ch = part * NFC + fc
                        while gctr["issued"] <= min(nw + 2, len(g_items) - 1):
                            g_issue()
                        wbl = wub[nw % 4]
                        nw += 1
                        z = zr2[part]
                        for T in range(8):
                            ps = pG[npg % 4]
                            npg += 1
                            for kc in range(8):
                                P.mm(ps[:], wbl[:, kc, :], x1mT[:, kc, T * 512:(T + 1) * 512], start=(kc == 0), stop=(kc == 7),
                                     sig=(kc == 7))
                            P.cp("act", z[:, 1 + T * 512:1 + (T + 1) * 512], ps[:])
                        dst = ta_ if part == 0 else tg
                        P.ts("pool", dst[:], z[:, 0:L], fcw[:, ch, 0:1], fcb[:, ch:ch + 1], ALU.mult, ALU.add)
                        P.stt("dve", dst[:], z[:, 1:L + 1], fcw[:, ch, 1:2], dst[:], ALU.mult, ALU.add)
                        P.stt("dve", dst[:], z[:, 2:L + 2], fcw[:, ch, 2:3], dst[:], ALU.mult, ALU.add)
                    P.act(tg[:], tg[:], AF.Silu)
                    hr = hrow[fc % 2]
                    P.tt("dve", hr[:], tg[:], ta_[:], ALU.mult)
                    if fc > 0:
                        P.ld("sp", hT_s.ap()[fc - 1], hrow[(fc - 1) % 2][:])
                P.ld("sp", hT_s.ap()[NFC - 1], hrow[(NFC - 1) % 2][:])
                P.barrier()

        with contextlib.ExitStack() as st:
            wfs = [sbt(st, "wfsH%d" % i, [128, 1024], F32) for i in range(2)]
            wd = sbt(st, "wd", [128, NFC, D], BF16)
            stream_weight_bf16(wd, wdn_d, NFC, D, wfs)
            l2g = sbt(st, "l2g", [128, D], F32); l2b = sbt(st, "l2b", [128, D], F32)
            P.ld("sp", l2g[:], l2g_d.ap().partition_broadcast(128))
            P.ld("sp", l2b[:], l2b_d.ap().partition_broadcast(128))
            ht = [sbt(st, "ht%d" % i, [128, NFC, 512], BF16) for i in range(2)]
            lnt = [(sbt(st, "rsH%d" % i, [128, D], F32), sbt(st, "rH%d" % i, [128, D], F32),
                    sbt(st, "statsH%d" % i, [128, 2, 6], F32), sbt(st, "mvH%d" % i, [128, 2], F32),
                    sbt(st, "rstdH%d" % i, [128, 1], F32), sbt(st, "tmpH%d" % i, [128, D], F32)) for i in range(2)]
            xo = [sbt(st, "xoH%d" % i, [128, D], F32) for i in range(2)]
            pD = [[pst(st, "pD%d_%d" % (s, i), [128, 512]) for i in range(2)] for s in range(2)]
            nt = 0

            def load_ht(T):
                P.ld("sp" if T % 2 == 0 else "act", ht[T % 2][:],
                     hT_s.ap()[:, :, T * 512:(T + 1) * 512].rearrange("f p t -> p f t"))

            load_ht(0)
            for T in range(8):
                i = T % 2
                if T + 1 < 8:
                    load_ht(T + 1)
                for tb in range(4):
                    tok0 = T * 512 + tb * 128
                    pp = pD[nt % 2]
                    for hf in range(2):
                        for fc in range(NFC):
                            P.mm(pp[hf][:], ht[i][:, fc, tb * 128:(tb + 1) * 128], wd[:, fc, hf * 512:(hf + 1) * 512],
                                 start=(fc == 0), stop=(fc == NFC - 1), sig=(fc == NFC - 1))
                    o_ = xo[nt % 2]
                    lnt_ = lnt[nt % 2]
                    nt += 1
                    residual_ln("H", lnt_, pp[0], pp[1], g2bc,
                                x1_s.ap()[tok0:tok0 + 128, :], l2g, l2b, o_)
                    P.ld("pool", out_d.ap()[tok0:tok0 + 128, :], o_[:])
            P.barrier()

        P.run()
    return nc


def host_constants():
    f32 = np.float32
    rows = L // 64
    row = np.repeat(np.arange(rows, dtype=f32), 64)
    col = np.tile(np.arange(64, dtype=f32), rows)
    inv = (10000.0 ** (-np.arange(0, 32, 2, dtype=f32) / 32.0)).astype(f32)
    cosT = np.zeros((128, L), f32)
    sinT = np.zeros((128, L), f32)
    for p in range(128):
        d = p % 64
        a = d // 32
        i = d % 32
        f = i % 16
        ang = (row if a == 0 else col) * inv[f]
        cosT[p] = np.cos(ang)
        sinT[p] = (-np.sin(ang)) if i < 16 else np.sin(ang)
    t = np.linspace(0.0, 1.0, L, dtype=f32)[:, None]
    w = (f32(2.0 * math.pi / L) * np.arange(L, dtype=f32))[:, None]
    f = np.linspace(1e-4, 15, 16, dtype=f32)[None, :]
    z = np.concatenate([t, np.cos(f * w), -np.sin(f * w)], -1).astype(f32)
    min_decay = math.log(1e-2) / 1.5
    max_decay = math.log(1e-2) / 0.3
    deltas = np.linspace(min_decay, max_decay, HYW, dtype=f32)
    decay = np.exp(-t * np.abs(deltas)[None, :]).astype(f32)
    return dict(
        rope_cos=cosT, rope_sin=sinT,
        zposT_f=np.ascontiguousarray(z.T), zposT_r=np.ascontiguousarray(z[::-1].T),
        decayT_f=np.ascontiguousarray(decay.T), decayT_r=np.ascontiguousarray(decay[::-1].T),
    )


def per_part(v, nch):
    return np.ascontiguousarray(np.asarray(v, np.float32).reshape(nch, 128).T)


def make_in_maps(inp):
    f32 = np.float32
    g = {k: np.asarray(v, f32) for k, v in inp.items()}
    const = host_constants()
    w_in = g["w_in"][0]
    perm = np.zeros(1024, np.int64)
    for cidx in range(1024):
        base = 1536 + cidx
        d = cidx % 64
        i = d % 32
        perm[cidx] = base + 16 if i < 16 else base - 16
    w_qkp = np.ascontiguousarray(w_in[:, perm])
    cw = g["hy_conv_w"][0]
    hy_cw = np.ascontiguousarray(np.stack([per_part(cw[k], 12) for k in range(3)], -1))
    fw = g["ffn_conv_w"][0]
    ffn_cw = np.ascontiguousarray(np.stack([per_part(fw[k], 44) for k in range(3)], -1))
    shared = dict(
        ln_in_g=g["ln_in_g"], ln_in_b=g["ln_in_b"],
        w_ada=g["w_ada"][0], b_ada=g["b_ada"][0], b_adaT=per_part(g["b_ada"][0], 48),
        w_in=w_in, w_qkp=w_qkp,
        hy_cw=hy_cw, hy_cb=per_part(g["hy_conv_b"][0], 12),
        f_w1=g["hy_f_w1"][0], f_w2=g["hy_f_w2"][0], f_w3=g["hy_f_w3"][0], f_wout=g["hy_f_wout"][0],
        f_freqT=np.ascontiguousarray(g["hy_f_freq"][0].T),
        f_bT=np.ascontiguousarray(np.stack([g["hy_f_b1"][0], g["hy_f_b2"][0], g["hy_f_b3"][0]], -1)),
        hy_bias=g["hy_bias"][0],
        lamv=np.ascontiguousarray(np.stack([g["lam_q1"][0], g["lam_k1"][0], g["lam_q2"][0], g["lam_k2"][0]], 0)),
        subln_g=np.ascontiguousarray(g["at_subln_g"][0].reshape(128, 1)),
        w_hy_o=g["w_hy_o"][0], w_at_o=g["w_at_o"][0], w_out=g["w_out"][0],
        ln1_g=g["ln1_g"][0], ln1_b=g["ln1_b"][0], ln2_g=g["ln2_g"][0], ln2_b=g["ln2_b"][0],
        ffn_w_up=g["ffn_w_up"][0], ffn_cw=ffn_cw, ffn_cb=per_part(g["ffn_conv_b"][0], 44),
        ffn_w_down=g["ffn_w_down"][0],
    )
    shared.update(const)
    maps = []
    cc = per_part(g["c_ctx"], 8)
    for b in range(8):
        m = dict(shared)
        m["x"] = np.ascontiguousarray(g["x"][b])
        m["ctx"] = np.ascontiguousarray(g["ctx"][b])
        m["cT"] = np.ascontiguousarray(np.stack([per_part(g["c"][b], 8), cc], -1))
        maps.append(m)
    return maps


_NC = None


def kernel(**inputs):
    global _NC
    if _NC is None:
        _NC = build_program()
    maps = make_in_maps(inputs)
    res = run_bass_kernel_spmd(_NC, maps, core_ids=list(range(8)))
    out = np.stack([np.asarray(r["out"], np.float32) for r in res.results], 0)
    return out
```

```python
import math
import contextlib
import numpy as np
import concourse.bass as bass
import concourse.mybir as mybir
from concourse.bass_utils import run_bass_kernel_spmd

F32 = mybir.dt.float32
BF16 = mybir.dt.bfloat16
ALU = mybir.AluOpType
AF = mybir.ActivationFunctionType
AX = mybir.AxisListType

ENGS = ("pe", "act", "dve", "pool", "sp")

D = 1024
L = 4096
CTX = 256
NTT = 34
LK = CTX + L
HYW = 512
DFF = 2816
NFC = DFF // 128
ALPHA = 2.0 ** 0.25
LAM_INIT = 0.2
PI = math.pi
DEBUG = False


class Prog:
    NDMA = 6

    def __init__(self, nc):
        self.nc = nc
        self.ops = {e: [] for e in ENGS}
        self.cnt = {e: 0 for e in ENGS}
        self.waited = {e: {} for e in ENGS}
        self.last_w = {}
        self.reads = {}
        self.dma_uses = {}
        self.dma_rr = {e: 0 for e in ENGS}
        self.semkeys = set()
        self.sems = {}

    def _deps(self, eng, reads, writes):
        deps = []
        for k in reads:
            if k in self.last_w:
                deps.append(self.last_w[k])
        for k in writes:
            if k in self.last_w:
                deps.append(self.last_w[k])
            for ev in self.reads.get(k, ()):
                if ev[0] == eng:
                    continue
                deps.append(ev)
        best = {}
        for sk, v in deps:
            if eng == "pe" and sk == "pe":
                continue
            if v > best.get(sk, 0):
                best[sk] = v
        out = []
        w = self.waited[eng]
        for sk, v in best.items():
            if w.get(sk, 0) >= v:
                continue
            w[sk] = v
            out.append((sk, v))
        return out

    def _commit(self, ev, reads, writes):
        for k in reads:
            self.reads.setdefault(k, []).append(ev)
        for k in writes:
            self.last_w[k] = ev
            self.reads[k] = []

    def op(self, eng, fn, reads=(), writes=(), sig=True):
        waits = self._deps(eng, reads, writes)
        if sig:
            self.cnt[eng] += 1
            ev = (eng, self.cnt[eng])
            self.semkeys.add(eng)
            self.ops[eng].append(("op", fn, waits, eng))
        else:
            ev = (eng, self.cnt[eng] + 1)
            self.ops[eng].append(("op", fn, waits, None))
        self._commit(ev, reads, writes)
        return ev

    def dma(self, q, fn, reads=(), writes=()):
        slot = self.dma_rr[q] % self.NDMA
        self.dma_rr[q] += 1
        sk = ("dma", q, slot)
        self.semkeys.add(sk)
        uses = self.dma_uses.get(sk, 0)
        waits = self._deps(q, reads, writes)
        if uses > 0 and self.waited[q].get(sk, 0) < 16 * uses:
            self.waited[q][sk] = 16 * uses
            waits.append((sk, 16 * uses))
        uses += 1
        self.dma_uses[sk] = uses
        ev = (sk, 16 * uses)
        self.ops[q].append(("dma", fn, waits, sk))
        self._commit(ev, reads, writes)
        return ev

    def barrier(self):
        evs = []
        for e in ENGS:
            if self.cnt[e] > 0:
                evs.append((e, self.cnt[e]))
        for sk, uses in self.dma_uses.items():
            evs.append((sk, 16 * uses))
        for e in ENGS:
            waits = []
            for sk, v in evs:
                if sk == e:
                    continue
                if self.waited[e].get(sk, 0) >= v:
                    continue
                self.waited[e][sk] = v
                waits.append((sk, v))
            if waits:
                self.ops[e].append(("wait", None, waits, None))
        self.last_w = {}
        self.reads = {}

    def run(self):
        nc = self.nc
        with contextlib.ExitStack() as st:
            for sk in sorted(self.semkeys, key=str):
                name = sk if isinstance(sk, str) else "d_%s_%d" % (sk[1], sk[2])
                self.sems[sk] = st.enter_context(nc.semaphore("s_" + name))
            block = st.enter_context(nc.Block())

            def mk(ename):
                def body(e):
                    for kind, fn, waits, sk in self.ops[ename]:
                        for wk, wv in waits:
                            e.wait_ge(self.sems[wk], wv)
                        if kind == "wait":
                            continue
                        ins = fn(e)
                        if sk is not None:
                            ins.then_inc(self.sems[sk], 16 if kind == "dma" else 1)
                return body

            block.tensor(mk("pe"))
            block.scalar(mk("act"))
            block.vector(mk("dve"))
            block.gpsimd(mk("pool"))
            block.sync(mk("sp"))

    @staticmethod
    def _k(*aps):
        out = []
        for a in aps:
            if a is None or isinstance(a, (int, float)):
                continue
            out.append(a.name)
        return out

    def mm(self, out, lhsT, rhs, start=True, stop=True, sig=True):
        return self.op("pe", lambda e: e.matmul(out, lhsT, rhs, start=start, stop=stop),
                       reads=self._k(lhsT, rhs), writes=self._k(out), sig=sig)

    def tr(self, out, in_, ident, sig=True):
        return self.op("pe", lambda e: e.transpose(out, in_, ident),
                       reads=self._k(in_, ident), writes=self._k(out), sig=sig)

    def act(self, out, in_, func, bias=None, scale=None, accum_out=None):
        kw = {}
        if bias is not None:
            kw["bias"] = bias
        if scale is not None:
            kw["scale"] = scale
        if accum_out is not None:
            kw["accum_out"] = accum_out
        return self.op("act", lambda e: e.activation(out=out, in_=in_, func=func, **kw),
                       reads=self._k(in_, bias, scale), writes=self._k(out, accum_out))

    def tt(self, eng, out, in0, in1, op):
        return self.op(eng, lambda e: e.tensor_tensor(out=out, in0=in0, in1=in1, op=op),
                       reads=self._k(in0, in1), writes=self._k(out))

    def ts(self, eng, out, in0, s1, s2, op0, op1=None):
        if op1 is None:
            return self.op(eng, lambda e: e.tensor_scalar(out=out, in0=in0, scalar1=s1, scalar2=None, op0=op0),
                           reads=self._k(in0, s1), writes=self._k(out))
        return self.op(eng, lambda e: e.tensor_scalar(out=out, in0=in0, scalar1=s1, scalar2=s2, op0=op0, op1=op1),
                       reads=self._k(in0, s1, s2), writes=self._k(out))

    def stt(self, eng, out, in0, scalar, in1, op0, op1):
        return self.op(eng, lambda e: e.scalar_tensor_tensor(out=out, in0=in0, scalar=scalar, in1=in1, op0=op0, op1=op1),
                       reads=self._k(in0, scalar, in1), writes=self._k(out))

    def cp(self, eng, out, in_):
        if eng == "act":
            return self.act(out, in_, AF.Copy)
        return self.op(eng, lambda e: e.tensor_copy(out=out, in_=in_), reads=self._k(in_), writes=self._k(out))

    def memset(self, eng, ap, val):
        return self.op(eng, lambda e: e.memset(ap, val), writes=self._k(ap))

    def ld(self, q, out, in_):
        return self.dma(q, lambda e: e.dma_start(out=out, in_=in_), reads=self._k(in_), writes=self._k(out))


def build_program():
    nc = bass.Bass("TRN2", target_bir_lowering=False)
    P = Prog(nc)

    def din(name, shape, dt=F32):
        return nc.dram_tensor(name, list(shape), dt, kind="ExternalInput")

    def dscr(name, shape, dt, dbg=False):
        if DEBUG and dbg:
            return nc.dram_tensor(name, list(shape), dt, kind="ExternalOutput")
        return nc.dram_tensor(name, list(shape), dt)

    x_d = din("x", [L, D]); ctx_d = din("ctx", [CTX, D])
    cT_d = din("cT", [128, 8, 2])
    lng_d = din("ln_in_g", [D]); lnb_d = din("ln_in_b", [D])
    wada_d = din("w_ada", [D, 6 * D]); bada_d = din("b_ada", [6 * D]); badaT_d = din("b_adaT", [128, 48])
    win_d = din("w_in", [D, 5120]); wqkp_d = din("w_qkp", [D, 1024])
    cos_d = din("rope_cos", [128, L]); sin_d = din("rope_sin", [128, L])
    hcw_d = din("hy_cw", [128, 12, 3]); hcb_d = din("hy_cb", [128, 12])
    fw1_d = din("f_w1", [33, 64]); fw2_d = din("f_w2", [64, 64]); fw3_d = din("f_w3", [64, 64])
    fwo_d = din("f_wout", [64, 2048]); ffr_d = din("f_freqT", [64, 3]); fb_d = din("f_bT", [64, 3])
    zpf_d = din("zposT_f", [33, L]); zpr_d = din("zposT_r", [33, L])
    dcf_d = din("decayT_f", [HYW, L]); dcr_d = din("decayT_r", [HYW, L])
    hyb_d = din("hy_bias", [2, HYW])
    lam_d = din("lamv", [4, 64])
    sub_d = din("subln_g", [128, 1])
    who_d = din("w_hy_o", [HYW, D]); wao_d = din("w_at_o", [512, D]); wout_d = din("w_out", [D, D])
    l1g_d = din("ln1_g", [D]); l1b_d = din("ln1_b", [D]); l2g_d = din("ln2_g", [D]); l2b_d = din("ln2_b", [D])
    wup_d = din("ffn_w_up", [D, 2 * DFF]); fcw_d = din("ffn_cw", [128, 44, 3]); fcb_d = din("ffn_cb", [128, 44])
    wdn_d = din("ffn_w_down", [DFF, D])
    out_d = nc.dram_tensor("out", [L, D], F32, kind="ExternalOutput")

    xln_s = dscr("xln_s", [L, D], F32, True)
    qT_s = dscr("qT_s", [4, 128, L], BF16, True)
    kT_s = dscr("kT_s", [4, 128, LK], BF16, True)
    V_s = dscr("V_s", [NTT, 128, 512], BF16, True)
    hyu_s = dscr("hyu_s", [128, 1536 * 32], BF16, True)
    gat_s = dscr("gat_s", [16, 128, L], BF16, True)
    A_s = dscr("A_s", [2 * HYW, 8192], BF16, True)
    oT_s = dscr("oT_s", [4, 128, L], BF16, True)
    yhT_s = dscr("yhT_s", [4, 128, L], BF16, True)
    x1_s = dscr("x1_s", [L, D], F32, True)
    hT_s = dscr("hT_s", [NFC, 128, L], BF16, True)

    top = contextlib.ExitStack()

    def sbt(st, name, shape, dt):
        return st.enter_context(nc.sbuf_tensor("sb_" + name, list(shape), dt))

    def pst(st, name, shape, dt=F32):
        return st.enter_context(nc.psum_tensor("ps_" + name, list(shape), dt))

    with top:
        ident = sbt(top, "ident", [128, 128], BF16)
        jrev = sbt(top, "jrev", [128, 128], BF16)
        identf = sbt(top, "identf", [128, 128], F32)
        epsb = sbt(top, "epsb", [128, 1], F32)
        npib = sbt(top, "npib", [128, 1], F32)
        modT = sbt(top, "modT", [128, 48, 2], F32)
        onep = sbt(top, "onep", [128, 16, 2], F32)
        g1bc = sbt(top, "g1bc", [128, D], F32)
        g2bc = sbt(top, "g2bc", [128, D], F32)
        neglam = sbt(top, "neglam", [128, 1], F32)
        subg = sbt(top, "subg", [128, 1], F32)

        P.memset("pool", identf[:], 0.0)
        P.op("pool", lambda e: e.affine_select(out=identf[:], in_=identf[:], pattern=[[-1, 128]],
                                               compare_op=ALU.not_equal, fill=1.0, base=0, channel_multiplier=1),
             reads=[identf.name], writes=[identf.name])
        P.cp("dve", ident[:], identf[:])
        P.memset("pool", identf[:], 0.0)
        P.op("pool", lambda e: e.affine_select(out=identf[:], in_=identf[:], pattern=[[1, 128]],
                                               compare_op=ALU.not_equal, fill=1.0, base=-127, channel_multiplier=1),
             reads=[identf.name], writes=[identf.name])
        P.cp("dve", jrev[:], identf[:])
        P.memset("pool", epsb[:], 1e-5)
        P.memset("pool", npib[:], -PI)

        with contextlib.ExitStack() as st:
            cT = sbt(st, "cT", [128, 8, 2], F32)
            sc = sbt(st, "sc", [128, 8, 2], F32)
            scb = sbt(st, "scb", [128, 8, 128], F32)
            wa = [sbt(st, "wa%d" % i, [128, 8, 512], F32) for i in range(4)]
            badaT = sbt(st, "badaT", [128, 48], F32)
            bg = sbt(st, "bg", [128, D], F32)
            lamv = sbt(st, "lamv_t", [128, 4, 64], F32)
            lamp = sbt(st, "lamp", [128, 2, 64], F32)
            lams = sbt(st, "lams", [128, 2], F32)
            pm = pst(st, "pm", [128, 48, 2])
            pb = pst(st, "pb", [128, 512])
            P.ld("sp", cT[:], cT_d.ap())
            P.ld("sp", badaT[:], badaT_d.ap())
            P.ld("sp", subg[:], sub_d.ap())
            P.ld("sp", lamv[:].rearrange("p a b -> p (a b)"),
                 lam_d.ap().rearrange("a b -> (a b)").partition_broadcast(128))
            P.act(sc[:], cT[:], AF.Silu)
            P.cp("dve", scb[:], sc[:, :, 0:1].to_broadcast([128, 8, 128]))
            P.ts("dve", subg[:], subg[:], 1.0 - LAM_INIT, None, ALU.mult)
            P.tt("dve", lamp[:, 0, :], lamv[:, 0, :], lamv[:, 1, :], ALU.mult)
            P.tt("dve", lamp[:, 1, :], lamv[:, 2, :], lamv[:, 3, :], ALU.mult)
            P.op("dve", lambda e: e.reduce_sum(out=lams[:], in_=lamp[:], axis=AX.X), reads=[lamp.name], writes=[lams.name])
            P.act(lams[:], lams[:], AF.Exp)
            P.tt("dve", neglam[:], lams[:, 1:2], lams[:, 0:1], ALU.subtract)
            P.ts("dve", neglam[:], neglam[:], -LAM_INIT, None, ALU.add)
            for g in range(12):
                w = wa[g % 4]
                P.ld("sp" if g % 2 == 0 else "act", w[:],
                     wada_d.ap()[:, g * 512:(g + 1) * 512].rearrange("(kc p) c -> p kc c", p=128))
                for jj in range(4):
                    j = g * 4 + jj
                    for kc in range(8):
                        P.mm(pm[:, j, :], w[:, kc, jj * 128:(jj + 1) * 128], sc[:, kc, :],
                             start=(kc == 0), stop=(kc == 7), sig=(kc == 7))
                if g in (4, 5, 10, 11):
                    for kc in range(8):
                        P.mm(pb[:], scb[:, kc, :], w[:, kc, :], start=(kc == 0), stop=(kc == 7), sig=(kc == 7))
                    dst = g1bc if g in (4, 5) else g2bc
                    half = g % 2
                    P.ld("sp", bg[:, 0:512], bada_d.ap()[g * 512:(g + 1) * 512].partition_broadcast(128))
                    P.tt("dve", dst[:, half * 512:(half + 1) * 512], pb[:], bg[:, 0:512], ALU.add)
            P.tt("dve", modT[:], pm[:], badaT[:].unsqueeze(2).to_broadcast([128, 48, 2]), ALU.add)
            P.ts("dve", onep[:, 0:8, :], modT[:, 8:16, :], 1.0, None, ALU.add)
            P.ts("dve", onep[:, 8:16, :], modT[:, 32:40, :], 1.0, None, ALU.add)
            P.barrier()

        def layer_norm_tile(pfx, src, gbc, bbc, dst, stats, mv, rstd, tmp):
            for h in range(2):
                P.op("dve", (lambda e, h=h: e.bn_stats(out=stats[:, h, :], in_=src[:, h * 512:(h + 1) * 512])),
                     reads=[src.name], writes=[stats.name])
            P.op("dve", lambda e: e.bn_aggr(out=mv[:], in_=stats[:].rearrange("p a b -> p (a b)")),
                 reads=[stats.name], writes=[mv.name])
            P.act(rstd[:], mv[:, 1:2], AF.Sqrt, bias=epsb[:], scale=1.0)
            P.op("dve", lambda e: e.reciprocal(out=rstd[:], in_=rstd[:]), reads=[rstd.name], writes=[rstd.name])
            P.ts("dve", tmp[:], src[:], mv[:, 0:1], rstd[:], ALU.subtract, ALU.mult)
            P.tt("pool", tmp[:], tmp[:], gbc[:], ALU.mult)
            P.tt("pool", dst[:], tmp[:], bbc[:], ALU.add)

        with contextlib.ExitStack() as stA:
            xmT = sbt(stA, "xmT", [128, 8, L], BF16)
            xcT = sbt(stA, "xcT", [128, 8, CTX], BF16)
            with contextlib.ExitStack() as st:
                gbc = sbt(st, "gbc", [128, D], F32)
                bbc = sbt(st, "bbc", [128, D], F32)
                xt = [sbt(st, "xt%d" % i, [128, D], F32) for i in range(3)]
                xh = [sbt(st, "xh%d" % i, [128, D], F32) for i in range(3)]
                xl = [sbt(st, "xl%d" % i, [128, D], F32) for i in range(3)]
                xb = [sbt(st, "xb%d" % i, [128, D], BF16) for i in range(2)]
                stats = [sbt(st, "stats%d" % i, [128, 2, 6], F32) for i in range(3)]
                mv = [sbt(st, "mv%d" % i, [128, 2], F32) for i in range(3)]
                rstd = [sbt(st, "rstd%d" % i, [128, 1], F32) for i in range(3)]
                pT = [pst(st, "pT%d" % i, [128, 8, 128], BF16) for i in range(2)]
                P.ld("sp", gbc[:], lng_d.ap().partition_broadcast(128))
                P.ld("sp", bbc[:], lnb_d.ap().partition_broadcast(128))
                def phase2(t):
                    i = t % 3
                    is_ctx = t < 2
                    if not is_ctx:
                        P.ld("pool", xln_s.ap()[(t - 2) * 128:(t - 1) * 128, :], xl[i][:])
                    P.cp("act", xb[t % 2][:], xl[i][:])
                    for kc in range(8):
                        P.tr(pT[t % 2][:, kc, :], xb[t % 2][:, kc * 128:(kc + 1) * 128], ident[:], sig=(kc == 7))
                    w = 1 if is_ctx else 0
                    for kc in range(8):
                        dst = xcT[:, kc, t * 128:(t + 1) * 128] if is_ctx else xmT[:, kc, (t - 2) * 128:(t - 1) * 128]
                        P.act(dst, pT[t % 2][:, kc, :], AF.Identity, bias=modT[:, kc, w:w + 1], scale=onep[:, kc, w:w + 1])

                for t in range(NTT):
                    i = t % 3
                    is_ctx = t < 2
                    src = ctx_d.ap()[t * 128:(t + 1) * 128, :] if is_ctx else x_d.ap()[(t - 2) * 128:(t - 1) * 128, :]
                    P.ld("sp", xt[i][:], src)
                    layer_norm_tile("A", xt[i], gbc, bbc, xl[i], stats[i], mv[i], rstd[i], xh[i])
                    if t > 0:
                        phase2(t - 1)
                phase2(NTT - 1)
                P.barrier()

            wf = [sbt(stA, "wf%d" % i, [128, 8, 128], F32) for i in range(2)]
            wb = [sbt(stA, "wb%d" % i, [128, 8, 128], BF16) for i in range(4)]
            psB = [pst(stA, "psB%d" % i, [128, 512]) for i in range(4)]
            ctr = {"w": 0, "ps": 0}

            w_items = []
            for which_ in range(2):
                for h_ in range(4):
                    w_items.append((win_d, 1536 + which_ * 512 + h_ * 128))
                    w_items.append((wqkp_d, which_ * 512 + h_ * 128))
            for cc_ in range(12):
                w_items.append((win_d, cc_ * 128))
            for gc_ in range(16):
                w_items.append((win_d, 3072 + gc_ * 128))
            for c4_ in range(4):
                w_items.append((win_d, 2560 + c4_ * 128))
            ctr["issued"] = 0

            def _issue_w():
                i = ctr["issued"]
                if i >= len(w_items):
                    return
                ctr["issued"] += 1
                src_d, col0 = w_items[i]
                P.ld("sp", wf[i % 2][:], src_d.ap()[:, col0:col0 + 128].rearrange("(kc p) c -> p kc c", p=128))
                P.cp("pool", wb[i % 4][:], wf[i % 2][:])

            def load_w(src_d, col0):
                i = ctr["w"]
                ctr["w"] += 1
                assert w_items[i][1] == col0 and w_items[i][0] is src_d
                while ctr["issued"] <= min(i + 2, len(w_items) - 1):
                    _issue_w()
                return wb[i % 4]

            def proj_fm(wt, T):
                ps = psB[ctr["ps"] % 4]
                ctr["ps"] += 1
                for kc in range(8):
                    P.mm(ps[:], wt[:, kc, :], xmT[:, kc, T * 512:(T + 1) * 512], start=(kc == 0), stop=(kc == 7),
                         sig=(kc == 7))
                return ps

            with contextlib.ExitStack() as st:
                cosT = sbt(st, "cosT", [128, L], F32)
                sinT = sbt(st, "sinT", [128, L], F32)
                ra = [sbt(st, "ra%d" % i, [128, 512], F32) for i in range(2)]
                rb = [sbt(st, "rb%d" % i, [128, 512], F32) for i in range(2)]
                qrow = [sbt(st, "qrow%d" % i, [128, LK], BF16) for i in range(2)]
                P.ld("sp", cosT[:], cos_d.ap())
                P.ld("sp", sinT[:], sin_d.ap())
                n = 0
                for which in range(2):
                    for h in range(4):
                        col = 1536 + which * 512 + h * 128
                        w_main = load_w(win_d, col)
                        w_perm = load_w(wqkp_d, which * 512 + h * 128)
                        row = qrow[n % 2]
                        off = CTX if which == 1 else 0
                        if which == 1:
                            ps = psB[ctr["ps"] % 4]
                            ctr["ps"] += 1
                            for kc in range(8):
                                P.mm(ps[:, 0:CTX], w_main[:, kc, :], xcT[:, kc, :], start=(kc == 0), stop=(kc == 7),
                                     sig=(kc == 7))
                            P.cp("act", row[:, 0:CTX], ps[:, 0:CTX])
                        for T in range(8):
                            pa = proj_fm(w_main, T)
                            pb_ = proj_fm(w_perm, T)
                            P.tt("dve", ra[T % 2][:], pa[:], cosT[:, T * 512:(T + 1) * 512], ALU.mult)
                            P.tt("dve", rb[T % 2][:], pb_[:], sinT[:, T * 512:(T + 1) * 512], ALU.mult)
                            P.tt("pool", row[:, off + T * 512:off + (T + 1) * 512], ra[T % 2][:], rb[T % 2][:], ALU.add)
                        if which == 0:
                            P.ld("sp", qT_s.ap()[h], row[:, 0:L])
                        else:
                            P.ld("sp", kT_s.ap()[h], row[:, 0:LK])
                        n += 1
                P.barrier()

            with contextlib.ExitStack() as st:
                zrows = [sbt(st, "zrow%d" % i, [128, L + 2], F32) for i in range(2)]
                t1 = sbt(st, "t1", [128, L], F32)
                urows = [sbt(st, "urow%d" % i, [128, L], BF16) for i in range(2)]
                ucc = [sbt(st, "ucc%d" % i, [128, 128, 32], BF16) for i in range(2)]
                hcw = sbt(st, "hcw", [128, 12, 3], F32)
                hcb = sbt(st, "hcb", [128, 12], F32)
                pU = [pst(st, "pU%d" % i, [128, 4, 128], BF16) for i in range(2)]
                P.ld("sp", hcw[:], hcw_d.ap())
                P.ld("sp", hcb[:], hcb_d.ap())
                for zrow in zrows:
                    P.memset("pool", zrow[:, 0:1], 0.0)
                    P.memset("pool", zrow[:, L + 1:L + 2], 0.0)
                def b2_phase2(cc):
                    urow = urows[cc % 2]
                    u = ucc[cc % 2]
                    for jq in range(8):
                        pp = pU[jq % 2]
                        for jj in range(4):
                            j = jq * 4 + jj
                            P.tr(pp[:, jj, :], urow[:, j * 128:(j + 1) * 128], ident[:], sig=(jj == 3))
                        P.cp("act", u[:, :, jq * 4:(jq + 1) * 4], pp[:].rearrange("p j c -> p c j"))
                    P.ld("sp", hyu_s.ap()[:, cc * 4096:(cc + 1) * 4096], u[:].rearrange("p c j -> p (c j)"))

                for cc in range(12):
                    zrow = zrows[cc % 2]
                    urow = urows[cc % 2]
                    wt = load_w(win_d, cc * 128)
                    for T in range(8):
                        ps = proj_fm(wt, T)
                        P.cp("act", zrow[:, 1 + T * 512:1 + (T + 1) * 512], ps[:])
                    P.ts("pool", t1[:], zrow[:, 0:L], hcw[:, cc, 0:1], hcb[:, cc:cc + 1], ALU.mult, ALU.add)
                    P.stt("dve", t1[:], zrow[:, 1:L + 1], hcw[:, cc, 1:2], t1[:], ALU.mult, ALU.add)
                    P.stt("dve", urow[:], zrow[:, 2:L + 2], hcw[:, cc, 2:3], t1[:], ALU.mult, ALU.add)
                    if cc > 0:
                        b2_phase2(cc - 1)
                    if cc == 11:
                        b2_phase2(cc)
                P.barrier()

            with contextlib.ExitStack() as st:
                grow = [sbt(st, "grow%d" % i, [128, L], BF16) for i in range(2)]
                wv = sbt(st, "wv", [128, 8, 512], BF16)
                vrow = [sbt(st, "vrow%d" % i, [128, 512], BF16) for i in range(2)]
                for gc in range(16):
                    wt = load_w(win_d, 3072 + gc * 128)
                    g = grow[gc % 2]
                    for T in range(8):
                        ps = proj_fm(wt, T)
                        P.act(g[:, T * 512:(T + 1) * 512], ps[:], AF.Sigmoid)
                    P.ld("sp", gat_s.ap()[gc], g[:])
                for c4 in range(4):
                    wt = load_w(win_d, 2560 + c4 * 128)
                    P.cp("dve", wv[:, :, c4 * 128:(c4 + 1) * 128], wt[:])
                for t in range(NTT):
                    ps = psB[ctr["ps"] % 4]
                    ctr["ps"] += 1
                    for kc in range(8):
                        lhs = xcT[:, kc, t * 128:(t + 1) * 128] if t < 2 else xmT[:, kc, (t - 2) * 128:(t - 1) * 128]
                        P.mm(ps[:], lhs, wv[:, kc, :], start=(kc == 0), stop=(kc == 7), sig=(kc == 7))
                    v = vrow[t % 2]
                    P.cp("act", v[:], ps[:])
                    P.ld("sp", V_s.ap()[t], v[:])
                P.barrier()

        with contextlib.ExitStack() as st:
            fw1 = sbt(st, "fw1", [33, 64], F32); fw2 = sbt(st, "fw2", [64, 64], F32); fw3 = sbt(st, "fw3", [64, 64], F32)
            fwo = sbt(st, "fwo", [64, 2048], F32)
            ffr = sbt(st, "ffr", [64, 3], F32); fbt = sbt(st, "fbt", [64, 3], F32); ffb = sbt(st, "ffb", [64, 3], F32)
            zp = sbt(st, "zp", [33, L], F32)
            hA = sbt(st, "hA", [64, L], F32); hB = sbt(st, "hB", [64, L], F32)
            targ = [sbt(st, "targ%d" % i, [64, 2048], F32) for i in range(2)]
            targm = [sbt(st, "targm%d" % i, [64, 2048], F32) for i in range(2)]
            dct = [sbt(st, "dct%d" % i, [128, L], F32) for i in range(2)]
            arow = [sbt(st, "arow%d" % i, [128, L], BF16) for i in range(2)]
            pf = [pst(st, "pf%d" % i, [128, 2048]) for i in range(2)]
            P.ld("sp", fw1[:], fw1_d.ap()); P.ld("sp", fw2[:], fw2_d.ap()); P.ld("sp", fw3[:], fw3_d.ap())
            P.ld("act", fwo[:], fwo_d.ap()); P.ld("sp", ffr[:], ffr_d.ap()); P.ld("sp", fbt[:], fb_d.ap())
            P.tt("dve", ffb[:], ffr[:], fbt[:], ALU.mult)
            npf = 0
            nrow = 0
            for ev in range(2):
                P.ld("sp", zp[:], (zpf_d if ev == 0 else zpr_d).ap())
                srcs = [(zp, fw1, 33), (hA, fw2, 64), (hB, fw3, 64)]
                dsts = [hA, hB, hA]
                for li in range(3):
                    src, wt, kk = srcs[li]
                    dst = dsts[li]
                    for hf in range(2):
                        ps = pf[npf % 2]
                        ta = targ[npf % 2]
                        tm = targm[npf % 2]
                        npf += 1
                        for q4 in range(4):
                            col = hf * 2048 + q4 * 512
                            P.mm(ps[0:64, q4 * 512:(q4 + 1) * 512], wt[0:kk, :], src[0:kk, col:col + 512], sig=(q4 == 3))
                        P.ts("dve", ta[:], ps[0:64, :], ffr[:, li:li + 1], ffb[:, li:li + 1], ALU.mult, ALU.add)
                        P.ts("dve", tm[:], ta[:], PI, -2.0 * PI, ALU.is_gt, ALU.mult)
                        P.tt("dve", ta[:], ta[:], tm[:], ALU.add)
                        P.ts("dve", tm[:], ta[:], -PI, 2.0 * PI, ALU.is_lt, ALU.mult)
                        P.tt("dve", ta[:], ta[:], tm[:], ALU.add)
                        P.ts("dve", ta[:], ta[:], PI, -PI, ALU.min, ALU.max)
                        P.act(dst[:, hf * 2048:(hf + 1) * 2048], ta[:], AF.Sin)
                for o in range(2):
                    for cc in range(4):
                        dc = dct[nrow % 2]
                        ar = arow[nrow % 2]
                        nrow += 1
                        P.ld("sp" if nrow % 2 == 0 else "act", dc[:],
                             (dcf_d if ev == 0 else dcr_d).ap()[cc * 128:(cc + 1) * 128, :])
                        col = o * 1024 + ev * 512 + cc * 128
                        for hf in range(2):
                            ps = pf[npf % 2]
                            npf += 1
                            for q4 in range(4):
                                c_ = hf * 2048 + q4 * 512
                                P.mm(ps[:, q4 * 512:(q4 + 1) * 512], fwo[:, col:col + 128], hA[:, c_:c_ + 512], sig=(q4 == 3))
                            P.tt("dve", ar[:, hf * 2048:(hf + 1) * 2048], ps[:], dc[:, hf * 2048:(hf + 1) * 2048], ALU.mult)
                        rows = A_s.ap()[o * HYW + cc * 128:o * HYW + (cc + 1) * 128, :]
                        if ev == 0:
                            P.ld("sp", rows[:, 4095:8191], ar[:])
                        else:
                            P.ld("sp", rows[:, 0:4095], ar[:, 0:4095])
            P.barrier()

        G = 16
        S = 4
        KB = 128 // S
        TW = 8192 - KB
        NG = G // S
        with contextlib.ExitStack() as stY:
            yall = sbt(stY, "yall", [128, HYW, 32], BF16)
            with contextlib.ExitStack() as st:
                qh = sbt(st, "qh", [128, L], BF16)
                kh = sbt(st, "kh", [128, LK], BF16)
                Vh = sbt(st, "Vh", [128, NTT, 128], BF16)
                ones_f = sbt(st, "ones_f", [128, 128], F32)
                ones_b = sbt(st, "ones_b", [128, 128], BF16)
                NR = 4
                PT = [sbt(st, "PT%d" % i, [128, 512], BF16) for i in range(NR)]
                rc = [sbt(st, "rc%d" % i, [128, 512], F32) for i in range(2)]
                acc = [sbt(st, "acc%d" % i, [128, 512], F32) for i in range(2)]
                on = [sbt(st, "on%d" % i, [128, 512], F32) for i in range(4)]
                oc = [sbt(st, "oc%d" % i, [128, 512], F32) for i in range(2)]
                osq = [sbt(st, "osq%d" % i, [128, 512], F32) for i in range(2)]
                rst = [sbt(st, "rst%d" % i, [128, 512], F32) for i in range(2)]
                orow = [sbt(st, "orow%d" % i, [128, 512], BF16) for i in range(2)]
                pS = [pst(st, "pS%d" % i, [128, 512]) for i in range(2)]
                pO = [pst(st, "pO%d" % i, [128, 512]) for i in range(2)]
                pSm = [pst(st, "pSm%d" % i, [128, 512]) for i in range(2)]
                NT = 4
                tsk = [sbt(st, "tsk%d" % i, [128, TW], BF16) for i in range(NT)]
                rself = sbt(st, "rself", [128, KB], F32)
                fsel = sbt(st, "fsel", [128, S, S, 128], BF16)
                b0 = sbt(st, "b0", [128, HYW], F32); b1 = sbt(st, "b1", [128, HYW], F32)
                vg = [sbt(st, "vg%d" % i, [128, G, 32], BF16) for i in range(2)]
                x1g = [sbt(st, "x1g%d" % i, [128, G, 32], BF16) for i in range(2)]
                x2g = [sbt(st, "x2g%d" % i, [128, G, 32], BF16) for i in range(2)]
                vr = sbt(st, "vr", [128, NG, S, 32, S], BF16)
                vb = sbt(st, "vb", [128, G, 32], F32)
                tmp = sbt(st, "tmpE", [128, G, 32], F32)
                zz = sbt(st, "zz", [128, G, 32], BF16)
                zr = sbt(st, "zr", [128, NG, S, 32, S], BF16)
                zb = sbt(st, "zb", [128, G, 32], F32)
                pc = [pst(st, "pc%d" % o, [128, NG, 32, S]) for o in range(2)]
                pj = pc[1]

                P.memset("pool", ones_f[:], 1.0)
                P.memset("pool", ones_b[:], 1.0)
                P.ld("sp", b0[:], hyb_d.ap()[0].partition_broadcast(128))
                P.ld("sp", b1[:], hyb_d.ap()[1].partition_broadcast(128))
                hy3 = hyu_s.ap().rearrange("p (c j) -> p c j", j=32)
                P.memset("pool", fsel[:], 0.0)
                for hi_ in range(S):
                    P.memset("pool", rself[:], 0.0)
                    P.op("pool", (lambda e, b_=-(KB * hi_ + KB - 1): e.affine_select(
                        out=rself[:], in_=rself[:], pattern=[[1, KB]], compare_op=ALU.not_equal, fill=1.0,
                        base=b_, channel_multiplier=1)), reads=[rself.name], writes=[rself.name])
                    for sl_ in range(S):
                        P.cp("dve", fsel[:, hi_, sl_, KB * sl_:KB * sl_ + KB], rself[:])

                steps = [(h, Q, c, kb) for h in range(4) for Q in range(8) for c in range(2) for kb in range(NTT)]
                NS = len(steps)

                def gen_C():
                    pending = []
                    state = {"head": -1, "qk": -1}

                    def load_head(h):
                        P.ld("sp", qh[:], qT_s.ap()[h])
                        P.ld("sp", kh[:], kT_s.ap()[h])
                        for t0_ in (0, 17):
                            P.ld("sp", Vh[:, t0_:t0_ + 17, :],
                                 V_s.ap()[t0_:t0_ + 17, :, h * 128:(h + 1) * 128].rearrange("t p v -> p t v"))

                    def ensure_qk(m):
                        if m <= state["qk"] or m >= NS:
                            return
                        h, Q, c, kb = steps[m]
                        if h != state["head"]:
                            load_head(h)
                            state["head"] = h
                        P.mm(pS[m % 2][:], kh[64 * c:64 * c + 64, kb * 128:(kb + 1) * 128],
                             qh[64 * c:64 * c + 64, Q * 512:(Q + 1) * 512])
                        P.act(PT[m % NR][:], pS[m % 2][:], AF.Exp, scale=0.125)
                        state["qk"] = m

                    def fin1(h, Q, c, s_):
                        g = (h * 8 + Q) % 2
                        P.mm(pSm[s_][:], ones_f[:], acc[s_][:])
                        P.op("dve", (lambda e, a=rc[c][:], b=pSm[s_][:]: e.reciprocal(out=a, in_=b)),
                             reads=[pSm[s_].name], writes=[rc[c].name])
                        P.tt("dve", on[2 * g + c][:], pO[s_][:], rc[c][:], ALU.mult)
                        if c == 1:
                            P.stt("dve", oc[g][:], on[2 * g + 1][:], neglam[:], on[2 * g][:], ALU.mult, ALU.add)
                            P.tt("pool", osq[g][:], oc[g][:], oc[g][:], ALU.mult)

                    def fin2(h, Q, s_):
                        g = (h * 8 + Q) % 2
                        P.mm(pSm[s_][:], ones_f[:], osq[g][:])
                        P.act(rst[g][:], pSm[s_][:], AF.Sqrt, bias=epsb[:], scale=1.0 / 128.0)
                        P.op("dve", (lambda e, a=rst[g][:]: e.reciprocal(out=a, in_=a)),
                             reads=[rst[g].name], writes=[rst[g].name])
                        P.stt("dve", orow[g][:], oc[g][:], subg[:], rst[g][:], ALU.mult, ALU.mult)
                        P.ld("pool", oT_s.ap()[h][:, Q * 512:(Q + 1) * 512], orow[g][:])

                    for n in range(NS):
                        h, Q, c, kb = steps[n]
                        s_ = (n // NTT) % 2
                        ensure_qk(n)
                        if n + 1 < NS and steps[n + 1][0] == h:
                            ensure_qk(n + 1)
                        P.mm(pO[s_][:], Vh[:, kb, :], PT[n % NR][:], start=(kb == 0), stop=(kb == NTT - 1),
                             sig=(kb == NTT - 1))
                        if kb == 0:
                            P.cp("dve", acc[s_][:], PT[n % NR][:])
                        else:
                            P.tt("dve", acc[s_][:], acc[s_][:], PT[n % NR][:], ALU.add)
                        for item in list(pending):
                            if item[0] <= n:
                                item[1]()
                                pending.remove(item)
                        if kb == NTT - 1:
                            pending.append((n + 3, (lambda h=h, Q=Q, c=c, s_=s_: fin1(h, Q, c, s_))))
                            if c == 1:
                                pending.append((n + 10, (lambda h=h, Q=Q, s_=s_: fin2(h, Q, s_))))
                            if n + 1 < NS and steps[n + 1][0] != h:
                                for item in pending:
                                    item[1]()
                                pending = []
                        yield
                    for item in pending:
                        item[1]()

                elist = [0] + [e for e in range(-(32 * S - 1), 31 * S + 1) if e != 0]
                ectr = {"tsk": 0}

                def conv_grp(o, c, rhs_t, grp, pb):
                    tk = tsk[ectr["tsk"] % NT]
                    ectr["tsk"] += 1
                    keys = []
                    for sl in range(S):
                        q = "sp"
                        key = tk.name + "_s%d" % sl
                        keys.append(key)
                        P.dma(q, (lambda e, o_=tk[KB * sl:KB * sl + KB, :],
                                         i_=bass.AP(A_s, (o * HYW + c + sl) * 8192, [[1, KB], [1, TW]]):
                                     e.dma_start(out=o_, in_=i_)),
                              reads=[], writes=[key])
                    last = len(elist) - 1
                    for n_, e in enumerate(elist):
                        i_lo = max(0, -((-e) // S))
                        i_hi = min(31, (32 * S - 1 + e) // S)
                        nn = i_hi - i_lo + 1
                        hi = (-e) % S
                        j0 = i_lo - (e + hi) // S
                        assert nn > 0 and (e + hi) % S == 0 and 0 <= j0 and j0 + nn <= 32
                        ce = KB * e + 4096 - KB
                        assert 0 <= ce and ce + 128 <= TW
                        P.op("pe", (lambda en, o_=pb[:, grp, i_lo:i_hi + 1, :].rearrange("p i s -> p (i s)"),
                                           l_=tk[:, ce:ce + 128],
                                           r_=rhs_t[:, grp, hi, j0:j0 + nn, :].rearrange("p j s -> p (j s)"),
                                           s0=(n_ == 0), s1=(n_ == last): en.matmul(o_, l_, r_, start=s0, stop=s1)),
                             reads=keys + [rhs_t.name], writes=[pb.name], sig=(n_ == last))
                        if n_ % 28 == 27:
                            yield

                def reverse(dst, src):
                    sv = src[:].rearrange("p (g s) j -> p g s j", s=S)
                    pjf = pj[:].rearrange("p g i s -> p (g i s)")
                    for hi in range(S):
                        for sl in range(S):
                            P.mm(pjf[:, sl * NG * 32:(sl + 1) * NG * 32], fsel[:, hi, sl, :], sv[:, :, sl, :],
                                 sig=(sl == S - 1))
                        P.cp("act", dst[:, :, hi, :, :],
                             pjf.rearrange("p (s g j) -> p g j s", s=S, g=NG))

                def gen_E():
                    for g in range(HYW // G):
                        i = g % 2
                        c0 = g * G
                        P.ld("sp", vg[i][:], hy3[:, c0:c0 + G, :])
                        P.ld("sp", x1g[i][:], hy3[:, HYW + c0:HYW + c0 + G, :])
                        P.ld("sp", x2g[i][:], hy3[:, 2 * HYW + c0:2 * HYW + c0 + G, :])
                        reverse(vr, vg[i])
                        P.tt("pool", vb[:], vg[i][:], b0[:, c0:c0 + G].unsqueeze(2).to_broadcast([128, G, 32]), ALU.mult)
                        for grp in range(NG):
                            yield from conv_grp(0, c0 + S * grp, vr, grp, pc[0])
                        P.tt("dve", tmp[:].rearrange("p (q h) i -> p q h i", h=S), pc[0][:].rearrange("p q i h -> p q h i"),
                             vb[:].rearrange("p (q h) i -> p q h i", h=S), ALU.add)
                        P.tt("dve", zz[:], tmp[:], x1g[i][:], ALU.mult)
                        reverse(zr, zz)
                        P.tt("pool", zb[:], zz[:], b1[:, c0:c0 + G].unsqueeze(2).to_broadcast([128, G, 32]), ALU.mult)
                        for grp in range(NG):
                            yield from conv_grp(1, c0 + S * grp, zr, grp, pc[1])
                        P.tt("dve", tmp[:].rearrange("p (q h) i -> p q h i", h=S), pc[1][:].rearrange("p q i h -> p q h i"),
                             zb[:].rearrange("p (q h) i -> p q h i", h=S), ALU.add)
                        P.tt("dve", yall[:, c0:c0 + G, :], tmp[:], x2g[i][:], ALU.mult)
                        yield

                gC = gen_C()
                gE = gen_E()
                doneC = doneE = False
                while not (doneC and doneE):
                    if not doneE:
                        try:
                            next(gE)
                        except StopIteration:
                            doneE = True
                    if not doneC:
                        try:
                            next(gC)
                        except StopIteration:
                            doneC = True
                P.barrier()

            with contextlib.ExitStack() as st:
                yrow = [sbt(st, "yrow%d" % i, [128, 512], BF16) for i in range(2)]
                pY = [pst(st, "pY%d" % i, [128, 4, 128], BF16) for i in range(2)]
                n = 0
                for cc in range(4):
                    for iq in range(8):
                        for ii in range(4):
                            P.tr(pY[n % 2][:, ii, :], yall[:, cc * 128:(cc + 1) * 128, iq * 4 + ii], ident[:], sig=(ii == 3))
                        r = yrow[n % 2]
                        P.cp("act", r[:], pY[n % 2][:].rearrange("p a b -> p (a b)"))
                        P.ld("sp", yhT_s.ap()[cc][:, iq * 512:(iq + 1) * 512], r[:])
                        n += 1
                P.barrier()

        def stream_weight_bf16(wt, src_d, nk, ncol, wfs):
            n = 0
            for kc in range(nk):
                for c0 in range(0, ncol, 1024):
                    s_ = wfs[n % 2]
                    n += 1
                    P.ld("sp" if n % 2 == 0 else "act", s_[:], src_d.ap()[kc * 128:(kc + 1) * 128, c0:c0 + 1024])
                    P.cp("dve" if n % 2 == 0 else "act", wt[:, kc, c0:c0 + 1024], s_[:])
            return wt

        def residual_ln(pfx, st_tiles, ps_lo, ps_hi, gbc_mod, res_src_ap, lg, lb, dst):
            rs, r, stats, mv, rstd, tmp = st_tiles
            P.ld("sp", rs[:], res_src_ap)
            P.tt("dve", r[:, 0:512], ps_lo[:], gbc_mod[:, 0:512], ALU.mult)
            P.tt("dve", r[:, 512:1024], ps_hi[:], gbc_mod[:, 512:1024], ALU.mult)
            P.stt("dve", r[:], rs[:], ALPHA, r[:], ALU.mult, ALU.add)
            layer_norm_tile(pfx, r, lg, lb, dst, stats, mv, rstd, tmp)

        with contextlib.ExitStack() as stF:
            x1mT = sbt(stF, "x1mT", [128, 8, L], BF16)
            with contextlib.ExitStack() as st:
                who = sbt(st, "who", [128, 4, D], BF16)
                wao = sbt(st, "wao", [128, 4, D], BF16)
                wo = sbt(st, "wo", [128, 8, D], BF16)
                with contextlib.ExitStack() as stw:
                    wfs = [sbt(stw, "wfs%d" % i, [128, 1024], F32) for i in range(2)]
                    stream_weight_bf16(who, who_d, 4, D, wfs)
                    stream_weight_bf16(wao, wao_d, 4, D, wfs)
                    stream_weight_bf16(wo, wout_d, 8, D, wfs)
                    P.barrier()
                l1g = sbt(st, "l1g", [128, D], F32); l1b = sbt(st, "l1b", [128, D], F32)
                P.ld("sp", l1g[:], l1g_d.ap().partition_broadcast(128))
                P.ld("sp", l1b[:], l1b_d.ap().partition_broadcast(128))
                yh = [sbt(st, "yh%d" % i, [128, 4, 512], BF16) for i in range(2)]
                ot = [sbt(st, "ot%d" % i, [128, 4, 512], BF16) for i in range(2)]
                gt = [sbt(st, "gt%d" % i, [128, 16, 512], BF16) for i in range(2)]
                mT = sbt(st, "mT", [128, 8, 512], BF16)
                m1 = sbt(st, "m1", [128, 512], F32); m2 = sbt(st, "m2", [128, 512], F32)
                lnt = [(sbt(st, "rsF%d" % i, [128, D], F32), sbt(st, "rF%d" % i, [128, D], F32),
                        sbt(st, "statsF%d" % i, [128, 2, 6], F32), sbt(st, "mvF%d" % i, [128, 2], F32),
                        sbt(st, "rstdF%d" % i, [128, 1], F32), sbt(st, "tmpF%d" % i, [128, D], F32)) for i in range(2)]
                x1t = [sbt(st, "x1t0", [128, D], F32)] * 2
                x1bs = [sbt(st, "x1b%d" % i, [128, D], BF16) for i in range(2)]
                pA = [pst(st, "pA%d" % i, [128, 512]) for i in range(2)]
                pB2 = [pst(st, "pB2%d" % i, [128, 512]) for i in range(2)]
                pYl = [pst(st, "pYl%d" % i, [128, 512]) for i in range(2)]
                pT2 = pst(st, "pT2", [128, 8, 128], BF16)

                def load_T(T):
                    i = T % 2
                    sl = slice(T * 512, (T + 1) * 512)
                    P.ld("sp", yh[i][:], yhT_s.ap()[:, :, sl].rearrange("c p t -> p c t"))
                    P.ld("act", ot[i][:], oT_s.ap()[:, :, sl].rearrange("c p t -> p c t"))
                    P.ld("sp", gt[i][:], gat_s.ap()[:, :, sl].rearrange("g p t -> p g t"))

                def transposes(x1b, tok0):
                    for kc in range(8):
                        P.tr(pT2[:, kc, :], x1b[:, kc * 128:(kc + 1) * 128], ident[:], sig=(kc == 7))
                    for kc in range(8):
                        P.act(x1mT[:, kc, tok0:tok0 + 128], pT2[:, kc, :], AF.Identity,
                              bias=modT[:, 24 + kc, 0:1], scale=onep[:, 8 + kc, 0:1])

                nt = 0
                prev = None
                load_T(0)
                for T in range(8):
                    i = T % 2
                    if T + 1 < 8:
                        load_T(T + 1)
                    for fc in range(8):
                        a = pA[fc % 2]; b = pB2[fc % 2]
                        for cc in range(4):
                            P.mm(a[:], who[:, cc, fc * 128:(fc + 1) * 128], yh[i][:, cc, :], start=(cc == 0), stop=(cc == 3),
                                 sig=(cc == 3))
                        for cc in range(4):
                            P.mm(b[:], wao[:, cc, fc * 128:(fc + 1) * 128], ot[i][:, cc, :], start=(cc == 0), stop=(cc == 3),
                                 sig=(cc == 3))
                        P.tt("dve", m1[:], a[:], gt[i][:, fc, :], ALU.mult)
                        P.tt("dve", m2[:], b[:], gt[i][:, 8 + fc, :], ALU.mult)
                        P.tt("pool", mT[:, fc, :], m1[:], m2[:], ALU.add)
                    for tb in range(4):
                        tok0 = T * 512 + tb * 128
                        for hf in range(2):
                            for kc in range(8):
                                P.mm(pYl[hf][:], mT[:, kc, tb * 128:(tb + 1) * 128], wo[:, kc, hf * 512:(hf + 1) * 512],
                                     start=(kc == 0), stop=(kc == 7), sig=(kc == 7))
                        if prev is not None:
                            transposes(*prev)
                        xo = x1t[nt % 2]
                        x1b = x1bs[nt % 2]
                        lnt_ = lnt[nt % 2]
                        nt += 1
                        residual_ln("F", lnt_, pYl[0], pYl[1], g1bc,
                                    xln_s.ap()[tok0:tok0 + 128, :], l1g, l1b, xo)
                        P.ld("pool", x1_s.ap()[tok0:tok0 + 128, :], xo[:])
                        P.cp("act", x1b[:], xo[:])
                        prev = (x1b, tok0)
                transposes(*prev)
                P.barrier()

            with contextlib.ExitStack() as st:
                wuf = [sbt(st, "wuf%d" % i, [128, 8, 128], F32) for i in range(2)]
                wub = [sbt(st, "wub%d" % i, [128, 8, 128], BF16) for i in range(4)]
                g_items = []
                for fc_ in range(NFC):
                    for part_ in range(2):
                        g_items.append(part_ * NFC + fc_)
                gctr = {"issued": 0}

                def g_issue():
                    i = gctr["issued"]
                    if i >= len(g_items):
                        return
                    gctr["issued"] += 1
                    ch_ = g_items[i]
                    P.ld("sp", wuf[i % 2][:], wup_d.ap()[:, ch_ * 128:(ch_ + 1) * 128].rearrange("(kc p) c -> p kc c", p=128))
                    P.cp("pool", wub[i % 4][:], wuf[i % 2][:])
                zr2 = [sbt(st, "zr2_%d" % i, [128, L + 2], F32) for i in range(2)]
                tgs = [sbt(st, "tg%d" % i, [128, L], F32) for i in range(2)]
                tas = [sbt(st, "ta_%d" % i, [128, L], F32) for i in range(2)]
                hrow = [sbt(st, "hrow%d" % i, [128, L], BF16) for i in range(2)]
                fcw = sbt(st, "fcw", [128, 44, 3], F32); fcb = sbt(st, "fcb", [128, 44], F32)
                pG = [pst(st, "pG%d" % i, [128, 512]) for i in range(4)]
                P.ld("sp", fcw[:], fcw_d.ap()); P.ld("sp", fcb[:], fcb_d.ap())
                for i in range(2):
                    P.memset("pool", zr2[i][:, 0:1], 0.0)
                    P.memset("pool", zr2[i][:, L + 1:L + 2], 0.0)
                nw = 0
                npg = 0
                for fc in range(NFC):
                    tg = tgs[fc % 2]
                    ta_ = tas[fc % 2]
                    for part in range(2):
                        ch = part * NFC + fc
                        while gctr["issued"] <= min(nw + 2, len(g_items) - 1):
                            g_issue()
                        wbl = wub[nw % 4]
                        nw += 1
                        z = zr2[part]
                        for T in range(8):
                            ps = pG[npg % 4]
                            npg += 1
                            for kc in range(8):
                                P.mm(ps[:], wbl[:, kc, :], x1mT[:, kc, T * 512:(T + 1) * 512], start=(kc == 0), stop=(kc == 7),
                                     sig=(kc == 7))
                            P.cp("act", z[:, 1 + T * 512:1 + (T + 1) * 512], ps[:])
                        dst = ta_ if part == 0 else tg
                        P.ts("pool", dst[:], z[:, 0:L], fcw[:, ch, 0:1], fcb[:, ch:ch + 1], ALU.mult, ALU.add)
                        P.stt("dve", dst[:], z[:, 1:L + 1], fcw[:, ch, 1:2], dst[:], ALU.mult, ALU.add)
                        P.stt("dve", dst[:], z[:, 2:L + 2], fcw[:, ch, 2:3], dst[:], ALU.mult, ALU.add)
                    P.act(tg[:], tg[:], AF.Silu)
                    hr = hrow[fc % 2]
                    P.tt("dve", hr[:], tg[:], ta_[:], ALU.mult)
                    if fc > 0:
                        P.ld("sp", hT_s.ap()[fc - 1], hrow[(fc - 1) % 2][:])
                P.ld("sp", hT_s.ap()[NFC - 1], hrow[(NFC - 1) % 2][:])
                P.barrier()

        with contextlib.ExitStack() as st:
            wfs = [sbt(st, "wfsH%d" % i, [128, 1024], F32) for i in range(2)]
            wd = sbt(st, "wd", [128, NFC, D], BF16)
            stream_weight_bf16(wd, wdn_d, NFC, D, wfs)
            l2g = sbt(st, "l2g", [128, D], F32); l2b = sbt(st, "l2b", [128, D], F32)
            P.ld("sp", l2g[:], l2g_d.ap().partition_broadcast(128))
            P.ld("sp", l2b[:], l2b_d.ap().partition_broadcast(128))
            ht = [sbt(st, "ht%d" % i, [128, NFC, 512], BF16) for i in range(2)]
            lnt = [(sbt(st, "rsH%d" % i, [128, D], F32), sbt(st, "rH%d" % i, [128, D], F32),
                    sbt(st, "statsH%d" % i, [128, 2, 6], F32), sbt(st, "mvH%d" % i, [128, 2], F32),
                    sbt(st, "rstdH%d" % i, [128, 1], F32), sbt(st, "tmpH%d" % i, [128, D], F32)) for i in range(2)]
            xo = [sbt(st, "xoH%d" % i, [128, D], F32) for i in range(2)]
            pD = [[pst(st, "pD%d_%d" % (s, i), [128, 512]) for i in range(2)] for s in range(2)]
            nt = 0

            def load_ht(T):
                P.ld("sp" if T % 2 == 0 else "act", ht[T % 2][:],
                     hT_s.ap()[:, :, T * 512:(T + 1) * 512].rearrange("f p t -> p f t"))

            load_ht(0)
            for T in range(8):
                i = T % 2
                if T + 1 < 8:
                    load_ht(T + 1)
                for tb in range(4):
                    tok0 = T * 512 + tb * 128
                    pp = pD[nt % 2]
                    for hf in range(2):
                        for fc in range(NFC):
                            P.mm(pp[hf][:], ht[i][:, fc, tb * 128:(tb + 1) * 128], wd[:, fc, hf * 512:(hf + 1) * 512],
                                 start=(fc == 0), stop=(fc == NFC - 1), sig=(fc == NFC - 1))
                    o_ = xo[nt % 2]
                    lnt_ = lnt[nt % 2]
                    nt += 1
                    residual_ln("H", lnt_, pp[0], pp[1], g2bc,
                                x1_s.ap()[tok0:tok0 + 128, :], l2g, l2b, o_)
                    P.ld("pool", out_d.ap()[tok0:tok0 + 128, :], o_[:])
            P.barrier()

        P.run()
    return nc


def host_constants():
    f32 = np.float32
    rows = L // 64
    row = np.repeat(np.arange(rows, dtype=f32), 64)
    col = np.tile(np.arange(64, dtype=f32), rows)
    inv = (10000.0 ** (-np.arange(0, 32, 2, dtype=f32) / 32.0)).astype(f32)
    cosT = np.zeros((128, L), f32)
    sinT = np.zeros((128, L), f32)
    for p in range(128):
        d = p % 64
        a = d // 32
        i = d % 32
        f = i % 16
        ang = (row if a == 0 else col) * inv[f]
        cosT[p] = np.cos(ang)
        sinT[p] = (-np.sin(ang)) if i < 16 else np.sin(ang)
    t = np.linspace(0.0, 1.0, L, dtype=f32)[:, None]
    w = (f32(2.0 * math.pi / L) * np.arange(L, dtype=f32))[:, None]
    f = np.linspace(1e-4, 15, 16, dtype=f32)[None, :]
    z = np.concatenate([t, np.cos(f * w), -np.sin(f * w)], -1).astype(f32)
    min_decay = math.log(1e-2) / 1.5
    max_decay = math.log(1e-2) / 0.3
    deltas = np.linspace(min_decay, max_decay, HYW, dtype=f32)
    decay = np.exp(-t * np.abs(deltas)[None, :]).astype(f32)
    return dict(
        rope_cos=cosT, rope_sin=sinT,
        zposT_f=np.ascontiguousarray(z.T), zposT_r=np.ascontiguousarray(z[::-1].T),
        decayT_f=np.ascontiguousarray(decay.T), decayT_r=np.ascontiguousarray(decay[::-1].T),
    )


def per_part(v, nch):
    return np.ascontiguousarray(np.asarray(v, np.float32).reshape(nch, 128).T)


def make_in_maps(inp):
    f32 = np.float32
    g = {k: np.asarray(v, f32) for k, v in inp.items()}
    const = host_constants()
    w_in = g["w_in"][0]
    perm = np.zeros(1024, np.int64)
    for cidx in range(1024):
        base = 1536 + cidx
        d = cidx % 64
        i = d % 32
        perm[cidx] = base + 16 if i < 16 else base - 16
    w_qkp = np.ascontiguousarray(w_in[:, perm])
    cw = g["hy_conv_w"][0]
    hy_cw = np.ascontiguousarray(np.stack([per_part(cw[k], 12) for k in range(3)], -1))
    fw = g["ffn_conv_w"][0]
    ffn_cw = np.ascontiguousarray(np.stack([per_part(fw[k], 44) for k in range(3)], -1))
    shared = dict(
        ln_in_g=g["ln_in_g"], ln_in_b=g["ln_in_b"],
        w_ada=g["w_ada"][0], b_ada=g["b_ada"][0], b_adaT=per_part(g["b_ada"][0], 48),
        w_in=w_in, w_qkp=w_qkp,
        hy_cw=hy_cw, hy_cb=per_part(g["hy_conv_b"][0], 12),
        f_w1=g["hy_f_w1"][0], f_w2=g["hy_f_w2"][0], f_w3=g["hy_f_w3"][0], f_wout=g["hy_f_wout"][0],
        f_freqT=np.ascontiguousarray(g["hy_f_freq"][0].T),
        f_bT=np.ascontiguousarray(np.stack([g["hy_f_b1"][0], g["hy_f_b2"][0], g["hy_f_b3"][0]], -1)),
        hy_bias=g["hy_bias"][0],
        lamv=np.ascontiguousarray(np.stack([g["lam_q1"][0], g["lam_k1"][0], g["lam_q2"][0], g["lam_k2"][0]], 0)),
        subln_g=np.ascontiguousarray(g["at_subln_g"][0].reshape(128, 1)),
        w_hy_o=g["w_hy_o"][0], w_at_o=g["w_at_o"][0], w_out=g["w_out"][0],
        ln1_g=g["ln1_g"][0], ln1_b=g["ln1_b"][0], ln2_g=g["ln2_g"][0], ln2_b=g["ln2_b"][0],
        ffn_w_up=g["ffn_w_up"][0], ffn_cw=ffn_cw, ffn_cb=per_part(g["ffn_conv_b"][0], 44),
        ffn_w_down=g["ffn_w_down"][0],
    )
    shared.update(const)
    maps = []
    cc = per_part(g["c_ctx"], 8)
    for b in range(8):
        m = dict(shared)
        m["x"] = np.ascontiguousarray(g["x"][b])
        m["ctx"] = np.ascontiguousarray(g["ctx"][b])
        m["cT"] = np.ascontiguousarray(np.stack([per_part(g["c"][b], 8), cc], -1))
        maps.append(m)
    return maps


_NC = None


def kernel(**inputs):
    global _NC
    if _NC is None:
        _NC = build_program()
    maps = make_in_maps(inputs)
    res = run_bass_kernel_spmd(_NC, maps, core_ids=list(range(8)))
    out = np.stack([np.asarray(r["out"], np.float32) for r in res.results], 0)
    return out
```

```python
import math
import contextlib
import numpy as np
import concourse.bass as bass
import concourse.mybir as mybir
from concourse.bass_utils import run_bass_kernel_spmd

F32 = mybir.dt.float32
BF16 = mybir.dt.bfloat16
ALU = mybir.AluOpType
AF = mybir.ActivationFunctionType
AX = mybir.AxisListType

ENGS = ("pe", "act", "dve", "pool", "sp")

D = 1024
L = 4096
CTX = 256
NTT = 34
LK = CTX + L
HYW = 512
DFF = 2816
NFC = DFF // 128
ALPHA = 2.0 ** 0.25
LAM_INIT = 0.2
PI = math.pi
DEBUG = False


class Prog:
    NDMA = 6

    def __init__(self, nc):
        self.nc = nc
        self.ops = {e: [] for e in ENGS}
        self.cnt = {e: 0 for e in ENGS}
        self.waited = {e: {} for e in ENGS}
        self.last_w = {}
        self.reads = {}
        self.dma_uses = {}
        self.dma_rr = {e: 0 for e in ENGS}
        self.semkeys = set()
        self.sems = {}

    def _deps(self, eng, reads, writes):
        deps = []
        for k in reads:
            if k in self.last_w:
                deps.append(self.last_w[k])
        for k in writes:
            if k in self.last_w:
                deps.append(self.last_w[k])
            for ev in self.reads.get(k, ()):
                if ev[0] == eng:
                    continue
                deps.append(ev)
        best = {}
        for sk, v in deps:
            if eng == "pe" and sk == "pe":
                continue
            if v > best.get(sk, 0):
                best[sk] = v
        out = []
        w = self.waited[eng]
        for sk, v in best.items():
            if w.get(sk, 0) >= v:
                continue
            w[sk] = v
            out.append((sk, v))
        return out

    def _commit(self, ev, reads, writes):
        for k in reads:
            self.reads.setdefault(k, []).append(ev)
        for k in writes:
            self.last_w[k] = ev
            self.reads[k] = []

    def op(self, eng, fn, reads=(), writes=(), sig=True):
        waits = self._deps(eng, reads, writes)
        if sig:
            self.cnt[eng] += 1
            ev = (eng, self.cnt[eng])
            self.semkeys.add(eng)
            self.ops[eng].append(("op", fn, waits, eng))
        else:
            ev = (eng, self.cnt[eng] + 1)
            self.ops[eng].append(("op", fn, waits, None))
        self._commit(ev, reads, writes)
        return ev

    def dma(self, q, fn, reads=(), writes=()):
        slot = self.dma_rr[q] % self.NDMA
        self.dma_rr[q] += 1
        sk = ("dma", q, slot)
        self.semkeys.add(sk)
        uses = self.dma_uses.get(sk, 0)
        waits = self._deps(q, reads, writes)
        if uses > 0 and self.waited[q].get(sk, 0) < 16 * uses:
            self.waited[q][sk] = 16 * uses
            waits.append((sk, 16 * uses))
        uses += 1
        self.dma_uses[sk] = uses
        ev = (sk, 16 * uses)
        self.ops[q].append(("dma", fn, waits, sk))
        self._commit(ev, reads, writes)
        return ev

    def barrier(self):
        evs = []
        for e in ENGS:
            if self.cnt[e] > 0:
                evs.append((e, self.cnt[e]))
        for sk, uses in self.dma_uses.items():
            evs.append((sk, 16 * uses))
        for e in ENGS:
            waits = []
            for sk, v in evs:
                if sk == e:
                    continue
                if self.waited[e].get(sk, 0) >= v:
                    continue
                self.waited[e][sk] = v
                waits.append((sk, v))
            if waits:
                self.ops[e].append(("wait", None, waits, None))
        self.last_w = {}
        self.reads = {}

    def run(self):
        nc = self.nc
        with contextlib.ExitStack() as st:
            for sk in sorted(self.semkeys, key=str):
                name = sk if isinstance(sk, str) else "d_%s_%d" % (sk[1], sk[2])
                self.sems[sk] = st.enter_context(nc.semaphore("s_" + name))
            block = st.enter_context(nc.Block())

            def mk(ename):
                def body(e):
                    for kind, fn, waits, sk in self.ops[ename]:
                        for wk, wv in waits:
                            e.wait_ge(self.sems[wk], wv)
                        if kind == "wait":
                            continue
                        ins = fn(e)
                        if sk is not None:
                            ins.then_inc(self.sems[sk], 16 if kind == "dma" else 1)
                return body

            block.tensor(mk("pe"))
            block.scalar(mk("act"))
            block.vector(mk("dve"))
            block.gpsimd(mk("pool"))
            block.sync(mk("sp"))

    @staticmethod
    def _k(*aps):
        out = []
        for a in aps:
            if a is None or isinstance(a, (int, float)):
                continue
            out.append(a.name)
        return out

    def mm(self, out, lhsT, rhs, start=True, stop=True, sig=True):
        return self.op("pe", lambda e: e.matmul(out, lhsT, rhs, start=start, stop=stop),
                       reads=self._k(lhsT, rhs), writes=self._k(out), sig=sig)

    def tr(self, out, in_, ident, sig=True):
        return self.op("pe", lambda e: e.transpose(out, in_, ident),
                       reads=self._k(in_, ident), writes=self._k(out), sig=sig)

    def act(self, out, in_, func, bias=None, scale=None, accum_out=None):
        kw = {}
        if bias is not None:
            kw["bias"] = bias
        if scale is not None:
            kw["scale"] = scale
        if accum_out is not None:
            kw["accum_out"] = accum_out
        return self.op("act", lambda e: e.activation(out=out, in_=in_, func=func, **kw),
                       reads=self._k(in_, bias, scale), writes=self._k(out, accum_out))

    def tt(self, eng, out, in0, in1, op):
        return self.op(eng, lambda e: e.tensor_tensor(out=out, in0=in0, in1=in1, op=op),
                       reads=self._k(in0, in1), writes=self._k(out))

    def ts(self, eng, out, in0, s1, s2, op0, op1=None):
        if op1 is None:
            return self.op(eng, lambda e: e.tensor_scalar(out=out, in0=in0, scalar1=s1, scalar2=None, op0=op0),
                           reads=self._k(in0, s1), writes=self._k(out))
        return self.op(eng, lambda e: e.tensor_scalar(out=out, in0=in0, scalar1=s1, scalar2=s2, op0=op0, op1=op1),
                       reads=self._k(in0, s1, s2), writes=self._k(out))

    def stt(self, eng, out, in0, scalar, in1, op0, op1):
        return self.op(eng, lambda e: e.scalar_tensor_tensor(out=out, in0=in0, scalar=scalar, in1=in1, op0=op0, op1=op1),
                       reads=self._k(in0, scalar, in1), writes=self._k(out))

    def cp(self, eng, out, in_):
        if eng == "act":
            return self.act(out, in_, AF.Copy)
        return self.op(eng, lambda e: e.tensor_copy(out=out, in_=in_), reads=self._k(in_), writes=self._k(out))

    def memset(self, eng, ap, val):
        return self.op(eng, lambda e: e.memset(ap, val), writes=self._k(ap))

    def ld(self, q, out, in_):
        return self.dma(q, lambda e: e.dma_start(out=out, in_=in_), reads=self._k(in_), writes=self._k(out))


def build_program():
    nc = bass.Bass("TRN2", target_bir_lowering=False)
    P = Prog(nc)

    def din(name, shape, dt=F32):
        return nc.dram_tensor(name, list(shape), dt, kind="ExternalInput")

    def dscr(name, shape, dt, dbg=False):
        if DEBUG and dbg:
            return nc.dram_tensor(name, list(shape), dt, kind="ExternalOutput")
        return nc.dram_tensor(name, list(shape), dt)

    x_d = din("x", [L, D]); ctx_d = din("ctx", [CTX, D])
    cT_d = din("cT", [128, 8, 2])
    lng_d = din("ln_in_g", [D]); lnb_d = din("ln_in_b", [D])
    wada_d = din("w_ada", [D, 6 * D]); bada_d = din("b_ada", [6 * D]); badaT_d = din("b_adaT", [128, 48])
    win_d = din("w_in", [D, 5120]); wqkp_d = din("w_qkp", [D, 1024])
    cos_d = din("rope_cos", [128, L]); sin_d = din("rope_sin", [128, L])
    hcw_d = din("hy_cw", [128, 12, 3]); hcb_d = din("hy_cb", [128, 12])
    fw1_d = din("f_w1", [33, 64]); fw2_d = din("f_w2", [64, 64]); fw3_d = din("f_w3", [64, 64])
    fwo_d = din("f_wout", [64, 2048]); ffr_d = din("f_freqT", [64, 3]); fb_d = din("f_bT", [64, 3])
    zpf_d = din("zposT_f", [33, L]); zpr_d = din("zposT_r", [33, L])
    dcf_d = din("decayT_f", [HYW, L]); dcr_d = din("decayT_r", [HYW, L])
    hyb_d = din("hy_bias", [2, HYW])
    lam_d = din("lamv", [4, 64])
    sub_d = din("subln_g", [128, 1])
    who_d = din("w_hy_o", [HYW, D]); wao_d = din("w_at_o", [512, D]); wout_d = din("w_out", [D, D])
    l1g_d = din("ln1_g", [D]); l1b_d = din("ln1_b", [D]); l2g_d = din("ln2_g", [D]); l2b_d = din("ln2_b", [D])
    wup_d = din("ffn_w_up", [D, 2 * DFF]); fcw_d = din("ffn_cw", [128, 44, 3]); fcb_d = din("ffn_cb", [128, 44])
    wdn_d = din("ffn_w_down", [DFF, D])
    out_d = nc.dram_tensor("out", [L, D], F32, kind="ExternalOutput")

    xln_s = dscr("xln_s", [L, D], F32, True)
    qT_s = dscr("qT_s", [4, 128, L], BF16, True)
    kT_s = dscr("kT_s", [4, 128, LK], BF16, True)
    V_s = dscr("V_s", [NTT, 128, 512], BF16, True)
    hyu_s = dscr("hyu_s", [128, 1536 * 32], BF16, True)
    gat_s = dscr("gat_s", [16, 128, L], BF16, True)
    A_s = dscr("A_s", [2 * HYW, 8192], BF16, True)
    oT_s = dscr("oT_s", [4, 128, L], BF16, True)
    yhT_s = dscr("yhT_s", [4, 128, L], BF16, True)
    x1_s = dscr("x1_s", [L, D], F32, True)
    hT_s = dscr("hT_s", [NFC, 128, L], BF16, True)

    top = contextlib.ExitStack()

    def sbt(st, name, shape, dt):
        return st.enter_context(nc.sbuf_tensor("sb_" + name, list(shape), dt))

    def pst(st, name, shape, dt=F32):
        return st.enter_context(nc.psum_tensor("ps_" + name, list(shape), dt))

    with top:
        ident = sbt(top, "ident", [128, 128], BF16)
        jrev = sbt(top, "jrev", [128, 128], BF16)
        identf = sbt(top, "identf", [128, 128], F32)
        epsb = sbt(top, "epsb", [128, 1], F32)
        npib = sbt(top, "npib", [128, 1], F32)
        modT = sbt(top, "modT", [128, 48, 2], F32)
        onep = sbt(top, "onep", [128, 16, 2], F32)
        g1bc = sbt(top, "g1bc", [128, D], F32)
        g2bc = sbt(top, "g2bc", [128, D], F32)
        neglam = sbt(top, "neglam", [128, 1], F32)
        subg = sbt(top, "subg", [128, 1], F32)

        P.memset("pool", identf[:], 0.0)
        P.op("pool", lambda e: e.affine_select(out=identf[:], in_=identf[:], pattern=[[-1, 128]],
                                               compare_op=ALU.not_equal, fill=1.0, base=0, channel_multiplier=1),
             reads=[identf.name], writes=[identf.name])
        P.cp("dve", ident[:], identf[:])
        P.memset("pool", identf[:], 0.0)
        P.op("pool", lambda e: e.affine_select(out=identf[:], in_=identf[:], pattern=[[1, 128]],
                                               compare_op=ALU.not_equal, fill=1.0, base=-127, channel_multiplier=1),
             reads=[identf.name], writes=[identf.name])
        P.cp("dve", jrev[:], identf[:])
        P.memset("pool", epsb[:], 1e-5)
        P.memset("pool", npib[:], -PI)

        with contextlib.ExitStack() as st:
            cT = sbt(st, "cT", [128, 8, 2], F32)
            sc = sbt(st, "sc", [128, 8, 2], F32)
            scb = sbt(st, "scb", [128, 8, 128], F32)
            wa = [sbt(st, "wa%d" % i, [128, 8, 512], F32) for i in range(4)]
            wab = [sbt(st, "wab%d" % i, [128, 8, 512], BF16) for i in range(2)]
            sc_b = sbt(st, "sc_b", [128, 8, 2], BF16)
            scb_b = sbt(st, "scb_b", [128, 8, 128], BF16)
            badaT = sbt(st, "badaT", [128, 48], F32)
            bg = sbt(st, "bg", [128, D], F32)
            lamv = sbt(st, "lamv_t", [128, 4, 64], F32)
            lamp = sbt(st, "lamp", [128, 2, 64], F32)
            lams = sbt(st, "lams", [128, 2], F32)
            pm = pst(st, "pm", [128, 48, 2])
            pb = pst(st, "pb", [128, 512])
            P.ld("sp", cT[:], cT_d.ap())
            P.ld("sp", badaT[:], badaT_d.ap())
            P.ld("sp", subg[:], sub_d.ap())
            P.ld("sp", lamv[:].rearrange("p a b -> p (a b)"),
                 lam_d.ap().rearrange("a b -> (a b)").partition_broadcast(128))
            P.act(sc[:], cT[:], AF.Silu)
            P.cp("dve", scb[:], sc[:, :, 0:1].to_broadcast([128, 8, 128]))
            P.cp("dve", sc_b[:], sc[:])
            P.cp("dve", scb_b[:], scb[:])
            P.ts("dve", subg[:], subg[:], 1.0 - LAM_INIT, None, ALU.mult)
            P.tt("dve", lamp[:, 0, :], lamv[:, 0, :], lamv[:, 1, :], ALU.mult)
            P.tt("dve", lamp[:, 1, :], lamv[:, 2, :], lamv[:, 3, :], ALU.mult)
            P.op("dve", lambda e: e.reduce_sum(out=lams[:], in_=lamp[:], axis=AX.X), reads=[lamp.name], writes=[lams.name])
            P.act(lams[:], lams[:], AF.Exp)
            P.tt("dve", neglam[:], lams[:, 1:2], lams[:, 0:1], ALU.subtract)
            P.ts("dve", neglam[:], neglam[:], -LAM_INIT, None, ALU.add)
            for g in range(12):
                w = wa[g % 4]
                P.ld("sp" if g % 2 == 0 else "act", w[:],
                     wada_d.ap()[:, g * 512:(g + 1) * 512].rearrange("(kc p) c -> p kc c", p=128))
                wq = wab[g % 2]
                P.cp("dve" if g % 2 == 0 else "act", wq[:], w[:])
                for jj in range(4):
                    j = g * 4 + jj
                    for kc in range(8):
                        P.mm(pm[:, j, :], wq[:, kc, jj * 128:(jj + 1) * 128], sc_b[:, kc, :],
                             start=(kc == 0), stop=(kc == 7), sig=(kc == 7))
                if g in (4, 5, 10, 11):
                    for kc in range(8):
                        P.mm(pb[:], scb_b[:, kc, :], wq[:, kc, :], start=(kc == 0), stop=(kc == 7), sig=(kc == 7))
                    dst = g1bc if g in (4, 5) else g2bc
                    half = g % 2
                    P.ld("sp", bg[:, 0:512], bada_d.ap()[g * 512:(g + 1) * 512].partition_broadcast(128))
                    P.tt("dve", dst[:, half * 512:(half + 1) * 512], pb[:], bg[:, 0:512], ALU.add)
            P.tt("dve", modT[:], pm[:], badaT[:].unsqueeze(2).to_broadcast([128, 48, 2]), ALU.add)
            P.ts("dve", onep[:, 0:8, :], modT[:, 8:16, :], 1.0, None, ALU.add)
            P.ts("dve", onep[:, 8:16, :], modT[:, 32:40, :], 1.0, None, ALU.add)
            P.barrier()

        def layer_norm_tile(pfx, src, gbc, bbc, dst, stats, mv, rstd, tmp):
            for h in range(2):
                P.op("dve", (lambda e, h=h: e.bn_stats(out=stats[:, h, :], in_=src[:, h * 512:(h + 1) * 512])),
                     reads=[src.name], writes=[stats.name])
            P.op("dve", lambda e: e.bn_aggr(out=mv[:], in_=stats[:].rearrange("p a b -> p (a b)")),
                 reads=[stats.name], writes=[mv.name])
            P.act(rstd[:], mv[:, 1:2], AF.Sqrt, bias=epsb[:], scale=1.0)
            P.op("dve", lambda e: e.reciprocal(out=rstd[:], in_=rstd[:]), reads=[rstd.name], writes=[rstd.name])
            P.ts("dve", tmp[:], src[:], mv[:, 0:1], rstd[:], ALU.subtract, ALU.mult)
            P.tt("pool", tmp[:], tmp[:], gbc[:], ALU.mult)
            P.tt("pool", dst[:], tmp[:], bbc[:], ALU.add)

        with contextlib.ExitStack() as stA:
            xmT = sbt(stA, "xmT", [128, 8, L], BF16)
            xcT = sbt(stA, "xcT", [128, 8, CTX], BF16)
            with contextlib.ExitStack() as st:
                gbc = sbt(st, "gbc", [128, D], F32)
                bbc = sbt(st, "bbc", [128, D], F32)
                xt = [sbt(st, "xt%d" % i, [128, D], F32) for i in range(3)]
                xh = [sbt(st, "xh%d" % i, [128, D], F32) for i in range(3)]
                xl = [sbt(st, "xl%d" % i, [128, D], F32) for i in range(3)]
                xb = [sbt(st, "xb%d" % i, [128, D], BF16) for i in range(2)]
                stats = [sbt(st, "stats%d" % i, [128, 2, 6], F32) for i in range(3)]
                mv = [sbt(st, "mv%d" % i, [128, 2], F32) for i in range(3)]
                rstd = [sbt(st, "rstd%d" % i, [128, 1], F32) for i in range(3)]
                pT = [pst(st, "pT%d" % i, [128, 8, 128], BF16) for i in range(2)]
                P.ld("sp", gbc[:], lng_d.ap().partition_broadcast(128))
                P.ld("sp", bbc[:], lnb_d.ap().partition_broadcast(128))
                def phase2(t):
                    i = t % 3
                    is_ctx = t < 2
                    if not is_ctx:
                        P.ld("pool", xln_s.ap()[(t - 2) * 128:(t - 1) * 128, :], xl[i][:])
                    P.cp("act", xb[t % 2][:], xl[i][:])
                    for kc in range(8):
                        P.tr(pT[t % 2][:, kc, :], xb[t % 2][:, kc * 128:(kc + 1) * 128], ident[:], sig=(kc == 7))
                    w = 1 if is_ctx else 0
                    for kc in range(8):
                        dst = xcT[:, kc, t * 128:(t + 1) * 128] if is_ctx else xmT[:, kc, (t - 2) * 128:(t - 1) * 128]
                        P.act(dst, pT[t % 2][:, kc, :], AF.Identity, bias=modT[:, kc, w:w + 1], scale=onep[:, kc, w:w + 1])

                for t in range(NTT):
                    i = t % 3
                    is_ctx = t < 2
                    src = ctx_d.ap()[t * 128:(t + 1) * 128, :] if is_ctx else x_d.ap()[(t - 2) * 128:(t - 1) * 128, :]
                    P.ld("sp", xt[i][:], src)
                    layer_norm_tile("A", xt[i], gbc, bbc, xl[i], stats[i], mv[i], rstd[i], xh[i])
                    if t > 0:
                        phase2(t - 1)
                phase2(NTT - 1)
                P.barrier()

            wf = [sbt(stA, "wf%d" % i, [128, 8, 128], F32) for i in range(2)]
            wb = [sbt(stA, "wb%d" % i, [128, 8, 128], BF16) for i in range(4)]
            psB = [pst(stA, "psB%d" % i, [128, 512]) for i in range(4)]
            ctr = {"w": 0, "ps": 0}

            w_items = []
            for which_ in range(2):
                for h_ in range(4):
                    w_items.append((win_d, 1536 + which_ * 512 + h_ * 128))
                    w_items.append((wqkp_d, which_ * 512 + h_ * 128))
            for cc_ in range(12):
                w_items.append((win_d, cc_ * 128))
            for gc_ in range(16):
                w_items.append((win_d, 3072 + gc_ * 128))
            for c4_ in range(4):
                w_items.append((win_d, 2560 + c4_ * 128))
            ctr["issued"] = 0

            def _issue_w():
                i = ctr["issued"]
                if i >= len(w_items):
                    return
                ctr["issued"] += 1
                src_d, col0 = w_items[i]
                P.ld("sp", wf[i % 2][:], src_d.ap()[:, col0:col0 + 128].rearrange("(kc p) c -> p kc c", p=128))
                P.cp("pool", wb[i % 4][:], wf[i % 2][:])

            def load_w(src_d, col0):
                i = ctr["w"]
                ctr["w"] += 1
                assert w_items[i][1] == col0 and w_items[i][0] is src_d
                while ctr["issued"] <= min(i + 2, len(w_items) - 1):
                    _issue_w()
                return wb[i % 4]

            def proj_fm(wt, T):
                ps = psB[ctr["ps"] % 4]
                ctr["ps"] += 1
                for kc in range(8):
                    P.mm(ps[:], wt[:, kc, :], xmT[:, kc, T * 512:(T + 1) * 512], start=(kc == 0), stop=(kc == 7),
                         sig=(kc == 7))
                return ps

            with contextlib.ExitStack() as st:
                cosT = sbt(st, "cosT", [128, L], F32)
                sinT = sbt(st, "sinT", [128, L], F32)
                ra = [sbt(st, "ra%d" % i, [128, 512], F32) for i in range(2)]
                rb = [sbt(st, "rb%d" % i, [128, 512], F32) for i in range(2)]
                qrow = [sbt(st, "qrow%d" % i, [128, LK], BF16) for i in range(2)]
                P.ld("sp", cosT[:], cos_d.ap())
                P.ld("sp", sinT[:], sin_d.ap())
                n = 0
                for which in range(2):
                    for h in range(4):
                        col = 1536 + which * 512 + h * 128
                        w_main = load_w(win_d, col)
                        w_perm = load_w(wqkp_d, which * 512 + h * 128)
                        row = qrow[n % 2]
                        off = CTX if which == 1 else 0
                        if which == 1:
                            ps = psB[ctr["ps"] % 4]
                            ctr["ps"] += 1
                            for kc in range(8):
                                P.mm(ps[:, 0:CTX], w_main[:, kc, :], xcT[:, kc, :], start=(kc == 0), stop=(kc == 7),
                                     sig=(kc == 7))
                            P.cp("act", row[:, 0:CTX], ps[:, 0:CTX])
                        for T in range(8):
                            pa = proj_fm(w_main, T)
                            pb_ = proj_fm(w_perm, T)
                            P.tt("dve", ra[T % 2][:], pa[:], cosT[:, T * 512:(T + 1) * 512], ALU.mult)
                            P.tt("dve", rb[T % 2][:], pb_[:], sinT[:, T * 512:(T + 1) * 512], ALU.mult)
                            P.tt("pool", row[:, off + T * 512:off + (T + 1) * 512], ra[T % 2][:], rb[T % 2][:], ALU.add)
                        if which == 0:
                            P.ld("sp", qT_s.ap()[h], row[:, 0:L])
                        else:
                            P.ld("sp", kT_s.ap()[h], row[:, 0:LK])
                        n += 1
                P.barrier()

            with contextlib.ExitStack() as st:
                zrows = [sbt(st, "zrow%d" % i, [128, L + 2], F32) for i in range(2)]
                t1 = sbt(st, "t1", [128, L], F32)
                urows = [sbt(st, "urow%d" % i, [128, L], BF16) for i in range(2)]
                ucc = [sbt(st, "ucc%d" % i, [128, 128, 32], BF16) for i in range(2)]
                hcw = sbt(st, "hcw", [128, 12, 3], F32)
                hcb = sbt(st, "hcb", [128, 12], F32)
                pU = [pst(st, "pU%d" % i, [128, 4, 128], BF16) for i in range(2)]
                P.ld("sp", hcw[:], hcw_d.ap())
                P.ld("sp", hcb[:], hcb_d.ap())
                for zrow in zrows:
                    P.memset("pool", zrow[:, 0:1], 0.0)
                    P.memset("pool", zrow[:, L + 1:L + 2], 0.0)
                def b2_phase2(cc):
                    urow = urows[cc % 2]
                    u = ucc[cc % 2]
                    for jq in range(8):
                        pp = pU[jq % 2]
                        for jj in range(4):
                            j = jq * 4 + jj
                            P.tr(pp[:, jj, :], urow[:, j * 128:(j + 1) * 128], ident[:], sig=(jj == 3))
                        P.cp("act", u[:, :, jq * 4:(jq + 1) * 4], pp[:].rearrange("p j c -> p c j"))
                    P.ld("sp", hyu_s.ap()[:, cc * 4096:(cc + 1) * 4096], u[:].rearrange("p c j -> p (c j)"))

                for cc in range(12):
                    zrow = zrows[cc % 2]
                    urow = urows[cc % 2]
                    wt = load_w(win_d, cc * 128)
                    for T in range(8):
                        ps = proj_fm(wt, T)
                        P.cp("act", zrow[:, 1 + T * 512:1 + (T + 1) * 512], ps[:])
                    P.ts("pool", t1[:], zrow[:, 0:L], hcw[:, cc, 0:1], hcb[:, cc:cc + 1], ALU.mult, ALU.add)
                    P.stt("dve", t1[:], zrow[:, 1:L + 1], hcw[:, cc, 1:2], t1[:], ALU.mult, ALU.add)
                    P.stt("dve", urow[:], zrow[:, 2:L + 2], hcw[:, cc, 2:3], t1[:], ALU.mult, ALU.add)
                    if cc > 0:
                        b2_phase2(cc - 1)
                    if cc == 11:
                        b2_phase2(cc)
                P.barrier()

            with contextlib.ExitStack() as st:
                grow = [sbt(st, "grow%d" % i, [128, L], BF16) for i in range(2)]
                wv = sbt(st, "wv", [128, 8, 512], BF16)
                vrow = [sbt(st, "vrow%d" % i, [128, 512], BF16) for i in range(2)]
                for gc in range(16):
                    wt = load_w(win_d, 3072 + gc * 128)
                    g = grow[gc % 2]
                    for T in range(8):
                        ps = proj_fm(wt, T)
                        P.act(g[:, T * 512:(T + 1) * 512], ps[:], AF.Sigmoid)
                    P.ld("sp", gat_s.ap()[gc], g[:])
                for c4 in range(4):
                    wt = load_w(win_d, 2560 + c4 * 128)
                    P.cp("dve", wv[:, :, c4 * 128:(c4 + 1) * 128], wt[:])
                for t in range(NTT):
                    ps = psB[ctr["ps"] % 4]
                    ctr["ps"] += 1
                    for kc in range(8):
                        lhs = xcT[:, kc, t * 128:(t + 1) * 128] if t < 2 else xmT[:, kc, (t - 2) * 128:(t - 1) * 128]
                        P.mm(ps[:], lhs, wv[:, kc, :], start=(kc == 0), stop=(kc == 7), sig=(kc == 7))
                    v = vrow[t % 2]
                    P.cp("act", v[:], ps[:])
                    P.ld("sp", V_s.ap()[t], v[:])
                P.barrier()

        with contextlib.ExitStack() as st:
            fw1 = sbt(st, "fw1", [33, 64], F32); fw2 = sbt(st, "fw2", [64, 64], F32); fw3 = sbt(st, "fw3", [64, 64], F32)
            fwo = sbt(st, "fwo", [64, 2048], F32)
            ffr = sbt(st, "ffr", [64, 3], F32); fbt = sbt(st, "fbt", [64, 3], F32); ffb = sbt(st, "ffb", [64, 3], F32)
            zp = sbt(st, "zp", [33, L], F32)
            hA = sbt(st, "hA", [64, L], F32); hB = sbt(st, "hB", [64, L], F32)
            targ = [sbt(st, "targ%d" % i, [64, 2048], F32) for i in range(2)]
            targm = [sbt(st, "targm%d" % i, [64, 2048], F32) for i in range(2)]
            dct = [sbt(st, "dct%d" % i, [128, L], F32) for i in range(2)]
            arow = [sbt(st, "arow%d" % i, [128, L], BF16) for i in range(2)]
            pf = [pst(st, "pf%d" % i, [128, 2048]) for i in range(2)]
            P.ld("sp", fw1[:], fw1_d.ap()); P.ld("sp", fw2[:], fw2_d.ap()); P.ld("sp", fw3[:], fw3_d.ap())
            P.ld("act", fwo[:], fwo_d.ap()); P.ld("sp", ffr[:], ffr_d.ap()); P.ld("sp", fbt[:], fb_d.ap())
            P.tt("dve", ffb[:], ffr[:], fbt[:], ALU.mult)
            npf = 0
            nrow = 0
            for ev in range(2):
                P.ld("sp", zp[:], (zpf_d if ev == 0 else zpr_d).ap())
                srcs = [(zp, fw1, 33), (hA, fw2, 64), (hB, fw3, 64)]
                dsts = [hA, hB, hA]
                for li in range(3):
                    src, wt, kk = srcs[li]
                    dst = dsts[li]
                    for hf in range(2):
                        ps = pf[npf % 2]
                        ta = targ[npf % 2]
                        tm = targm[npf % 2]
                        npf += 1
                        for q4 in range(4):
                            col = hf * 2048 + q4 * 512
                            P.mm(ps[0:64, q4 * 512:(q4 + 1) * 512], wt[0:kk, :], src[0:kk, col:col + 512], sig=(q4 == 3))
                        P.ts("dve", ta[:], ps[0:64, :], ffr[:, li:li + 1], ffb[:, li:li + 1], ALU.mult, ALU.add)
                        P.ts("dve", tm[:], ta[:], PI, -2.0 * PI, ALU.is_gt, ALU.mult)
                        P.tt("dve", ta[:], ta[:], tm[:], ALU.add)
                        P.ts("dve", tm[:], ta[:], -PI, 2.0 * PI, ALU.is_lt, ALU.mult)
                        P.tt("dve", ta[:], ta[:], tm[:], ALU.add)
                        P.ts("dve", ta[:], ta[:], PI, -PI, ALU.min, ALU.max)
                        P.act(dst[:, hf * 2048:(hf + 1) * 2048], ta[:], AF.Sin)
                for o in range(2):
                    for cc in range(4):
                        dc = dct[nrow % 2]
                        ar = arow[nrow % 2]
                        nrow += 1
                        P.ld("sp" if nrow % 2 == 0 else "act", dc[:],
                             (dcf_d if ev == 0 else dcr_d).ap()[cc * 128:(cc + 1) * 128, :])
                        col = o * 1024 + ev * 512 + cc * 128
                        for hf in range(2):
                            ps = pf[npf % 2]
                            npf += 1
                            for q4 in range(4):
                                c_ = hf * 2048 + q4 * 512
                                P.mm(ps[:, q4 * 512:(q4 + 1) * 512], fwo[:, col:col + 128], hA[:, c_:c_ + 512], sig=(q4 == 3))
                            P.tt("dve", ar[:, hf * 2048:(hf + 1) * 2048], ps[:], dc[:, hf * 2048:(hf + 1) * 2048], ALU.mult)
                        rows = A_s.ap()[o * HYW + cc * 128:o * HYW + (cc + 1) * 128, :]
                        if ev == 0:
                            P.ld("sp", rows[:, 4095:8191], ar[:])
                        else:
                            P.ld("sp", rows[:, 0:4095], ar[:, 0:4095])
            P.barrier()

        G = 16
        S = 4
        KB = 128 // S
        TW = 8192 - KB
        NG = G // S
        with contextlib.ExitStack() as stY:
            yall = sbt(stY, "yall", [128, HYW, 32], BF16)
            with contextlib.ExitStack() as st:
                qh = sbt(st, "qh", [128, L], BF16)
                kh = sbt(st, "kh", [128, LK], BF16)
                Vh = sbt(st, "Vh", [128, NTT, 128], BF16)
                ones_f = sbt(st, "ones_f", [128, 128], F32)
                ones_b = sbt(st, "ones_b", [128, 128], BF16)
                NR = 4
                PT = [sbt(st, "PT%d" % i, [128, 512], BF16) for i in range(NR)]
                rc = [sbt(st, "rc%d" % i, [128, 512], F32) for i in range(2)]
                acc = [sbt(st, "acc%d" % i, [128, 512], F32) for i in range(2)]
                on = [sbt(st, "on%d" % i, [128, 512], F32) for i in range(4)]
                oc = [sbt(st, "oc%d" % i, [128, 512], F32) for i in range(2)]
                osq = [sbt(st, "osq%d" % i, [128, 512], F32) for i in range(2)]
                rst = [sbt(st, "rst%d" % i, [128, 512], F32) for i in range(2)]
                orow = [sbt(st, "orow%d" % i, [128, 512], BF16) for i in range(2)]
                pS = [pst(st, "pS%d" % i, [128, 512]) for i in range(2)]
                pO = [pst(st, "pO%d" % i, [128, 512]) for i in range(2)]
                pSm = [pst(st, "pSm%d" % i, [128, 512]) for i in range(2)]
                NT = 4
                tsk = [sbt(st, "tsk%d" % i, [128, TW], BF16) for i in range(NT)]
                rself = sbt(st, "rself", [128, KB], F32)
                fsel = sbt(st, "fsel", [128, S, S, 128], BF16)
                b0 = sbt(st, "b0", [128, HYW], F32); b1 = sbt(st, "b1", [128, HYW], F32)
                vg = [sbt(st, "vg%d" % i, [128, G, 32], BF16) for i in range(2)]
                x1g = [sbt(st, "x1g%d" % i, [128, G, 32], BF16) for i in range(2)]
                x2g = [sbt(st, "x2g%d" % i, [128, G, 32], BF16) for i in range(2)]
                vr = sbt(st, "vr", [128, NG, S, 32, S], BF16)
                vb = sbt(st, "vb", [128, G, 32], F32)
                tmp = sbt(st, "tmpE", [128, G, 32], F32)
                zz = sbt(st, "zz", [128, G, 32], BF16)
                zr = sbt(st, "zr", [128, NG, S, 32, S], BF16)
                zb = sbt(st, "zb", [128, G, 32], F32)
                pc = [pst(st, "pc%d" % o, [128, NG, 32, S]) for o in range(2)]
                pj = pc[1]

                P.memset("pool", ones_f[:], 1.0)
                P.memset("pool", ones_b[:], 1.0)
                P.ld("sp", b0[:], hyb_d.ap()[0].partition_broadcast(128))
                P.ld("sp", b1[:], hyb_d.ap()[1].partition_broadcast(128))
                hy3 = hyu_s.ap().rearrange("p (c j) -> p c j", j=32)
                P.memset("pool", fsel[:], 0.0)
                for hi_ in range(S):
                    P.memset("pool", rself[:], 0.0)
                    P.op("pool", (lambda e, b_=-(KB * hi_ + KB - 1): e.affine_select(
                        out=rself[:], in_=rself[:], pattern=[[1, KB]], compare_op=ALU.not_equal, fill=1.0,
                        base=b_, channel_multiplier=1)), reads=[rself.name], writes=[rself.name])
                    for sl_ in range(S):
                        P.cp("dve", fsel[:, hi_, sl_, KB * sl_:KB * sl_ + KB], rself[:])

                steps = [(h, Q, c, kb) for h in range(4) for Q in range(8) for c in range(2) for kb in range(NTT)]
                NS = len(steps)

                def gen_C():
                    pending = []
                    state = {"head": -1, "qk": -1}

                    def load_head(h):
                        P.ld("sp", qh[:], qT_s.ap()[h])
                        P.ld("sp", kh[:], kT_s.ap()[h])
                        for t0_ in (0, 17):
                            P.ld("sp", Vh[:, t0_:t0_ + 17, :],
                                 V_s.ap()[t0_:t0_ + 17, :, h * 128:(h + 1) * 128].rearrange("t p v -> p t v"))

                    def ensure_qk(m):
                        if m <= state["qk"] or m >= NS:
                            return
                        h, Q, c, kb = steps[m]
                        if h != state["head"]:
                            load_head(h)
                            state["head"] = h
                        P.mm(pS[m % 2][:], kh[64 * c:64 * c + 64, kb * 128:(kb + 1) * 128],
                             qh[64 * c:64 * c + 64, Q * 512:(Q + 1) * 512])
                        P.act(PT[m % NR][:], pS[m % 2][:], AF.Exp, scale=0.125)
                        state["qk"] = m

                    def fin1(h, Q, c, s_):
                        g = (h * 8 + Q) % 2
                        P.mm(pSm[s_][:], ones_f[:], acc[s_][:])
                        P.op("dve", (lambda e, a=rc[c][:], b=pSm[s_][:]: e.reciprocal(out=a, in_=b)),
                             reads=[pSm[s_].name], writes=[rc[c].name])
                        P.tt("dve", on[2 * g + c][:], pO[s_][:], rc[c][:], ALU.mult)
                        if c == 1:
                            P.stt("dve", oc[g][:], on[2 * g + 1][:], neglam[:], on[2 * g][:], ALU.mult, ALU.add)
                            P.tt("pool", osq[g][:], oc[g][:], oc[g][:], ALU.mult)

                    def fin2(h, Q, s_):
                        g = (h * 8 + Q) % 2
                        P.mm(pSm[s_][:], ones_f[:], osq[g][:])
                        P.act(rst[g][:], pSm[s_][:], AF.Sqrt, bias=epsb[:], scale=1.0 / 128.0)
                        P.op("dve", (lambda e, a=rst[g][:]: e.reciprocal(out=a, in_=a)),
                             reads=[rst[g].name], writes=[rst[g].name])
                        P.stt("dve", orow[g][:], oc[g][:], subg[:], rst[g][:], ALU.mult, ALU.mult)
                        P.ld("pool", oT_s.ap()[h][:, Q * 512:(Q + 1) * 512], orow[g][:])

                    for n in range(NS):
                        h, Q, c, kb = steps[n]
                        s_ = (n // NTT) % 2
                        ensure_qk(n)
                        if n + 1 < NS and steps[n + 1][0] == h:
                            ensure_qk(n + 1)
                        P.mm(pO[s_][:], Vh[:, kb, :], PT[n % NR][:], start=(kb == 0), stop=(kb == NTT - 1),
                             sig=(kb == NTT - 1))
                        if kb == 0:
                            P.cp("dve", acc[s_][:], PT[n % NR][:])
                        else:
                            P.tt("dve", acc[s_][:], acc[s_][:], PT[n % NR][:], ALU.add)
                        for item in list(pending):
                            if item[0] <= n:
                                item[1]()
                                pending.remove(item)
                        if kb == NTT - 1:
                            pending.append((n + 3, (lambda h=h, Q=Q, c=c, s_=s_: fin1(h, Q, c, s_))))
                            if c == 1:
                                pending.append((n + 10, (lambda h=h, Q=Q, s_=s_: fin2(h, Q, s_))))
                            if n + 1 < NS and steps[n + 1][0] != h:
                                for item in pending:
                                    item[1]()
                                pending = []
                        yield
                    for item in pending:
                        item[1]()

                elist = [0] + [e for e in range(-(32 * S - 1), 31 * S + 1) if e != 0]
                ectr = {"tsk": 0}

                def conv_grp(o, c, rhs_t, grp, pb):
                    tk = tsk[ectr["tsk"] % NT]
                    ectr["tsk"] += 1
                    keys = []
                    for sl in range(S):
                        q = "sp"
                        key = tk.name + "_s%d" % sl
                        keys.append(key)
                        P.dma(q, (lambda e, o_=tk[KB * sl:KB * sl + KB, :],
                                         i_=bass.AP(A_s, (o * HYW + c + sl) * 8192, [[1, KB], [1, TW]]):
                                     e.dma_start(out=o_, in_=i_)),
                              reads=[], writes=[key])
                    last = len(elist) - 1
                    for n_, e in enumerate(elist):
                        i_lo = max(0, -((-e) // S))
                        i_hi = min(31, (32 * S - 1 + e) // S)
                        nn = i_hi - i_lo + 1
                        hi = (-e) % S
                        j0 = i_lo - (e + hi) // S
                        assert nn > 0 and (e + hi) % S == 0 and 0 <= j0 and j0 + nn <= 32
                        ce = KB * e + 4096 - KB
                        assert 0 <= ce and ce + 128 <= TW
                        P.op("pe", (lambda en, o_=pb[:, grp, i_lo:i_hi + 1, :].rearrange("p i s -> p (i s)"),
                                           l_=tk[:, ce:ce + 128],
                                           r_=rhs_t[:, grp, hi, j0:j0 + nn, :].rearrange("p j s -> p (j s)"),
                                           s0=(n_ == 0), s1=(n_ == last): en.matmul(o_, l_, r_, start=s0, stop=s1)),
                             reads=keys + [rhs_t.name], writes=[pb.name], sig=(n_ == last))
                        if n_ % 28 == 27:
                            yield

                def reverse(dst, src):
                    sv = src[:].rearrange("p (g s) j -> p g s j", s=S)
                    pjf = pj[:].rearrange("p g i s -> p (g i s)")
                    for hi in range(S):
                        for sl in range(S):
                            P.mm(pjf[:, sl * NG * 32:(sl + 1) * NG * 32], fsel[:, hi, sl, :], sv[:, :, sl, :],
                                 sig=(sl == S - 1))
                        P.cp("act", dst[:, :, hi, :, :],
                             pjf.rearrange("p (s g j) -> p g j s", s=S, g=NG))

                def gen_E():
                    for g in range(HYW // G):
                        i = g % 2
                        c0 = g * G
                        P.ld("sp", vg[i][:], hy3[:, c0:c0 + G, :])
                        P.ld("sp", x1g[i][:], hy3[:, HYW + c0:HYW + c0 + G, :])
                        P.ld("sp", x2g[i][:], hy3[:, 2 * HYW + c0:2 * HYW + c0 + G, :])
                        reverse(vr, vg[i])
                        P.tt("pool", vb[:], vg[i][:], b0[:, c0:c0 + G].unsqueeze(2).to_broadcast([128, G, 32]), ALU.mult)
                        for grp in range(NG):
                            yield from conv_grp(0, c0 + S * grp, vr, grp, pc[0])
                        P.tt("dve", tmp[:].rearrange("p (q h) i -> p q h i", h=S), pc[0][:].rearrange("p q i h -> p q h i"),
                             vb[:].rearrange("p (q h) i -> p q h i", h=S), ALU.add)
                        P.tt("dve", zz[:], tmp[:], x1g[i][:], ALU.mult)
                        reverse(zr, zz)
                        P.tt("pool", zb[:], zz[:], b1[:, c0:c0 + G].unsqueeze(2).to_broadcast([128, G, 32]), ALU.mult)
                        for grp in range(NG):
                            yield from conv_grp(1, c0 + S * grp, zr, grp, pc[1])
                        P.tt("dve", tmp[:].rearrange("p (q h) i -> p q h i", h=S), pc[1][:].rearrange("p q i h -> p q h i"),
                             zb[:].rearrange("p (q h) i -> p q h i", h=S), ALU.add)
                        P.tt("dve", yall[:, c0:c0 + G, :], tmp[:], x2g[i][:], ALU.mult)
                        yield

                gC = gen_C()
                gE = gen_E()
                doneC = doneE = False
                while not (doneC and doneE):
                    if not doneE:
                        try:
                            next(gE)
                        except StopIteration:
                            doneE = True
                    if not doneC:
                        try:
                            next(gC)
                        except StopIteration:
                            doneC = True
                P.barrier()

            with contextlib.ExitStack() as st:
                yrow = [sbt(st, "yrow%d" % i, [128, 512], BF16) for i in range(2)]
                pY = [pst(st, "pY%d" % i, [128, 4, 128], BF16) for i in range(2)]
                n = 0
                for cc in range(4):
                    for iq in range(8):
                        for ii in range(4):
                            P.tr(pY[n % 2][:, ii, :], yall[:, cc * 128:(cc + 1) * 128, iq * 4 + ii], ident[:], sig=(ii == 3))
                        r = yrow[n % 2]
                        P.cp("act", r[:], pY[n % 2][:].rearrange("p a b -> p (a b)"))
                        P.ld("sp", yhT_s.ap()[cc][:, iq * 512:(iq + 1) * 512], r[:])
                        n += 1
                P.barrier()

        def stream_weight_bf16(wt, src_d, nk, ncol, wfs):
            n = 0
            for kc in range(nk):
                for c0 in range(0, ncol, 1024):
                    s_ = wfs[n % 2]
                    n += 1
                    P.ld("sp" if n % 2 == 0 else "act", s_[:], src_d.ap()[kc * 128:(kc + 1) * 128, c0:c0 + 1024])
                    P.cp("dve" if n % 2 == 0 else "act", wt[:, kc, c0:c0 + 1024], s_[:])
            return wt

        def residual_ln(pfx, st_tiles, ps_lo, ps_hi, gbc_mod, res_src_ap, lg, lb, dst):
            rs, r, stats, mv, rstd, tmp = st_tiles
            P.ld("sp", rs[:], res_src_ap)
            P.tt("dve", r[:, 0:512], ps_lo[:], gbc_mod[:, 0:512], ALU.mult)
            P.tt("dve", r[:, 512:1024], ps_hi[:], gbc_mod[:, 512:1024], ALU.mult)
            P.stt("dve", r[:], rs[:], ALPHA, r[:], ALU.mult, ALU.add)
            layer_norm_tile(pfx, r, lg, lb, dst, stats, mv, rstd, tmp)

        with contextlib.ExitStack() as stF:
            x1mT = sbt(stF, "x1mT", [128, 8, L], BF16)
            with contextlib.ExitStack() as st:
                who = sbt(st, "who", [128, 4, D], BF16)
                wao = sbt(st, "wao", [128, 4, D], BF16)
                wo = sbt(st, "wo", [128, 8, D], BF16)
                with contextlib.ExitStack() as stw:
                    wfs = [sbt(stw, "wfs%d" % i, [128, 1024], F32) for i in range(2)]
                    stream_weight_bf16(who, who_d, 4, D, wfs)
                    stream_weight_bf16(wao, wao_d, 4, D, wfs)
                    stream_weight_bf16(wo, wout_d, 8, D, wfs)
                    P.barrier()
                l1g = sbt(st, "l1g", [128, D], F32); l1b = sbt(st, "l1b", [128, D], F32)
                P.ld("sp", l1g[:], l1g_d.ap().partition_broadcast(128))
                P.ld("sp", l1b[:], l1b_d.ap().partition_broadcast(128))
                yh = [sbt(st, "yh%d" % i, [128, 4, 512], BF16) for i in range(2)]
                ot = [sbt(st, "ot%d" % i, [128, 4, 512], BF16) for i in range(2)]
                gt = [sbt(st, "gt%d" % i, [128, 16, 512], BF16) for i in range(2)]
                mT = sbt(st, "mT", [128, 8, 512], BF16)
                m1 = sbt(st, "m1", [128, 512], F32); m2 = sbt(st, "m2", [128, 512], F32)
                lnt = [(sbt(st, "rsF%d" % i, [128, D], F32), sbt(st, "rF%d" % i, [128, D], F32),
                        sbt(st, "statsF%d" % i, [128, 2, 6], F32), sbt(st, "mvF%d" % i, [128, 2], F32),
                        sbt(st, "rstdF%d" % i, [128, 1], F32), sbt(st, "tmpF%d" % i, [128, D], F32)) for i in range(2)]
                x1t = [sbt(st, "x1t0", [128, D], F32)] * 2
                x1bs = [sbt(st, "x1b%d" % i, [128, D], BF16) for i in range(2)]
                pA = [pst(st, "pA%d" % i, [128, 512]) for i in range(2)]
                pB2 = [pst(st, "pB2%d" % i, [128, 512]) for i in range(2)]
                pYl = [pst(st, "pYl%d" % i, [128, 512]) for i in range(2)]
                pT2 = pst(st, "pT2", [128, 8, 128], BF16)

                def load_T(T):
                    i = T % 2
                    sl = slice(T * 512, (T + 1) * 512)
                    P.ld("sp", yh[i][:], yhT_s.ap()[:, :, sl].rearrange("c p t -> p c t"))
                    P.ld("act", ot[i][:], oT_s.ap()[:, :, sl].rearrange("c p t -> p c t"))
                    P.ld("sp", gt[i][:], gat_s.ap()[:, :, sl].rearrange("g p t -> p g t"))

                def transposes(x1b, tok0):
                    for kc in range(8):
                        P.tr(pT2[:, kc, :], x1b[:, kc * 128:(kc + 1) * 128], ident[:], sig=(kc == 7))
                    for kc in range(8):
                        P.act(x1mT[:, kc, tok0:tok0 + 128], pT2[:, kc, :], AF.Identity,
                              bias=modT[:, 24 + kc, 0:1], scale=onep[:, 8 + kc, 0:1])

                nt = 0
                prev = None
                load_T(0)
                for T in range(8):
                    i = T % 2
                    if T + 1 < 8:
                        load_T(T + 1)
                    for fc in range(8):
                        a = pA[fc % 2]; b = pB2[fc % 2]
                        for cc in range(4):
                            P.mm(a[:], who[:, cc, fc * 128:(fc + 1) * 128], yh[i][:, cc, :], start=(cc == 0), stop=(cc == 3),
                                 sig=(cc == 3))
                        for cc in range(4):
                            P.mm(b[:], wao[:, cc, fc * 128:(fc + 1) * 128], ot[i][:, cc, :], start=(cc == 0), stop=(cc == 3),
                                 sig=(cc == 3))
                        P.tt("dve", m1[:], a[:], gt[i][:, fc, :], ALU.mult)
                        P.tt("dve", m2[:], b[:], gt[i][:, 8 + fc, :], ALU.mult)
                        P.tt("pool", mT[:, fc, :], m1[:], m2[:], ALU.add)
                    for tb in range(4):
                        tok0 = T * 512 + tb * 128
                        for hf in range(2):
                            for kc in range(8):
                                P.mm(pYl[hf][:], mT[:, kc, tb * 128:(tb + 1) * 128], wo[:, kc, hf * 512:(hf + 1) * 512],
                                     start=(kc == 0), stop=(kc == 7), sig=(kc == 7))
                        if prev is not None:
                            transposes(*prev)
                        xo = x1t[nt % 2]
                        x1b = x1bs[nt % 2]
                        lnt_ = lnt[nt % 2]
                        nt += 1
                        residual_ln("F", lnt_, pYl[0], pYl[1], g1bc,
                                    xln_s.ap()[tok0:tok0 + 128, :], l1g, l1b, xo)
                        P.ld("pool", x1_s.ap()[tok0:tok0 + 128, :], xo[:])
                        P.cp("act", x1b[:], xo[:])
                        prev = (x1b, tok0)
                transposes(*prev)
                P.barrier()

            with contextlib.ExitStack() as st:
                wuf = [sbt(st, "wuf%d" % i, [128, 8, 128], F32) for i in range(2)]
                wub = [sbt(st, "wub%d" % i, [128, 8, 128], BF16) for i in range(4)]
                g_items = []
                for fc_ in range(NFC):
                    for part_ in range(2):
                        g_items.append(part_ * NFC + fc_)
                gctr = {"issued": 0}

                def g_issue():
                    i = gctr["issued"]
                    if i >= len(g_items):
                        return
                    gctr["issued"] += 1
                    ch_ = g_items[i]
                    P.ld("sp", wuf[i % 2][:], wup_d.ap()[:, ch_ * 128:(ch_ + 1) * 128].rearrange("(kc p) c -> p kc c", p=128))
                    P.cp("pool", wub[i % 4][:], wuf[i % 2][:])
                zr2 = [sbt(st, "zr2_%d" % i, [128, L + 2], F32) for i in range(2)]
                tgs = [sbt(st, "tg%d" % i, [128, L], F32) for i in range(2)]
                tas = [sbt(st, "ta_%d" % i, [128, L], F32) for i in range(2)]
                hrow = [sbt(st, "hrow%d" % i, [128, L], BF16) for i in range(2)]
                fcw = sbt(st, "fcw", [128, 44, 3], F32); fcb = sbt(st, "fcb", [128, 44], F32)
                pG = [pst(st, "pG%d" % i, [128, 512]) for i in range(4)]
                P.ld("sp", fcw[:], fcw_d.ap()); P.ld("sp", fcb[:], fcb_d.ap())
                for i in range(2):
                    P.memset("pool", zr2[i][:, 0:1], 0.0)
                    P.memset("pool", zr2[i][:, L + 1:L + 2], 0.0)
                nw = 0
                npg = 0
                for fc in range(NFC):
                    tg = tgs[fc % 2]
                    ta_ = tas[fc % 2]
                    for part in range(2):
                        ch = part * NFC + fc
                        while gctr["issued"] <= min(nw + 2, len(g_items) - 1):
                            g_issue()
                        wbl = wub[nw % 4]
                        nw += 1
                        z = zr2[part]
                        for T in range(8):
                            ps = pG[npg % 4]
                            npg += 1
                            for kc in range(8):
                                P.mm(ps[:], wbl[:, kc, :], x1mT[:, kc, T * 512:(T + 1) * 512], start=(kc == 0), stop=(kc == 7),
                                     sig=(kc == 7))
                            P.cp("act", z[:, 1 + T * 512:1 + (T + 1) * 512], ps[:])
                        dst = ta_ if part == 0 else tg
                        P.ts("pool", dst[:], z[:, 0:L], fcw[:, ch, 0:1], fcb[:, ch:ch + 1], ALU.mult, ALU.add)
                        P.stt("dve", dst[:], z[:, 1:L + 1], fcw[:, ch, 1:2], dst[:], ALU.mult, ALU.add)
                        P.stt("dve", dst[:], z[:, 2:L + 2], fcw[:, ch, 2:3], dst[:], ALU.mult, ALU.add)
                    P.act(tg[:], tg[:], AF.Silu)
                    hr = hrow[fc % 2]
                    P.tt("dve", hr[:], tg[:], ta_[:], ALU.mult)
                    if fc > 0:
                        P.ld("sp", hT_s.ap()[fc - 1], hrow[(fc - 1) % 2][:])
                P.ld("sp", hT_s.ap()[NFC - 1], hrow[(NFC - 1) % 2][:])
                P.barrier()

        with contextlib.ExitStack() as st:
            wfs = [sbt(st, "wfsH%d" % i, [128, 1024], F32) for i in range(2)]
            wd = sbt(st, "wd", [128, NFC, D], BF16)
            stream_weight_bf16(wd, wdn_d, NFC, D, wfs)
            l2g = sbt(st, "l2g", [128, D], F32); l2b = sbt(st, "l2b", [128, D], F32)
            P.ld("sp", l2g[:], l2g_d.ap().partition_broadcast(128))
            P.ld("sp", l2b[:], l2b_d.ap().partition_broadcast(128))
            ht = [sbt(st, "ht%d" % i, [128, NFC, 512], BF16) for i in range(2)]
            lnt = [(sbt(st, "rsH%d" % i, [128, D], F32), sbt(st, "rH%d" % i, [128, D], F32),
                    sbt(st, "statsH%d" % i, [128, 2, 6], F32), sbt(st, "mvH%d" % i, [128, 2], F32),
                    sbt(st, "rstdH%d" % i, [128, 1], F32), sbt(st, "tmpH%d" % i, [128, D], F32)) for i in range(2)]
            xo = [sbt(st, "xoH%d" % i, [128, D], F32) for i in range(2)]
            pD = [[pst(st, "pD%d_%d" % (s, i), [128, 512]) for i in range(2)] for s in range(2)]
            nt = 0

            def load_ht(T):
                P.ld("sp" if T % 2 == 0 else "act", ht[T % 2][:],
                     hT_s.ap()[:, :, T * 512:(T + 1) * 512].rearrange("f p t -> p f t"))

            load_ht(0)
            for T in range(8):
                i = T % 2
                if T + 1 < 8:
                    load_ht(T + 1)
                for tb in range(4):
                    tok0 = T * 512 + tb * 128
                    pp = pD[nt % 2]
                    for hf in range(2):
                        for fc in range(NFC):
                            P.mm(pp[hf][:], ht[i][:, fc, tb * 128:(tb + 1) * 128], wd[:, fc, hf * 512:(hf + 1) * 512],
                                 start=(fc == 0), stop=(fc == NFC - 1), sig=(fc == NFC - 1))
                    o_ = xo[nt % 2]
                    lnt_ = lnt[nt % 2]
                    nt += 1
                    residual_ln("H", lnt_, pp[0], pp[1], g2bc,
                                x1_s.ap()[tok0:tok0 + 128, :], l2g, l2b, o_)
                    P.ld("pool", out_d.ap()[tok0:tok0 + 128, :], o_[:])
            P.barrier()

        P.run()
    return nc


def host_constants():
    f32 = np.float32
    rows = L // 64
    row = np.repeat(np.arange(rows, dtype=f32), 64)
    col = np.tile(np.arange(64, dtype=f32), rows)
    inv = (10000.0 ** (-np.arange(0, 32, 2, dtype=f32) / 32.0)).astype(f32)
    cosT = np.zeros((128, L), f32)
    sinT = np.zeros((128, L), f32)
    for p in range(128):
        d = p % 64
        a = d // 32
        i = d % 32
        f = i % 16
        ang = (row if a == 0 else col) * inv[f]
        cosT[p] = np.cos(ang)
        sinT[p] = (-np.sin(ang)) if i < 16 else np.sin(ang)
    t = np.linspace(0.0, 1.0, L, dtype=f32)[:, None]
    w = (f32(2.0 * math.pi / L) * np.arange(L, dtype=f32))[:, None]
    f = np.linspace(1e-4, 15, 16, dtype=f32)[None, :]
    z = np.concatenate([t, np.cos(f * w), -np.sin(f * w)], -1).astype(f32)
    min_decay = math.log(1e-2) / 1.5
    max_decay = math.log(1e-2) / 0.3
    deltas = np.linspace(min_decay, max_decay, HYW, dtype=f32)
    decay = np.exp(-t * np.abs(deltas)[None, :]).astype(f32)
    return dict(
        rope_cos=cosT, rope_sin=sinT,
        zposT_f=np.ascontiguousarray(z.T), zposT_r=np.ascontiguousarray(z[::-1].T),
        decayT_f=np.ascontiguousarray(decay.T), decayT_r=np.ascontiguousarray(decay[::-1].T),
    )


def per_part(v, nch):
    return np.ascontiguousarray(np.asarray(v, np.float32).reshape(nch, 128).T)


def make_in_maps(inp):
    f32 = np.float32
    g = {k: np.asarray(v, f32) for k, v in inp.items()}
    const = host_constants()
    w_in = g["w_in"][0]
    perm = np.zeros(1024, np.int64)
    for cidx in range(1024):
        base = 1536 + cidx
        d = cidx % 64
        i = d % 32
        perm[cidx] = base + 16 if i < 16 else base - 16
    w_qkp = np.ascontiguousarray(w_in[:, perm])
    cw = g["hy_conv_w"][0]
    hy_cw = np.ascontiguousarray(np.stack([per_part(cw[k], 12) for k in range(3)], -1))
    fw = g["ffn_conv_w"][0]
    ffn_cw = np.ascontiguousarray(np.stack([per_part(fw[k], 44) for k in range(3)], -1))
    shared = dict(
        ln_in_g=g["ln_in_g"], ln_in_b=g["ln_in_b"],
        w_ada=g["w_ada"][0], b_ada=g["b_ada"][0], b_adaT=per_part(g["b_ada"][0], 48),
        w_in=w_in, w_qkp=w_qkp,
        hy_cw=hy_cw, hy_cb=per_part(g["hy_conv_b"][0], 12),
        f_w1=g["hy_f_w1"][0], f_w2=g["hy_f_w2"][0], f_w3=g["hy_f_w3"][0], f_wout=g["hy_f_wout"][0],
        f_freqT=np.ascontiguousarray(g["hy_f_freq"][0].T),
        f_bT=np.ascontiguousarray(np.stack([g["hy_f_b1"][0], g["hy_f_b2"][0], g["hy_f_b3"][0]], -1)),
        hy_bias=g["hy_bias"][0],
        lamv=np.ascontiguousarray(np.stack([g["lam_q1"][0], g["lam_k1"][0], g["lam_q2"][0], g["lam_k2"][0]], 0)),
        subln_g=np.ascontiguousarray(g["at_subln_g"][0].reshape(128, 1)),
        w_hy_o=g["w_hy_o"][0], w_at_o=g["w_at_o"][0], w_out=g["w_out"][0],
        ln1_g=g["ln1_g"][0], ln1_b=g["ln1_b"][0], ln2_g=g["ln2_g"][0], ln2_b=g["ln2_b"][0],
        ffn_w_up=g["ffn_w_up"][0], ffn_cw=ffn_cw, ffn_cb=per_part(g["ffn_conv_b"][0], 44),
        ffn_w_down=g["ffn_w_down"][0],
    )
    shared.update(const)
    maps = []
    cc = per_part(g["c_ctx"], 8)
    for b in range(8):
        m = dict(shared)
        m["x"] = np.ascontiguousarray(g["x"][b])
        m["ctx"] = np.ascontiguousarray(g["ctx"][b])
        m["cT"] = np.ascontiguousarray(np.stack([per_part(g["c"][b], 8), cc], -1))
        maps.append(m)
    return maps


_NC = None


def kernel(**inputs):
    global _NC
    if _NC is None:
        _NC = build_program()
    maps = make_in_maps(inputs)
    res = run_bass_kernel_spmd(_NC, maps, core_ids=list(range(8)))
    out = np.stack([np.asarray(r["out"], np.float32) for r in res.results], 0)
    return out
```

```python
import math
import contextlib
import numpy as np
import concourse.bass as bass
import concourse.mybir as mybir
from concourse.bass_utils import run_bass_kernel_spmd

F32 = mybir.dt.float32
BF16 = mybir.dt.bfloat16
ALU = mybir.AluOpType
AF = mybir.ActivationFunctionType
AX = mybir.AxisListType

ENGS = ("pe", "act", "dve", "pool", "sp")

D = 1024
L = 4096
CTX = 256
NTT = 34
LK = CTX + L
HYW = 512
DFF = 2816
NFC = DFF // 128
ALPHA = 2.0 ** 0.25
LAM_INIT = 0.2
PI = math.pi
DEBUG = False


class Prog:
    NDMA = 6

    def __init__(self, nc):
        self.nc = nc
        self.ops = {e: [] for e in ENGS}
        self.cnt = {e: 0 for e in ENGS}
        self.waited = {e: {} for e in ENGS}
        self.last_w = {}
        self.reads = {}
        self.dma_uses = {}
        self.dma_rr = {e: 0 for e in ENGS}
        self.semkeys = set()
        self.sems = {}

    def _deps(self, eng, reads, writes):
        deps = []
        for k in reads:
            if k in self.last_w:
                deps.append(self.last_w[k])
        for k in writes:
            if k in self.last_w:
                deps.append(self.last_w[k])
            for ev in self.reads.get(k, ()):
                if ev[0] == eng:
                    continue
                deps.append(ev)
        best = {}
        for sk, v in deps:
            if eng == "pe" and sk == "pe":
                continue
            if v > best.get(sk, 0):
                best[sk] = v
        out = []
        w = self.waited[eng]
        for sk, v in best.items():
            if w.get(sk, 0) >= v:
                continue
            w[sk] = v
            out.append((sk, v))
        return out

    def _commit(self, ev, reads, writes):
        for k in reads:
            self.reads.setdefault(k, []).append(ev)
        for k in writes:
            self.last_w[k] = ev
            self.reads[k] = []

    def op(self, eng, fn, reads=(), writes=(), sig=True):
        waits = self._deps(eng, reads, writes)
        if sig:
            self.cnt[eng] += 1
            ev = (eng, self.cnt[eng])
            self.semkeys.add(eng)
            self.ops[eng].append(("op", fn, waits, eng))
        else:
            ev = (eng, self.cnt[eng] + 1)
            self.ops[eng].append(("op", fn, waits, None))
        self._commit(ev, reads, writes)
        return ev

    def dma(self, q, fn, reads=(), writes=()):
        slot = self.dma_rr[q] % self.NDMA
        self.dma_rr[q] += 1
        sk = ("dma", q, slot)
        self.semkeys.add(sk)
        uses = self.dma_uses.get(sk, 0)
        waits = self._deps(q, reads, writes)
        if uses > 0 and self.waited[q].get(sk, 0) < 16 * uses:
            self.waited[q][sk] = 16 * uses
            waits.append((sk, 16 * uses))
        uses += 1
        self.dma_uses[sk] = uses
        ev = (sk, 16 * uses)
        self.ops[q].append(("dma", fn, waits, sk))
        self._commit(ev, reads, writes)
        return ev

    def barrier(self):
        evs = []
        for e in ENGS:
            if self.cnt[e] > 0:
                evs.append((e, self.cnt[e]))
        for sk, uses in self.dma_uses.items():
            evs.append((sk, 16 * uses))
        for e in ENGS:
            waits = []
            for sk, v in evs:
                if sk == e:
                    continue
                if self.waited[e].get(sk, 0) >= v:
                    continue
                self.waited[e][sk] = v
                waits.append((sk, v))
            if waits:
                self.ops[e].append(("wait", None, waits, None))
        self.last_w = {}
        self.reads = {}

    def run(self):
        nc = self.nc
        with contextlib.ExitStack() as st:
            for sk in sorted(self.semkeys, key=str):
                name = sk if isinstance(sk, str) else "d_%s_%d" % (sk[1], sk[2])
                self.sems[sk] = st.enter_context(nc.semaphore("s_" + name))
            block = st.enter_context(nc.Block())

            def mk(ename):
                def body(e):
                    for kind, fn, waits, sk in self.ops[ename]:
                        for wk, wv in waits:
                            e.wait_ge(self.sems[wk], wv)
                        if kind == "wait":
                            continue
                        ins = fn(e)
                        if sk is not None:
                            ins.then_inc(self.sems[sk], 16 if kind == "dma" else 1)
                return body

            block.tensor(mk("pe"))
            block.scalar(mk("act"))
            block.vector(mk("dve"))
            block.gpsimd(mk("pool"))
            block.sync(mk("sp"))

    @staticmethod
    def _k(*aps):
        out = []
        for a in aps:
            if a is None or isinstance(a, (int, float)):
                continue
            out.append(a.name)
        return out

    def mm(self, out, lhsT, rhs, start=True, stop=True, sig=True):
        return self.op("pe", lambda e: e.matmul(out, lhsT, rhs, start=start, stop=stop),
                       reads=self._k(lhsT, rhs), writes=self._k(out), sig=sig)

    def tr(self, out, in_, ident, sig=True):
        return self.op("pe", lambda e: e.transpose(out, in_, ident),
                       reads=self._k(in_, ident), writes=self._k(out), sig=sig)

    def act(self, out, in_, func, bias=None, scale=None, accum_out=None):
        kw = {}
        if bias is not None:
            kw["bias"] = bias
        if scale is not None:
            kw["scale"] = scale
        if accum_out is not None:
            kw["accum_out"] = accum_out
        return self.op("act", lambda e: e.activation(out=out, in_=in_, func=func, **kw),
                       reads=self._k(in_, bias, scale), writes=self._k(out, accum_out))

    def tt(self, eng, out, in0, in1, op):
        return self.op(eng, lambda e: e.tensor_tensor(out=out, in0=in0, in1=in1, op=op),
                       reads=self._k(in0, in1), writes=self._k(out))

    def ts(self, eng, out, in0, s1, s2, op0, op1=None):
        if op1 is None:
            return self.op(eng, lambda e: e.tensor_scalar(out=out, in0=in0, scalar1=s1, scalar2=None, op0=op0),
                           reads=self._k(in0, s1), writes=self._k(out))
        return self.op(eng, lambda e: e.tensor_scalar(out=out, in0=in0, scalar1=s1, scalar2=s2, op0=op0, op1=op1),
                       reads=self._k(in0, s1, s2), writes=self._k(out))

    def stt(self, eng, out, in0, scalar, in1, op0, op1):
        return self.op(eng, lambda e: e.scalar_tensor_tensor(out=out, in0=in0, scalar=scalar, in1=in1, op0=op0, op1=op1),
                       reads=self._k(in0, scalar, in1), writes=self._k(out))

    def cp(self, eng, out, in_):
        if eng == "act":
            return self.act(out, in_, AF.Copy)
        return self.op(eng, lambda e: e.tensor_copy(out=out, in_=in_), reads=self._k(in_), writes=self._k(out))

    def memset(self, eng, ap, val):
        return self.op(eng, lambda e: e.memset(ap, val), writes=self._k(ap))

    def ld(self, q, out, in_):
        return self.dma(q, lambda e: e.dma_start(out=out, in_=in_), reads=self._k(in_), writes=self._k(out))


def build_program():
    nc = bass.Bass("TRN2", target_bir_lowering=False)
    P = Prog(nc)

    def din(name, shape, dt=F32):
        return nc.dram_tensor(name, list(shape), dt, kind="ExternalInput")

    def dscr(name, shape, dt, dbg=False):
        if DEBUG and dbg:
            return nc.dram_tensor(name, list(shape), dt, kind="ExternalOutput")
        return nc.dram_tensor(name, list(shape), dt)

    x_d = din("x", [L, D]); ctx_d = din("ctx", [CTX, D])
    cT_d = din("cT", [128, 8, 2])
    lng_d = din("ln_in_g", [D]); lnb_d = din("ln_in_b", [D])
    wada_d = din("w_ada", [D, 6 * D]); bada_d = din("b_ada", [6 * D]); badaT_d = din("b_adaT", [128, 48])
    win_d = din("w_in", [D, 5120]); wqkp_d = din("w_qkp", [D, 1024])
    cos_d = din("rope_cos", [128, L]); sin_d = din("rope_sin", [128, L])
    hcw_d = din("hy_cw", [128, 12, 3]); hcb_d = din("hy_cb", [128, 12])
    fw1_d = din("f_w1", [33, 64]); fw2_d = din("f_w2", [64, 64]); fw3_d = din("f_w3", [64, 64])
    fwo_d = din("f_wout", [64, 2048]); ffr_d = din("f_freqT", [64, 3]); fb_d = din("f_bT", [64, 3])
    zpf_d = din("zposT_f", [33, L]); zpr_d = din("zposT_r", [33, L])
    dcf_d = din("decayT_f", [HYW, L]); dcr_d = din("decayT_r", [HYW, L])
    hyb_d = din("hy_bias", [2, HYW])
    lam_d = din("lamv", [4, 64])
    sub_d = din("subln_g", [128, 1])
    who_d = din("w_hy_o", [HYW, D]); wao_d = din("w_at_o", [512, D]); wout_d = din("w_out", [D, D])
    l1g_d = din("ln1_g", [D]); l1b_d = din("ln1_b", [D]); l2g_d = din("ln2_g", [D]); l2b_d = din("ln2_b", [D])
    wup_d = din("ffn_w_up", [D, 2 * DFF]); fcw_d = din("ffn_cw", [128, 44, 3]); fcb_d = din("ffn_cb", [128, 44])
    wdn_d = din("ffn_w_down", [DFF, D])
    out_d = nc.dram_tensor("out", [L, D], F32, kind="ExternalOutput")

    xln_s = dscr("xln_s", [L, D], F32, True)
    qT_s = dscr("qT_s", [4, 128, L], BF16, True)
    kT_s = dscr("kT_s", [4, 128, LK], BF16, True)
    V_s = dscr("V_s", [NTT, 128, 512], BF16, True)
    hyu_s = dscr("hyu_s", [128, 1536 * 32], BF16, True)
    gat_s = dscr("gat_s", [16, 128, L], BF16, True)
    A_s = dscr("A_s", [2 * HYW, 8192], BF16, True)
    oT_s = dscr("oT_s", [4, 128, L], BF16, True)
    yhT_s = dscr("yhT_s", [4, 128, L], BF16, True)
    x1_s = dscr("x1_s", [L, D], F32, True)
    hT_s = dscr("hT_s", [NFC, 128, L], BF16, True)

    top = contextlib.ExitStack()

    def sbt(st, name, shape, dt):
        return st.enter_context(nc.sbuf_tensor("sb_" + name, list(shape), dt))

    def pst(st, name, shape, dt=F32):
        return st.enter_context(nc.psum_tensor("ps_" + name, list(shape), dt))

    with top:
        ident = sbt(top, "ident", [128, 128], BF16)
        jrev = sbt(top, "jrev", [128, 128], BF16)
        identf = sbt(top, "identf", [128, 128], F32)
        epsb = sbt(top, "epsb", [128, 1], F32)
        npib = sbt(top, "npib", [128, 1], F32)
        modT = sbt(top, "modT", [128, 48, 2], F32)
        onep = sbt(top, "onep", [128, 16, 2], F32)
        g1bc = sbt(top, "g1bc", [128, D], F32)
        g2bc = sbt(top, "g2bc", [128, D], F32)
        neglam = sbt(top, "neglam", [128, 1], F32)
        subg = sbt(top, "subg", [128, 1], F32)

        P.memset("pool", identf[:], 0.0)
        P.op("pool", lambda e: e.affine_select(out=identf[:], in_=identf[:], pattern=[[-1, 128]],
                                               compare_op=ALU.not_equal, fill=1.0, base=0, channel_multiplier=1),
             reads=[identf.name], writes=[identf.name])
        P.cp("dve", ident[:], identf[:])
        P.memset("pool", identf[:], 0.0)
        P.op("pool", lambda e: e.affine_select(out=identf[:], in_=identf[:], pattern=[[1, 128]],
                                               compare_op=ALU.not_equal, fill=1.0, base=-127, channel_multiplier=1),
             reads=[identf.name], writes=[identf.name])
        P.cp("dve", jrev[:], identf[:])
        P.memset("pool", epsb[:], 1e-5)
        P.memset("pool", npib[:], -PI)

        with contextlib.ExitStack() as st:
            cT = sbt(st, "cT", [128, 8, 2], F32)
            sc = sbt(st, "sc", [128, 8, 2], F32)
            scb = sbt(st, "scb", [128, 8, 128], F32)
            wa = [sbt(st, "wa%d" % i, [128, 8, 512], F32) for i in range(4)]
            wab = [sbt(st, "wab%d" % i, [128, 8, 512], BF16) for i in range(2)]
            sc_b = sbt(st, "sc_b", [128, 8, 2], BF16)
            scb_b = sbt(st, "scb_b", [128, 8, 128], BF16)
            badaT = sbt(st, "badaT", [128, 48], F32)
            bg = sbt(st, "bg", [128, D], F32)
            lamv = sbt(st, "lamv_t", [128, 4, 64], F32)
            lamp = sbt(st, "lamp", [128, 2, 64], F32)
            lams = sbt(st, "lams", [128, 2], F32)
            pm = pst(st, "pm", [128, 48, 2])
            pb = pst(st, "pb", [128, 512])
            P.ld("sp", cT[:], cT_d.ap())
            P.ld("sp", badaT[:], badaT_d.ap())
            P.ld("sp", subg[:], sub_d.ap())
            P.ld("sp", lamv[:].rearrange("p a b -> p (a b)"),
                 lam_d.ap().rearrange("a b -> (a b)").partition_broadcast(128))
            P.act(sc[:], cT[:], AF.Silu)
            P.cp("dve", scb[:], sc[:, :, 0:1].to_broadcast([128, 8, 128]))
            P.cp("dve", sc_b[:], sc[:])
            P.cp("dve", scb_b[:], scb[:])
            P.ts("dve", subg[:], subg[:], 1.0 - LAM_INIT, None, ALU.mult)
            P.tt("dve", lamp[:, 0, :], lamv[:, 0, :], lamv[:, 1, :], ALU.mult)
            P.tt("dve", lamp[:, 1, :], lamv[:, 2, :], lamv[:, 3, :], ALU.mult)
            P.op("dve", lambda e: e.reduce_sum(out=lams[:], in_=lamp[:], axis=AX.X), reads=[lamp.name], writes=[lams.name])
            P.act(lams[:], lams[:], AF.Exp)
            P.tt("dve", neglam[:], lams[:, 1:2], lams[:, 0:1], ALU.subtract)
            P.ts("dve", neglam[:], neglam[:], -LAM_INIT, None, ALU.add)
            for g in range(12):
                w = wa[g % 4]
                P.ld("sp" if g % 2 == 0 else "act", w[:],
                     wada_d.ap()[:, g * 512:(g + 1) * 512].rearrange("(kc p) c -> p kc c", p=128))
                wq = wab[g % 2]
                P.cp("dve" if g % 2 == 0 else "act", wq[:], w[:])
                for jj in range(4):
                    j = g * 4 + jj
                    for kc in range(8):
                        P.mm(pm[:, j, :], wq[:, kc, jj * 128:(jj + 1) * 128], sc_b[:, kc, :],
                             start=(kc == 0), stop=(kc == 7), sig=(kc == 7))
                if g in (4, 5, 10, 11):
                    for kc in range(8):
                        P.mm(pb[:], scb_b[:, kc, :], wq[:, kc, :], start=(kc == 0), stop=(kc == 7), sig=(kc == 7))
                    dst = g1bc if g in (4, 5) else g2bc
                    half = g % 2
                    P.ld("sp", bg[:, 0:512], bada_d.ap()[g * 512:(g + 1) * 512].partition_broadcast(128))
                    P.tt("dve", dst[:, half * 512:(half + 1) * 512], pb[:], bg[:, 0:512], ALU.add)
            P.tt("dve", modT[:], pm[:], badaT[:].unsqueeze(2).to_broadcast([128, 48, 2]), ALU.add)
            P.ts("dve", onep[:, 0:8, :], modT[:, 8:16, :], 1.0, None, ALU.add)
            P.ts("dve", onep[:, 8:16, :], modT[:, 32:40, :], 1.0, None, ALU.add)
            P.barrier()

        def layer_norm_tile(pfx, src, gbc, bbc, dst, stats, mv, rstd, tmp):
            for h in range(2):
                P.op("dve", (lambda e, h=h: e.bn_stats(out=stats[:, h, :], in_=src[:, h * 512:(h + 1) * 512])),
                     reads=[src.name], writes=[stats.name])
            P.op("dve", lambda e: e.bn_aggr(out=mv[:], in_=stats[:].rearrange("p a b -> p (a b)")),
                 reads=[stats.name], writes=[mv.name])
            P.act(rstd[:], mv[:, 1:2], AF.Sqrt, bias=epsb[:], scale=1.0)
            P.op("dve", lambda e: e.reciprocal(out=rstd[:], in_=rstd[:]), reads=[rstd.name], writes=[rstd.name])
            P.ts("dve", tmp[:], src[:], mv[:, 0:1], rstd[:], ALU.subtract, ALU.mult)
            P.tt("pool", tmp[:], tmp[:], gbc[:], ALU.mult)
            P.tt("pool", dst[:], tmp[:], bbc[:], ALU.add)

        with contextlib.ExitStack() as stA:
            xmT = sbt(stA, "xmT", [128, 8, L], BF16)
            xcT = sbt(stA, "xcT", [128, 8, CTX], BF16)
            with contextlib.ExitStack() as st:
                gbc = sbt(st, "gbc", [128, D], F32)
                bbc = sbt(st, "bbc", [128, D], F32)
                xt = [sbt(st, "xt%d" % i, [128, D], F32) for i in range(3)]
                xh = [sbt(st, "xh%d" % i, [128, D], F32) for i in range(3)]
                xl = [sbt(st, "xl%d" % i, [128, D], F32) for i in range(3)]
                xb = [sbt(st, "xb%d" % i, [128, D], BF16) for i in range(2)]
                stats = [sbt(st, "stats%d" % i, [128, 2, 6], F32) for i in range(3)]
                mv = [sbt(st, "mv%d" % i, [128, 2], F32) for i in range(3)]
                rstd = [sbt(st, "rstd%d" % i, [128, 1], F32) for i in range(3)]
                pT = [pst(st, "pT%d" % i, [128, 8, 128], BF16) for i in range(2)]
                P.ld("sp", gbc[:], lng_d.ap().partition_broadcast(128))
                P.ld("sp", bbc[:], lnb_d.ap().partition_broadcast(128))
                def phase2(t):
                    i = t % 3
                    is_ctx = t < 2
                    if not is_ctx:
                        P.ld("pool", xln_s.ap()[(t - 2) * 128:(t - 1) * 128, :], xl[i][:])
                    P.cp("act", xb[t % 2][:], xl[i][:])
                    for kc in range(8):
                        P.tr(pT[t % 2][:, kc, :], xb[t % 2][:, kc * 128:(kc + 1) * 128], ident[:], sig=(kc == 7))
                    w = 1 if is_ctx else 0
                    for kc in range(8):
                        dst = xcT[:, kc, t * 128:(t + 1) * 128] if is_ctx else xmT[:, kc, (t - 2) * 128:(t - 1) * 128]
                        P.act(dst, pT[t % 2][:, kc, :], AF.Identity, bias=modT[:, kc, w:w + 1], scale=onep[:, kc, w:w + 1])

                for t in range(NTT):
                    i = t % 3
                    is_ctx = t < 2
                    src = ctx_d.ap()[t * 128:(t + 1) * 128, :] if is_ctx else x_d.ap()[(t - 2) * 128:(t - 1) * 128, :]
                    P.ld("sp", xt[i][:], src)
                    layer_norm_tile("A", xt[i], gbc, bbc, xl[i], stats[i], mv[i], rstd[i], xh[i])
                    if t > 0:
                        phase2(t - 1)
                phase2(NTT - 1)
                P.barrier()

            wf = [sbt(stA, "wf%d" % i, [128, 8, 128], F32) for i in range(2)]
            wb = [sbt(stA, "wb%d" % i, [128, 8, 128], BF16) for i in range(4)]
            psB = [pst(stA, "psB%d" % i, [128, 512]) for i in range(4)]
            ctr = {"w": 0, "ps": 0}

            w_items = []
            for which_ in range(2):
                for h_ in range(4):
                    w_items.append((win_d, 1536 + which_ * 512 + h_ * 128))
                    w_items.append((wqkp_d, which_ * 512 + h_ * 128))
            for cc_ in range(12):
                w_items.append((win_d, cc_ * 128))
            for gc_ in range(16):
                w_items.append((win_d, 3072 + gc_ * 128))
            for c4_ in range(4):
                w_items.append((win_d, 2560 + c4_ * 128))
            ctr["issued"] = 0

            def _issue_w():
                i = ctr["issued"]
                if i >= len(w_items):
                    return
                ctr["issued"] += 1
                src_d, col0 = w_items[i]
                P.ld("sp", wf[i % 2][:], src_d.ap()[:, col0:col0 + 128].rearrange("(kc p) c -> p kc c", p=128))
                P.cp("pool", wb[i % 4][:], wf[i % 2][:])

            def load_w(src_d, col0):
                i = ctr["w"]
                ctr["w"] += 1
                assert w_items[i][1] == col0 and w_items[i][0] is src_d
                while ctr["issued"] <= min(i + 2, len(w_items) - 1):
                    _issue_w()
                return wb[i % 4]

            def proj_fm(wt, T):
                ps = psB[ctr["ps"] % 4]
                ctr["ps"] += 1
                for kc in range(8):
                    P.mm(ps[:], wt[:, kc, :], xmT[:, kc, T * 512:(T + 1) * 512], start=(kc == 0), stop=(kc == 7),
                         sig=(kc == 7))
                return ps

            with contextlib.ExitStack() as st:
                cosT = sbt(st, "cosT", [128, L], F32)
                sinT = sbt(st, "sinT", [128, L], F32)
                ra = [sbt(st, "ra%d" % i, [128, 512], F32) for i in range(2)]
                rb = [sbt(st, "rb%d" % i, [128, 512], F32) for i in range(2)]
                qrow = [sbt(st, "qrow%d" % i, [128, LK], BF16) for i in range(2)]
                P.ld("sp", cosT[:], cos_d.ap())
                P.ld("sp", sinT[:], sin_d.ap())
                n = 0
                for which in range(2):
                    for h in range(4):
                        col = 1536 + which * 512 + h * 128
                        w_main = load_w(win_d, col)
                        w_perm = load_w(wqkp_d, which * 512 + h * 128)
                        row = qrow[n % 2]
                        off = CTX if which == 1 else 0
                        if which == 1:
                            ps = psB[ctr["ps"] % 4]
                            ctr["ps"] += 1
                            for kc in range(8):
                                P.mm(ps[:, 0:CTX], w_main[:, kc, :], xcT[:, kc, :], start=(kc == 0), stop=(kc == 7),
                                     sig=(kc == 7))
                            P.cp("act", row[:, 0:CTX], ps[:, 0:CTX])
                        for T in range(8):
                            pa = proj_fm(w_main, T)
                            pb_ = proj_fm(w_perm, T)
                            P.tt("dve", ra[T % 2][:], pa[:], cosT[:, T * 512:(T + 1) * 512], ALU.mult)
                            P.tt("dve", rb[T % 2][:], pb_[:], sinT[:, T * 512:(T + 1) * 512], ALU.mult)
                            P.tt("pool", row[:, off + T * 512:off + (T + 1) * 512], ra[T % 2][:], rb[T % 2][:], ALU.add)
                        if which == 0:
                            P.ld("sp", qT_s.ap()[h], row[:, 0:L])
                        else:
                            P.ld("sp", kT_s.ap()[h], row[:, 0:LK])
                        n += 1
                P.barrier()

            with contextlib.ExitStack() as st:
                zrows = [sbt(st, "zrow%d" % i, [128, L + 2], F32) for i in range(2)]
                t1 = sbt(st, "t1", [128, L], F32)
                urows = [sbt(st, "urow%d" % i, [128, L], BF16) for i in range(2)]
                ucc = [sbt(st, "ucc%d" % i, [128, 128, 32], BF16) for i in range(2)]
                hcw = sbt(st, "hcw", [128, 12, 3], F32)
                hcb = sbt(st, "hcb", [128, 12], F32)
                pU = [pst(st, "pU%d" % i, [128, 4, 128], BF16) for i in range(2)]
                P.ld("sp", hcw[:], hcw_d.ap())
                P.ld("sp", hcb[:], hcb_d.ap())
                for zrow in zrows:
                    P.memset("pool", zrow[:, 0:1], 0.0)
                    P.memset("pool", zrow[:, L + 1:L + 2], 0.0)
                def b2_phase2(cc):
                    urow = urows[cc % 2]
                    u = ucc[cc % 2]
                    for jq in range(8):
                        pp = pU[jq % 2]
                        for jj in range(4):
                            j = jq * 4 + jj
                            P.tr(pp[:, jj, :], urow[:, j * 128:(j + 1) * 128], ident[:], sig=(jj == 3))
                        P.cp("act", u[:, :, jq * 4:(jq + 1) * 4], pp[:].rearrange("p j c -> p c j"))
                    P.ld("sp", hyu_s.ap()[:, cc * 4096:(cc + 1) * 4096], u[:].rearrange("p c j -> p (c j)"))

                for cc in range(12):
                    zrow = zrows[cc % 2]
                    urow = urows[cc % 2]
                    wt = load_w(win_d, cc * 128)
                    for T in range(8):
                        ps = proj_fm(wt, T)
                        P.cp("act", zrow[:, 1 + T * 512:1 + (T + 1) * 512], ps[:])
                    P.ts("pool", t1[:], zrow[:, 0:L], hcw[:, cc, 0:1], hcb[:, cc:cc + 1], ALU.mult, ALU.add)
                    P.stt("dve", t1[:], zrow[:, 1:L + 1], hcw[:, cc, 1:2], t1[:], ALU.mult, ALU.add)
                    P.stt("dve", urow[:], zrow[:, 2:L + 2], hcw[:, cc, 2:3], t1[:], ALU.mult, ALU.add)
                    if cc > 0:
                        b2_phase2(cc - 1)
                    if cc == 11:
                        b2_phase2(cc)
                P.barrier()

            with contextlib.ExitStack() as st:
                grow = [sbt(st, "grow%d" % i, [128, L], BF16) for i in range(2)]
                wv = sbt(st, "wv", [128, 8, 512], BF16)
                vrow = [sbt(st, "vrow%d" % i, [128, 512], BF16) for i in range(2)]
                for gc in range(16):
                    wt = load_w(win_d, 3072 + gc * 128)
                    g = grow[gc % 2]
                    for T in range(8):
                        ps = proj_fm(wt, T)
                        P.act(g[:, T * 512:(T + 1) * 512], ps[:], AF.Sigmoid)
                    P.ld("sp", gat_s.ap()[gc], g[:])
                for c4 in range(4):
                    wt = load_w(win_d, 2560 + c4 * 128)
                    P.cp("dve", wv[:, :, c4 * 128:(c4 + 1) * 128], wt[:])
                for t in range(NTT):
                    ps = psB[ctr["ps"] % 4]
                    ctr["ps"] += 1
                    for kc in range(8):
                        lhs = xcT[:, kc, t * 128:(t + 1) * 128] if t < 2 else xmT[:, kc, (t - 2) * 128:(t - 1) * 128]
                        P.mm(ps[:], lhs, wv[:, kc, :], start=(kc == 0), stop=(kc == 7), sig=(kc == 7))
                    v = vrow[t % 2]
                    P.cp("act", v[:], ps[:])
                    P.ld("sp", V_s.ap()[t], v[:])
                P.barrier()

        with contextlib.ExitStack() as st:
            fw1 = sbt(st, "fw1", [33, 64], F32); fw2 = sbt(st, "fw2", [64, 64], F32); fw3 = sbt(st, "fw3", [64, 64], F32)
            fwo = sbt(st, "fwo", [64, 2048], F32)
            ffr = sbt(st, "ffr", [64, 3], F32); fbt = sbt(st, "fbt", [64, 3], F32); ffb = sbt(st, "ffb", [64, 3], F32)
            zp = sbt(st, "zp", [33, L], F32)
            hA = sbt(st, "hA", [64, L], F32); hB = sbt(st, "hB", [64, L], F32)
            targ = [sbt(st, "targ%d" % i, [64, 2048], F32) for i in range(2)]
            targm = [sbt(st, "targm%d" % i, [64, 2048], F32) for i in range(2)]
            dct = [sbt(st, "dct%d" % i, [128, L], F32) for i in range(2)]
            arow = [sbt(st, "arow%d" % i, [128, L], BF16) for i in range(2)]
            pf = [pst(st, "pf%d" % i, [128, 2048]) for i in range(2)]
            fwob = sbt(st, "fwob", [64, 2048], BF16)
            hAb = sbt(st, "hAb", [64, L], BF16)
            P.ld("sp", fw1[:], fw1_d.ap()); P.ld("sp", fw2[:], fw2_d.ap()); P.ld("sp", fw3[:], fw3_d.ap())
            P.ld("act", fwo[:], fwo_d.ap()); P.ld("sp", ffr[:], ffr_d.ap()); P.ld("sp", fbt[:], fb_d.ap())
            P.tt("dve", ffb[:], ffr[:], fbt[:], ALU.mult)
            P.cp("act", fwob[:], fwo[:])
            npf = 0
            nrow = 0
            for ev in range(2):
                P.ld("sp", zp[:], (zpf_d if ev == 0 else zpr_d).ap())
                srcs = [(zp, fw1, 33), (hA, fw2, 64), (hB, fw3, 64)]
                dsts = [hA, hB, hA]
                for li in range(3):
                    src, wt, kk = srcs[li]
                    dst = dsts[li]
                    for hf in range(2):
                        ps = pf[npf % 2]
                        ta = targ[npf % 2]
                        tm = targm[npf % 2]
                        npf += 1
                        for q4 in range(4):
                            col = hf * 2048 + q4 * 512
                            P.mm(ps[0:64, q4 * 512:(q4 + 1) * 512], wt[0:kk, :], src[0:kk, col:col + 512], sig=(q4 == 3))
                        P.ts("dve", ta[:], ps[0:64, :], ffr[:, li:li + 1], ffb[:, li:li + 1], ALU.mult, ALU.add)
                        P.ts("dve", tm[:], ta[:], PI, -2.0 * PI, ALU.is_gt, ALU.mult)
                        P.tt("dve", ta[:], ta[:], tm[:], ALU.add)
                        P.ts("dve", tm[:], ta[:], -PI, 2.0 * PI, ALU.is_lt, ALU.mult)
                        P.tt("dve", ta[:], ta[:], tm[:], ALU.add)
                        P.ts("dve", ta[:], ta[:], PI, -PI, ALU.min, ALU.max)
                        P.act(dst[:, hf * 2048:(hf + 1) * 2048], ta[:], AF.Sin)
                P.cp("act", hAb[:, 0:2048], hA[:, 0:2048])
                P.cp("dve", hAb[:, 2048:4096], hA[:, 2048:4096])
                for o in range(2):
                    for cc in range(4):
                        dc = dct[nrow % 2]
                        ar = arow[nrow % 2]
                        nrow += 1
                        P.ld("sp" if nrow % 2 == 0 else "act", dc[:],
                             (dcf_d if ev == 0 else dcr_d).ap()[cc * 128:(cc + 1) * 128, :])
                        col = o * 1024 + ev * 512 + cc * 128
                        for hf in range(2):
                            ps = pf[npf % 2]
                            npf += 1
                            for q4 in range(4):
                                c_ = hf * 2048 + q4 * 512
                                P.mm(ps[:, q4 * 512:(q4 + 1) * 512], fwob[:, col:col + 128], hAb[:, c_:c_ + 512], sig=(q4 == 3))
                            P.tt("dve", ar[:, hf * 2048:(hf + 1) * 2048], ps[:], dc[:, hf * 2048:(hf + 1) * 2048], ALU.mult)
                        rows = A_s.ap()[o * HYW + cc * 128:o * HYW + (cc + 1) * 128, :]
                        if ev == 0:
                            P.ld("sp", rows[:, 4095:8191], ar[:])
                        else:
                            P.ld("sp", rows[:, 0:4095], ar[:, 0:4095])
            P.barrier()

        G = 16
        S = 4
        KB = 128 // S
        TW = 8192 - KB
        NG = G // S
        with contextlib.ExitStack() as stY:
            yall = sbt(stY, "yall", [128, HYW, 32], BF16)
            with contextlib.ExitStack() as st:
                qh = sbt(st, "qh", [128, L], BF16)
                kh = sbt(st, "kh", [128, LK], BF16)
                Vh = sbt(st, "Vh", [128, NTT, 128], BF16)
                ones_f = sbt(st, "ones_f", [128, 128], F32)
                ones_b = sbt(st, "ones_b", [128, 128], BF16)
                NR = 4
                PT = [sbt(st, "PT%d" % i, [128, 512], BF16) for i in range(NR)]
                rc = [sbt(st, "rc%d" % i, [128, 512], F32) for i in range(2)]
                acc = [sbt(st, "acc%d" % i, [128, 512], F32) for i in range(2)]
                on = [sbt(st, "on%d" % i, [128, 512], F32) for i in range(4)]
                oc = [sbt(st, "oc%d" % i, [128, 512], F32) for i in range(2)]
                osq = [sbt(st, "osq%d" % i, [128, 512], F32) for i in range(2)]
                rst = [sbt(st, "rst%d" % i, [128, 512], F32) for i in range(2)]
                orow = [sbt(st, "orow%d" % i, [128, 512], BF16) for i in range(2)]
                pS = [pst(st, "pS%d" % i, [128, 512]) for i in range(2)]
                pO = [pst(st, "pO%d" % i, [128, 512]) for i in range(2)]
                pSm = [pst(st, "pSm%d" % i, [128, 512]) for i in range(2)]
                NT = 4
                tsk = [sbt(st, "tsk%d" % i, [128, TW], BF16) for i in range(NT)]
                rself = sbt(st, "rself", [128, KB], F32)
                fsel = sbt(st, "fsel", [128, S, S, 128], BF16)
                b0 = sbt(st, "b0", [128, HYW], F32); b1 = sbt(st, "b1", [128, HYW], F32)
                vg = [sbt(st, "vg%d" % i, [128, G, 32], BF16) for i in range(2)]
                x1g = [sbt(st, "x1g%d" % i, [128, G, 32], BF16) for i in range(2)]
                x2g = [sbt(st, "x2g%d" % i, [128, G, 32], BF16) for i in range(2)]
                vr = sbt(st, "vr", [128, NG, S, 32, S], BF16)
                vb = sbt(st, "vb", [128, G, 32], F32)
                tmp = sbt(st, "tmpE", [128, G, 32], F32)
                zz = sbt(st, "zz", [128, G, 32], BF16)
                zr = sbt(st, "zr", [128, NG, S, 32, S], BF16)
                zb = sbt(st, "zb", [128, G, 32], F32)
                pc = [pst(st, "pc%d" % o, [128, NG, 32, S]) for o in range(2)]
                pj = pc[1]

                P.memset("pool", ones_f[:], 1.0)
                P.memset("pool", ones_b[:], 1.0)
                P.ld("sp", b0[:], hyb_d.ap()[0].partition_broadcast(128))
                P.ld("sp", b1[:], hyb_d.ap()[1].partition_broadcast(128))
                hy3 = hyu_s.ap().rearrange("p (c j) -> p c j", j=32)
                P.memset("pool", fsel[:], 0.0)
                for hi_ in range(S):
                    P.memset("pool", rself[:], 0.0)
                    P.op("pool", (lambda e, b_=-(KB * hi_ + KB - 1): e.affine_select(
                        out=rself[:], in_=rself[:], pattern=[[1, KB]], compare_op=ALU.not_equal, fill=1.0,
                        base=b_, channel_multiplier=1)), reads=[rself.name], writes=[rself.name])
                    for sl_ in range(S):
                        P.cp("dve", fsel[:, hi_, sl_, KB * sl_:KB * sl_ + KB], rself[:])

                steps = [(h, Q, c, kb) for h in range(4) for Q in range(8) for c in range(2) for kb in range(NTT)]
                NS = len(steps)

                def gen_C():
                    pending = []
                    state = {"head": -1, "qk": -1}

                    def load_head(h):
                        P.ld("sp", qh[:], qT_s.ap()[h])
                        P.ld("sp", kh[:], kT_s.ap()[h])
                        for t0_ in (0, 17):
                            P.ld("sp", Vh[:, t0_:t0_ + 17, :],
                                 V_s.ap()[t0_:t0_ + 17, :, h * 128:(h + 1) * 128].rearrange("t p v -> p t v"))

                    def ensure_qk(m):
                        if m <= state["qk"] or m >= NS:
                            return
                        h, Q, c, kb = steps[m]
                        if h != state["head"]:
                            load_head(h)
                            state["head"] = h
                        P.mm(pS[m % 2][:], kh[64 * c:64 * c + 64, kb * 128:(kb + 1) * 128],
                             qh[64 * c:64 * c + 64, Q * 512:(Q + 1) * 512])
                        P.act(PT[m % NR][:], pS[m % 2][:], AF.Exp, scale=0.125)
                        state["qk"] = m

                    def fin1(h, Q, c, s_):
                        g = (h * 8 + Q) % 2
                        P.mm(pSm[s_][:], ones_f[:], acc[s_][:])
                        P.op("dve", (lambda e, a=rc[c][:], b=pSm[s_][:]: e.reciprocal(out=a, in_=b)),
                             reads=[pSm[s_].name], writes=[rc[c].name])
                        P.tt("dve", on[2 * g + c][:], pO[s_][:], rc[c][:], ALU.mult)
                        if c == 1:
                            P.stt("dve", oc[g][:], on[2 * g + 1][:], neglam[:], on[2 * g][:], ALU.mult, ALU.add)
                            P.tt("pool", osq[g][:], oc[g][:], oc[g][:], ALU.mult)

                    def fin2(h, Q, s_):
                        g = (h * 8 + Q) % 2
                        P.mm(pSm[s_][:], ones_f[:], osq[g][:])
                        P.act(rst[g][:], pSm[s_][:], AF.Sqrt, bias=epsb[:], scale=1.0 / 128.0)
                        P.op("dve", (lambda e, a=rst[g][:]: e.reciprocal(out=a, in_=a)),
                             reads=[rst[g].name], writes=[rst[g].name])
                        P.stt("dve", orow[g][:], oc[g][:], subg[:], rst[g][:], ALU.mult, ALU.mult)
                        P.ld("pool", oT_s.ap()[h][:, Q * 512:(Q + 1) * 512], orow[g][:])

                    for n in range(NS):
                        h, Q, c, kb = steps[n]
                        s_ = (n // NTT) % 2
                        ensure_qk(n)
                        if n + 1 < NS and steps[n + 1][0] == h:
                            ensure_qk(n + 1)
                        P.mm(pO[s_][:], Vh[:, kb, :], PT[n % NR][:], start=(kb == 0), stop=(kb == NTT - 1),
                             sig=(kb == NTT - 1))
                        if kb == 0:
                            P.cp("dve", acc[s_][:], PT[n % NR][:])
                        else:
                            P.tt("dve", acc[s_][:], acc[s_][:], PT[n % NR][:], ALU.add)
                        for item in list(pending):
                            if item[0] <= n:
                                item[1]()
                                pending.remove(item)
                        if kb == NTT - 1:
                            pending.append((n + 3, (lambda h=h, Q=Q, c=c, s_=s_: fin1(h, Q, c, s_))))
                            if c == 1:
                                pending.append((n + 10, (lambda h=h, Q=Q, s_=s_: fin2(h, Q, s_))))
                            if n + 1 < NS and steps[n + 1][0] != h:
                                for item in pending:
                                    item[1]()
                                pending = []
                        yield
                    for item in pending:
                        item[1]()

                elist = [0] + [e for e in range(-(32 * S - 1), 31 * S + 1) if e != 0]
                ectr = {"tsk": 0}

                def conv_grp(o, c, rhs_t, grp, pb):
                    tk = tsk[ectr["tsk"] % NT]
                    ectr["tsk"] += 1
                    keys = []
                    for sl in range(S):
                        q = "sp"
                        key = tk.name + "_s%d" % sl
                        keys.append(key)
                        P.dma(q, (lambda e, o_=tk[KB * sl:KB * sl + KB, :],
                                         i_=bass.AP(A_s, (o * HYW + c + sl) * 8192, [[1, KB], [1, TW]]):
                                     e.dma_start(out=o_, in_=i_)),
                              reads=[], writes=[key])
                    last = len(elist) - 1
                    for n_, e in enumerate(elist):
                        i_lo = max(0, -((-e) // S))
                        i_hi = min(31, (32 * S - 1 + e) // S)
                        nn = i_hi - i_lo + 1
                        hi = (-e) % S
                        j0 = i_lo - (e + hi) // S
                        assert nn > 0 and (e + hi) % S == 0 and 0 <= j0 and j0 + nn <= 32
                        ce = KB * e + 4096 - KB
                        assert 0 <= ce and ce + 128 <= TW
                        P.op("pe", (lambda en, o_=pb[:, grp, i_lo:i_hi + 1, :].rearrange("p i s -> p (i s)"),
                                           l_=tk[:, ce:ce + 128],
                                           r_=rhs_t[:, grp, hi, j0:j0 + nn, :].rearrange("p j s -> p (j s)"),
                                           s0=(n_ == 0), s1=(n_ == last): en.matmul(o_, l_, r_, start=s0, stop=s1)),
                             reads=keys + [rhs_t.name], writes=[pb.name], sig=(n_ == last))
                        if n_ % 28 == 27:
                            yield

                def reverse(dst, src):
                    sv = src[:].rearrange("p (g s) j -> p g s j", s=S)
                    pjf = pj[:].rearrange("p g i s -> p (g i s)")
                    for hi in range(S):
                        for sl in range(S):
                            P.mm(pjf[:, sl * NG * 32:(sl + 1) * NG * 32], fsel[:, hi, sl, :], sv[:, :, sl, :],
                                 sig=(sl == S - 1))
                        P.cp("act", dst[:, :, hi, :, :],
                             pjf.rearrange("p (s g j) -> p g j s", s=S, g=NG))

                def gen_E():
                    for g in range(HYW // G):
                        i = g % 2
                        c0 = g * G
                        P.ld("sp", vg[i][:], hy3[:, c0:c0 + G, :])
                        P.ld("sp", x1g[i][:], hy3[:, HYW + c0:HYW + c0 + G, :])
                        P.ld("sp", x2g[i][:], hy3[:, 2 * HYW + c0:2 * HYW + c0 + G, :])
                        reverse(vr, vg[i])
                        P.tt("pool", vb[:], vg[i][:], b0[:, c0:c0 + G].unsqueeze(2).to_broadcast([128, G, 32]), ALU.mult)
                        for grp in range(NG):
                            yield from conv_grp(0, c0 + S * grp, vr, grp, pc[0])
                        P.tt("dve", tmp[:].rearrange("p (q h) i -> p q h i", h=S), pc[0][:].rearrange("p q i h -> p q h i"),
                             vb[:].rearrange("p (q h) i -> p q h i", h=S), ALU.add)
                        P.tt("dve", zz[:], tmp[:], x1g[i][:], ALU.mult)
                        reverse(zr, zz)
                        P.tt("pool", zb[:], zz[:], b1[:, c0:c0 + G].unsqueeze(2).to_broadcast([128, G, 32]), ALU.mult)
                        for grp in range(NG):
                            yield from conv_grp(1, c0 + S * grp, zr, grp, pc[1])
                        P.tt("dve", tmp[:].rearrange("p (q h) i -> p q h i", h=S), pc[1][:].rearrange("p q i h -> p q h i"),
                             zb[:].rearrange("p (q h) i -> p q h i", h=S), ALU.add)
                        P.tt("dve", yall[:, c0:c0 + G, :], tmp[:], x2g[i][:], ALU.mult)
                        yield

                gC = gen_C()
                gE = gen_E()
                doneC = doneE = False
                while not (doneC and doneE):
                    if not doneE:
                        try:
                            next(gE)
                        except StopIteration:
                            doneE = True
                    if not doneC:
                        try:
                            next(gC)
                        except StopIteration:
                            doneC = True
                P.barrier()

            with contextlib.ExitStack() as st:
                yrow = [sbt(st, "yrow%d" % i, [128, 512], BF16) for i in range(2)]
                pY = [pst(st, "pY%d" % i, [128, 4, 128], BF16) for i in range(2)]
                n = 0
                for cc in range(4):
                    for iq in range(8):
                        for ii in range(4):
                            P.tr(pY[n % 2][:, ii, :], yall[:, cc * 128:(cc + 1) * 128, iq * 4 + ii], ident[:], sig=(ii == 3))
                        r = yrow[n % 2]
                        P.cp("act", r[:], pY[n % 2][:].rearrange("p a b -> p (a b)"))
                        P.ld("sp", yhT_s.ap()[cc][:, iq * 512:(iq + 1) * 512], r[:])
                        n += 1
                P.barrier()

        def stream_weight_bf16(wt, src_d, nk, ncol, wfs):
            n = 0
            for kc in range(nk):
                for c0 in range(0, ncol, 1024):
                    s_ = wfs[n % 2]
                    n += 1
                    P.ld("sp" if n % 2 == 0 else "act", s_[:], src_d.ap()[kc * 128:(kc + 1) * 128, c0:c0 + 1024])
                    P.cp("dve" if n % 2 == 0 else "act", wt[:, kc, c0:c0 + 1024], s_[:])
            return wt

        def residual_ln(pfx, st_tiles, ps_lo, ps_hi, gbc_mod, res_src_ap, lg, lb, dst):
            rs, r, stats, mv, rstd, tmp = st_tiles
            P.ld("sp", rs[:], res_src_ap)
            P.tt("dve", r[:, 0:512], ps_lo[:], gbc_mod[:, 0:512], ALU.mult)
            P.tt("dve", r[:, 512:1024], ps_hi[:], gbc_mod[:, 512:1024], ALU.mult)
            P.stt("dve", r[:], rs[:], ALPHA, r[:], ALU.mult, ALU.add)
            layer_norm_tile(pfx, r, lg, lb, dst, stats, mv, rstd, tmp)

        with contextlib.ExitStack() as stF:
            x1mT = sbt(stF, "x1mT", [128, 8, L], BF16)
            with contextlib.ExitStack() as st:
                who = sbt(st, "who", [128, 4, D], BF16)
                wao = sbt(st, "wao", [128, 4, D], BF16)
                wo = sbt(st, "wo", [128, 8, D], BF16)
                with contextlib.ExitStack() as stw:
                    wfs = [sbt(stw, "wfs%d" % i, [128, 1024], F32) for i in range(2)]
                    stream_weight_bf16(who, who_d, 4, D, wfs)
                    stream_weight_bf16(wao, wao_d, 4, D, wfs)
                    stream_weight_bf16(wo, wout_d, 8, D, wfs)
                    P.barrier()
                l1g = sbt(st, "l1g", [128, D], F32); l1b = sbt(st, "l1b", [128, D], F32)
                P.ld("sp", l1g[:], l1g_d.ap().partition_broadcast(128))
                P.ld("sp", l1b[:], l1b_d.ap().partition_broadcast(128))
                yh = [sbt(st, "yh%d" % i, [128, 4, 512], BF16) for i in range(2)]
                ot = [sbt(st, "ot%d" % i, [128, 4, 512], BF16) for i in range(2)]
                gt = [sbt(st, "gt%d" % i, [128, 16, 512], BF16) for i in range(2)]
                mT = sbt(st, "mT", [128, 8, 512], BF16)
                m1 = sbt(st, "m1", [128, 512], F32); m2 = sbt(st, "m2", [128, 512], F32)
                lnt = [(sbt(st, "rsF%d" % i, [128, D], F32), sbt(st, "rF%d" % i, [128, D], F32),
                        sbt(st, "statsF%d" % i, [128, 2, 6], F32), sbt(st, "mvF%d" % i, [128, 2], F32),
                        sbt(st, "rstdF%d" % i, [128, 1], F32), sbt(st, "tmpF%d" % i, [128, D], F32)) for i in range(2)]
                x1t = [sbt(st, "x1t0", [128, D], F32)] * 2
                x1bs = [sbt(st, "x1b%d" % i, [128, D], BF16) for i in range(2)]
                pA = [pst(st, "pA%d" % i, [128, 512]) for i in range(2)]
                pB2 = [pst(st, "pB2%d" % i, [128, 512]) for i in range(2)]
                pYl = [pst(st, "pYl%d" % i, [128, 512]) for i in range(2)]
                pT2 = pst(st, "pT2", [128, 8, 128], BF16)

                def load_T(T):
                    i = T % 2
                    sl = slice(T * 512, (T + 1) * 512)
                    P.ld("sp", yh[i][:], yhT_s.ap()[:, :, sl].rearrange("c p t -> p c t"))
                    P.ld("act", ot[i][:], oT_s.ap()[:, :, sl].rearrange("c p t -> p c t"))
                    P.ld("sp", gt[i][:], gat_s.ap()[:, :, sl].rearrange("g p t -> p g t"))

                def transposes(x1b, tok0):
                    for kc in range(8):
                        P.tr(pT2[:, kc, :], x1b[:, kc * 128:(kc + 1) * 128], ident[:], sig=(kc == 7))
                    for kc in range(8):
                        P.act(x1mT[:, kc, tok0:tok0 + 128], pT2[:, kc, :], AF.Identity,
                              bias=modT[:, 24 + kc, 0:1], scale=onep[:, 8 + kc, 0:1])

                nt = 0
                prev = None
                load_T(0)
                for T in range(8):
                    i = T % 2
                    if T + 1 < 8:
                        load_T(T + 1)
                    for fc in range(8):
                        a = pA[fc % 2]; b = pB2[fc % 2]
                        for cc in range(4):
                            P.mm(a[:], who[:, cc, fc * 128:(fc + 1) * 128], yh[i][:, cc, :], start=(cc == 0), stop=(cc == 3),
                                 sig=(cc == 3))
                        for cc in range(4):
                            P.mm(b[:], wao[:, cc, fc * 128:(fc + 1) * 128], ot[i][:, cc, :], start=(cc == 0), stop=(cc == 3),
                                 sig=(cc == 3))
                        P.tt("dve", m1[:], a[:], gt[i][:, fc, :], ALU.mult)
                        P.tt("dve", m2[:], b[:], gt[i][:, 8 + fc, :], ALU.mult)
                        P.tt("pool", mT[:, fc, :], m1[:], m2[:], ALU.add)
                    for tb in range(4):
                        tok0 = T * 512 + tb * 128
                        for hf in range(2):
                            for kc in range(8):
                                P.mm(pYl[hf][:], mT[:, kc, tb * 128:(tb + 1) * 128], wo[:, kc, hf * 512:(hf + 1) * 512],
                                     start=(kc == 0), stop=(kc == 7), sig=(kc == 7))
                        if prev is not None:
                            transposes(*prev)
                        xo = x1t[nt % 2]
                        x1b = x1bs[nt % 2]
                        lnt_ = lnt[nt % 2]
                        nt += 1
                        residual_ln("F", lnt_, pYl[0], pYl[1], g1bc,
                                    xln_s.ap()[tok0:tok0 + 128, :], l1g, l1b, xo)
                        P.ld("pool", x1_s.ap()[tok0:tok0 + 128, :], xo[:])
                        P.cp("act", x1b[:], xo[:])
                        prev = (x1b, tok0)
                transposes(*prev)
                P.barrier()

            with contextlib.ExitStack() as st:
                wuf = [sbt(st, "wuf%d" % i, [128, 8, 128], F32) for i in range(2)]
                wub = [sbt(st, "wub%d" % i, [128, 8, 128], BF16) for i in range(4)]
                g_items = []
                for fc_ in range(NFC):
                    for part_ in range(2):
                        g_items.append(part_ * NFC + fc_)
                gctr = {"issued": 0}

                def g_issue():
                    i = gctr["issued"]
                    if i >= len(g_items):
                        return
                    gctr["issued"] += 1
                    ch_ = g_items[i]
                    P.ld("sp", wuf[i % 2][:], wup_d.ap()[:, ch_ * 128:(ch_ + 1) * 128].rearrange("(kc p) c -> p kc c", p=128))
                    P.cp("pool", wub[i % 4][:], wuf[i % 2][:])
                zr2 = [sbt(st, "zr2_%d" % i, [128, L + 2], F32) for i in range(2)]
                tgs = [sbt(st, "tg%d" % i, [128, L], F32) for i in range(2)]
                tas = [sbt(st, "ta_%d" % i, [128, L], F32) for i in range(2)]
                hrow = [sbt(st, "hrow%d" % i, [128, L], BF16) for i in range(2)]
                fcw = sbt(st, "fcw", [128, 44, 3], F32); fcb = sbt(st, "fcb", [128, 44], F32)
                pG = [pst(st, "pG%d" % i, [128, 512]) for i in range(4)]
                P.ld("sp", fcw[:], fcw_d.ap()); P.ld("sp", fcb[:], fcb_d.ap())
                for i in range(2):
                    P.memset("pool", zr2[i][:, 0:1], 0.0)
                    P.memset("pool", zr2[i][:, L + 1:L + 2], 0.0)
                nw = 0
                npg = 0
                for fc in range(NFC):
                    tg = tgs[fc % 2]
                    ta_ = tas[fc % 2]
                    for part in range(2):
                        ch = part * NFC + fc
                        while gctr["issued"] <= min(nw + 2, len(g_items) - 1):
                            g_issue()
                        wbl = wub[nw % 4]
                        nw += 1
                        z = zr2[part]
                        for T in range(8):
                            ps = pG[npg % 4]
                            npg += 1
                            for kc in range(8):
                                P.mm(ps[:], wbl[:, kc, :], x1mT[:, kc, T * 512:(T + 1) * 512], start=(kc == 0), stop=(kc == 7),
                                     sig=(kc == 7))
                            P.cp("act", z[:, 1 + T * 512:1 + (T + 1) * 512], ps[:])
                        dst = ta_ if part == 0 else tg
                        P.ts("pool", dst[:], z[:, 0:L], fcw[:, ch, 0:1], fcb[:, ch:ch + 1], ALU.mult, ALU.add)
                        P.stt("dve", dst[:], z[:, 1:L + 1], fcw[:, ch, 1:2], dst[:], ALU.mult, ALU.add)
                        P.stt("dve", dst[:], z[:, 2:L + 2], fcw[:, ch, 2:3], dst[:], ALU.mult, ALU.add)
                    P.act(tg[:], tg[:], AF.Silu)
                    hr = hrow[fc % 2]
                    P.tt("dve", hr[:], tg[:], ta_[:], ALU.mult)
                    if fc > 0:
                        P.ld("sp", hT_s.ap()[fc - 1], hrow[(fc - 1) % 2][:])
                P.ld("sp", hT_s.ap()[NFC - 1], hrow[(NFC - 1) % 2][:])
                P.barrier()

        with contextlib.ExitStack() as st:
            wfs = [sbt(st, "wfsH%d" % i, [128, 1024], F32) for i in range(2)]
            wd = sbt(st, "wd", [128, NFC, D], BF16)
            stream_weight_bf16(wd, wdn_d, NFC, D, wfs)
            l2g = sbt(st, "l2g", [128, D], F32); l2b = sbt(st, "l2b", [128, D], F32)
            P.ld("sp", l2g[:], l2g_d.ap().partition_broadcast(128))
            P.ld("sp", l2b[:], l2b_d.ap().partition_broadcast(128))
            ht = [sbt(st, "ht%d" % i, [128, NFC, 512], BF16) for i in range(2)]
            lnt = [(sbt(st, "rsH%d" % i, [128, D], F32), sbt(st, "rH%d" % i, [128, D], F32),
                    sbt(st, "statsH%d" % i, [128, 2, 6], F32), sbt(st, "mvH%d" % i, [128, 2], F32),
                    sbt(st, "rstdH%d" % i, [128, 1], F32), sbt(st, "tmpH%d" % i, [128, D], F32)) for i in range(2)]
            xo = [sbt(st, "xoH%d" % i, [128, D], F32) for i in range(2)]
            pD = [[pst(st, "pD%d_%d" % (s, i), [128, 512]) for i in range(2)] for s in range(2)]
            nt = 0

            def load_ht(T):
                P.ld("sp" if T % 2 == 0 else "act", ht[T % 2][:],
                     hT_s.ap()[:, :, T * 512:(T + 1) * 512].rearrange("f p t -> p f t"))

            load_ht(0)
            for T in range(8):
                i = T % 2
                if T + 1 < 8:
                    load_ht(T + 1)
                for tb in range(4):
                    tok0 = T * 512 + tb * 128
                    pp = pD[nt % 2]
                    for hf in range(2):
                        for fc in range(NFC):
                            P.mm(pp[hf][:], ht[i][:, fc, tb * 128:(tb + 1) * 128], wd[:, fc, hf * 512:(hf + 1) * 512],
                                 start=(fc == 0), stop=(fc == NFC - 1), sig=(fc == NFC - 1))
                    o_ = xo[nt % 2]
                    lnt_ = lnt[nt % 2]
                    nt += 1
                    residual_ln("H", lnt_, pp[0], pp[1], g2bc,
                                x1_s.ap()[tok0:tok0 + 128, :], l2g, l2b, o_)
                    P.ld("pool", out_d.ap()[tok0:tok0 + 128, :], o_[:])
            P.barrier()

        P.run()
    return nc


def host_constants():
    f32 = np.float32
    rows = L // 64
    row = np.repeat(np.arange(rows, dtype=f32), 64)
    col = np.tile(np.arange(64, dtype=f32), rows)
    inv = (10000.0 ** (-np.arange(0, 32, 2, dtype=f32) / 32.0)).astype(f32)
    cosT = np.zeros((128, L), f32)
    sinT = np.zeros((128, L), f32)
    for p in range(128):
        d = p % 64
        a = d // 32
        i = d % 32
        f = i % 16
        ang = (row if a == 0 else col) * inv[f]
        cosT[p] = np.cos(ang)
        sinT[p] = (-np.sin(ang)) if i < 16 else np.sin(ang)
    t = np.linspace(0.0, 1.0, L, dtype=f32)[:, None]
    w = (f32(2.0 * math.pi / L) * np.arange(L, dtype=f32))[:, None]
    f = np.linspace(1e-4, 15, 16, dtype=f32)[None, :]
    z = np.concatenate([t, np.cos(f * w), -np.sin(f * w)], -1).astype(f32)
    min_decay = math.log(1e-2) / 1.5
    max_decay = math.log(1e-2) / 0.3
    deltas = np.linspace(min_decay, max_decay, HYW, dtype=f32)
    decay = np.exp(-t * np.abs(deltas)[None, :]).astype(f32)
    return dict(
        rope_cos=cosT, rope_sin=sinT,
        zposT_f=np.ascontiguousarray(z.T), zposT_r=np.ascontiguousarray(z[::-1].T),
        decayT_f=np.ascontiguousarray(decay.T), decayT_r=np.ascontiguousarray(decay[::-1].T),
    )


def per_part(v, nch):
    return np.ascontiguousarray(np.asarray(v, np.float32).reshape(nch, 128).T)


def make_in_maps(inp):
    f32 = np.float32
    g = {k: np.asarray(v, f32) for k, v in inp.items()}
    const = host_constants()
    w_in = g["w_in"][0]
    perm = np.zeros(1024, np.int64)
    for cidx in range(1024):
        base = 1536 + cidx
        d = cidx % 64
        i = d % 32
        perm[cidx] = base + 16 if i < 16 else base - 16
    w_qkp = np.ascontiguousarray(w_in[:, perm])
    cw = g["hy_conv_w"][0]
    hy_cw = np.ascontiguousarray(np.stack([per_part(cw[k], 12) for k in range(3)], -1))
    fw = g["ffn_conv_w"][0]
    ffn_cw = np.ascontiguousarray(np.stack([per_part(fw[k], 44) for k in range(3)], -1))
    shared = dict(
        ln_in_g=g["ln_in_g"], ln_in_b=g["ln_in_b"],
        w_ada=g["w_ada"][0], b_ada=g["b_ada"][0], b_adaT=per_part(g["b_ada"][0], 48),
        w_in=w_in, w_qkp=w_qkp,
        hy_cw=hy_cw, hy_cb=per_part(g["hy_conv_b"][0], 12),
        f_w1=g["hy_f_w1"][0], f_w2=g["hy_f_w2"][0], f_w3=g["hy_f_w3"][0], f_wout=g["hy_f_wout"][0],
        f_freqT=np.ascontiguousarray(g["hy_f_freq"][0].T),
        f_bT=np.ascontiguousarray(np.stack([g["hy_f_b1"][0], g["hy_f_b2"][0], g["hy_f_b3"][0]], -1)),
        hy_bias=g["hy_bias"][0],
        lamv=np.ascontiguousarray(np.stack([g["lam_q1"][0], g["lam_k1"][0], g["lam_q2"][0], g["lam_k2"][0]], 0)),
        subln_g=np.ascontiguousarray(g["at_subln_g"][0].reshape(128, 1)),
        w_hy_o=g["w_hy_o"][0], w_at_o=g["w_at_o"][0], w_out=g["w_out"][0],
        ln1_g=g["ln1_g"][0], ln1_b=g["ln1_b"][0], ln2_g=g["ln2_g"][0], ln2_b=g["ln2_b"][0],
        ffn_w_up=g["ffn_w_up"][0], ffn_cw=ffn_cw, ffn_cb=per_part(g["ffn_conv_b"][0], 44),
        ffn_w_down=g["ffn_w_down"][0],
    )
    shared.update(const)
    maps = []
    cc = per_part(g["c_ctx"], 8)
    for b in range(8):
        m = dict(shared)
        m["x"] = np.ascontiguousarray(g["x"][b])
        m["ctx"] = np.ascontiguousarray(g["ctx"][b])
        m["cT"] = np.ascontiguousarray(np.stack([per_part(g["c"][b], 8), cc], -1))
        maps.append(m)
    return maps


_NC = None


def kernel(**inputs):
    global _NC
    if _NC is None:
        _NC = build_program()
    maps = make_in_maps(inputs)
    res = run_bass_kernel_spmd(_NC, maps, core_ids=list(range(8)))
    out = np.stack([np.asarray(r["out"], np.float32) for r in res.results], 0)
    return out
```

```python
import math
import contextlib
import numpy as np
import concourse.bass as bass
import concourse.mybir as mybir
from concourse.bass_utils import run_bass_kernel_spmd

F32 = mybir.dt.float32
BF16 = mybir.dt.bfloat16
ALU = mybir.AluOpType
AF = mybir.ActivationFunctionType
AX = mybir.AxisListType

ENGS = ("pe", "act", "dve", "pool", "sp")

D = 1024
L = 4096
CTX = 256
NTT = 34
LK = CTX + L
HYW = 512
DFF = 2816
NFC = DFF // 128
ALPHA = 2.0 ** 0.25
LAM_INIT = 0.2
PI = math.pi
DEBUG = False


class Prog:
    NDMA = 6

    def __init__(self, nc):
        self.nc = nc
        self.ops = {e: [] for e in ENGS}
        self.cnt = {e: 0 for e in ENGS}
        self.waited = {e: {} for e in ENGS}
        self.last_w = {}
        self.reads = {}
        self.dma_uses = {}
        self.dma_rr = {e: 0 for e in ENGS}
        self.semkeys = set()
        self.sems = {}

    def _deps(self, eng, reads, writes):
        deps = []
        for k in reads:
            if k in self.last_w:
                deps.append(self.last_w[k])
        for k in writes:
            if k in self.last_w:
                deps.append(self.last_w[k])
            for ev in self.reads.get(k, ()):
                if ev[0] == eng:
                    continue
                deps.append(ev)
        best = {}
        for sk, v in deps:
            if eng == "pe" and sk == "pe":
                continue
            if v > best.get(sk, 0):
                best[sk] = v
        out = []
        w = self.waited[eng]
        for sk, v in best.items():
            if w.get(sk, 0) >= v:
                continue
            w[sk] = v
            out.append((sk, v))
        return out

    def _commit(self, ev, reads, writes):
        for k in reads:
            self.reads.setdefault(k, []).append(ev)
        for k in writes:
            self.last_w[k] = ev
            self.reads[k] = []

    def op(self, eng, fn, reads=(), writes=(), sig=True):
        waits = self._deps(eng, reads, writes)
        if sig:
            self.cnt[eng] += 1
            ev = (eng, self.cnt[eng])
            self.semkeys.add(eng)
            self.ops[eng].append(("op", fn, waits, eng))
        else:
            ev = (eng, self.cnt[eng] + 1)
            self.ops[eng].append(("op", fn, waits, None))
        self._commit(ev, reads, writes)
        return ev

    def dma(self, q, fn, reads=(), writes=()):
        slot = self.dma_rr[q] % self.NDMA
        self.dma_rr[q] += 1
        sk = ("dma", q, slot)
        self.semkeys.add(sk)
        uses = self.dma_uses.get(sk, 0)
        waits = self._deps(q, reads, writes)
        if uses > 0 and self.waited[q].get(sk, 0) < 16 * uses:
            self.waited[q][sk] = 16 * uses
            waits.append((sk, 16 * uses))
        uses += 1
        self.dma_uses[sk] = uses
        ev = (sk, 16 * uses)
        self.ops[q].append(("dma", fn, waits, sk))
        self._commit(ev, reads, writes)
        return ev

    def barrier(self):
        evs = []
        for e in ENGS:
            if self.cnt[e] > 0:
                evs.append((e, self.cnt[e]))
        for sk, uses in self.dma_uses.items():
            evs.append((sk, 16 * uses))
        for e in ENGS:
            waits = []
            for sk, v in evs:
                if sk == e:
                    continue
                if self.waited[e].get(sk, 0) >= v:
                    continue
                self.waited[e][sk] = v
                waits.append((sk, v))
            if waits:
                self.ops[e].append(("wait", None, waits, None))
        self.last_w = {}
        self.reads = {}

    def run(self):
        nc = self.nc
        with contextlib.ExitStack() as st:
            for sk in sorted(self.semkeys, key=str):
                name = sk if isinstance(sk, str) else "d_%s_%d" % (sk[1], sk[2])
                self.sems[sk] = st.enter_context(nc.semaphore("s_" + name))
            block = st.enter_context(nc.Block())

            def mk(ename):
                def body(e):
                    for kind, fn, waits, sk in self.ops[ename]:
                        for wk, wv in waits:
                            e.wait_ge(self.sems[wk], wv)
                        if kind == "wait":
                            continue
                        ins = fn(e)
                        if sk is not None:
                            ins.then_inc(self.sems[sk], 16 if kind == "dma" else 1)
                return body

            block.tensor(mk("pe"))
            block.scalar(mk("act"))
            block.vector(mk("dve"))
            block.gpsimd(mk("pool"))
            block.sync(mk("sp"))

    @staticmethod
    def _k(*aps):
        out = []
        for a in aps:
            if a is None or isinstance(a, (int, float)):
                continue
            out.append(a.name)
        return out

    def mm(self, out, lhsT, rhs, start=True, stop=True, sig=True):
        return self.op("pe", lambda e: e.matmul(out, lhsT, rhs, start=start, stop=stop),
                       reads=self._k(lhsT, rhs), writes=self._k(out), sig=sig)

    def tr(self, out, in_, ident, sig=True):
        return self.op("pe", lambda e: e.transpose(out, in_, ident),
                       reads=self._k(in_, ident), writes=self._k(out), sig=sig)

    def act(self, out, in_, func, bias=None, scale=None, accum_out=None):
        kw = {}
        if bias is not None:
            kw["bias"] = bias
        if scale is not None:
            kw["scale"] = scale
        if accum_out is not None:
            kw["accum_out"] = accum_out
        return self.op("act", lambda e: e.activation(out=out, in_=in_, func=func, **kw),
                       reads=self._k(in_, bias, scale), writes=self._k(out, accum_out))

    def tt(self, eng, out, in0, in1, op):
        return self.op(eng, lambda e: e.tensor_tensor(out=out, in0=in0, in1=in1, op=op),
                       reads=self._k(in0, in1), writes=self._k(out))

    def ts(self, eng, out, in0, s1, s2, op0, op1=None):
        if op1 is None:
            return self.op(eng, lambda e: e.tensor_scalar(out=out, in0=in0, scalar1=s1, scalar2=None, op0=op0),
                           reads=self._k(in0, s1), writes=self._k(out))
        return self.op(eng, lambda e: e.tensor_scalar(out=out, in0=in0, scalar1=s1, scalar2=s2, op0=op0, op1=op1),
                       reads=self._k(in0, s1, s2), writes=self._k(out))

    def stt(self, eng, out, in0, scalar, in1, op0, op1):
        return self.op(eng, lambda e: e.scalar_tensor_tensor(out=out, in0=in0, scalar=scalar, in1=in1, op0=op0, op1=op1),
                       reads=self._k(in0, scalar, in1), writes=self._k(out))

    def cp(self, eng, out, in_):
        if eng == "act":
            return self.act(out, in_, AF.Copy)
        return self.op(eng, lambda e: e.tensor_copy(out=out, in_=in_), reads=self._k(in_), writes=self._k(out))

    def memset(self, eng, ap, val):
        return self.op(eng, lambda e: e.memset(ap, val), writes=self._k(ap))

    def ld(self, q, out, in_):
        return self.dma(q, lambda e: e.dma_start(out=out, in_=in_), reads=self._k(in_), writes=self._k(out))


def build_program():
    nc = bass.Bass("TRN2", target_bir_lowering=False)
    P = Prog(nc)

    def din(name, shape, dt=F32):
        return nc.dram_tensor(name, list(shape), dt, kind="ExternalInput")

    def dscr(name, shape, dt, dbg=False):
        if DEBUG and dbg:
            return nc.dram_tensor(name, list(shape), dt, kind="ExternalOutput")
        return nc.dram_tensor(name, list(shape), dt)

    x_d = din("x", [L, D]); ctx_d = din("ctx", [CTX, D])
    cT_d = din("cT", [128, 8, 2])
    lng_d = din("ln_in_g", [D]); lnb_d = din("ln_in_b", [D])
    wada_d = din("w_ada", [D, 6 * D]); bada_d = din("b_ada", [6 * D]); badaT_d = din("b_adaT", [128, 48])
    win_d = din("w_in", [D, 5120]); wqkp_d = din("w_qkp", [D, 1024])
    cos_d = din("rope_cos", [128, L]); sin_d = din("rope_sin", [128, L])
    hcw_d = din("hy_cw", [128, 12, 3]); hcb_d = din("hy_cb", [128, 12])
    fw1_d = din("f_w1", [33, 64]); fw2_d = din("f_w2", [64, 64]); fw3_d = din("f_w3", [64, 64])
    fwo_d = din("f_wout", [64, 2048]); ffr_d = din("f_freqT", [64, 3]); fb_d = din("f_bT", [64, 3])
    zpf_d = din("zposT_f", [33, L]); zpr_d = din("zposT_r", [33, L])
    dcf_d = din("decayT_f", [HYW, L]); dcr_d = din("decayT_r", [HYW, L])
    hyb_d = din("hy_bias", [2, HYW])
    lam_d = din("lamv", [4, 64])
    sub_d = din("subln_g", [128, 1])
    who_d = din("w_hy_o", [HYW, D]); wao_d = din("w_at_o", [512, D]); wout_d = din("w_out", [D, D])
    l1g_d = din("ln1_g", [D]); l1b_d = din("ln1_b", [D]); l2g_d = din("ln2_g", [D]); l2b_d = din("ln2_b", [D])
    wup_d = din("ffn_w_up", [D, 2 * DFF]); fcw_d = din("ffn_cw", [128, 44, 3]); fcb_d = din("ffn_cb", [128, 44])
    wdn_d = din("ffn_w_down", [DFF, D])
    out_d = nc.dram_tensor("out", [L, D], F32, kind="ExternalOutput")

    xln_s = dscr("xln_s", [L, D], F32, True)
    qT_s = dscr("qT_s", [4, 128, L], BF16, True)
    kT_s = dscr("kT_s", [4, 128, LK], BF16, True)
    V_s = dscr("V_s", [NTT, 128, 512], BF16, True)
    hyu_s = dscr("hyu_s", [128, 1536 * 32], BF16, True)
    gat_s = dscr("gat_s", [16, 128, L], BF16, True)
    A_s = dscr("A_s", [2 * HYW, 8192], BF16, True)
    oT_s = dscr("oT_s", [4, 128, L], BF16, True)
    yhT_s = dscr("yhT_s", [4, 128, L], BF16, True)
    x1_s = dscr("x1_s", [L, D], F32, True)
    hT_s = dscr("hT_s", [NFC, 128, L], BF16, True)

    top = contextlib.ExitStack()

    def sbt(st, name, shape, dt):
        return st.enter_context(nc.sbuf_tensor("sb_" + name, list(shape), dt))

    def pst(st, name, shape, dt=F32):
        return st.enter_context(nc.psum_tensor("ps_" + name, list(shape), dt))

    with top:
        ident = sbt(top, "ident", [128, 128], BF16)
        jrev = sbt(top, "jrev", [128, 128], BF16)
        identf = sbt(top, "identf", [128, 128], F32)
        epsb = sbt(top, "epsb", [128, 1], F32)
        npib = sbt(top, "npib", [128, 1], F32)
        modT = sbt(top, "modT", [128, 48, 2], F32)
        onep = sbt(top, "onep", [128, 16, 2], F32)
        g1bc = sbt(top, "g1bc", [128, D], F32)
        g2bc = sbt(top, "g2bc", [128, D], F32)
        neglam = sbt(top, "neglam", [128, 1], F32)
        subg = sbt(top, "subg", [128, 1], F32)

        P.memset("pool", identf[:], 0.0)
        P.op("pool", lambda e: e.affine_select(out=identf[:], in_=identf[:], pattern=[[-1, 128]],
                                               compare_op=ALU.not_equal, fill=1.0, base=0, channel_multiplier=1),
             reads=[identf.name], writes=[identf.name])
        P.cp("dve", ident[:], identf[:])
        P.memset("pool", identf[:], 0.0)
        P.op("pool", lambda e: e.affine_select(out=identf[:], in_=identf[:], pattern=[[1, 128]],
                                               compare_op=ALU.not_equal, fill=1.0, base=-127, channel_multiplier=1),
             reads=[identf.name], writes=[identf.name])
        P.cp("dve", jrev[:], identf[:])
        P.memset("pool", epsb[:], 1e-5)
        P.memset("pool", npib[:], -PI)

        with contextlib.ExitStack() as st:
            cT = sbt(st, "cT", [128, 8, 2], F32)
            sc = sbt(st, "sc", [128, 8, 2], F32)
            scb = sbt(st, "scb", [128, 8, 128], F32)
            wa = [sbt(st, "wa%d" % i, [128, 8, 512], F32) for i in range(4)]
            wab = [sbt(st, "wab%d" % i, [128, 8, 512], BF16) for i in range(2)]
            sc_b = sbt(st, "sc_b", [128, 8, 2], BF16)
            scb_b = sbt(st, "scb_b", [128, 8, 128], BF16)
            badaT = sbt(st, "badaT", [128, 48], F32)
            bg = sbt(st, "bg", [128, D], F32)
            lamv = sbt(st, "lamv_t", [128, 4, 64], F32)
            lamp = sbt(st, "lamp", [128, 2, 64], F32)
            lams = sbt(st, "lams", [128, 2], F32)
            pm = pst(st, "pm", [128, 48, 2])
            pb = pst(st, "pb", [128, 512])
            P.ld("sp", cT[:], cT_d.ap())
            P.ld("sp", badaT[:], badaT_d.ap())
            P.ld("sp", subg[:], sub_d.ap())
            P.ld("sp", lamv[:].rearrange("p a b -> p (a b)"),
                 lam_d.ap().rearrange("a b -> (a b)").partition_broadcast(128))
            P.act(sc[:], cT[:], AF.Silu)
            P.cp("dve", scb[:], sc[:, :, 0:1].to_broadcast([128, 8, 128]))
            P.cp("dve", sc_b[:], sc[:])
            P.cp("dve", scb_b[:], scb[:])
            P.ts("dve", subg[:], subg[:], 1.0 - LAM_INIT, None, ALU.mult)
            P.tt("dve", lamp[:, 0, :], lamv[:, 0, :], lamv[:, 1, :], ALU.mult)
            P.tt("dve", lamp[:, 1, :], lamv[:, 2, :], lamv[:, 3, :], ALU.mult)
            P.op("dve", lambda e: e.reduce_sum(out=lams[:], in_=lamp[:], axis=AX.X), reads=[lamp.name], writes=[lams.name])
            P.act(lams[:], lams[:], AF.Exp)
            P.tt("dve", neglam[:], lams[:, 1:2], lams[:, 0:1], ALU.subtract)
            P.ts("dve", neglam[:], neglam[:], -LAM_INIT, None, ALU.add)
            for g in range(12):
                w = wa[g % 4]
                P.ld("sp" if g % 2 == 0 else "act", w[:],
                     wada_d.ap()[:, g * 512:(g + 1) * 512].rearrange("(kc p) c -> p kc c", p=128))
                wq = wab[g % 2]
                P.cp("dve" if g % 2 == 0 else "act", wq[:], w[:])
                for jj in range(4):
                    j = g * 4 + jj
                    for kc in range(8):
                        P.mm(pm[:, j, :], wq[:, kc, jj * 128:(jj + 1) * 128], sc_b[:, kc, :],
                             start=(kc == 0), stop=(kc == 7), sig=(kc == 7))
                if g in (4, 5, 10, 11):
                    for kc in range(8):
                        P.mm(pb[:], scb_b[:, kc, :], wq[:, kc, :], start=(kc == 0), stop=(kc == 7), sig=(kc == 7))
                    dst = g1bc if g in (4, 5) else g2bc
                    half = g % 2
                    P.ld("sp", bg[:, 0:512], bada_d.ap()[g * 512:(g + 1) * 512].partition_broadcast(128))
                    P.tt("dve", dst[:, half * 512:(half + 1) * 512], pb[:], bg[:, 0:512], ALU.add)
            P.tt("dve", modT[:], pm[:], badaT[:].unsqueeze(2).to_broadcast([128, 48, 2]), ALU.add)
            P.ts("dve", onep[:, 0:8, :], modT[:, 8:16, :], 1.0, None, ALU.add)
            P.ts("dve", onep[:, 8:16, :], modT[:, 32:40, :], 1.0, None, ALU.add)
            P.barrier()

        def layer_norm_tile(pfx, src, gbc, bbc, dst, stats, mv, rstd, tmp):
            for h in range(2):
                P.op("dve", (lambda e, h=h: e.bn_stats(out=stats[:, h, :], in_=src[:, h * 512:(h + 1) * 512])),
                     reads=[src.name], writes=[stats.name])
            P.op("dve", lambda e: e.bn_aggr(out=mv[:], in_=stats[:].rearrange("p a b -> p (a b)")),
                 reads=[stats.name], writes=[mv.name])
            P.act(rstd[:], mv[:, 1:2], AF.Sqrt, bias=epsb[:], scale=1.0)
            P.op("dve", lambda e: e.reciprocal(out=rstd[:], in_=rstd[:]), reads=[rstd.name], writes=[rstd.name])
            P.ts("dve", tmp[:], src[:], mv[:, 0:1], rstd[:], ALU.subtract, ALU.mult)
            P.tt("pool", tmp[:], tmp[:], gbc[:], ALU.mult)
            P.tt("dve", dst[:], tmp[:], bbc[:], ALU.add)

        with contextlib.ExitStack() as stA:
            xmT = sbt(stA, "xmT", [128, 8, L], BF16)
            xcT = sbt(stA, "xcT", [128, 8, CTX], BF16)
            with contextlib.ExitStack() as st:
                gbc = sbt(st, "gbc", [128, D], F32)
                bbc = sbt(st, "bbc", [128, D], F32)
                xt = [sbt(st, "xt%d" % i, [128, D], F32) for i in range(3)]
                xh = [sbt(st, "xh%d" % i, [128, D], F32) for i in range(3)]
                xl = [sbt(st, "xl%d" % i, [128, D], F32) for i in range(3)]
                xb = [sbt(st, "xb%d" % i, [128, D], BF16) for i in range(2)]
                stats = [sbt(st, "stats%d" % i, [128, 2, 6], F32) for i in range(3)]
                mv = [sbt(st, "mv%d" % i, [128, 2], F32) for i in range(3)]
                rstd = [sbt(st, "rstd%d" % i, [128, 1], F32) for i in range(3)]
                pT = [pst(st, "pT%d" % i, [128, 8, 128], BF16) for i in range(2)]
                P.ld("sp", gbc[:], lng_d.ap().partition_broadcast(128))
                P.ld("sp", bbc[:], lnb_d.ap().partition_broadcast(128))
                def phase2(t):
                    i = t % 3
                    is_ctx = t < 2
                    if not is_ctx:
                        P.ld("pool", xln_s.ap()[(t - 2) * 128:(t - 1) * 128, :], xl[i][:])
                    P.cp("act", xb[t % 2][:], xl[i][:])
                    for kc in range(8):
                        P.tr(pT[t % 2][:, kc, :], xb[t % 2][:, kc * 128:(kc + 1) * 128], ident[:], sig=(kc == 7))
                    w = 1 if is_ctx else 0
                    for kc in range(8):
                        dst = xcT[:, kc, t * 128:(t + 1) * 128] if is_ctx else xmT[:, kc, (t - 2) * 128:(t - 1) * 128]
                        P.act(dst, pT[t % 2][:, kc, :], AF.Identity, bias=modT[:, kc, w:w + 1], scale=onep[:, kc, w:w + 1])

                for t in range(NTT):
                    i = t % 3
                    is_ctx = t < 2
                    src = ctx_d.ap()[t * 128:(t + 1) * 128, :] if is_ctx else x_d.ap()[(t - 2) * 128:(t - 1) * 128, :]
                    P.ld("sp", xt[i][:], src)
                    layer_norm_tile("A", xt[i], gbc, bbc, xl[i], stats[i], mv[i], rstd[i], xh[i])
                    if t > 0:
                        phase2(t - 1)
                phase2(NTT - 1)
                P.barrier()

            wf = [sbt(stA, "wf%d" % i, [128, 8, 128], F32) for i in range(2)]
            wb = [sbt(stA, "wb%d" % i, [128, 8, 128], BF16) for i in range(4)]
            psB = [pst(stA, "psB%d" % i, [128, 512]) for i in range(4)]
            ctr = {"w": 0, "ps": 0}

            w_items = []
            for which_ in range(2):
                for h_ in range(4):
                    w_items.append((win_d, 1536 + which_ * 512 + h_ * 128))
                    w_items.append((wqkp_d, which_ * 512 + h_ * 128))
            for cc_ in range(12):
                w_items.append((win_d, cc_ * 128))
            for gc_ in range(16):
                w_items.append((win_d, 3072 + gc_ * 128))
            for c4_ in range(4):
                w_items.append((win_d, 2560 + c4_ * 128))
            ctr["issued"] = 0

            def _issue_w():
                i = ctr["issued"]
                if i >= len(w_items):
                    return
                ctr["issued"] += 1
                src_d, col0 = w_items[i]
                P.ld("sp", wf[i % 2][:], src_d.ap()[:, col0:col0 + 128].rearrange("(kc p) c -> p kc c", p=128))
                P.cp("pool", wb[i % 4][:], wf[i % 2][:])

            def load_w(src_d, col0):
                i = ctr["w"]
                ctr["w"] += 1
                assert w_items[i][1] == col0 and w_items[i][0] is src_d
                while ctr["issued"] <= min(i + 2, len(w_items) - 1):
                    _issue_w()
                return wb[i % 4]

            def proj_fm(wt, T):
                ps = psB[ctr["ps"] % 4]
                ctr["ps"] += 1
                for kc in range(8):
                    P.mm(ps[:], wt[:, kc, :], xmT[:, kc, T * 512:(T + 1) * 512], start=(kc == 0), stop=(kc == 7),
                         sig=(kc == 7))
                return ps

            with contextlib.ExitStack() as st:
                cosT = sbt(st, "cosT", [128, L], F32)
                sinT = sbt(st, "sinT", [128, L], F32)
                ra = [sbt(st, "ra%d" % i, [128, 512], F32) for i in range(2)]
                rb = [sbt(st, "rb%d" % i, [128, 512], F32) for i in range(2)]
                qrow = [sbt(st, "qrow%d" % i, [128, LK], BF16) for i in range(2)]
                P.ld("sp", cosT[:], cos_d.ap())
                P.ld("sp", sinT[:], sin_d.ap())
                n = 0
                for which in range(2):
                    for h in range(4):
                        col = 1536 + which * 512 + h * 128
                        w_main = load_w(win_d, col)
                        w_perm = load_w(wqkp_d, which * 512 + h * 128)
                        row = qrow[n % 2]
                        off = CTX if which == 1 else 0
                        if which == 1:
                            ps = psB[ctr["ps"] % 4]
                            ctr["ps"] += 1
                            for kc in range(8):
                                P.mm(ps[:, 0:CTX], w_main[:, kc, :], xcT[:, kc, :], start=(kc == 0), stop=(kc == 7),
                                     sig=(kc == 7))
                            P.cp("act", row[:, 0:CTX], ps[:, 0:CTX])
                        for T in range(8):
                            pa = proj_fm(w_main, T)
                            pb_ = proj_fm(w_perm, T)
                            P.tt("dve", ra[T % 2][:], pa[:], cosT[:, T * 512:(T + 1) * 512], ALU.mult)
                            P.tt("dve", rb[T % 2][:], pb_[:], sinT[:, T * 512:(T + 1) * 512], ALU.mult)
                            P.tt("pool", row[:, off + T * 512:off + (T + 1) * 512], ra[T % 2][:], rb[T % 2][:], ALU.add)
                        if which == 0:
                            P.ld("sp", qT_s.ap()[h], row[:, 0:L])
                        else:
                            P.ld("sp", kT_s.ap()[h], row[:, 0:LK])
                        n += 1
                P.barrier()

            with contextlib.ExitStack() as st:
                zrows = [sbt(st, "zrow%d" % i, [128, L + 2], F32) for i in range(2)]
                t1 = sbt(st, "t1", [128, L], F32)
                urows = [sbt(st, "urow%d" % i, [128, L], BF16) for i in range(2)]
                ucc = [sbt(st, "ucc%d" % i, [128, 128, 32], BF16) for i in range(2)]
                hcw = sbt(st, "hcw", [128, 12, 3], F32)
                hcb = sbt(st, "hcb", [128, 12], F32)
                pU = [pst(st, "pU%d" % i, [128, 4, 128], BF16) for i in range(2)]
                P.ld("sp", hcw[:], hcw_d.ap())
                P.ld("sp", hcb[:], hcb_d.ap())
                for zrow in zrows:
                    P.memset("pool", zrow[:, 0:1], 0.0)
                    P.memset("pool", zrow[:, L + 1:L + 2], 0.0)
                def b2_phase2(cc):
                    urow = urows[cc % 2]
                    u = ucc[cc % 2]
                    for jq in range(8):
                        pp = pU[jq % 2]
                        for jj in range(4):
                            j = jq * 4 + jj
                            P.tr(pp[:, jj, :], urow[:, j * 128:(j + 1) * 128], ident[:], sig=(jj == 3))
                        P.cp("act", u[:, :, jq * 4:(jq + 1) * 4], pp[:].rearrange("p j c -> p c j"))
                    P.ld("sp", hyu_s.ap()[:, cc * 4096:(cc + 1) * 4096], u[:].rearrange("p c j -> p (c j)"))

                for cc in range(12):
                    zrow = zrows[cc % 2]
                    urow = urows[cc % 2]
                    wt = load_w(win_d, cc * 128)
                    for T in range(8):
                        ps = proj_fm(wt, T)
                        P.cp("act", zrow[:, 1 + T * 512:1 + (T + 1) * 512], ps[:])
                    P.ts("pool", t1[:], zrow[:, 0:L], hcw[:, cc, 0:1], hcb[:, cc:cc + 1], ALU.mult, ALU.add)
                    P.stt("dve", t1[:], zrow[:, 1:L + 1], hcw[:, cc, 1:2], t1[:], ALU.mult, ALU.add)
                    P.stt("dve", urow[:], zrow[:, 2:L + 2], hcw[:, cc, 2:3], t1[:], ALU.mult, ALU.add)
                    if cc > 0:
                        b2_phase2(cc - 1)
                    if cc == 11:
                        b2_phase2(cc)
                P.barrier()

            with contextlib.ExitStack() as st:
                grow = [sbt(st, "grow%d" % i, [128, L], BF16) for i in range(2)]
                wv = sbt(st, "wv", [128, 8, 512], BF16)
                vrow = [sbt(st, "vrow%d" % i, [128, 512], BF16) for i in range(2)]
                for gc in range(16):
                    wt = load_w(win_d, 3072 + gc * 128)
                    g = grow[gc % 2]
                    for T in range(8):
                        ps = proj_fm(wt, T)
                        P.act(g[:, T * 512:(T + 1) * 512], ps[:], AF.Sigmoid)
                    P.ld("sp", gat_s.ap()[gc], g[:])
                for c4 in range(4):
                    wt = load_w(win_d, 2560 + c4 * 128)
                    P.cp("dve", wv[:, :, c4 * 128:(c4 + 1) * 128], wt[:])
                for t in range(NTT):
                    ps = psB[ctr["ps"] % 4]
                    ctr["ps"] += 1
                    for kc in range(8):
                        lhs = xcT[:, kc, t * 128:(t + 1) * 128] if t < 2 else xmT[:, kc, (t - 2) * 128:(t - 1) * 128]
                        P.mm(ps[:], lhs, wv[:, kc, :], start=(kc == 0), stop=(kc == 7), sig=(kc == 7))
                    v = vrow[t % 2]
                    P.cp("act", v[:], ps[:])
                    P.ld("sp", V_s.ap()[t], v[:])
                P.barrier()

        with contextlib.ExitStack() as st:
            fw1 = sbt(st, "fw1", [33, 64], F32); fw2 = sbt(st, "fw2", [64, 64], F32); fw3 = sbt(st, "fw3", [64, 64], F32)
            fwo = sbt(st, "fwo", [64, 2048], F32)
            ffr = sbt(st, "ffr", [64, 3], F32); fbt = sbt(st, "fbt", [64, 3], F32); ffb = sbt(st, "ffb", [64, 3], F32)
            zp = sbt(st, "zp", [33, L], F32)
            hA = sbt(st, "hA", [64, L], F32); hB = sbt(st, "hB", [64, L], F32)
            targ = [sbt(st, "targ%d" % i, [64, 2048], F32) for i in range(2)]
            targm = [sbt(st, "targm%d" % i, [64, 2048], F32) for i in range(2)]
            dct = [sbt(st, "dct%d" % i, [128, L], F32) for i in range(2)]
            arow = [sbt(st, "arow%d" % i, [128, L], BF16) for i in range(2)]
            pf = [pst(st, "pf%d" % i, [128, 2048]) for i in range(2)]
            fwob = sbt(st, "fwob", [64, 2048], BF16)
            hAb = sbt(st, "hAb", [64, L], BF16)
            P.ld("sp", fw1[:], fw1_d.ap()); P.ld("sp", fw2[:], fw2_d.ap()); P.ld("sp", fw3[:], fw3_d.ap())
            P.ld("act", fwo[:], fwo_d.ap()); P.ld("sp", ffr[:], ffr_d.ap()); P.ld("sp", fbt[:], fb_d.ap())
            P.tt("dve", ffb[:], ffr[:], fbt[:], ALU.mult)
            P.cp("act", fwob[:], fwo[:])
            npf = 0
            nrow = 0
            for ev in range(2):
                P.ld("sp", zp[:], (zpf_d if ev == 0 else zpr_d).ap())
                srcs = [(zp, fw1, 33), (hA, fw2, 64), (hB, fw3, 64)]
                dsts = [hA, hB, hA]
                for li in range(3):
                    src, wt, kk = srcs[li]
                    dst = dsts[li]
                    for hf in range(2):
                        ps = pf[npf % 2]
                        ta = targ[npf % 2]
                        tm = targm[npf % 2]
                        npf += 1
                        for q4 in range(4):
                            col = hf * 2048 + q4 * 512
                            P.mm(ps[0:64, q4 * 512:(q4 + 1) * 512], wt[0:kk, :], src[0:kk, col:col + 512], sig=(q4 == 3))
                        P.ts("dve", ta[:], ps[0:64, :], ffr[:, li:li + 1], ffb[:, li:li + 1], ALU.mult, ALU.add)
                        P.ts("dve", tm[:], ta[:], PI, -2.0 * PI, ALU.is_gt, ALU.mult)
                        P.tt("dve", ta[:], ta[:], tm[:], ALU.add)
                        P.ts("dve", tm[:], ta[:], -PI, 2.0 * PI, ALU.is_lt, ALU.mult)
                        P.tt("dve", ta[:], ta[:], tm[:], ALU.add)
                        P.ts("dve", ta[:], ta[:], PI, -PI, ALU.min, ALU.max)
                        P.act(dst[:, hf * 2048:(hf + 1) * 2048], ta[:], AF.Sin)
                P.cp("act", hAb[:, 0:2048], hA[:, 0:2048])
                P.cp("dve", hAb[:, 2048:4096], hA[:, 2048:4096])
                for o in range(2):
                    for cc in range(4):
                        dc = dct[nrow % 2]
                        ar = arow[nrow % 2]
                        nrow += 1
                        P.ld("sp" if nrow % 2 == 0 else "act", dc[:],
                             (dcf_d if ev == 0 else dcr_d).ap()[cc * 128:(cc + 1) * 128, :])
                        col = o * 1024 + ev * 512 + cc * 128
                        for hf in range(2):
                            ps = pf[npf % 2]
                            npf += 1
                            for q4 in range(4):
                                c_ = hf * 2048 + q4 * 512
                                P.mm(ps[:, q4 * 512:(q4 + 1) * 512], fwob[:, col:col + 128], hAb[:, c_:c_ + 512], sig=(q4 == 3))
                            P.tt("dve", ar[:, hf * 2048:(hf + 1) * 2048], ps[:], dc[:, hf * 2048:(hf + 1) * 2048], ALU.mult)
                        rows = A_s.ap()[o * HYW + cc * 128:o * HYW + (cc + 1) * 128, :]
                        if ev == 0:
                            P.ld("sp", rows[:, 4095:8191], ar[:])
                        else:
                            P.ld("sp", rows[:, 0:4095], ar[:, 0:4095])
            P.barrier()

        G = 16
        S = 4
        KB = 128 // S
        TW = 8192 - KB
        NG = G // S
        with contextlib.ExitStack() as stY:
            yall = sbt(stY, "yall", [128, HYW, 32], BF16)
            with contextlib.ExitStack() as st:
                qh = sbt(st, "qh", [128, L], BF16)
                kh = sbt(st, "kh", [128, LK], BF16)
                Vh = sbt(st, "Vh", [128, NTT, 128], BF16)
                ones_f = sbt(st, "ones_f", [128, 128], F32)
                ones_b = sbt(st, "ones_b", [128, 128], BF16)
                NR = 4
                PT = [sbt(st, "PT%d" % i, [128, 512], BF16) for i in range(NR)]
                rc = [sbt(st, "rc%d" % i, [128, 512], F32) for i in range(2)]
                acc = [sbt(st, "acc%d" % i, [128, 512], F32) for i in range(2)]
                on = [sbt(st, "on%d" % i, [128, 512], F32) for i in range(4)]
                oc = [sbt(st, "oc%d" % i, [128, 512], F32) for i in range(2)]
                osq = [sbt(st, "osq%d" % i, [128, 512], F32) for i in range(2)]
                rst = [sbt(st, "rst%d" % i, [128, 512], F32) for i in range(2)]
                orow = [sbt(st, "orow%d" % i, [128, 512], BF16) for i in range(2)]
                pS = [pst(st, "pS%d" % i, [128, 512]) for i in range(2)]
                pO = [pst(st, "pO%d" % i, [128, 512]) for i in range(2)]
                pSm = [pst(st, "pSm%d" % i, [128, 512]) for i in range(2)]
                NT = 4
                tsk = [sbt(st, "tsk%d" % i, [128, TW], BF16) for i in range(NT)]
                rself = sbt(st, "rself", [128, KB], F32)
                fsel = sbt(st, "fsel", [128, S, S, 128], BF16)
                b0 = sbt(st, "b0", [128, HYW], F32); b1 = sbt(st, "b1", [128, HYW], F32)
                vg = [sbt(st, "vg%d" % i, [128, G, 32], BF16) for i in range(2)]
                x1g = [sbt(st, "x1g%d" % i, [128, G, 32], BF16) for i in range(2)]
                x2g = [sbt(st, "x2g%d" % i, [128, G, 32], BF16) for i in range(2)]
                vr = sbt(st, "vr", [128, NG, S, 32, S], BF16)
                vb = sbt(st, "vb", [128, G, 32], F32)
                tmp = sbt(st, "tmpE", [128, G, 32], F32)
                zz = sbt(st, "zz", [128, G, 32], BF16)
                zr = sbt(st, "zr", [128, NG, S, 32, S], BF16)
                zb = sbt(st, "zb", [128, G, 32], F32)
                pc = [pst(st, "pc%d" % o, [128, NG, 32, S]) for o in range(2)]
                pj = pc[1]

                P.memset("pool", ones_f[:], 1.0)
                P.memset("pool", ones_b[:], 1.0)
                P.ld("sp", b0[:], hyb_d.ap()[0].partition_broadcast(128))
                P.ld("sp", b1[:], hyb_d.ap()[1].partition_broadcast(128))
                hy3 = hyu_s.ap().rearrange("p (c j) -> p c j", j=32)
                P.memset("pool", fsel[:], 0.0)
                for hi_ in range(S):
                    P.memset("pool", rself[:], 0.0)
                    P.op("pool", (lambda e, b_=-(KB * hi_ + KB - 1): e.affine_select(
                        out=rself[:], in_=rself[:], pattern=[[1, KB]], compare_op=ALU.not_equal, fill=1.0,
                        base=b_, channel_multiplier=1)), reads=[rself.name], writes=[rself.name])
                    for sl_ in range(S):
                        P.cp("dve", fsel[:, hi_, sl_, KB * sl_:KB * sl_ + KB], rself[:])

                steps = [(h, Q, c, kb) for h in range(4) for Q in range(8) for c in range(2) for kb in range(NTT)]
                NS = len(steps)

                def gen_C():
                    pending = []
                    state = {"head": -1, "qk": -1}

                    def load_head(h):
                        P.ld("sp", qh[:], qT_s.ap()[h])
                        P.ld("sp", kh[:], kT_s.ap()[h])
                        for t0_ in (0, 17):
                            P.ld("sp", Vh[:, t0_:t0_ + 17, :],
                                 V_s.ap()[t0_:t0_ + 17, :, h * 128:(h + 1) * 128].rearrange("t p v -> p t v"))

                    def ensure_qk(m):
                        if m <= state["qk"] or m >= NS:
                            return
                        h, Q, c, kb = steps[m]
                        if h != state["head"]:
                            load_head(h)
                            state["head"] = h
                        P.mm(pS[m % 2][:], kh[64 * c:64 * c + 64, kb * 128:(kb + 1) * 128],
                             qh[64 * c:64 * c + 64, Q * 512:(Q + 1) * 512])
                        P.act(PT[m % NR][:], pS[m % 2][:], AF.Exp, scale=0.125)
                        state["qk"] = m

                    def fin1(h, Q, c, s_):
                        g = (h * 8 + Q) % 2
                        P.mm(pSm[s_][:], ones_f[:], acc[s_][:])
                        P.op("dve", (lambda e, a=rc[c][:], b=pSm[s_][:]: e.reciprocal(out=a, in_=b)),
                             reads=[pSm[s_].name], writes=[rc[c].name])
                        P.tt("dve", on[2 * g + c][:], pO[s_][:], rc[c][:], ALU.mult)
                        if c == 1:
                            P.stt("dve", oc[g][:], on[2 * g + 1][:], neglam[:], on[2 * g][:], ALU.mult, ALU.add)
                            P.tt("pool", osq[g][:], oc[g][:], oc[g][:], ALU.mult)

                    def fin2(h, Q, s_):
                        g = (h * 8 + Q) % 2
                        P.mm(pSm[s_][:], ones_f[:], osq[g][:])
                        P.act(rst[g][:], pSm[s_][:], AF.Sqrt, bias=epsb[:], scale=1.0 / 128.0)
                        P.op("dve", (lambda e, a=rst[g][:]: e.reciprocal(out=a, in_=a)),
                             reads=[rst[g].name], writes=[rst[g].name])
                        P.stt("dve", orow[g][:], oc[g][:], subg[:], rst[g][:], ALU.mult, ALU.mult)
                        P.ld("pool", oT_s.ap()[h][:, Q * 512:(Q + 1) * 512], orow[g][:])

                    for n in range(NS):
                        h, Q, c, kb = steps[n]
                        s_ = (n // NTT) % 2
                        ensure_qk(n)
                        if n + 1 < NS and steps[n + 1][0] == h:
                            ensure_qk(n + 1)
                        P.mm(pO[s_][:], Vh[:, kb, :], PT[n % NR][:], start=(kb == 0), stop=(kb == NTT - 1),
                             sig=(kb == NTT - 1))
                        if kb == 0:
                            P.cp("dve", acc[s_][:], PT[n % NR][:])
                        else:
                            P.tt("dve", acc[s_][:], acc[s_][:], PT[n % NR][:], ALU.add)
                        for item in list(pending):
                            if item[0] <= n:
                                item[1]()
                                pending.remove(item)
                        if kb == NTT - 1:
                            pending.append((n + 3, (lambda h=h, Q=Q, c=c, s_=s_: fin1(h, Q, c, s_))))
                            if c == 1:
                                pending.append((n + 10, (lambda h=h, Q=Q, s_=s_: fin2(h, Q, s_))))
                            if n + 1 < NS and steps[n + 1][0] != h:
                                for item in pending:
                                    item[1]()
                                pending = []
                        yield
                    for item in pending:
                        item[1]()

                elist = [0] + [e for e in range(-(32 * S - 1), 31 * S + 1) if e != 0]
                ectr = {"tsk": 0}

                def conv_grp(o, c, rhs_t, grp, pb):
                    tk = tsk[ectr["tsk"] % NT]
                    ectr["tsk"] += 1
                    keys = []
                    for sl in range(S):
                        q = "sp"
                        key = tk.name + "_s%d" % sl
                        keys.append(key)
                        P.dma(q, (lambda e, o_=tk[KB * sl:KB * sl + KB, :],
                                         i_=bass.AP(A_s, (o * HYW + c + sl) * 8192, [[1, KB], [1, TW]]):
                                     e.dma_start(out=o_, in_=i_)),
                              reads=[], writes=[key])
                    last = len(elist) - 1
                    for n_, e in enumerate(elist):
                        i_lo = max(0, -((-e) // S))
                        i_hi = min(31, (32 * S - 1 + e) // S)
                        nn = i_hi - i_lo + 1
                        hi = (-e) % S
                        j0 = i_lo - (e + hi) // S
                        assert nn > 0 and (e + hi) % S == 0 and 0 <= j0 and j0 + nn <= 32
                        ce = KB * e + 4096 - KB
                        assert 0 <= ce and ce + 128 <= TW
                        P.op("pe", (lambda en, o_=pb[:, grp, i_lo:i_hi + 1, :].rearrange("p i s -> p (i s)"),
                                           l_=tk[:, ce:ce + 128],
                                           r_=rhs_t[:, grp, hi, j0:j0 + nn, :].rearrange("p j s -> p (j s)"),
                                           s0=(n_ == 0), s1=(n_ == last): en.matmul(o_, l_, r_, start=s0, stop=s1)),
                             reads=keys + [rhs_t.name], writes=[pb.name], sig=(n_ == last))
                        if n_ % 28 == 27:
                            yield

                def reverse(dst, src):
                    sv = src[:].rearrange("p (g s) j -> p g s j", s=S)
                    pjf = pj[:].rearrange("p g i s -> p (g i s)")
                    for hi in range(S):
                        for sl in range(S):
                            P.mm(pjf[:, sl * NG * 32:(sl + 1) * NG * 32], fsel[:, hi, sl, :], sv[:, :, sl, :],
                                 sig=(sl == S - 1))
                        P.cp("act", dst[:, :, hi, :, :],
                             pjf.rearrange("p (s g j) -> p g j s", s=S, g=NG))

                def gen_E():
                    for g in range(HYW // G):
                        i = g % 2
                        c0 = g * G
                        P.ld("sp", vg[i][:], hy3[:, c0:c0 + G, :])
                        P.ld("sp", x1g[i][:], hy3[:, HYW + c0:HYW + c0 + G, :])
                        P.ld("sp", x2g[i][:], hy3[:, 2 * HYW + c0:2 * HYW + c0 + G, :])
                        reverse(vr, vg[i])
                        P.tt("pool", vb[:], vg[i][:], b0[:, c0:c0 + G].unsqueeze(2).to_broadcast([128, G, 32]), ALU.mult)
                        for grp in range(NG):
                            yield from conv_grp(0, c0 + S * grp, vr, grp, pc[0])
                        P.tt("dve", tmp[:].rearrange("p (q h) i -> p q h i", h=S), pc[0][:].rearrange("p q i h -> p q h i"),
                             vb[:].rearrange("p (q h) i -> p q h i", h=S), ALU.add)
                        P.tt("dve", zz[:], tmp[:], x1g[i][:], ALU.mult)
                        reverse(zr, zz)
                        P.tt("pool", zb[:], zz[:], b1[:, c0:c0 + G].unsqueeze(2).to_broadcast([128, G, 32]), ALU.mult)
                        for grp in range(NG):
                            yield from conv_grp(1, c0 + S * grp, zr, grp, pc[1])
                        P.tt("dve", tmp[:].rearrange("p (q h) i -> p q h i", h=S), pc[1][:].rearrange("p q i h -> p q h i"),
                             zb[:].rearrange("p (q h) i -> p q h i", h=S), ALU.add)
                        P.tt("dve", yall[:, c0:c0 + G, :], tmp[:], x2g[i][:], ALU.mult)
                        yield

                gC = gen_C()
                gE = gen_E()
                doneC = doneE = False
                while not (doneC and doneE):
                    if not doneE:
                        try:
                            next(gE)
                        except StopIteration:
                            doneE = True
                    if not doneC:
                        try:
                            next(gC)
                        except StopIteration:
                            doneC = True
                P.barrier()

            with contextlib.ExitStack() as st:
                yrow = [sbt(st, "yrow%d" % i, [128, 512], BF16) for i in range(2)]
                pY = [pst(st, "pY%d" % i, [128, 4, 128], BF16) for i in range(2)]
                n = 0
                for cc in range(4):
                    for iq in range(8):
                        for ii in range(4):
                            P.tr(pY[n % 2][:, ii, :], yall[:, cc * 128:(cc + 1) * 128, iq * 4 + ii], ident[:], sig=(ii == 3))
                        r = yrow[n % 2]
                        P.cp("act", r[:], pY[n % 2][:].rearrange("p a b -> p (a b)"))
                        P.ld("sp", yhT_s.ap()[cc][:, iq * 512:(iq + 1) * 512], r[:])
                        n += 1
                P.barrier()

        def stream_weight_bf16(wt, src_d, nk, ncol, wfs):
            n = 0
            for kc in range(nk):
                for c0 in range(0, ncol, 1024):
                    s_ = wfs[n % 2]
                    n += 1
                    P.ld("sp" if n % 2 == 0 else "act", s_[:], src_d.ap()[kc * 128:(kc + 1) * 128, c0:c0 + 1024])
                    P.cp("dve" if n % 2 == 0 else "act", wt[:, kc, c0:c0 + 1024], s_[:])
            return wt

        def residual_ln(pfx, st_tiles, ps_lo, ps_hi, gbc_mod, res_src_ap, lg, lb, dst):
            rs, r, stats, mv, rstd, tmp = st_tiles
            P.ld("sp", rs[:], res_src_ap)
            P.tt("dve", r[:, 0:512], ps_lo[:], gbc_mod[:, 0:512], ALU.mult)
            P.tt("dve", r[:, 512:1024], ps_hi[:], gbc_mod[:, 512:1024], ALU.mult)
            P.stt("dve", r[:], rs[:], ALPHA, r[:], ALU.mult, ALU.add)
            layer_norm_tile(pfx, r, lg, lb, dst, stats, mv, rstd, tmp)

        with contextlib.ExitStack() as stF:
            x1mT = sbt(stF, "x1mT", [128, 8, L], BF16)
            with contextlib.ExitStack() as st:
                who = sbt(st, "who", [128, 4, D], BF16)
                wao = sbt(st, "wao", [128, 4, D], BF16)
                wo = sbt(st, "wo", [128, 8, D], BF16)
                with contextlib.ExitStack() as stw:
                    wfs = [sbt(stw, "wfs%d" % i, [128, 1024], F32) for i in range(2)]
                    stream_weight_bf16(who, who_d, 4, D, wfs)
                    stream_weight_bf16(wao, wao_d, 4, D, wfs)
                    stream_weight_bf16(wo, wout_d, 8, D, wfs)
                    P.barrier()
                l1g = sbt(st, "l1g", [128, D], F32); l1b = sbt(st, "l1b", [128, D], F32)
                P.ld("sp", l1g[:], l1g_d.ap().partition_broadcast(128))
                P.ld("sp", l1b[:], l1b_d.ap().partition_broadcast(128))
                yh = [sbt(st, "yh%d" % i, [128, 4, 512], BF16) for i in range(2)]
                ot = [sbt(st, "ot%d" % i, [128, 4, 512], BF16) for i in range(2)]
                gt = [sbt(st, "gt%d" % i, [128, 16, 512], BF16) for i in range(2)]
                mT = sbt(st, "mT", [128, 8, 512], BF16)
                m1 = sbt(st, "m1", [128, 512], F32); m2 = sbt(st, "m2", [128, 512], F32)
                lnt = [(sbt(st, "rsF%d" % i, [128, D], F32), sbt(st, "rF%d" % i, [128, D], F32),
                        sbt(st, "statsF%d" % i, [128, 2, 6], F32), sbt(st, "mvF%d" % i, [128, 2], F32),
                        sbt(st, "rstdF%d" % i, [128, 1], F32), sbt(st, "tmpF%d" % i, [128, D], F32)) for i in range(2)]
                x1t = [sbt(st, "x1t0", [128, D], F32)] * 2
                x1bs = [sbt(st, "x1b%d" % i, [128, D], BF16) for i in range(2)]
                pA = [pst(st, "pA%d" % i, [128, 512]) for i in range(2)]
                pB2 = [pst(st, "pB2%d" % i, [128, 512]) for i in range(2)]
                pYl = [pst(st, "pYl%d" % i, [128, 512]) for i in range(2)]
                pT2 = pst(st, "pT2", [128, 8, 128], BF16)

                def load_T(T):
                    i = T % 2
                    sl = slice(T * 512, (T + 1) * 512)
                    P.ld("sp", yh[i][:], yhT_s.ap()[:, :, sl].rearrange("c p t -> p c t"))
                    P.ld("act", ot[i][:], oT_s.ap()[:, :, sl].rearrange("c p t -> p c t"))
                    P.ld("sp", gt[i][:], gat_s.ap()[:, :, sl].rearrange("g p t -> p g t"))

                def transposes(x1b, tok0):
                    for kc in range(8):
                        P.tr(pT2[:, kc, :], x1b[:, kc * 128:(kc + 1) * 128], ident[:], sig=(kc == 7))
                    for kc in range(8):
                        P.act(x1mT[:, kc, tok0:tok0 + 128], pT2[:, kc, :], AF.Identity,
                              bias=modT[:, 24 + kc, 0:1], scale=onep[:, 8 + kc, 0:1])

                nt = 0
                prev = None
                load_T(0)
                for T in range(8):
                    i = T % 2
                    if T + 1 < 8:
                        load_T(T + 1)
                    for fc in range(8):
                        a = pA[fc % 2]; b = pB2[fc % 2]
                        for cc in range(4):
                            P.mm(a[:], who[:, cc, fc * 128:(fc + 1) * 128], yh[i][:, cc, :], start=(cc == 0), stop=(cc == 3),
                                 sig=(cc == 3))
                        for cc in range(4):
                            P.mm(b[:], wao[:, cc, fc * 128:(fc + 1) * 128], ot[i][:, cc, :], start=(cc == 0), stop=(cc == 3),
                                 sig=(cc == 3))
                        P.tt("dve", m1[:], a[:], gt[i][:, fc, :], ALU.mult)
                        P.tt("dve", m2[:], b[:], gt[i][:, 8 + fc, :], ALU.mult)
                        P.tt("pool", mT[:, fc, :], m1[:], m2[:], ALU.add)
                    for tb in range(4):
                        tok0 = T * 512 + tb * 128
                        for hf in range(2):
                            for kc in range(8):
                                P.mm(pYl[hf][:], mT[:, kc, tb * 128:(tb + 1) * 128], wo[:, kc, hf * 512:(hf + 1) * 512],
                                     start=(kc == 0), stop=(kc == 7), sig=(kc == 7))
                        if prev is not None:
                            transposes(*prev)
                        xo = x1t[nt % 2]
                        x1b = x1bs[nt % 2]
                        lnt_ = lnt[nt % 2]
                        nt += 1
                        residual_ln("F", lnt_, pYl[0], pYl[1], g1bc,
                                    xln_s.ap()[tok0:tok0 + 128, :], l1g, l1b, xo)
                        P.ld("pool", x1_s.ap()[tok0:tok0 + 128, :], xo[:])
                        P.cp("act", x1b[:], xo[:])
                        prev = (x1b, tok0)
                transposes(*prev)
                P.barrier()

            with contextlib.ExitStack() as st:
                wuf = [sbt(st, "wuf%d" % i, [128, 8, 128], F32) for i in range(2)]
                wub = [sbt(st, "wub%d" % i, [128, 8, 128], BF16) for i in range(4)]
                g_items = []
                for fc_ in range(NFC):
                    for part_ in range(2):
                        g_items.append(part_ * NFC + fc_)
                gctr = {"issued": 0}

                def g_issue():
                    i = gctr["issued"]
                    if i >= len(g_items):
                        return
                    gctr["issued"] += 1
                    ch_ = g_items[i]
                    P.ld("sp", wuf[i % 2][:], wup_d.ap()[:, ch_ * 128:(ch_ + 1) * 128].rearrange("(kc p) c -> p kc c", p=128))
                    P.cp("pool", wub[i % 4][:], wuf[i % 2][:])
                zr2 = [sbt(st, "zr2_%d" % i, [128, L + 2], F32) for i in range(2)]
                tgs = [sbt(st, "tg%d" % i, [128, L], F32) for i in range(2)]
                tas = [sbt(st, "ta_%d" % i, [128, L], F32) for i in range(2)]
                hrow = [sbt(st, "hrow%d" % i, [128, L], BF16) for i in range(2)]
                fcw = sbt(st, "fcw", [128, 44, 3], F32); fcb = sbt(st, "fcb", [128, 44], F32)
                pG = [pst(st, "pG%d" % i, [128, 512]) for i in range(4)]
                P.ld("sp", fcw[:], fcw_d.ap()); P.ld("sp", fcb[:], fcb_d.ap())
                for i in range(2):
                    P.memset("pool", zr2[i][:, 0:1], 0.0)
                    P.memset("pool", zr2[i][:, L + 1:L + 2], 0.0)
                nw = 0
                npg = 0
                for fc in range(NFC):
                    tg = tgs[fc % 2]
                    ta_ = tas[fc % 2]
                    for part in range(2):
                        ch = part * NFC + fc
                        while gctr["issued"] <= min(nw + 2, len(g_items) - 1):
                            g_issue()
                        wbl = wub[nw % 4]
                        nw += 1
                        z = zr2[part]
                        for T in range(8):
                            ps = pG[npg % 4]
                            npg += 1
                            for kc in range(8):
                                P.mm(ps[:], wbl[:, kc, :], x1mT[:, kc, T * 512:(T + 1) * 512], start=(kc == 0), stop=(kc == 7),
                                     sig=(kc == 7))
                            P.cp("act", z[:, 1 + T * 512:1 + (T + 1) * 512], ps[:])
                        dst = ta_ if part == 0 else tg
                        P.ts("pool", dst[:], z[:, 0:L], fcw[:, ch, 0:1], fcb[:, ch:ch + 1], ALU.mult, ALU.add)
                        P.stt("dve", dst[:], z[:, 1:L + 1], fcw[:, ch, 1:2], dst[:], ALU.mult, ALU.add)
                        P.stt("dve", dst[:], z[:, 2:L + 2], fcw[:, ch, 2:3], dst[:], ALU.mult, ALU.add)
                    P.act(tg[:], tg[:], AF.Silu)
                    hr = hrow[fc % 2]
                    P.tt("dve", hr[:], tg[:], ta_[:], ALU.mult)
                    if fc > 0:
                        P.ld("sp", hT_s.ap()[fc - 1], hrow[(fc - 1) % 2][:])
                P.ld("sp", hT_s.ap()[NFC - 1], hrow[(NFC - 1) % 2][:])
                P.barrier()

        with contextlib.ExitStack() as st:
            wfs = [sbt(st, "wfsH%d" % i, [128, 1024], F32) for i in range(2)]
            wd = sbt(st, "wd", [128, NFC, D], BF16)
            stream_weight_bf16(wd, wdn_d, NFC, D, wfs)
            l2g = sbt(st, "l2g", [128, D], F32); l2b = sbt(st, "l2b", [128, D], F32)
            P.ld("sp", l2g[:], l2g_d.ap().partition_broadcast(128))
            P.ld("sp", l2b[:], l2b_d.ap().partition_broadcast(128))
            ht = [sbt(st, "ht%d" % i, [128, NFC, 512], BF16) for i in range(2)]
            lnt = [(sbt(st, "rsH%d" % i, [128, D], F32), sbt(st, "rH%d" % i, [128, D], F32),
                    sbt(st, "statsH%d" % i, [128, 2, 6], F32), sbt(st, "mvH%d" % i, [128, 2], F32),
                    sbt(st, "rstdH%d" % i, [128, 1], F32), sbt(st, "tmpH%d" % i, [128, D], F32)) for i in range(2)]
            xo = [sbt(st, "xoH%d" % i, [128, D], F32) for i in range(2)]
            pD = [[pst(st, "pD%d_%d" % (s, i), [128, 512]) for i in range(2)] for s in range(2)]
            nt = 0

            def load_ht(T):
                P.ld("sp" if T % 2 == 0 else "act", ht[T % 2][:],
                     hT_s.ap()[:, :, T * 512:(T + 1) * 512].rearrange("f p t -> p f t"))

            load_ht(0)
            for T in range(8):
                i = T % 2
                if T + 1 < 8:
                    load_ht(T + 1)
                for tb in range(4):
                    tok0 = T * 512 + tb * 128
                    pp = pD[nt % 2]
                    for hf in range(2):
                        for fc in range(NFC):
                            P.mm(pp[hf][:], ht[i][:, fc, tb * 128:(tb + 1) * 128], wd[:, fc, hf * 512:(hf + 1) * 512],
                                 start=(fc == 0), stop=(fc == NFC - 1), sig=(fc == NFC - 1))
                    o_ = xo[nt % 2]
                    lnt_ = lnt[nt % 2]
                    nt += 1
                    residual_ln("H", lnt_, pp[0], pp[1], g2bc,
                                x1_s.ap()[tok0:tok0 + 128, :], l2g, l2b, o_)
                    P.ld("pool", out_d.ap()[tok0:tok0 + 128, :], o_[:])
            P.barrier()

        P.run()
    return nc


def host_constants():
    f32 = np.float32
    rows = L // 64
    row = np.repeat(np.arange(rows, dtype=f32), 64)
    col = np.tile(np.arange(64, dtype=f32), rows)
    inv = (10000.0 ** (-np.arange(0, 32, 2, dtype=f32) / 32.0)).astype(f32)
    cosT = np.zeros((128, L), f32)
    sinT = np.zeros((128, L), f32)
    for p in range(128):
        d = p % 64
        a = d // 32
        i = d % 32
        f = i % 16
        ang = (row if a == 0 else col) * inv[f]
        cosT[p] = np.cos(ang)
        sinT[p] = (-np.sin(ang)) if i < 16 else np.sin(ang)
    t = np.linspace(0.0, 1.0, L, dtype=f32)[:, None]
    w = (f32(2.0 * math.pi / L) * np.arange(L, dtype=f32))[:, None]
    f = np.linspace(1e-4, 15, 16, dtype=f32)[None, :]
    z = np.concatenate([t, np.cos(f * w), -np.sin(f * w)], -1).astype(f32)
    min_decay = math.log(1e-2) / 1.5
    max_decay = math.log(1e-2) / 0.3
    deltas = np.linspace(min_decay, max_decay, HYW, dtype=f32)
    decay = np.exp(-t * np.abs(deltas)[None, :]).astype(f32)
    return dict(
        rope_cos=cosT, rope_sin=sinT,
        zposT_f=np.ascontiguousarray(z.T), zposT_r=np.ascontiguousarray(z[::-1].T),
        decayT_f=np.ascontiguousarray(decay.T), decayT_r=np.ascontiguousarray(decay[::-1].T),
    )


def per_part(v, nch):
    return np.ascontiguousarray(np.asarray(v, np.float32).reshape(nch, 128).T)


def make_in_maps(inp):
    f32 = np.float32
    g = {k: np.asarray(v, f32) for k, v in inp.items()}
    const = host_constants()
    w_in = g["w_in"][0]
    perm = np.zeros(1024, np.int64)
    for cidx in range(1024):
        base = 1536 + cidx
        d = cidx % 64
        i = d % 32
        perm[cidx] = base + 16 if i < 16 else base - 16
    w_qkp = np.ascontiguousarray(w_in[:, perm])
    cw = g["hy_conv_w"][0]
    hy_cw = np.ascontiguousarray(np.stack([per_part(cw[k], 12) for k in range(3)], -1))
    fw = g["ffn_conv_w"][0]
    ffn_cw = np.ascontiguousarray(np.stack([per_part(fw[k], 44) for k in range(3)], -1))
    shared = dict(
        ln_in_g=g["ln_in_g"], ln_in_b=g["ln_in_b"],
        w_ada=g["w_ada"][0], b_ada=g["b_ada"][0], b_adaT=per_part(g["b_ada"][0], 48),
        w_in=w_in, w_qkp=w_qkp,
        hy_cw=hy_cw, hy_cb=per_part(g["hy_conv_b"][0], 12),
        f_w1=g["hy_f_w1"][0], f_w2=g["hy_f_w2"][0], f_w3=g["hy_f_w3"][0], f_wout=g["hy_f_wout"][0],
        f_freqT=np.ascontiguousarray(g["hy_f_freq"][0].T),
        f_bT=np.ascontiguousarray(np.stack([g["hy_f_b1"][0], g["hy_f_b2"][0], g["hy_f_b3"][0]], -1)),
        hy_bias=g["hy_bias"][0],
        lamv=np.ascontiguousarray(np.stack([g["lam_q1"][0], g["lam_k1"][0], g["lam_q2"][0], g["lam_k2"][0]], 0)),
        subln_g=np.ascontiguousarray(g["at_subln_g"][0].reshape(128, 1)),
        w_hy_o=g["w_hy_o"][0], w_at_o=g["w_at_o"][0], w_out=g["w_out"][0],
        ln1_g=g["ln1_g"][0], ln1_b=g["ln1_b"][0], ln2_g=g["ln2_g"][0], ln2_b=g["ln2_b"][0],
        ffn_w_up=g["ffn_w_up"][0], ffn_cw=ffn_cw, ffn_cb=per_part(g["ffn_conv_b"][0], 44),
        ffn_w_down=g["ffn_w_down"][0],
    )
    shared.update(const)
    maps = []
    cc = per_part(g["c_ctx"], 8)
    for b in range(8):
        m = dict(shared)
        m["x"] = np.ascontiguousarray(g["x"][b])
        m["ctx"] = np.ascontiguousarray(g["ctx"][b])
        m["cT"] = np.ascontiguousarray(np.stack([per_part(g["c"][b], 8), cc], -1))
        maps.append(m)
    return maps


_NC = None


def kernel(**inputs):
    global _NC
    if _NC is None:
        _NC = build_program()
    maps = make_in_maps(inputs)
    res = run_bass_kernel_spmd(_NC, maps, core_ids=list(range(8)))
    out = np.stack([np.asarray(r["out"], np.float32) for r in res.results], 0)
    return out
```

```python
import math
import contextlib
import numpy as np
import concourse.bass as bass
import concourse.mybir as mybir
from concourse.bass_utils import run_bass_kernel_spmd

F32 = mybir.dt.float32
BF16 = mybir.dt.bfloat16
ALU = mybir.AluOpType
AF = mybir.ActivationFunctionType
AX = mybir.AxisListType

ENGS = ("pe", "act", "dve", "pool", "sp")

D = 1024
L = 4096
CTX = 256
NTT = 34
LK = CTX + L
HYW = 512
DFF = 2816
NFC = DFF // 128
ALPHA = 2.0 ** 0.25
LAM_INIT = 0.2
PI = math.pi
DEBUG = False


class Prog:
    NDMA = 6

    def __init__(self, nc):
        self.nc = nc
        self.ops = {e: [] for e in ENGS}
        self.cnt = {e: 0 for e in ENGS}
        self.waited = {e: {} for e in ENGS}
        self.last_w = {}
        self.reads = {}
        self.dma_uses = {}
        self.dma_rr = {e: 0 for e in ENGS}
        self.semkeys = set()
        self.sems = {}

    def _deps(self, eng, reads, writes):
        deps = []
        for k in reads:
            if k in self.last_w:
                deps.append(self.last_w[k])
        for k in writes:
            if k in self.last_w:
                deps.append(self.last_w[k])
            for ev in self.reads.get(k, ()):
                if ev[0] == eng:
                    continue
                deps.append(ev)
        best = {}
        for sk, v in deps:
            if eng == "pe" and sk == "pe":
                continue
            if v > best.get(sk, 0):
                best[sk] = v
        out = []
        w = self.waited[eng]
        for sk, v in best.items():
            if w.get(sk, 0) >= v:
                continue
            w[sk] = v
            out.append((sk, v))
        return out

    def _commit(self, ev, reads, writes):
        for k in reads:
            self.reads.setdefault(k, []).append(ev)
        for k in writes:
            self.last_w[k] = ev
            self.reads[k] = []

    def op(self, eng, fn, reads=(), writes=(), sig=True):
        waits = self._deps(eng, reads, writes)
        if sig:
            self.cnt[eng] += 1
            ev = (eng, self.cnt[eng])
            self.semkeys.add(eng)
            self.ops[eng].append(("op", fn, waits, eng))
        else:
            ev = (eng, self.cnt[eng] + 1)
            self.ops[eng].append(("op", fn, waits, None))
        self._commit(ev, reads, writes)
        return ev

    def dma(self, q, fn, reads=(), writes=()):
        slot = self.dma_rr[q] % self.NDMA
        self.dma_rr[q] += 1
        sk = ("dma", q, slot)
        self.semkeys.add(sk)
        uses = self.dma_uses.get(sk, 0)
        waits = self._deps(q, reads, writes)
        if uses > 0 and self.waited[q].get(sk, 0) < 16 * uses:
            self.waited[q][sk] = 16 * uses
            waits.append((sk, 16 * uses))
        uses += 1
        self.dma_uses[sk] = uses
        ev = (sk, 16 * uses)
        self.ops[q].append(("dma", fn, waits, sk))
        self._commit(ev, reads, writes)
        return ev

    def barrier(self):
        evs = []
        for e in ENGS:
            if self.cnt[e] > 0:
                evs.append((e, self.cnt[e]))
        for sk, uses in self.dma_uses.items():
            evs.append((sk, 16 * uses))
        for e in ENGS:
            waits = []
            for sk, v in evs:
                if sk == e:
                    continue
                if self.waited[e].get(sk, 0) >= v:
                    continue
                self.waited[e][sk] = v
                waits.append((sk, v))
            if waits:
                self.ops[e].append(("wait", None, waits, None))
        self.last_w = {}
        self.reads = {}

    def run(self):
        nc = self.nc
        with contextlib.ExitStack() as st:
            for sk in sorted(self.semkeys, key=str):
                name = sk if isinstance(sk, str) else "d_%s_%d" % (sk[1], sk[2])
                self.sems[sk] = st.enter_context(nc.semaphore("s_" + name))
            block = st.enter_context(nc.Block())

            def mk(ename):
                def body(e):
                    for kind, fn, waits, sk in self.ops[ename]:
                        for wk, wv in waits:
                            e.wait_ge(self.sems[wk], wv)
                        if kind == "wait":
                            continue
                        ins = fn(e)
                        if sk is not None:
                            ins.then_inc(self.sems[sk], 16 if kind == "dma" else 1)
                return body

            block.tensor(mk("pe"))
            block.scalar(mk("act"))
            block.vector(mk("dve"))
            block.gpsimd(mk("pool"))
            block.sync(mk("sp"))

    @staticmethod
    def _k(*aps):
        out = []
        for a in aps:
            if a is None or isinstance(a, (int, float)):
                continue
            out.append(a.name)
        return out

    def mm(self, out, lhsT, rhs, start=True, stop=True, sig=True):
        return self.op("pe", lambda e: e.matmul(out, lhsT, rhs, start=start, stop=stop),
                       reads=self._k(lhsT, rhs), writes=self._k(out), sig=sig)

    def tr(self, out, in_, ident, sig=True):
        return self.op("pe", lambda e: e.transpose(out, in_, ident),
                       reads=self._k(in_, ident), writes=self._k(out), sig=sig)

    def act(self, out, in_, func, bias=None, scale=None, accum_out=None):
        kw = {}
        if bias is not None:
            kw["bias"] = bias
        if scale is not None:
            kw["scale"] = scale
        if accum_out is not None:
            kw["accum_out"] = accum_out
        return self.op("act", lambda e: e.activation(out=out, in_=in_, func=func, **kw),
                       reads=self._k(in_, bias, scale), writes=self._k(out, accum_out))

    def tt(self, eng, out, in0, in1, op):
        return self.op(eng, lambda e: e.tensor_tensor(out=out, in0=in0, in1=in1, op=op),
                       reads=self._k(in0, in1), writes=self._k(out))

    def ts(self, eng, out, in0, s1, s2, op0, op1=None):
        if op1 is None:
            return self.op(eng, lambda e: e.tensor_scalar(out=out, in0=in0, scalar1=s1, scalar2=None, op0=op0),
                           reads=self._k(in0, s1), writes=self._k(out))
        return self.op(eng, lambda e: e.tensor_scalar(out=out, in0=in0, scalar1=s1, scalar2=s2, op0=op0, op1=op1),
                       reads=self._k(in0, s1, s2), writes=self._k(out))

    def stt(self, eng, out, in0, scalar, in1, op0, op1):
        return self.op(eng, lambda e: e.scalar_tensor_tensor(out=out, in0=in0, scalar=scalar, in1=in1, op0=op0, op1=op1),
                       reads=self._k(in0, scalar, in1), writes=self._k(out))

    def cp(self, eng, out, in_):
        if eng == "act":
            return self.act(out, in_, AF.Copy)
        return self.op(eng, lambda e: e.tensor_copy(out=out, in_=in_), reads=self._k(in_), writes=self._k(out))

    def memset(self, eng, ap, val):
        return self.op(eng, lambda e: e.memset(ap, val), writes=self._k(ap))

    def ld(self, q, out, in_):
        return self.dma(q, lambda e: e.dma_start(out=out, in_=in_), reads=self._k(in_), writes=self._k(out))


def build_program():
    nc = bass.Bass("TRN2", target_bir_lowering=False)
    P = Prog(nc)

    def din(name, shape, dt=F32):
        return nc.dram_tensor(name, list(shape), dt, kind="ExternalInput")

    def dscr(name, shape, dt, dbg=False):
        if DEBUG and dbg:
            return nc.dram_tensor(name, list(shape), dt, kind="ExternalOutput")
        return nc.dram_tensor(name, list(shape), dt)

    x_d = din("x", [L, D]); ctx_d = din("ctx", [CTX, D])
    cT_d = din("cT", [128, 8, 2])
    lng_d = din("ln_in_g", [D]); lnb_d = din("ln_in_b", [D])
    wada_d = din("w_ada", [D, 6 * D]); bada_d = din("b_ada", [6 * D]); badaT_d = din("b_adaT", [128, 48])
    win_d = din("w_in", [D, 5120]); wqkp_d = din("w_qkp", [D, 1024])
    cos_d = din("rope_cos", [128, L]); sin_d = din("rope_sin", [128, L])
    hcw_d = din("hy_cw", [128, 12, 3]); hcb_d = din("hy_cb", [128, 12])
    fw1_d = din("f_w1", [33, 64]); fw2_d = din("f_w2", [64, 64]); fw3_d = din("f_w3", [64, 64])
    fwo_d = din("f_wout", [64, 2048]); ffr_d = din("f_freqT", [64, 3]); fb_d = din("f_bT", [64, 3])
    zpf_d = din("zposT_f", [33, L]); zpr_d = din("zposT_r", [33, L])
    dcf_d = din("decayT_f", [HYW, L]); dcr_d = din("decayT_r", [HYW, L])
    hyb_d = din("hy_bias", [2, HYW])
    lam_d = din("lamv", [4, 64])
    sub_d = din("subln_g", [128, 1])
    who_d = din("w_hy_o", [HYW, D]); wao_d = din("w_at_o", [512, D]); wout_d = din("w_out", [D, D])
    l1g_d = din("ln1_g", [D]); l1b_d = din("ln1_b", [D]); l2g_d = din("ln2_g", [D]); l2b_d = din("ln2_b", [D])
    wup_d = din("ffn_w_up", [D, 2 * DFF]); fcw_d = din("ffn_cw", [128, 44, 3]); fcb_d = din("ffn_cb", [128, 44])
    wdn_d = din("ffn_w_down", [DFF, D])
    out_d = nc.dram_tensor("out", [L, D], F32, kind="ExternalOutput")

    xln_s = dscr("xln_s", [L, D], F32, True)
    qT_s = dscr("qT_s", [4, 128, L], BF16, True)
    kT_s = dscr("kT_s", [4, 128, LK], BF16, True)
    V_s = dscr("V_s", [NTT, 128, 512], BF16, True)
    hyu_s = dscr("hyu_s", [128, 1536 * 32], BF16, True)
    gat_s = dscr("gat_s", [16, 128, L], BF16, True)
    A_s = dscr("A_s", [2 * HYW, 8192], BF16, True)
    oT_s = dscr("oT_s", [4, 128, L], BF16, True)
    yhT_s = dscr("yhT_s", [4, 128, L], BF16, True)
    x1_s = dscr("x1_s", [L, D], F32, True)
    hT_s = dscr("hT_s", [NFC, 128, L], BF16, True)

    top = contextlib.ExitStack()

    def sbt(st, name, shape, dt):
        return st.enter_context(nc.sbuf_tensor("sb_" + name, list(shape), dt))

    def pst(st, name, shape, dt=F32):
        return st.enter_context(nc.psum_tensor("ps_" + name, list(shape), dt))

    with top:
        ident = sbt(top, "ident", [128, 128], BF16)
        jrev = sbt(top, "jrev", [128, 128], BF16)
        identf = sbt(top, "identf", [128, 128], F32)
        epsb = sbt(top, "epsb", [128, 1], F32)
        npib = sbt(top, "npib", [128, 1], F32)
        modT = sbt(top, "modT", [128, 48, 2], F32)
        onep = sbt(top, "onep", [128, 16, 2], F32)
        g1bc = sbt(top, "g1bc", [128, D], F32)
        g2bc = sbt(top, "g2bc", [128, D], F32)
        neglam = sbt(top, "neglam", [128, 1], F32)
        subg = sbt(top, "subg", [128, 1], F32)

        P.memset("pool", identf[:], 0.0)
        P.op("pool", lambda e: e.affine_select(out=identf[:], in_=identf[:], pattern=[[-1, 128]],
                                               compare_op=ALU.not_equal, fill=1.0, base=0, channel_multiplier=1),
             reads=[identf.name], writes=[identf.name])
        P.cp("dve", ident[:], identf[:])
        P.memset("pool", identf[:], 0.0)
        P.op("pool", lambda e: e.affine_select(out=identf[:], in_=identf[:], pattern=[[1, 128]],
                                               compare_op=ALU.not_equal, fill=1.0, base=-127, channel_multiplier=1),
             reads=[identf.name], writes=[identf.name])
        P.cp("dve", jrev[:], identf[:])
        P.memset("pool", epsb[:], 1e-5)
        P.memset("pool", npib[:], -PI)

        with contextlib.ExitStack() as st:
            cT = sbt(st, "cT", [128, 8, 2], F32)
            sc = sbt(st, "sc", [128, 8, 2], F32)
            scb = sbt(st, "scb", [128, 8, 128], F32)
            wa = [sbt(st, "wa%d" % i, [128, 8, 512], F32) for i in range(4)]
            wab = [sbt(st, "wab%d" % i, [128, 8, 512], BF16) for i in range(2)]
            sc_b = sbt(st, "sc_b", [128, 8, 2], BF16)
            scb_b = sbt(st, "scb_b", [128, 8, 128], BF16)
            badaT = sbt(st, "badaT", [128, 48], F32)
            bg = sbt(st, "bg", [128, D], F32)
            lamv = sbt(st, "lamv_t", [128, 4, 64], F32)
            lamp = sbt(st, "lamp", [128, 2, 64], F32)
            lams = sbt(st, "lams", [128, 2], F32)
            pm = pst(st, "pm", [128, 48, 2])
            pb = pst(st, "pb", [128, 512])
            P.ld("sp", cT[:], cT_d.ap())
            P.ld("sp", badaT[:], badaT_d.ap())
            P.ld("sp", subg[:], sub_d.ap())
            P.ld("sp", lamv[:].rearrange("p a b -> p (a b)"),
                 lam_d.ap().rearrange("a b -> (a b)").partition_broadcast(128))
            P.act(sc[:], cT[:], AF.Silu)
            P.cp("dve", scb[:], sc[:, :, 0:1].to_broadcast([128, 8, 128]))
            P.cp("dve", sc_b[:], sc[:])
            P.cp("dve", scb_b[:], scb[:])
            P.ts("dve", subg[:], subg[:], 1.0 - LAM_INIT, None, ALU.mult)
            P.tt("dve", lamp[:, 0, :], lamv[:, 0, :], lamv[:, 1, :], ALU.mult)
            P.tt("dve", lamp[:, 1, :], lamv[:, 2, :], lamv[:, 3, :], ALU.mult)
            P.op("dve", lambda e: e.reduce_sum(out=lams[:], in_=lamp[:], axis=AX.X), reads=[lamp.name], writes=[lams.name])
            P.act(lams[:], lams[:], AF.Exp)
            P.tt("dve", neglam[:], lams[:, 1:2], lams[:, 0:1], ALU.subtract)
            P.ts("dve", neglam[:], neglam[:], -LAM_INIT, None, ALU.add)
            for g in range(12):
                w = wa[g % 4]
                P.ld("sp" if g % 2 == 0 else "act", w[:],
                     wada_d.ap()[:, g * 512:(g + 1) * 512].rearrange("(kc p) c -> p kc c", p=128))
                wq = wab[g % 2]
                P.cp("dve" if g % 2 == 0 else "act", wq[:], w[:])
                for jj in range(4):
                    j = g * 4 + jj
                    for kc in range(8):
                        P.mm(pm[:, j, :], wq[:, kc, jj * 128:(jj + 1) * 128], sc_b[:, kc, :],
                             start=(kc == 0), stop=(kc == 7), sig=(kc == 7))
                if g in (4, 5, 10, 11):
                    for kc in range(8):
                        P.mm(pb[:], scb_b[:, kc, :], wq[:, kc, :], start=(kc == 0), stop=(kc == 7), sig=(kc == 7))
                    dst = g1bc if g in (4, 5) else g2bc
                    half = g % 2
                    P.ld("sp", bg[:, 0:512], bada_d.ap()[g * 512:(g + 1) * 512].partition_broadcast(128))
                    P.tt("dve", dst[:, half * 512:(half + 1) * 512], pb[:], bg[:, 0:512], ALU.add)
            P.tt("dve", modT[:], pm[:], badaT[:].unsqueeze(2).to_broadcast([128, 48, 2]), ALU.add)
            P.ts("dve", onep[:, 0:8, :], modT[:, 8:16, :], 1.0, None, ALU.add)
            P.ts("dve", onep[:, 8:16, :], modT[:, 32:40, :], 1.0, None, ALU.add)
            P.barrier()

        def layer_norm_tile(pfx, src, gbc, bbc, dst, stats, mv, rstd, tmp, add_eng="pool"):
            for h in range(2):
                P.op("dve", (lambda e, h=h: e.bn_stats(out=stats[:, h, :], in_=src[:, h * 512:(h + 1) * 512])),
                     reads=[src.name], writes=[stats.name])
            P.op("dve", lambda e: e.bn_aggr(out=mv[:], in_=stats[:].rearrange("p a b -> p (a b)")),
                 reads=[stats.name], writes=[mv.name])
            P.act(rstd[:], mv[:, 1:2], AF.Sqrt, bias=epsb[:], scale=1.0)
            P.op("dve", lambda e: e.reciprocal(out=rstd[:], in_=rstd[:]), reads=[rstd.name], writes=[rstd.name])
            P.ts("dve", tmp[:], src[:], mv[:, 0:1], rstd[:], ALU.subtract, ALU.mult)
            P.tt("pool", tmp[:], tmp[:], gbc[:], ALU.mult)
            P.tt(add_eng, dst[:], tmp[:], bbc[:], ALU.add)

        with contextlib.ExitStack() as stA:
            xmT = sbt(stA, "xmT", [128, 8, L], BF16)
            xcT = sbt(stA, "xcT", [128, 8, CTX], BF16)
            with contextlib.ExitStack() as st:
                gbc = sbt(st, "gbc", [128, D], F32)
                bbc = sbt(st, "bbc", [128, D], F32)
                xt = [sbt(st, "xt%d" % i, [128, D], F32) for i in range(3)]
                xh = [sbt(st, "xh%d" % i, [128, D], F32) for i in range(3)]
                xl = [sbt(st, "xl%d" % i, [128, D], F32) for i in range(3)]
                xb = [sbt(st, "xb%d" % i, [128, D], BF16) for i in range(2)]
                stats = [sbt(st, "stats%d" % i, [128, 2, 6], F32) for i in range(3)]
                mv = [sbt(st, "mv%d" % i, [128, 2], F32) for i in range(3)]
                rstd = [sbt(st, "rstd%d" % i, [128, 1], F32) for i in range(3)]
                pT = [pst(st, "pT%d" % i, [128, 8, 128], BF16) for i in range(2)]
                P.ld("sp", gbc[:], lng_d.ap().partition_broadcast(128))
                P.ld("sp", bbc[:], lnb_d.ap().partition_broadcast(128))
                def phase2(t):
                    i = t % 3
                    is_ctx = t < 2
                    if not is_ctx:
                        P.ld("pool", xln_s.ap()[(t - 2) * 128:(t - 1) * 128, :], xl[i][:])
                    P.cp("act", xb[t % 2][:], xl[i][:])
                    for kc in range(8):
                        P.tr(pT[t % 2][:, kc, :], xb[t % 2][:, kc * 128:(kc + 1) * 128], ident[:], sig=(kc == 7))
                    w = 1 if is_ctx else 0
                    for kc in range(8):
                        dst = xcT[:, kc, t * 128:(t + 1) * 128] if is_ctx else xmT[:, kc, (t - 2) * 128:(t - 1) * 128]
                        P.act(dst, pT[t % 2][:, kc, :], AF.Identity, bias=modT[:, kc, w:w + 1], scale=onep[:, kc, w:w + 1])

                for t in range(NTT):
                    i = t % 3
                    is_ctx = t < 2
                    src = ctx_d.ap()[t * 128:(t + 1) * 128, :] if is_ctx else x_d.ap()[(t - 2) * 128:(t - 1) * 128, :]
                    P.ld("sp", xt[i][:], src)
                    layer_norm_tile("A", xt[i], gbc, bbc, xl[i], stats[i], mv[i], rstd[i], xh[i])
                    if t > 0:
                        phase2(t - 1)
                phase2(NTT - 1)
                P.barrier()

            wf = [sbt(stA, "wf%d" % i, [128, 8, 128], F32) for i in range(2)]
            wb = [sbt(stA, "wb%d" % i, [128, 8, 128], BF16) for i in range(4)]
            psB = [pst(stA, "psB%d" % i, [128, 512]) for i in range(4)]
            ctr = {"w": 0, "ps": 0}

            w_items = []
            for which_ in range(2):
                for h_ in range(4):
                    w_items.append((win_d, 1536 + which_ * 512 + h_ * 128))
                    w_items.append((wqkp_d, which_ * 512 + h_ * 128))
            for cc_ in range(12):
                w_items.append((win_d, cc_ * 128))
            for gc_ in range(16):
                w_items.append((win_d, 3072 + gc_ * 128))
            for c4_ in range(4):
                w_items.append((win_d, 2560 + c4_ * 128))
            ctr["issued"] = 0

            def _issue_w():
                i = ctr["issued"]
                if i >= len(w_items):
                    return
                ctr["issued"] += 1
                src_d, col0 = w_items[i]
                P.ld("sp", wf[i % 2][:], src_d.ap()[:, col0:col0 + 128].rearrange("(kc p) c -> p kc c", p=128))
                P.cp("pool", wb[i % 4][:], wf[i % 2][:])

            def load_w(src_d, col0):
                i = ctr["w"]
                ctr["w"] += 1
                assert w_items[i][1] == col0 and w_items[i][0] is src_d
                while ctr["issued"] <= min(i + 2, len(w_items) - 1):
                    _issue_w()
                return wb[i % 4]

            def proj_fm(wt, T):
                ps = psB[ctr["ps"] % 4]
                ctr["ps"] += 1
                for kc in range(8):
                    P.mm(ps[:], wt[:, kc, :], xmT[:, kc, T * 512:(T + 1) * 512], start=(kc == 0), stop=(kc == 7),
                         sig=(kc == 7))
                return ps

            with contextlib.ExitStack() as st:
                cosT = sbt(st, "cosT", [128, L], F32)
                sinT = sbt(st, "sinT", [128, L], F32)
                ra = [sbt(st, "ra%d" % i, [128, 512], F32) for i in range(2)]
                rb = [sbt(st, "rb%d" % i, [128, 512], F32) for i in range(2)]
                qrow = [sbt(st, "qrow%d" % i, [128, LK], BF16) for i in range(2)]
                P.ld("sp", cosT[:], cos_d.ap())
                P.ld("sp", sinT[:], sin_d.ap())
                n = 0
                for which in range(2):
                    for h in range(4):
                        col = 1536 + which * 512 + h * 128
                        w_main = load_w(win_d, col)
                        w_perm = load_w(wqkp_d, which * 512 + h * 128)
                        row = qrow[n % 2]
                        off = CTX if which == 1 else 0
                        if which == 1:
                            ps = psB[ctr["ps"] % 4]
                            ctr["ps"] += 1
                            for kc in range(8):
                                P.mm(ps[:, 0:CTX], w_main[:, kc, :], xcT[:, kc, :], start=(kc == 0), stop=(kc == 7),
                                     sig=(kc == 7))
                            P.cp("act", row[:, 0:CTX], ps[:, 0:CTX])
                        for T in range(8):
                            pa = proj_fm(w_main, T)
                            pb_ = proj_fm(w_perm, T)
                            P.tt("dve", ra[T % 2][:], pa[:], cosT[:, T * 512:(T + 1) * 512], ALU.mult)
                            P.tt("dve", rb[T % 2][:], pb_[:], sinT[:, T * 512:(T + 1) * 512], ALU.mult)
                            P.tt("pool", row[:, off + T * 512:off + (T + 1) * 512], ra[T % 2][:], rb[T % 2][:], ALU.add)
                        if which == 0:
                            P.ld("sp", qT_s.ap()[h], row[:, 0:L])
                        else:
                            P.ld("sp", kT_s.ap()[h], row[:, 0:LK])
                        n += 1
                P.barrier()

            with contextlib.ExitStack() as st:
                zrows = [sbt(st, "zrow%d" % i, [128, L + 2], F32) for i in range(2)]
                t1 = sbt(st, "t1", [128, L], F32)
                urows = [sbt(st, "urow%d" % i, [128, L], BF16) for i in range(2)]
                ucc = [sbt(st, "ucc%d" % i, [128, 128, 32], BF16) for i in range(2)]
                hcw = sbt(st, "hcw", [128, 12, 3], F32)
                hcb = sbt(st, "hcb", [128, 12], F32)
                pU = [pst(st, "pU%d" % i, [128, 4, 128], BF16) for i in range(2)]
                P.ld("sp", hcw[:], hcw_d.ap())
                P.ld("sp", hcb[:], hcb_d.ap())
                for zrow in zrows:
                    P.memset("pool", zrow[:, 0:1], 0.0)
                    P.memset("pool", zrow[:, L + 1:L + 2], 0.0)
                def b2_phase2(cc):
                    urow = urows[cc % 2]
                    u = ucc[cc % 2]
                    for jq in range(8):
                        pp = pU[jq % 2]
                        for jj in range(4):
                            j = jq * 4 + jj
                            P.tr(pp[:, jj, :], urow[:, j * 128:(j + 1) * 128], ident[:], sig=(jj == 3))
                        P.cp("act", u[:, :, jq * 4:(jq + 1) * 4], pp[:].rearrange("p j c -> p c j"))
                    P.ld("sp", hyu_s.ap()[:, cc * 4096:(cc + 1) * 4096], u[:].rearrange("p c j -> p (c j)"))

                for cc in range(12):
                    zrow = zrows[cc % 2]
                    urow = urows[cc % 2]
                    wt = load_w(win_d, cc * 128)
                    for T in range(8):
                        ps = proj_fm(wt, T)
                        P.cp("act", zrow[:, 1 + T * 512:1 + (T + 1) * 512], ps[:])
                    P.ts("pool", t1[:], zrow[:, 0:L], hcw[:, cc, 0:1], hcb[:, cc:cc + 1], ALU.mult, ALU.add)
                    P.stt("dve", t1[:], zrow[:, 1:L + 1], hcw[:, cc, 1:2], t1[:], ALU.mult, ALU.add)
                    P.stt("dve", urow[:], zrow[:, 2:L + 2], hcw[:, cc, 2:3], t1[:], ALU.mult, ALU.add)
                    if cc > 0:
                        b2_phase2(cc - 1)
                    if cc == 11:
                        b2_phase2(cc)
                P.barrier()

            with contextlib.ExitStack() as st:
                grow = [sbt(st, "grow%d" % i, [128, L], BF16) for i in range(2)]
                wv = sbt(st, "wv", [128, 8, 512], BF16)
                vrow = [sbt(st, "vrow%d" % i, [128, 512], BF16) for i in range(2)]
                for gc in range(16):
                    wt = load_w(win_d, 3072 + gc * 128)
                    g = grow[gc % 2]
                    for T in range(8):
                        ps = proj_fm(wt, T)
                        P.act(g[:, T * 512:(T + 1) * 512], ps[:], AF.Sigmoid)
                    P.ld("sp", gat_s.ap()[gc], g[:])
                for c4 in range(4):
                    wt = load_w(win_d, 2560 + c4 * 128)
                    P.cp("dve", wv[:, :, c4 * 128:(c4 + 1) * 128], wt[:])
                for t in range(NTT):
                    ps = psB[ctr["ps"] % 4]
                    ctr["ps"] += 1
                    for kc in range(8):
                        lhs = xcT[:, kc, t * 128:(t + 1) * 128] if t < 2 else xmT[:, kc, (t - 2) * 128:(t - 1) * 128]
                        P.mm(ps[:], lhs, wv[:, kc, :], start=(kc == 0), stop=(kc == 7), sig=(kc == 7))
                    v = vrow[t % 2]
                    P.cp("act", v[:], ps[:])
                    P.ld("sp", V_s.ap()[t], v[:])
                P.barrier()

        with contextlib.ExitStack() as st:
            fw1 = sbt(st, "fw1", [33, 64], F32); fw2 = sbt(st, "fw2", [64, 64], F32); fw3 = sbt(st, "fw3", [64, 64], F32)
            fwo = sbt(st, "fwo", [64, 2048], F32)
            ffr = sbt(st, "ffr", [64, 3], F32); fbt = sbt(st, "fbt", [64, 3], F32); ffb = sbt(st, "ffb", [64, 3], F32)
            zp = sbt(st, "zp", [33, L], F32)
            hA = sbt(st, "hA", [64, L], F32); hB = sbt(st, "hB", [64, L], F32)
            targ = [sbt(st, "targ%d" % i, [64, 2048], F32) for i in range(2)]
            targm = [sbt(st, "targm%d" % i, [64, 2048], F32) for i in range(2)]
            dct = [sbt(st, "dct%d" % i, [128, L], F32) for i in range(2)]
            arow = [sbt(st, "arow%d" % i, [128, L], BF16) for i in range(2)]
            pf = [pst(st, "pf%d" % i, [128, 2048]) for i in range(2)]
            fwob = sbt(st, "fwob", [64, 2048], BF16)
            hAb = sbt(st, "hAb", [64, L], BF16)
            P.ld("sp", fw1[:], fw1_d.ap()); P.ld("sp", fw2[:], fw2_d.ap()); P.ld("sp", fw3[:], fw3_d.ap())
            P.ld("act", fwo[:], fwo_d.ap()); P.ld("sp", ffr[:], ffr_d.ap()); P.ld("sp", fbt[:], fb_d.ap())
            P.tt("dve", ffb[:], ffr[:], fbt[:], ALU.mult)
            P.cp("act", fwob[:], fwo[:])
            npf = 0
            nrow = 0
            for ev in range(2):
                P.ld("sp", zp[:], (zpf_d if ev == 0 else zpr_d).ap())
                srcs = [(zp, fw1, 33), (hA, fw2, 64), (hB, fw3, 64)]
                dsts = [hA, hB, hA]
                for li in range(3):
                    src, wt, kk = srcs[li]
                    dst = dsts[li]
                    for hf in range(2):
                        ps = pf[npf % 2]
                        ta = targ[npf % 2]
                        tm = targm[npf % 2]
                        npf += 1
                        for q4 in range(4):
                            col = hf * 2048 + q4 * 512
                            P.mm(ps[0:64, q4 * 512:(q4 + 1) * 512], wt[0:kk, :], src[0:kk, col:col + 512], sig=(q4 == 3))
                        P.ts("dve", ta[:], ps[0:64, :], ffr[:, li:li + 1], ffb[:, li:li + 1], ALU.mult, ALU.add)
                        P.ts("dve", tm[:], ta[:], PI, -2.0 * PI, ALU.is_gt, ALU.mult)
                        P.tt("dve", ta[:], ta[:], tm[:], ALU.add)
                        P.ts("dve", tm[:], ta[:], -PI, 2.0 * PI, ALU.is_lt, ALU.mult)
                        P.tt("dve", ta[:], ta[:], tm[:], ALU.add)
                        P.ts("dve", ta[:], ta[:], PI, -PI, ALU.min, ALU.max)
                        P.act(dst[:, hf * 2048:(hf + 1) * 2048], ta[:], AF.Sin)
                P.cp("act", hAb[:, 0:2048], hA[:, 0:2048])
                P.cp("dve", hAb[:, 2048:4096], hA[:, 2048:4096])
                for o in range(2):
                    for cc in range(4):
                        dc = dct[nrow % 2]
                        ar = arow[nrow % 2]
                        nrow += 1
                        P.ld("sp" if nrow % 2 == 0 else "act", dc[:],
                             (dcf_d if ev == 0 else dcr_d).ap()[cc * 128:(cc + 1) * 128, :])
                        col = o * 1024 + ev * 512 + cc * 128
                        for hf in range(2):
                            ps = pf[npf % 2]
                            npf += 1
                            for q4 in range(4):
                                c_ = hf * 2048 + q4 * 512
                                P.mm(ps[:, q4 * 512:(q4 + 1) * 512], fwob[:, col:col + 128], hAb[:, c_:c_ + 512], sig=(q4 == 3))
                            P.tt("dve", ar[:, hf * 2048:(hf + 1) * 2048], ps[:], dc[:, hf * 2048:(hf + 1) * 2048], ALU.mult)
                        rows = A_s.ap()[o * HYW + cc * 128:o * HYW + (cc + 1) * 128, :]
                        if ev == 0:
                            P.ld("sp", rows[:, 4095:8191], ar[:])
                        else:
                            P.ld("sp", rows[:, 0:4095], ar[:, 0:4095])
            P.barrier()

        G = 16
        S = 4
        KB = 128 // S
        TW = 8192 - KB
        NG = G // S
        with contextlib.ExitStack() as stY:
            yall = sbt(stY, "yall", [128, HYW, 32], BF16)
            with contextlib.ExitStack() as st:
                qh = sbt(st, "qh", [128, L], BF16)
                kh = sbt(st, "kh", [128, LK], BF16)
                Vh = sbt(st, "Vh", [128, NTT, 128], BF16)
                ones_f = sbt(st, "ones_f", [128, 128], F32)
                ones_b = sbt(st, "ones_b", [128, 128], BF16)
                NR = 4
                PT = [sbt(st, "PT%d" % i, [128, 512], BF16) for i in range(NR)]
                rc = [sbt(st, "rc%d" % i, [128, 512], F32) for i in range(2)]
                acc = [sbt(st, "acc%d" % i, [128, 512], F32) for i in range(2)]
                on = [sbt(st, "on%d" % i, [128, 512], F32) for i in range(4)]
                oc = [sbt(st, "oc%d" % i, [128, 512], F32) for i in range(2)]
                osq = [sbt(st, "osq%d" % i, [128, 512], F32) for i in range(2)]
                rst = [sbt(st, "rst%d" % i, [128, 512], F32) for i in range(2)]
                orow = [sbt(st, "orow%d" % i, [128, 512], BF16) for i in range(2)]
                pS = [pst(st, "pS%d" % i, [128, 512]) for i in range(2)]
                pO = [pst(st, "pO%d" % i, [128, 512]) for i in range(2)]
                pSm = [pst(st, "pSm%d" % i, [128, 512]) for i in range(2)]
                NT = 4
                tsk = [sbt(st, "tsk%d" % i, [128, TW], BF16) for i in range(NT)]
                rself = sbt(st, "rself", [128, KB], F32)
                fsel = sbt(st, "fsel", [128, S, S, 128], BF16)
                b0 = sbt(st, "b0", [128, HYW], F32); b1 = sbt(st, "b1", [128, HYW], F32)
                vg = [sbt(st, "vg%d" % i, [128, G, 32], BF16) for i in range(2)]
                x1g = [sbt(st, "x1g%d" % i, [128, G, 32], BF16) for i in range(2)]
                x2g = [sbt(st, "x2g%d" % i, [128, G, 32], BF16) for i in range(2)]
                vr = sbt(st, "vr", [128, NG, S, 32, S], BF16)
                vb = sbt(st, "vb", [128, G, 32], F32)
                tmp = sbt(st, "tmpE", [128, G, 32], F32)
                zz = sbt(st, "zz", [128, G, 32], BF16)
                zr = sbt(st, "zr", [128, NG, S, 32, S], BF16)
                zb = sbt(st, "zb", [128, G, 32], F32)
                pc = [pst(st, "pc%d" % o, [128, NG, 32, S]) for o in range(2)]
                pj = pc[1]

                P.memset("pool", ones_f[:], 1.0)
                P.memset("pool", ones_b[:], 1.0)
                P.ld("sp", b0[:], hyb_d.ap()[0].partition_broadcast(128))
                P.ld("sp", b1[:], hyb_d.ap()[1].partition_broadcast(128))
                hy3 = hyu_s.ap().rearrange("p (c j) -> p c j", j=32)
                P.memset("pool", fsel[:], 0.0)
                for hi_ in range(S):
                    P.memset("pool", rself[:], 0.0)
                    P.op("pool", (lambda e, b_=-(KB * hi_ + KB - 1): e.affine_select(
                        out=rself[:], in_=rself[:], pattern=[[1, KB]], compare_op=ALU.not_equal, fill=1.0,
                        base=b_, channel_multiplier=1)), reads=[rself.name], writes=[rself.name])
                    for sl_ in range(S):
                        P.cp("dve", fsel[:, hi_, sl_, KB * sl_:KB * sl_ + KB], rself[:])

                steps = [(h, Q, c, kb) for h in range(4) for Q in range(8) for c in range(2) for kb in range(NTT)]
                NS = len(steps)

                def gen_C():
                    pending = []
                    state = {"head": -1, "qk": -1}

                    def load_head(h):
                        P.ld("sp", qh[:], qT_s.ap()[h])
                        P.ld("sp", kh[:], kT_s.ap()[h])
                        for t0_ in (0, 17):
                            P.ld("sp", Vh[:, t0_:t0_ + 17, :],
                                 V_s.ap()[t0_:t0_ + 17, :, h * 128:(h + 1) * 128].rearrange("t p v -> p t v"))

                    def ensure_qk(m):
                        if m <= state["qk"] or m >= NS:
                            return
                        h, Q, c, kb = steps[m]
                        if h != state["head"]:
                            load_head(h)
                            state["head"] = h
                        P.mm(pS[m % 2][:], kh[64 * c:64 * c + 64, kb * 128:(kb + 1) * 128],
                             qh[64 * c:64 * c + 64, Q * 512:(Q + 1) * 512])
                        P.act(PT[m % NR][:], pS[m % 2][:], AF.Exp, scale=0.125)
                        state["qk"] = m

                    def fin1(h, Q, c, s_):
                        g = (h * 8 + Q) % 2
                        P.mm(pSm[s_][:], ones_f[:], acc[s_][:])
                        P.op("dve", (lambda e, a=rc[c][:], b=pSm[s_][:]: e.reciprocal(out=a, in_=b)),
                             reads=[pSm[s_].name], writes=[rc[c].name])
                        P.tt("dve", on[2 * g + c][:], pO[s_][:], rc[c][:], ALU.mult)
                        if c == 1:
                            P.stt("dve", oc[g][:], on[2 * g + 1][:], neglam[:], on[2 * g][:], ALU.mult, ALU.add)
                            P.tt("pool", osq[g][:], oc[g][:], oc[g][:], ALU.mult)

                    def fin2(h, Q, s_):
                        g = (h * 8 + Q) % 2
                        P.mm(pSm[s_][:], ones_f[:], osq[g][:])
                        P.act(rst[g][:], pSm[s_][:], AF.Sqrt, bias=epsb[:], scale=1.0 / 128.0)
                        P.op("dve", (lambda e, a=rst[g][:]: e.reciprocal(out=a, in_=a)),
                             reads=[rst[g].name], writes=[rst[g].name])
                        P.stt("dve", orow[g][:], oc[g][:], subg[:], rst[g][:], ALU.mult, ALU.mult)
                        P.ld("pool", oT_s.ap()[h][:, Q * 512:(Q + 1) * 512], orow[g][:])

                    for n in range(NS):
                        h, Q, c, kb = steps[n]
                        s_ = (n // NTT) % 2
                        ensure_qk(n)
                        if n + 1 < NS and steps[n + 1][0] == h:
                            ensure_qk(n + 1)
                        P.mm(pO[s_][:], Vh[:, kb, :], PT[n % NR][:], start=(kb == 0), stop=(kb == NTT - 1),
                             sig=(kb == NTT - 1))
                        if kb == 0:
                            P.cp("dve", acc[s_][:], PT[n % NR][:])
                        else:
                            P.tt("dve", acc[s_][:], acc[s_][:], PT[n % NR][:], ALU.add)
                        for item in list(pending):
                            if item[0] <= n:
                                item[1]()
                                pending.remove(item)
                        if kb == NTT - 1:
                            pending.append((n + 3, (lambda h=h, Q=Q, c=c, s_=s_: fin1(h, Q, c, s_))))
                            if c == 1:
                                pending.append((n + 10, (lambda h=h, Q=Q, s_=s_: fin2(h, Q, s_))))
                            if n + 1 < NS and steps[n + 1][0] != h:
                                for item in pending:
                                    item[1]()
                                pending = []
                        yield
                    for item in pending:
                        item[1]()

                elist = [0] + [e for e in range(-(32 * S - 1), 31 * S + 1) if e != 0]
                ectr = {"tsk": 0}

                def conv_grp(o, c, rhs_t, grp, pb):
                    tk = tsk[ectr["tsk"] % NT]
                    ectr["tsk"] += 1
                    keys = []
                    for sl in range(S):
                        q = "sp"
                        key = tk.name + "_s%d" % sl
                        keys.append(key)
                        P.dma(q, (lambda e, o_=tk[KB * sl:KB * sl + KB, :],
                                         i_=bass.AP(A_s, (o * HYW + c + sl) * 8192, [[1, KB], [1, TW]]):
                                     e.dma_start(out=o_, in_=i_)),
                              reads=[], writes=[key])
                    last = len(elist) - 1
                    for n_, e in enumerate(elist):
                        i_lo = max(0, -((-e) // S))
                        i_hi = min(31, (32 * S - 1 + e) // S)
                        nn = i_hi - i_lo + 1
                        hi = (-e) % S
                        j0 = i_lo - (e + hi) // S
                        assert nn > 0 and (e + hi) % S == 0 and 0 <= j0 and j0 + nn <= 32
                        ce = KB * e + 4096 - KB
                        assert 0 <= ce and ce + 128 <= TW
                        P.op("pe", (lambda en, o_=pb[:, grp, i_lo:i_hi + 1, :].rearrange("p i s -> p (i s)"),
                                           l_=tk[:, ce:ce + 128],
                                           r_=rhs_t[:, grp, hi, j0:j0 + nn, :].rearrange("p j s -> p (j s)"),
                                           s0=(n_ == 0), s1=(n_ == last): en.matmul(o_, l_, r_, start=s0, stop=s1)),
                             reads=keys + [rhs_t.name], writes=[pb.name], sig=(n_ == last))
                        if n_ % 28 == 27:
                            yield

                def reverse(dst, src):
                    sv = src[:].rearrange("p (g s) j -> p g s j", s=S)
                    pjf = pj[:].rearrange("p g i s -> p (g i s)")
                    for hi in range(S):
                        for sl in range(S):
                            P.mm(pjf[:, sl * NG * 32:(sl + 1) * NG * 32], fsel[:, hi, sl, :], sv[:, :, sl, :],
                                 sig=(sl == S - 1))
                        P.cp("act", dst[:, :, hi, :, :],
                             pjf.rearrange("p (s g j) -> p g j s", s=S, g=NG))

                def gen_E():
                    for g in range(HYW // G):
                        i = g % 2
                        c0 = g * G
                        P.ld("sp", vg[i][:], hy3[:, c0:c0 + G, :])
                        P.ld("sp", x1g[i][:], hy3[:, HYW + c0:HYW + c0 + G, :])
                        P.ld("sp", x2g[i][:], hy3[:, 2 * HYW + c0:2 * HYW + c0 + G, :])
                        reverse(vr, vg[i])
                        P.tt("pool", vb[:], vg[i][:], b0[:, c0:c0 + G].unsqueeze(2).to_broadcast([128, G, 32]), ALU.mult)
                        for grp in range(NG):
                            yield from conv_grp(0, c0 + S * grp, vr, grp, pc[0])
                        P.tt("dve", tmp[:].rearrange("p (q h) i -> p q h i", h=S), pc[0][:].rearrange("p q i h -> p q h i"),
                             vb[:].rearrange("p (q h) i -> p q h i", h=S), ALU.add)
                        P.tt("dve", zz[:], tmp[:], x1g[i][:], ALU.mult)
                        reverse(zr, zz)
                        P.tt("pool", zb[:], zz[:], b1[:, c0:c0 + G].unsqueeze(2).to_broadcast([128, G, 32]), ALU.mult)
                        for grp in range(NG):
                            yield from conv_grp(1, c0 + S * grp, zr, grp, pc[1])
                        P.tt("dve", tmp[:].rearrange("p (q h) i -> p q h i", h=S), pc[1][:].rearrange("p q i h -> p q h i"),
                             zb[:].rearrange("p (q h) i -> p q h i", h=S), ALU.add)
                        P.tt("dve", yall[:, c0:c0 + G, :], tmp[:], x2g[i][:], ALU.mult)
                        yield

                gC = gen_C()
                gE = gen_E()
                doneC = doneE = False
                while not (doneC and doneE):
                    if not doneE:
                        try:
                            next(gE)
                        except StopIteration:
                            doneE = True
                    if not doneC:
                        try:
                            next(gC)
                        except StopIteration:
                            doneC = True
                P.barrier()

            with contextlib.ExitStack() as st:
                yrow = [sbt(st, "yrow%d" % i, [128, 512], BF16) for i in range(2)]
                pY = [pst(st, "pY%d" % i, [128, 4, 128], BF16) for i in range(2)]
                n = 0
                for cc in range(4):
                    for iq in range(8):
                        for ii in range(4):
                            P.tr(pY[n % 2][:, ii, :], yall[:, cc * 128:(cc + 1) * 128, iq * 4 + ii], ident[:], sig=(ii == 3))
                        r = yrow[n % 2]
                        P.cp("act", r[:], pY[n % 2][:].rearrange("p a b -> p (a b)"))
                        P.ld("sp", yhT_s.ap()[cc][:, iq * 512:(iq + 1) * 512], r[:])
                        n += 1
                P.barrier()

        def stream_weight_bf16(wt, src_d, nk, ncol, wfs):
            n = 0
            for kc in range(nk):
                for c0 in range(0, ncol, 1024):
                    s_ = wfs[n % 2]
                    n += 1
                    P.ld("sp" if n % 2 == 0 else "act", s_[:], src_d.ap()[kc * 128:(kc + 1) * 128, c0:c0 + 1024])
                    P.cp("dve" if n % 2 == 0 else "act", wt[:, kc, c0:c0 + 1024], s_[:])
            return wt

        def residual_ln(pfx, st_tiles, ps_lo, ps_hi, gbc_mod, res_src_ap, lg, lb, dst):
            rs, r, stats, mv, rstd, tmp = st_tiles
            P.ld("sp", rs[:], res_src_ap)
            P.tt("dve", r[:, 0:512], ps_lo[:], gbc_mod[:, 0:512], ALU.mult)
            P.tt("dve", r[:, 512:1024], ps_hi[:], gbc_mod[:, 512:1024], ALU.mult)
            P.stt("dve", r[:], rs[:], ALPHA, r[:], ALU.mult, ALU.add)
            layer_norm_tile(pfx, r, lg, lb, dst, stats, mv, rstd, tmp, add_eng=("dve" if pfx == "F" else "pool"))

        with contextlib.ExitStack() as stF:
            x1mT = sbt(stF, "x1mT", [128, 8, L], BF16)
            with contextlib.ExitStack() as st:
                who = sbt(st, "who", [128, 4, D], BF16)
                wao = sbt(st, "wao", [128, 4, D], BF16)
                wo = sbt(st, "wo", [128, 8, D], BF16)
                with contextlib.ExitStack() as stw:
                    wfs = [sbt(stw, "wfs%d" % i, [128, 1024], F32) for i in range(2)]
                    stream_weight_bf16(who, who_d, 4, D, wfs)
                    stream_weight_bf16(wao, wao_d, 4, D, wfs)
                    stream_weight_bf16(wo, wout_d, 8, D, wfs)
                    P.barrier()
                l1g = sbt(st, "l1g", [128, D], F32); l1b = sbt(st, "l1b", [128, D], F32)
                P.ld("sp", l1g[:], l1g_d.ap().partition_broadcast(128))
                P.ld("sp", l1b[:], l1b_d.ap().partition_broadcast(128))
                yh = [sbt(st, "yh%d" % i, [128, 4, 512], BF16) for i in range(2)]
                ot = [sbt(st, "ot%d" % i, [128, 4, 512], BF16) for i in range(2)]
                gt = [sbt(st, "gt%d" % i, [128, 16, 512], BF16) for i in range(2)]
                mT = sbt(st, "mT", [128, 8, 512], BF16)
                m1 = sbt(st, "m1", [128, 512], F32); m2 = sbt(st, "m2", [128, 512], F32)
                lnt = [(sbt(st, "rsF%d" % i, [128, D], F32), sbt(st, "rF%d" % i, [128, D], F32),
                        sbt(st, "statsF%d" % i, [128, 2, 6], F32), sbt(st, "mvF%d" % i, [128, 2], F32),
                        sbt(st, "rstdF%d" % i, [128, 1], F32), sbt(st, "tmpF%d" % i, [128, D], F32)) for i in range(2)]
                x1t = [sbt(st, "x1t0", [128, D], F32)] * 2
                x1bs = [sbt(st, "x1b%d" % i, [128, D], BF16) for i in range(2)]
                pA = [pst(st, "pA%d" % i, [128, 512]) for i in range(2)]
                pB2 = [pst(st, "pB2%d" % i, [128, 512]) for i in range(2)]
                pYl = [pst(st, "pYl%d" % i, [128, 512]) for i in range(2)]
                pT2 = pst(st, "pT2", [128, 8, 128], BF16)

                def load_T(T):
                    i = T % 2
                    sl = slice(T * 512, (T + 1) * 512)
                    P.ld("sp", yh[i][:], yhT_s.ap()[:, :, sl].rearrange("c p t -> p c t"))
                    P.ld("act", ot[i][:], oT_s.ap()[:, :, sl].rearrange("c p t -> p c t"))
                    P.ld("sp", gt[i][:], gat_s.ap()[:, :, sl].rearrange("g p t -> p g t"))

                def transposes(x1b, tok0):
                    for kc in range(8):
                        P.tr(pT2[:, kc, :], x1b[:, kc * 128:(kc + 1) * 128], ident[:], sig=(kc == 7))
                    for kc in range(8):
                        P.act(x1mT[:, kc, tok0:tok0 + 128], pT2[:, kc, :], AF.Identity,
                              bias=modT[:, 24 + kc, 0:1], scale=onep[:, 8 + kc, 0:1])

                nt = 0
                prev = None
                load_T(0)
                for T in range(8):
                    i = T % 2
                    if T + 1 < 8:
                        load_T(T + 1)
                    for fc in range(8):
                        a = pA[fc % 2]; b = pB2[fc % 2]
                        for cc in range(4):
                            P.mm(a[:], who[:, cc, fc * 128:(fc + 1) * 128], yh[i][:, cc, :], start=(cc == 0), stop=(cc == 3),
                                 sig=(cc == 3))
                        for cc in range(4):
                            P.mm(b[:], wao[:, cc, fc * 128:(fc + 1) * 128], ot[i][:, cc, :], start=(cc == 0), stop=(cc == 3),
                                 sig=(cc == 3))
                        P.tt("dve", m1[:], a[:], gt[i][:, fc, :], ALU.mult)
                        P.tt("dve", m2[:], b[:], gt[i][:, 8 + fc, :], ALU.mult)
                        P.tt("pool", mT[:, fc, :], m1[:], m2[:], ALU.add)
                    for tb in range(4):
                        tok0 = T * 512 + tb * 128
                        for hf in range(2):
                            for kc in range(8):
                                P.mm(pYl[hf][:], mT[:, kc, tb * 128:(tb + 1) * 128], wo[:, kc, hf * 512:(hf + 1) * 512],
                                     start=(kc == 0), stop=(kc == 7), sig=(kc == 7))
                        if prev is not None:
                            transposes(*prev)
                        xo = x1t[nt % 2]
                        x1b = x1bs[nt % 2]
                        lnt_ = lnt[nt % 2]
                        nt += 1
                        residual_ln("F", lnt_, pYl[0], pYl[1], g1bc,
                                    xln_s.ap()[tok0:tok0 + 128, :], l1g, l1b, xo)
                        P.ld("pool", x1_s.ap()[tok0:tok0 + 128, :], xo[:])
                        P.cp("act", x1b[:], xo[:])
                        prev = (x1b, tok0)
                transposes(*prev)
                P.barrier()

            with contextlib.ExitStack() as st:
                wuf = [sbt(st, "wuf%d" % i, [128, 8, 128], F32) for i in range(2)]
                wub = [sbt(st, "wub%d" % i, [128, 8, 128], BF16) for i in range(4)]
                g_items = []
                for fc_ in range(NFC):
                    for part_ in range(2):
                        g_items.append(part_ * NFC + fc_)
                gctr = {"issued": 0}

                def g_issue():
                    i = gctr["issued"]
                    if i >= len(g_items):
                        return
                    gctr["issued"] += 1
                    ch_ = g_items[i]
                    P.ld("sp", wuf[i % 2][:], wup_d.ap()[:, ch_ * 128:(ch_ + 1) * 128].rearrange("(kc p) c -> p kc c", p=128))
                    P.cp("pool", wub[i % 4][:], wuf[i % 2][:])
                zr2 = [sbt(st, "zr2_%d" % i, [128, L + 2], F32) for i in range(2)]
                tgs = [sbt(st, "tg%d" % i, [128, L], F32) for i in range(2)]
                tas = [sbt(st, "ta_%d" % i, [128, L], F32) for i in range(2)]
                hrow = [sbt(st, "hrow%d" % i, [128, L], BF16) for i in range(2)]
                fcw = sbt(st, "fcw", [128, 44, 3], F32); fcb = sbt(st, "fcb", [128, 44], F32)
                pG = [pst(st, "pG%d" % i, [128, 512]) for i in range(4)]
                P.ld("sp", fcw[:], fcw_d.ap()); P.ld("sp", fcb[:], fcb_d.ap())
                for i in range(2):
                    P.memset("pool", zr2[i][:, 0:1], 0.0)
                    P.memset("pool", zr2[i][:, L + 1:L + 2], 0.0)
                nw = 0
                npg = 0
                for fc in range(NFC):
                    tg = tgs[fc % 2]
                    ta_ = tas[fc % 2]
                    for part in range(2):
                        ch = part * NFC + fc
                        while gctr["issued"] <= min(nw + 2, len(g_items) - 1):
                            g_issue()
                        wbl = wub[nw % 4]
                        nw += 1
                        z = zr2[part]
                        for T in range(8):
                            ps = pG[npg % 4]
                            npg += 1
                            for kc in range(8):
                                P.mm(ps[:], wbl[:, kc, :], x1mT[:, kc, T * 512:(T + 1) * 512], start=(kc == 0), stop=(kc == 7),
                                     sig=(kc == 7))
                            P.cp("act", z[:, 1 + T * 512:1 + (T + 1) * 512], ps[:])
                        dst = ta_ if part == 0 else tg
                        P.ts("pool", dst[:], z[:, 0:L], fcw[:, ch, 0:1], fcb[:, ch:ch + 1], ALU.mult, ALU.add)
                        P.stt("dve", dst[:], z[:, 1:L + 1], fcw[:, ch, 1:2], dst[:], ALU.mult, ALU.add)
                        P.stt("dve", dst[:], z[:, 2:L + 2], fcw[:, ch, 2:3], dst[:], ALU.mult, ALU.add)
                    P.act(tg[:], tg[:], AF.Silu)
                    hr = hrow[fc % 2]
                    P.tt("dve", hr[:], tg[:], ta_[:], ALU.mult)
                    if fc > 0:
                        P.ld("sp", hT_s.ap()[fc - 1], hrow[(fc - 1) % 2][:])
                P.ld("sp", hT_s.ap()[NFC - 1], hrow[(NFC - 1) % 2][:])
                P.barrier()

        with contextlib.ExitStack() as st:
            wfs = [sbt(st, "wfsH%d" % i, [128, 1024], F32) for i in range(2)]
            wd = sbt(st, "wd", [128, NFC, D], BF16)
            stream_weight_bf16(wd, wdn_d, NFC, D, wfs)
            l2g = sbt(st, "l2g", [128, D], F32); l2b = sbt(st, "l2b", [128, D], F32)
            P.ld("sp", l2g[:], l2g_d.ap().partition_broadcast(128))
            P.ld("sp", l2b[:], l2b_d.ap().partition_broadcast(128))
            ht = [sbt(st, "ht%d" % i, [128, NFC, 512], BF16) for i in range(2)]
            lnt = [(sbt(st, "rsH%d" % i, [128, D], F32), sbt(st, "rH%d" % i, [128, D], F32),
                    sbt(st, "statsH%d" % i, [128, 2, 6], F32), sbt(st, "mvH%d" % i, [128, 2], F32),
                    sbt(st, "rstdH%d" % i, [128, 1], F32), sbt(st, "tmpH%d" % i, [128, D], F32)) for i in range(2)]
            xo = [sbt(st, "xoH%d" % i, [128, D], F32) for i in range(2)]
            pD = [[pst(st, "pD%d_%d" % (s, i), [128, 512]) for i in range(2)] for s in range(2)]
            nt = 0

            def load_ht(T):
                P.ld("sp" if T % 2 == 0 else "act", ht[T % 2][:],
                     hT_s.ap()[:, :, T * 512:(T + 1) * 512].rearrange("f p t -> p f t"))

            load_ht(0)
            for T in range(8):
                i = T % 2
                if T + 1 < 8:
                    load_ht(T + 1)
                for tb in range(4):
                    tok0 = T * 512 + tb * 128
                    pp = pD[nt % 2]
                    for hf in range(2):
                        for fc in range(NFC):
                            P.mm(pp[hf][:], ht[i][:, fc, tb * 128:(tb + 1) * 128], wd[:, fc, hf * 512:(hf + 1) * 512],
                                 start=(fc == 0), stop=(fc == NFC - 1), sig=(fc == NFC - 1))
                    o_ = xo[nt % 2]
                    lnt_ = lnt[nt % 2]
                    nt += 1
                    residual_ln("H", lnt_, pp[0], pp[1], g2bc,
                                x1_s.ap()[tok0:tok0 + 128, :], l2g, l2b, o_)
                    P.ld("pool", out_d.ap()[tok0:tok0 + 128, :], o_[:])
            P.barrier()

        P.run()
    return nc


def host_constants():
    f32 = np.float32
    rows = L // 64
    row = np.repeat(np.arange(rows, dtype=f32), 64)
    col = np.tile(np.arange(64, dtype=f32), rows)
    inv = (10000.0 ** (-np.arange(0, 32, 2, dtype=f32) / 32.0)).astype(f32)
    cosT = np.zeros((128, L), f32)
    sinT = np.zeros((128, L), f32)
    for p in range(128):
        d = p % 64
        a = d // 32
        i = d % 32
        f = i % 16
        ang = (row if a == 0 else col) * inv[f]
        cosT[p] = np.cos(ang)
        sinT[p] = (-np.sin(ang)) if i < 16 else np.sin(ang)
    t = np.linspace(0.0, 1.0, L, dtype=f32)[:, None]
    w = (f32(2.0 * math.pi / L) * np.arange(L, dtype=f32))[:, None]
    f = np.linspace(1e-4, 15, 16, dtype=f32)[None, :]
    z = np.concatenate([t, np.cos(f * w), -np.sin(f * w)], -1).astype(f32)
    min_decay = math.log(1e-2) / 1.5
    max_decay = math.log(1e-2) / 0.3
    deltas = np.linspace(min_decay, max_decay, HYW, dtype=f32)
    decay = np.exp(-t * np.abs(deltas)[None, :]).astype(f32)
    return dict(
        rope_cos=cosT, rope_sin=sinT,
        zposT_f=np.ascontiguousarray(z.T), zposT_r=np.ascontiguousarray(z[::-1].T),
        decayT_f=np.ascontiguousarray(decay.T), decayT_r=np.ascontiguousarray(decay[::-1].T),
    )


def per_part(v, nch):
    return np.ascontiguousarray(np.asarray(v, np.float32).reshape(nch, 128).T)


def make_in_maps(inp):
    f32 = np.float32
    g = {k: np.asarray(v, f32) for k, v in inp.items()}
    const = host_constants()
    w_in = g["w_in"][0]
    perm = np.zeros(1024, np.int64)
    for cidx in range(1024):
        base = 1536 + cidx
        d = cidx % 64
        i = d % 32
        perm[cidx] = base + 16 if i < 16 else base - 16
    w_qkp = np.ascontiguousarray(w_in[:, perm])
    cw = g["hy_conv_w"][0]
    hy_cw = np.ascontiguousarray(np.stack([per_part(cw[k], 12) for k in range(3)], -1))
    fw = g["ffn_conv_w"][0]
    ffn_cw = np.ascontiguousarray(np.stack([per_part(fw[k], 44) for k in range(3)], -1))
    shared = dict(
        ln_in_g=g["ln_in_g"], ln_in_b=g["ln_in_b"],
        w_ada=g["w_ada"][0], b_ada=g["b_ada"][0], b_adaT=per_part(g["b_ada"][0], 48),
        w_in=w_in, w_qkp=w_qkp,
        hy_cw=hy_cw, hy_cb=per_part(g["hy_conv_b"][0], 12),
        f_w1=g["hy_f_w1"][0], f_w2=g["hy_f_w2"][0], f_w3=g["hy_f_w3"][0], f_wout=g["hy_f_wout"][0],
        f_freqT=np.ascontiguousarray(g["hy_f_freq"][0].T),
        f_bT=np.ascontiguousarray(np.stack([g["hy_f_b1"][0], g["hy_f_b2"][0], g["hy_f_b3"][0]], -1)),
        hy_bias=g["hy_bias"][0],
        lamv=np.ascontiguousarray(np.stack([g["lam_q1"][0], g["lam_k1"][0], g["lam_q2"][0], g["lam_k2"][0]], 0)),
        subln_g=np.ascontiguousarray(g["at_subln_g"][0].reshape(128, 1)),
        w_hy_o=g["w_hy_o"][0], w_at_o=g["w_at_o"][0], w_out=g["w_out"][0],
        ln1_g=g["ln1_g"][0], ln1_b=g["ln1_b"][0], ln2_g=g["ln2_g"][0], ln2_b=g["ln2_b"][0],
        ffn_w_up=g["ffn_w_up"][0], ffn_cw=ffn_cw, ffn_cb=per_part(g["ffn_conv_b"][0], 44),
        ffn_w_down=g["ffn_w_down"][0],
    )
    shared.update(const)
    maps = []
    cc = per_part(g["c_ctx"], 8)
    for b in range(8):
        m = dict(shared)
        m["x"] = np.ascontiguousarray(g["x"][b])
        m["ctx"] = np.ascontiguousarray(g["ctx"][b])
        m["cT"] = np.ascontiguousarray(np.stack([per_part(g["c"][b], 8), cc], -1))
        maps.append(m)
    return maps


_NC = None


def kernel(**inputs):
    global _NC
    if _NC is None:
        _NC = build_program()
    maps = make_in_maps(inputs)
    res = run_bass_kernel_spmd(_NC, maps, core_ids=list(range(8)))
    out = np.stack([np.asarray(r["out"], np.float32) for r in res.results], 0)
    return out
```

```python
import math
import contextlib
import numpy as np
import concourse.bass as bass
import concourse.mybir as mybir
from concourse.bass_utils import run_bass_kernel_spmd

F32 = mybir.dt.float32
BF16 = mybir.dt.bfloat16
ALU = mybir.AluOpType
AF = mybir.ActivationFunctionType
AX = mybir.AxisListType

ENGS = ("pe", "act", "dve", "pool", "sp")

D = 1024
L = 4096
CTX = 256
NTT = 34
LK = CTX + L
HYW = 512
DFF = 2816
NFC = DFF // 128
ALPHA = 2.0 ** 0.25
LAM_INIT = 0.2
PI = math.pi
DEBUG = False


class Prog:
    NDMA = 6

    def __init__(self, nc):
        self.nc = nc
        self.ops = {e: [] for e in ENGS}
        self.cnt = {e: 0 for e in ENGS}
        self.waited = {e: {} for e in ENGS}
        self.last_w = {}
        self.reads = {}
        self.dma_uses = {}
        self.dma_rr = {e: 0 for e in ENGS}
        self.semkeys = set()
        self.sems = {}

    def _deps(self, eng, reads, writes):
        deps = []
        for k in reads:
            if k in self.last_w:
                deps.append(self.last_w[k])
        for k in writes:
            if k in self.last_w:
                deps.append(self.last_w[k])
            for ev in self.reads.get(k, ()):
                if ev[0] == eng:
                    continue
                deps.append(ev)
        best = {}
        for sk, v in deps:
            if eng == "pe" and sk == "pe":
                continue
            if v > best.get(sk, 0):
                best[sk] = v
        out = []
        w = self.waited[eng]
        for sk, v in best.items():
            if w.get(sk, 0) >= v:
                continue
            w[sk] = v
            out.append((sk, v))
        return out

    def _commit(self, ev, reads, writes):
        for k in reads:
            self.reads.setdefault(k, []).append(ev)
        for k in writes:
            self.last_w[k] = ev
            self.reads[k] = []

    def op(self, eng, fn, reads=(), writes=(), sig=True):
        waits = self._deps(eng, reads, writes)
        if sig:
            self.cnt[eng] += 1
            ev = (eng, self.cnt[eng])
            self.semkeys.add(eng)
            self.ops[eng].append(("op", fn, waits, eng))
        else:
            ev = (eng, self.cnt[eng] + 1)
            self.ops[eng].append(("op", fn, waits, None))
        self._commit(ev, reads, writes)
        return ev

    def dma(self, q, fn, reads=(), writes=()):
        slot = self.dma_rr[q] % self.NDMA
        self.dma_rr[q] += 1
        sk = ("dma", q, slot)
        self.semkeys.add(sk)
        uses = self.dma_uses.get(sk, 0)
        waits = self._deps(q, reads, writes)
        if uses > 0 and self.waited[q].get(sk, 0) < 16 * uses:
            self.waited[q][sk] = 16 * uses
            waits.append((sk, 16 * uses))
        uses += 1
        self.dma_uses[sk] = uses
        ev = (sk, 16 * uses)
        self.ops[q].append(("dma", fn, waits, sk))
        self._commit(ev, reads, writes)
        return ev

    def barrier(self):
        evs = []
        for e in ENGS:
            if self.cnt[e] > 0:
                evs.append((e, self.cnt[e]))
        for sk, uses in self.dma_uses.items():
            evs.append((sk, 16 * uses))
        for e in ENGS:
            waits = []
            for sk, v in evs:
                if sk == e:
                    continue
                if self.waited[e].get(sk, 0) >= v:
                    continue
                self.waited[e][sk] = v
                waits.append((sk, v))
            if waits:
                self.ops[e].append(("wait", None, waits, None))
        self.last_w = {}
        self.reads = {}

    def run(self):
        nc = self.nc
        with contextlib.ExitStack() as st:
            for sk in sorted(self.semkeys, key=str):
                name = sk if isinstance(sk, str) else "d_%s_%d" % (sk[1], sk[2])
                self.sems[sk] = st.enter_context(nc.semaphore("s_" + name))
            block = st.enter_context(nc.Block())

            def mk(ename):
                def body(e):
                    for kind, fn, waits, sk in self.ops[ename]:
                        for wk, wv in waits:
                            e.wait_ge(self.sems[wk], wv)
                        if kind == "wait":
                            continue
                        ins = fn(e)
                        if sk is not None:
                            ins.then_inc(self.sems[sk], 16 if kind == "dma" else 1)
                return body

            block.tensor(mk("pe"))
            block.scalar(mk("act"))
            block.vector(mk("dve"))
            block.gpsimd(mk("pool"))
            block.sync(mk("sp"))

    @staticmethod
    def _k(*aps):
        out = []
        for a in aps:
            if a is None or isinstance(a, (int, float)):
                continue
            out.append(a.name)
        return out

    def mm(self, out, lhsT, rhs, start=True, stop=True, sig=True):
        return self.op("pe", lambda e: e.matmul(out, lhsT, rhs, start=start, stop=stop),
                       reads=self._k(lhsT, rhs), writes=self._k(out), sig=sig)

    def tr(self, out, in_, ident, sig=True):
        return self.op("pe", lambda e: e.transpose(out, in_, ident),
                       reads=self._k(in_, ident), writes=self._k(out), sig=sig)

    def act(self, out, in_, func, bias=None, scale=None, accum_out=None):
        kw = {}
        if bias is not None:
            kw["bias"] = bias
        if scale is not None:
            kw["scale"] = scale
        if accum_out is not None:
            kw["accum_out"] = accum_out
        return self.op("act", lambda e: e.activation(out=out, in_=in_, func=func, **kw),
                       reads=self._k(in_, bias, scale), writes=self._k(out, accum_out))

    def tt(self, eng, out, in0, in1, op):
        return self.op(eng, lambda e: e.tensor_tensor(out=out, in0=in0, in1=in1, op=op),
                       reads=self._k(in0, in1), writes=self._k(out))

    def ts(self, eng, out, in0, s1, s2, op0, op1=None):
        if op1 is None:
            return self.op(eng, lambda e: e.tensor_scalar(out=out, in0=in0, scalar1=s1, scalar2=None, op0=op0),
                           reads=self._k(in0, s1), writes=self._k(out))
        return self.op(eng, lambda e: e.tensor_scalar(out=out, in0=in0, scalar1=s1, scalar2=s2, op0=op0, op1=op1),
                       reads=self._k(in0, s1, s2), writes=self._k(out))

    def stt(self, eng, out, in0, scalar, in1, op0, op1):
        return self.op(eng, lambda e: e.scalar_tensor_tensor(out=out, in0=in0, scalar=scalar, in1=in1, op0=op0, op1=op1),
                       reads=self._k(in0, scalar, in1), writes=self._k(out))

    def cp(self, eng, out, in_):
        if eng == "act":
            return self.act(out, in_, AF.Copy)
        return self.op(eng, lambda e: e.tensor_copy(out=out, in_=in_), reads=self._k(in_), writes=self._k(out))

    def memset(self, eng, ap, val):
        return self.op(eng, lambda e: e.memset(ap, val), writes=self._k(ap))

    def ld(self, q, out, in_):
        return self.dma(q, lambda e: e.dma_start(out=out, in_=in_), reads=self._k(in_), writes=self._k(out))


def build_program():
    nc = bass.Bass("TRN2", target_bir_lowering=False)
    P = Prog(nc)

    def din(name, shape, dt=F32):
        return nc.dram_tensor(name, list(shape), dt, kind="ExternalInput")

    def dscr(name, shape, dt, dbg=False):
        if DEBUG and dbg:
            return nc.dram_tensor(name, list(shape), dt, kind="ExternalOutput")
        return nc.dram_tensor(name, list(shape), dt)

    x_d = din("x", [L, D]); ctx_d = din("ctx", [CTX, D])
    cT_d = din("cT", [128, 8, 2])
    lng_d = din("ln_in_g", [D]); lnb_d = din("ln_in_b", [D])
    wada_d = din("w_ada", [D, 6 * D]); bada_d = din("b_ada", [6 * D]); badaT_d = din("b_adaT", [128, 48])
    win_d = din("w_in", [D, 5120]); wqkp_d = din("w_qkp", [D, 1024])
    cos_d = din("rope_cos", [128, L]); sin_d = din("rope_sin", [128, L])
    hcw_d = din("hy_cw", [128, 12, 3]); hcb_d = din("hy_cb", [128, 12])
    fw1_d = din("f_w1", [33, 64]); fw2_d = din("f_w2", [64, 64]); fw3_d = din("f_w3", [64, 64])
    fwo_d = din("f_wout", [64, 2048]); ffr_d = din("f_freqT", [64, 3]); fb_d = din("f_bT", [64, 3])
    zpf_d = din("zposT_f", [33, L]); zpr_d = din("zposT_r", [33, L])
    dcf_d = din("decayT_f", [HYW, L]); dcr_d = din("decayT_r", [HYW, L])
    hyb_d = din("hy_bias", [2, HYW])
    lam_d = din("lamv", [4, 64])
    sub_d = din("subln_g", [128, 1])
    who_d = din("w_hy_o", [HYW, D]); wao_d = din("w_at_o", [512, D]); wout_d = din("w_out", [D, D])
    l1g_d = din("ln1_g", [D]); l1b_d = din("ln1_b", [D]); l2g_d = din("ln2_g", [D]); l2b_d = din("ln2_b", [D])
    wup_d = din("ffn_w_up", [D, 2 * DFF]); fcw_d = din("ffn_cw", [128, 44, 3]); fcb_d = din("ffn_cb", [128, 44])
    wdn_d = din("ffn_w_down", [DFF, D])
    out_d = nc.dram_tensor("out", [L, D], F32, kind="ExternalOutput")

    xln_s = dscr("xln_s", [L, D], F32, True)
    qT_s = dscr("qT_s", [4, 128, L], BF16, True)
    kT_s = dscr("kT_s", [4, 128, LK], BF16, True)
    V_s = dscr("V_s", [NTT, 128, 512], BF16, True)
    hyu_s = dscr("hyu_s", [128, 1536 * 32], BF16, True)
    gat_s = dscr("gat_s", [16, 128, L], BF16, True)
    A_s = dscr("A_s", [2 * HYW, 8192], BF16, True)
    oT_s = dscr("oT_s", [4, 128, L], BF16, True)
    yhT_s = dscr("yhT_s", [4, 128, L], BF16, True)
    x1_s = dscr("x1_s", [L, D], F32, True)
    hT_s = dscr("hT_s", [NFC, 128, L], BF16, True)

    top = contextlib.ExitStack()

    def sbt(st, name, shape, dt):
        return st.enter_context(nc.sbuf_tensor("sb_" + name, list(shape), dt))

    def pst(st, name, shape, dt=F32):
        return st.enter_context(nc.psum_tensor("ps_" + name, list(shape), dt))

    with top:
        ident = sbt(top, "ident", [128, 128], BF16)
        jrev = sbt(top, "jrev", [128, 128], BF16)
        identf = sbt(top, "identf", [128, 128], F32)
        epsb = sbt(top, "epsb", [128, 1], F32)
        npib = sbt(top, "npib", [128, 1], F32)
        modT = sbt(top, "modT", [128, 48, 2], F32)
        onep = sbt(top, "onep", [128, 16, 2], F32)
        g1bc = sbt(top, "g1bc", [128, D], F32)
        g2bc = sbt(top, "g2bc", [128, D], F32)
        neglam = sbt(top, "neglam", [128, 1], F32)
        subg = sbt(top, "subg", [128, 1], F32)

        P.memset("pool", identf[:], 0.0)
        P.op("pool", lambda e: e.affine_select(out=identf[:], in_=identf[:], pattern=[[-1, 128]],
                                               compare_op=ALU.not_equal, fill=1.0, base=0, channel_multiplier=1),
             reads=[identf.name], writes=[identf.name])
        P.cp("dve", ident[:], identf[:])
        P.memset("pool", identf[:], 0.0)
        P.op("pool", lambda e: e.affine_select(out=identf[:], in_=identf[:], pattern=[[1, 128]],
                                               compare_op=ALU.not_equal, fill=1.0, base=-127, channel_multiplier=1),
             reads=[identf.name], writes=[identf.name])
        P.cp("dve", jrev[:], identf[:])
        P.memset("pool", epsb[:], 1e-5)
        P.memset("pool", npib[:], -PI)

        with contextlib.ExitStack() as st:
            cT = sbt(st, "cT", [128, 8, 2], F32)
            sc = sbt(st, "sc", [128, 8, 2], F32)
            scb = sbt(st, "scb", [128, 8, 128], F32)
            wa = [sbt(st, "wa%d" % i, [128, 8, 512], F32) for i in range(4)]
            wab = [sbt(st, "wab%d" % i, [128, 8, 512], BF16) for i in range(2)]
            sc_b = sbt(st, "sc_b", [128, 8, 2], BF16)
            scb_b = sbt(st, "scb_b", [128, 8, 128], BF16)
            badaT = sbt(st, "badaT", [128, 48], F32)
            bg = sbt(st, "bg", [128, D], F32)
            lamv = sbt(st, "lamv_t", [128, 4, 64], F32)
            lamp = sbt(st, "lamp", [128, 2, 64], F32)
            lams = sbt(st, "lams", [128, 2], F32)
            pm = pst(st, "pm", [128, 48, 2])
            pb = pst(st, "pb", [128, 512])
            P.ld("sp", cT[:], cT_d.ap())
            P.ld("sp", badaT[:], badaT_d.ap())
            P.ld("sp", subg[:], sub_d.ap())
            P.ld("sp", lamv[:].rearrange("p a b -> p (a b)"),
                 lam_d.ap().rearrange("a b -> (a b)").partition_broadcast(128))
            P.act(sc[:], cT[:], AF.Silu)
            P.cp("dve", scb[:], sc[:, :, 0:1].to_broadcast([128, 8, 128]))
            P.cp("dve", sc_b[:], sc[:])
            P.cp("dve", scb_b[:], scb[:])
            P.ts("dve", subg[:], subg[:], 1.0 - LAM_INIT, None, ALU.mult)
            P.tt("dve", lamp[:, 0, :], lamv[:, 0, :], lamv[:, 1, :], ALU.mult)
            P.tt("dve", lamp[:, 1, :], lamv[:, 2, :], lamv[:, 3, :], ALU.mult)
            P.op("dve", lambda e: e.reduce_sum(out=lams[:], in_=lamp[:], axis=AX.X), reads=[lamp.name], writes=[lams.name])
            P.act(lams[:], lams[:], AF.Exp)
            P.tt("dve", neglam[:], lams[:, 1:2], lams[:, 0:1], ALU.subtract)
            P.ts("dve", neglam[:], neglam[:], -LAM_INIT, None, ALU.add)
            for g in range(12):
                w = wa[g % 4]
                P.ld("sp" if g % 2 == 0 else "act", w[:],
                     wada_d.ap()[:, g * 512:(g + 1) * 512].rearrange("(kc p) c -> p kc c", p=128))
                wq = wab[g % 2]
                P.cp("dve" if g % 2 == 0 else "act", wq[:], w[:])
                for jj in range(4):
                    j = g * 4 + jj
                    for kc in range(8):
                        P.mm(pm[:, j, :], wq[:, kc, jj * 128:(jj + 1) * 128], sc_b[:, kc, :],
                             start=(kc == 0), stop=(kc == 7), sig=(kc == 7))
                if g in (4, 5, 10, 11):
                    for kc in range(8):
                        P.mm(pb[:], scb_b[:, kc, :], wq[:, kc, :], start=(kc == 0), stop=(kc == 7), sig=(kc == 7))
                    dst = g1bc if g in (4, 5) else g2bc
                    half = g % 2
                    P.ld("sp", bg[:, 0:512], bada_d.ap()[g * 512:(g + 1) * 512].partition_broadcast(128))
                    P.tt("dve", dst[:, half * 512:(half + 1) * 512], pb[:], bg[:, 0:512], ALU.add)
            P.tt("dve", modT[:], pm[:], badaT[:].unsqueeze(2).to_broadcast([128, 48, 2]), ALU.add)
            P.ts("dve", onep[:, 0:8, :], modT[:, 8:16, :], 1.0, None, ALU.add)
            P.ts("dve", onep[:, 8:16, :], modT[:, 32:40, :], 1.0, None, ALU.add)
            P.barrier()

        def layer_norm_tile(pfx, src, gbc, bbc, dst, stats, mv, rstd, tmp, add_eng="pool"):
            for h in range(2):
                P.op("dve", (lambda e, h=h: e.bn_stats(out=stats[:, h, :], in_=src[:, h * 512:(h + 1) * 512])),
                     reads=[src.name], writes=[stats.name])
            P.op("dve", lambda e: e.bn_aggr(out=mv[:], in_=stats[:].rearrange("p a b -> p (a b)")),
                 reads=[stats.name], writes=[mv.name])
            P.act(rstd[:], mv[:, 1:2], AF.Sqrt, bias=epsb[:], scale=1.0)
            P.op("dve", lambda e: e.reciprocal(out=rstd[:], in_=rstd[:]), reads=[rstd.name], writes=[rstd.name])
            P.ts("dve", tmp[:], src[:], mv[:, 0:1], rstd[:], ALU.subtract, ALU.mult)
            P.tt("pool", tmp[:], tmp[:], gbc[:], ALU.mult)
            P.tt(add_eng, dst[:], tmp[:], bbc[:], ALU.add)

        with contextlib.ExitStack() as stA:
            xmT = sbt(stA, "xmT", [128, 8, L], BF16)
            xcT = sbt(stA, "xcT", [128, 8, CTX], BF16)
            with contextlib.ExitStack() as st:
                gbc = sbt(st, "gbc", [128, D], F32)
                bbc = sbt(st, "bbc", [128, D], F32)
                xt = [sbt(st, "xt%d" % i, [128, D], F32) for i in range(3)]
                xh = [sbt(st, "xh%d" % i, [128, D], F32) for i in range(3)]
                xl = [sbt(st, "xl%d" % i, [128, D], F32) for i in range(3)]
                xb = [sbt(st, "xb%d" % i, [128, D], BF16) for i in range(2)]
                stats = [sbt(st, "stats%d" % i, [128, 2, 6], F32) for i in range(3)]
                mv = [sbt(st, "mv%d" % i, [128, 2], F32) for i in range(3)]
                rstd = [sbt(st, "rstd%d" % i, [128, 1], F32) for i in range(3)]
                pT = [pst(st, "pT%d" % i, [128, 8, 128], BF16) for i in range(2)]
                P.ld("sp", gbc[:], lng_d.ap().partition_broadcast(128))
                P.ld("sp", bbc[:], lnb_d.ap().partition_broadcast(128))
                def phase2(t):
                    i = t % 3
                    is_ctx = t < 2
                    if not is_ctx:
                        P.ld("pool", xln_s.ap()[(t - 2) * 128:(t - 1) * 128, :], xl[i][:])
                    P.cp("act", xb[t % 2][:], xl[i][:])
                    for kc in range(8):
                        P.tr(pT[t % 2][:, kc, :], xb[t % 2][:, kc * 128:(kc + 1) * 128], ident[:], sig=(kc == 7))
                    w = 1 if is_ctx else 0
                    for kc in range(8):
                        dst = xcT[:, kc, t * 128:(t + 1) * 128] if is_ctx else xmT[:, kc, (t - 2) * 128:(t - 1) * 128]
                        P.act(dst, pT[t % 2][:, kc, :], AF.Identity, bias=modT[:, kc, w:w + 1], scale=onep[:, kc, w:w + 1])

                for t in range(NTT):
                    i = t % 3
                    is_ctx = t < 2
                    src = ctx_d.ap()[t * 128:(t + 1) * 128, :] if is_ctx else x_d.ap()[(t - 2) * 128:(t - 1) * 128, :]
                    P.ld("sp", xt[i][:], src)
                    layer_norm_tile("A", xt[i], gbc, bbc, xl[i], stats[i], mv[i], rstd[i], xh[i])
                    if t > 0:
                        phase2(t - 1)
                phase2(NTT - 1)
                P.barrier()

            wf = [sbt(stA, "wf%d" % i, [128, 8, 128], F32) for i in range(2)]
            wb = [sbt(stA, "wb%d" % i, [128, 8, 128], BF16) for i in range(4)]
            psB = [pst(stA, "psB%d" % i, [128, 512]) for i in range(4)]
            ctr = {"w": 0, "ps": 0}

            w_items = []
            for which_ in range(2):
                for h_ in range(4):
                    w_items.append((win_d, 1536 + which_ * 512 + h_ * 128))
                    w_items.append((wqkp_d, which_ * 512 + h_ * 128))
            for cc_ in range(12):
                w_items.append((win_d, cc_ * 128))
            for gc_ in range(16):
                w_items.append((win_d, 3072 + gc_ * 128))
            for c4_ in range(4):
                w_items.append((win_d, 2560 + c4_ * 128))
            ctr["issued"] = 0

            def _issue_w():
                i = ctr["issued"]
                if i >= len(w_items):
                    return
                ctr["issued"] += 1
                src_d, col0 = w_items[i]
                P.ld("sp", wf[i % 2][:], src_d.ap()[:, col0:col0 + 128].rearrange("(kc p) c -> p kc c", p=128))
                P.cp("pool", wb[i % 4][:], wf[i % 2][:])

            def load_w(src_d, col0):
                i = ctr["w"]
                ctr["w"] += 1
                assert w_items[i][1] == col0 and w_items[i][0] is src_d
                while ctr["issued"] <= min(i + 2, len(w_items) - 1):
                    _issue_w()
                return wb[i % 4]

            def proj_fm(wt, T):
                ps = psB[ctr["ps"] % 4]
                ctr["ps"] += 1
                for kc in range(8):
                    P.mm(ps[:], wt[:, kc, :], xmT[:, kc, T * 512:(T + 1) * 512], start=(kc == 0), stop=(kc == 7),
                         sig=(kc == 7))
                return ps

            with contextlib.ExitStack() as st:
                cosT = sbt(st, "cosT", [128, L], F32)
                sinT = sbt(st, "sinT", [128, L], F32)
                ra = [sbt(st, "ra%d" % i, [128, 512], F32) for i in range(2)]
                rb = [sbt(st, "rb%d" % i, [128, 512], F32) for i in range(2)]
                qrow = [sbt(st, "qrow%d" % i, [128, LK], BF16) for i in range(2)]
                P.ld("sp", cosT[:], cos_d.ap())
                P.ld("sp", sinT[:], sin_d.ap())
                n = 0
                for which in range(2):
                    for h in range(4):
                        col = 1536 + which * 512 + h * 128
                        w_main = load_w(win_d, col)
                        w_perm = load_w(wqkp_d, which * 512 + h * 128)
                        row = qrow[n % 2]
                        off = CTX if which == 1 else 0
                        if which == 1:
                            ps = psB[ctr["ps"] % 4]
                            ctr["ps"] += 1
                            for kc in range(8):
                                P.mm(ps[:, 0:CTX], w_main[:, kc, :], xcT[:, kc, :], start=(kc == 0), stop=(kc == 7),
                                     sig=(kc == 7))
                            P.cp("act", row[:, 0:CTX], ps[:, 0:CTX])
                        for T in range(8):
                            pa = proj_fm(w_main, T)
                            pb_ = proj_fm(w_perm, T)
                            P.tt("dve", ra[T % 2][:], pa[:], cosT[:, T * 512:(T + 1) * 512], ALU.mult)
                            P.tt("dve", rb[T % 2][:], pb_[:], sinT[:, T * 512:(T + 1) * 512], ALU.mult)
                            P.tt("pool", row[:, off + T * 512:off + (T + 1) * 512], ra[T % 2][:], rb[T % 2][:], ALU.add)
                        if which == 0:
                            P.ld("sp", qT_s.ap()[h], row[:, 0:L])
                        else:
                            P.ld("sp", kT_s.ap()[h], row[:, 0:LK])
                        n += 1
                P.barrier()

            with contextlib.ExitStack() as st:
                zrows = [sbt(st, "zrow%d" % i, [128, L + 2], F32) for i in range(2)]
                t1 = sbt(st, "t1", [128, L], F32)
                urows = [sbt(st, "urow%d" % i, [128, L], BF16) for i in range(2)]
                ucc = [sbt(st, "ucc%d" % i, [128, 128, 32], BF16) for i in range(2)]
                hcw = sbt(st, "hcw", [128, 12, 3], F32)
                hcb = sbt(st, "hcb", [128, 12], F32)
                pU = [pst(st, "pU%d" % i, [128, 4, 128], BF16) for i in range(2)]
                P.ld("sp", hcw[:], hcw_d.ap())
                P.ld("sp", hcb[:], hcb_d.ap())
                for zrow in zrows:
                    P.memset("pool", zrow[:, 0:1], 0.0)
                    P.memset("pool", zrow[:, L + 1:L + 2], 0.0)
                def b2_phase2(cc):
                    urow = urows[cc % 2]
                    u = ucc[cc % 2]
                    for jq in range(8):
                        pp = pU[jq % 2]
                        for jj in range(4):
                            j = jq * 4 + jj
                            P.tr(pp[:, jj, :], urow[:, j * 128:(j + 1) * 128], ident[:], sig=(jj == 3))
                        P.cp("act", u[:, :, jq * 4:(jq + 1) * 4], pp[:].rearrange("p j c -> p c j"))
                    P.ld("sp", hyu_s.ap()[:, cc * 4096:(cc + 1) * 4096], u[:].rearrange("p c j -> p (c j)"))

                for cc in range(12):
                    zrow = zrows[cc % 2]
                    urow = urows[cc % 2]
                    wt = load_w(win_d, cc * 128)
                    for T in range(8):
                        ps = proj_fm(wt, T)
                        P.cp("act", zrow[:, 1 + T * 512:1 + (T + 1) * 512], ps[:])
                    P.ts("pool", t1[:], zrow[:, 0:L], hcw[:, cc, 0:1], hcb[:, cc:cc + 1], ALU.mult, ALU.add)
                    P.stt("dve", t1[:], zrow[:, 1:L + 1], hcw[:, cc, 1:2], t1[:], ALU.mult, ALU.add)
                    P.stt("dve", urow[:], zrow[:, 2:L + 2], hcw[:, cc, 2:3], t1[:], ALU.mult, ALU.add)
                    if cc > 0:
                        b2_phase2(cc - 1)
                    if cc == 11:
                        b2_phase2(cc)
                P.barrier()

            with contextlib.ExitStack() as st:
                grow = [sbt(st, "grow%d" % i, [128, L], BF16) for i in range(2)]
                wv = sbt(st, "wv", [128, 8, 512], BF16)
                vrow = [sbt(st, "vrow%d" % i, [128, 512], BF16) for i in range(2)]
                for gc in range(16):
                    wt = load_w(win_d, 3072 + gc * 128)
                    g = grow[gc % 2]
                    for T in range(8):
                        ps = proj_fm(wt, T)
                        P.act(g[:, T * 512:(T + 1) * 512], ps[:], AF.Sigmoid)
                    P.ld("sp", gat_s.ap()[gc], g[:])
                for c4 in range(4):
                    wt = load_w(win_d, 2560 + c4 * 128)
                    P.cp("dve", wv[:, :, c4 * 128:(c4 + 1) * 128], wt[:])
                for t in range(NTT):
                    ps = psB[ctr["ps"] % 4]
                    ctr["ps"] += 1
                    for kc in range(8):
                        lhs = xcT[:, kc, t * 128:(t + 1) * 128] if t < 2 else xmT[:, kc, (t - 2) * 128:(t - 1) * 128]
                        P.mm(ps[:], lhs, wv[:, kc, :], start=(kc == 0), stop=(kc == 7), sig=(kc == 7))
                    v = vrow[t % 2]
                    P.cp("act", v[:], ps[:])
                    P.ld("sp", V_s.ap()[t], v[:])
                P.barrier()

        with contextlib.ExitStack() as st:
            fw1 = sbt(st, "fw1", [33, 64], F32); fw2 = sbt(st, "fw2", [64, 64], F32); fw3 = sbt(st, "fw3", [64, 64], F32)
            fwo = sbt(st, "fwo", [64, 2048], F32)
            ffr = sbt(st, "ffr", [64, 3], F32); fbt = sbt(st, "fbt", [64, 3], F32); ffb = sbt(st, "ffb", [64, 3], F32)
            zp = sbt(st, "zp", [33, L], F32)
            hA = sbt(st, "hA", [64, L], F32); hB = sbt(st, "hB", [64, L], F32)
            targ = [sbt(st, "targ%d" % i, [64, 2048], F32) for i in range(2)]
            targm = [sbt(st, "targm%d" % i, [64, 2048], F32) for i in range(2)]
            dct = [sbt(st, "dct%d" % i, [128, L], F32) for i in range(2)]
            arow = [sbt(st, "arow%d" % i, [128, L], BF16) for i in range(2)]
            pf = [pst(st, "pf%d" % i, [128, 2048]) for i in range(2)]
            fwob = sbt(st, "fwob", [64, 2048], BF16)
            hAb = sbt(st, "hAb", [64, L], BF16)
            P.ld("sp", fw1[:], fw1_d.ap()); P.ld("sp", fw2[:], fw2_d.ap()); P.ld("sp", fw3[:], fw3_d.ap())
            P.ld("act", fwo[:], fwo_d.ap()); P.ld("sp", ffr[:], ffr_d.ap()); P.ld("sp", fbt[:], fb_d.ap())
            P.tt("dve", ffb[:], ffr[:], fbt[:], ALU.mult)
            P.cp("act", fwob[:], fwo[:])
            npf = 0
            nrow = 0
            for ev in range(2):
                P.ld("sp", zp[:], (zpf_d if ev == 0 else zpr_d).ap())
                srcs = [(zp, fw1, 33), (hA, fw2, 64), (hB, fw3, 64)]
                dsts = [hA, hB, hA]
                for li in range(3):
                    src, wt, kk = srcs[li]
                    dst = dsts[li]
                    for hf in range(2):
                        ps = pf[npf % 2]
                        ta = targ[npf % 2]
                        tm = targm[npf % 2]
                        npf += 1
                        for q4 in range(4):
                            col = hf * 2048 + q4 * 512
                            P.mm(ps[0:64, q4 * 512:(q4 + 1) * 512], wt[0:kk, :], src[0:kk, col:col + 512], sig=(q4 == 3))
                        P.ts("dve", ta[:], ps[0:64, :], ffr[:, li:li + 1], ffb[:, li:li + 1], ALU.mult, ALU.add)
                        P.ts("dve", tm[:], ta[:], PI, -2.0 * PI, ALU.is_gt, ALU.mult)
                        P.tt("dve", ta[:], ta[:], tm[:], ALU.add)
                        P.ts("dve", tm[:], ta[:], -PI, 2.0 * PI, ALU.is_lt, ALU.mult)
                        P.tt("dve", ta[:], ta[:], tm[:], ALU.add)
                        P.ts("dve", ta[:], ta[:], PI, -PI, ALU.min, ALU.max)
                        P.act(dst[:, hf * 2048:(hf + 1) * 2048], ta[:], AF.Sin)
                P.cp("act", hAb[:, 0:2048], hA[:, 0:2048])
                P.cp("dve", hAb[:, 2048:4096], hA[:, 2048:4096])
                for o in range(2):
                    for cc in range(4):
                        dc = dct[nrow % 2]
                        ar = arow[nrow % 2]
                        nrow += 1
                        P.ld("sp" if nrow % 2 == 0 else "act", dc[:],
                             (dcf_d if ev == 0 else dcr_d).ap()[cc * 128:(cc + 1) * 128, :])
                        col = o * 1024 + ev * 512 + cc * 128
                        for hf in range(2):
                            ps = pf[npf % 2]
                            npf += 1
                            for q4 in range(4):
                                c_ = hf * 2048 + q4 * 512
                                P.mm(ps[:, q4 * 512:(q4 + 1) * 512], fwob[:, col:col + 128], hAb[:, c_:c_ + 512], sig=(q4 == 3))
                            P.tt("dve", ar[:, hf * 2048:(hf + 1) * 2048], ps[:], dc[:, hf * 2048:(hf + 1) * 2048], ALU.mult)
                        rows = A_s.ap()[o * HYW + cc * 128:o * HYW + (cc + 1) * 128, :]
                        if ev == 0:
                            P.ld("sp", rows[:, 4095:8191], ar[:])
                        else:
                            P.ld("sp", rows[:, 0:4095], ar[:, 0:4095])
            P.barrier()

        G = 16
        S = 4
        KB = 128 // S
        TW = 8192 - KB
        NG = G // S
        with contextlib.ExitStack() as stY:
            yall = sbt(stY, "yall", [128, HYW, 32], BF16)
            with contextlib.ExitStack() as st:
                qh = sbt(st, "qh", [128, L], BF16)
                kh = sbt(st, "kh", [128, LK], BF16)
                Vh = sbt(st, "Vh", [128, NTT, 128], BF16)
                ones_f = sbt(st, "ones_f", [128, 128], F32)
                ones_b = sbt(st, "ones_b", [128, 128], BF16)
                NR = 4
                PT = [sbt(st, "PT%d" % i, [128, 512], BF16) for i in range(NR)]
                rc = [sbt(st, "rc%d" % i, [128, 512], F32) for i in range(2)]
                acc = [sbt(st, "acc%d" % i, [128, 512], F32) for i in range(2)]
                on = [sbt(st, "on%d" % i, [128, 512], F32) for i in range(4)]
                oc = [sbt(st, "oc%d" % i, [128, 512], F32) for i in range(2)]
                osq = [sbt(st, "osq%d" % i, [128, 512], F32) for i in range(2)]
                rst = [sbt(st, "rst%d" % i, [128, 512], F32) for i in range(2)]
                orow = [sbt(st, "orow%d" % i, [128, 512], BF16) for i in range(2)]
                pS = [pst(st, "pS%d" % i, [128, 512]) for i in range(2)]
                pO = [pst(st, "pO%d" % i, [128, 512]) for i in range(2)]
                pSm = [pst(st, "pSm%d" % i, [128, 512]) for i in range(2)]
                NT = 4
                tsk = [sbt(st, "tsk%d" % i, [128, TW], BF16) for i in range(NT)]
                rself = sbt(st, "rself", [128, KB], F32)
                fsel = sbt(st, "fsel", [128, S, S, 128], BF16)
                b0 = sbt(st, "b0", [128, HYW], F32); b1 = sbt(st, "b1", [128, HYW], F32)
                vg = [sbt(st, "vg%d" % i, [128, G, 32], BF16) for i in range(2)]
                x1g = [sbt(st, "x1g%d" % i, [128, G, 32], BF16) for i in range(2)]
                x2g = [sbt(st, "x2g%d" % i, [128, G, 32], BF16) for i in range(2)]
                vr = sbt(st, "vr", [128, NG, S, 32, S], BF16)
                vb = sbt(st, "vb", [128, G, 32], F32)
                tmp = sbt(st, "tmpE", [128, G, 32], F32)
                zz = sbt(st, "zz", [128, G, 32], BF16)
                zr = sbt(st, "zr", [128, NG, S, 32, S], BF16)
                zb = sbt(st, "zb", [128, G, 32], F32)
                pc = [pst(st, "pc%d" % o, [128, NG, 32, S]) for o in range(2)]
                pj = pc[1]

                P.memset("pool", ones_f[:], 1.0)
                P.memset("pool", ones_b[:], 1.0)
                P.ld("sp", b0[:], hyb_d.ap()[0].partition_broadcast(128))
                P.ld("sp", b1[:], hyb_d.ap()[1].partition_broadcast(128))
                hy3 = hyu_s.ap().rearrange("p (c j) -> p c j", j=32)
                P.memset("pool", fsel[:], 0.0)
                for hi_ in range(S):
                    P.memset("pool", rself[:], 0.0)
                    P.op("pool", (lambda e, b_=-(KB * hi_ + KB - 1): e.affine_select(
                        out=rself[:], in_=rself[:], pattern=[[1, KB]], compare_op=ALU.not_equal, fill=1.0,
                        base=b_, channel_multiplier=1)), reads=[rself.name], writes=[rself.name])
                    for sl_ in range(S):
                        P.cp("dve", fsel[:, hi_, sl_, KB * sl_:KB * sl_ + KB], rself[:])

                steps = [(h, Q, c, kb) for h in range(4) for Q in range(8) for c in range(2) for kb in range(NTT)]
                NS = len(steps)

                def gen_C():
                    pending = []
                    state = {"head": -1, "qk": -1}

                    def load_head(h):
                        P.ld("sp", qh[:], qT_s.ap()[h])
                        P.ld("sp", kh[:], kT_s.ap()[h])
                        for t0_ in (0, 17):
                            P.ld("sp", Vh[:, t0_:t0_ + 17, :],
                                 V_s.ap()[t0_:t0_ + 17, :, h * 128:(h + 1) * 128].rearrange("t p v -> p t v"))

                    def ensure_qk(m):
                        if m <= state["qk"] or m >= NS:
                            return
                        h, Q, c, kb = steps[m]
                        if h != state["head"]:
                            load_head(h)
                            state["head"] = h
                        P.mm(pS[m % 2][:], kh[64 * c:64 * c + 64, kb * 128:(kb + 1) * 128],
                             qh[64 * c:64 * c + 64, Q * 512:(Q + 1) * 512])
                        P.act(PT[m % NR][:], pS[m % 2][:], AF.Exp, scale=0.125)
                        state["qk"] = m

                    def fin1(h, Q, c, s_):
                        g = (h * 8 + Q) % 2
                        P.mm(pSm[s_][:], ones_f[:], acc[s_][:])
                        P.op("dve", (lambda e, a=rc[c][:], b=pSm[s_][:]: e.reciprocal(out=a, in_=b)),
                             reads=[pSm[s_].name], writes=[rc[c].name])
                        P.tt("dve", on[2 * g + c][:], pO[s_][:], rc[c][:], ALU.mult)
                        if c == 1:
                            P.stt("dve", oc[g][:], on[2 * g + 1][:], neglam[:], on[2 * g][:], ALU.mult, ALU.add)
                            P.tt("pool", osq[g][:], oc[g][:], oc[g][:], ALU.mult)

                    def fin2(h, Q, s_):
                        g = (h * 8 + Q) % 2
                        P.mm(pSm[s_][:], ones_f[:], osq[g][:])
                        P.act(rst[g][:], pSm[s_][:], AF.Sqrt, bias=epsb[:], scale=1.0 / 128.0)
                        P.op("dve", (lambda e, a=rst[g][:]: e.reciprocal(out=a, in_=a)),
                             reads=[rst[g].name], writes=[rst[g].name])
                        P.stt("dve", orow[g][:], oc[g][:], subg[:], rst[g][:], ALU.mult, ALU.mult)
                        P.ld("pool", oT_s.ap()[h][:, Q * 512:(Q + 1) * 512], orow[g][:])

                    for n in range(NS):
                        h, Q, c, kb = steps[n]
                        s_ = (n // NTT) % 2
                        ensure_qk(n)
                        if n + 1 < NS and steps[n + 1][0] == h:
                            ensure_qk(n + 1)
                        P.mm(pO[s_][:], Vh[:, kb, :], PT[n % NR][:], start=(kb == 0), stop=(kb == NTT - 1),
                             sig=(kb == NTT - 1))
                        if kb == 0:
                            P.cp("dve", acc[s_][:], PT[n % NR][:])
                        else:
                            P.tt("dve", acc[s_][:], acc[s_][:], PT[n % NR][:], ALU.add)
                        for item in list(pending):
                            if item[0] <= n:
                                item[1]()
                                pending.remove(item)
                        if kb == NTT - 1:
                            pending.append((n + 3, (lambda h=h, Q=Q, c=c, s_=s_: fin1(h, Q, c, s_))))
                            if c == 1:
                                pending.append((n + 10, (lambda h=h, Q=Q, s_=s_: fin2(h, Q, s_))))
                            if n + 1 < NS and steps[n + 1][0] != h:
                                for item in pending:
                                    item[1]()
                                pending = []
                        yield
                    for item in pending:
                        item[1]()

                elist = [0] + [e for e in range(-(32 * S - 1), 31 * S + 1) if e != 0]
                ectr = {"tsk": 0}

                def conv_grp(o, c, rhs_t, grp, pb):
                    tk = tsk[ectr["tsk"] % NT]
                    ectr["tsk"] += 1
                    keys = []
                    for sl in range(S):
                        q = "sp"
                        key = tk.name + "_s%d" % sl
                        keys.append(key)
                        P.dma(q, (lambda e, o_=tk[KB * sl:KB * sl + KB, :],
                                         i_=bass.AP(A_s, (o * HYW + c + sl) * 8192, [[1, KB], [1, TW]]):
                                     e.dma_start(out=o_, in_=i_)),
                              reads=[], writes=[key])
                    last = len(elist) - 1
                    for n_, e in enumerate(elist):
                        i_lo = max(0, -((-e) // S))
                        i_hi = min(31, (32 * S - 1 + e) // S)
                        nn = i_hi - i_lo + 1
                        hi = (-e) % S
                        j0 = i_lo - (e + hi) // S
                        assert nn > 0 and (e + hi) % S == 0 and 0 <= j0 and j0 + nn <= 32
                        ce = KB * e + 4096 - KB
                        assert 0 <= ce and ce + 128 <= TW
                        P.op("pe", (lambda en, o_=pb[:, grp, i_lo:i_hi + 1, :].rearrange("p i s -> p (i s)"),
                                           l_=tk[:, ce:ce + 128],
                                           r_=rhs_t[:, grp, hi, j0:j0 + nn, :].rearrange("p j s -> p (j s)"),
                                           s0=(n_ == 0), s1=(n_ == last): en.matmul(o_, l_, r_, start=s0, stop=s1)),
                             reads=keys + [rhs_t.name], writes=[pb.name], sig=(n_ == last))
                        if n_ % 28 == 27:
                            yield

                def reverse(dst, src):
                    sv = src[:].rearrange("p (g s) j -> p g s j", s=S)
                    pjf = pj[:].rearrange("p g i s -> p (g i s)")
                    for hi in range(S):
                        for sl in range(S):
                            P.mm(pjf[:, sl * NG * 32:(sl + 1) * NG * 32], fsel[:, hi, sl, :], sv[:, :, sl, :],
                                 sig=(sl == S - 1))
                        P.cp("act", dst[:, :, hi, :, :],
                             pjf.rearrange("p (s g j) -> p g j s", s=S, g=NG))

                def gen_E():
                    for g in range(HYW // G):
                        i = g % 2
                        c0 = g * G
                        P.ld("sp", vg[i][:], hy3[:, c0:c0 + G, :])
                        P.ld("sp", x1g[i][:], hy3[:, HYW + c0:HYW + c0 + G, :])
                        P.ld("sp", x2g[i][:], hy3[:, 2 * HYW + c0:2 * HYW + c0 + G, :])
                        reverse(vr, vg[i])
                        P.tt("pool", vb[:], vg[i][:], b0[:, c0:c0 + G].unsqueeze(2).to_broadcast([128, G, 32]), ALU.mult)
                        for grp in range(NG):
                            yield from conv_grp(0, c0 + S * grp, vr, grp, pc[0])
                        P.tt("dve", tmp[:].rearrange("p (q h) i -> p q h i", h=S), pc[0][:].rearrange("p q i h -> p q h i"),
                             vb[:].rearrange("p (q h) i -> p q h i", h=S), ALU.add)
                        P.tt("dve", zz[:], tmp[:], x1g[i][:], ALU.mult)
                        reverse(zr, zz)
                        P.tt("pool", zb[:], zz[:], b1[:, c0:c0 + G].unsqueeze(2).to_broadcast([128, G, 32]), ALU.mult)
                        for grp in range(NG):
                            yield from conv_grp(1, c0 + S * grp, zr, grp, pc[1])
                        P.tt("dve", tmp[:].rearrange("p (q h) i -> p q h i", h=S), pc[1][:].rearrange("p q i h -> p q h i"),
                             zb[:].rearrange("p (q h) i -> p q h i", h=S), ALU.add)
                        P.tt("dve", yall[:, c0:c0 + G, :], tmp[:], x2g[i][:], ALU.mult)
                        yield

                gC = gen_C()
                gE = gen_E()
                doneC = doneE = False
                while not (doneC and doneE):
                    if not doneE:
                        try:
                            next(gE)
                        except StopIteration:
                            doneE = True
                    if not doneC:
                        try:
                            next(gC)
                        except StopIteration:
                            doneC = True
                P.barrier()

            with contextlib.ExitStack() as st:
                yrow = [sbt(st, "yrow%d" % i, [128, 512], BF16) for i in range(2)]
                pY = [pst(st, "pY%d" % i, [128, 4, 128], BF16) for i in range(2)]
                n = 0
                for cc in range(4):
                    for iq in range(8):
                        for ii in range(4):
                            P.tr(pY[n % 2][:, ii, :], yall[:, cc * 128:(cc + 1) * 128, iq * 4 + ii], ident[:], sig=(ii == 3))
                        r = yrow[n % 2]
                        P.cp("act", r[:], pY[n % 2][:].rearrange("p a b -> p (a b)"))
                        P.ld("sp", yhT_s.ap()[cc][:, iq * 512:(iq + 1) * 512], r[:])
                        n += 1
                P.barrier()

        def stream_weight_bf16(wt, src_d, nk, ncol, wfs):
            n = 0
            for kc in range(nk):
                for c0 in range(0, ncol, 1024):
                    s_ = wfs[n % 2]
                    n += 1
                    P.ld("sp" if n % 2 == 0 else "act", s_[:], src_d.ap()[kc * 128:(kc + 1) * 128, c0:c0 + 1024])
                    P.cp("dve" if n % 2 == 0 else "act", wt[:, kc, c0:c0 + 1024], s_[:])
            return wt

        def residual_ln(pfx, st_tiles, ps_lo, ps_hi, gbc_mod, res_src_ap, lg, lb, dst):
            rs, r, stats, mv, rstd, tmp = st_tiles
            P.ld("sp", rs[:], res_src_ap)
            P.tt("dve", r[:, 0:512], ps_lo[:], gbc_mod[:, 0:512], ALU.mult)
            P.tt("dve", r[:, 512:1024], ps_hi[:], gbc_mod[:, 512:1024], ALU.mult)
            P.stt("dve", r[:], rs[:], ALPHA, r[:], ALU.mult, ALU.add)
            layer_norm_tile(pfx, r, lg, lb, dst, stats, mv, rstd, tmp, add_eng=("dve" if pfx == "F" else "pool"))

        with contextlib.ExitStack() as stF:
            x1mT = sbt(stF, "x1mT", [128, 8, L], BF16)
            with contextlib.ExitStack() as st:
                who = sbt(st, "who", [128, 4, D], BF16)
                wao = sbt(st, "wao", [128, 4, D], BF16)
                wo = sbt(st, "wo", [128, 8, D], BF16)
                with contextlib.ExitStack() as stw:
                    wfs = [sbt(stw, "wfs%d" % i, [128, 1024], F32) for i in range(2)]
                    stream_weight_bf16(who, who_d, 4, D, wfs)
                    stream_weight_bf16(wao, wao_d, 4, D, wfs)
                    stream_weight_bf16(wo, wout_d, 8, D, wfs)
                    P.barrier()
                l1g = sbt(st, "l1g", [128, D], F32); l1b = sbt(st, "l1b", [128, D], F32)
                P.ld("sp", l1g[:], l1g_d.ap().partition_broadcast(128))
                P.ld("sp", l1b[:], l1b_d.ap().partition_broadcast(128))
                yh = [sbt(st, "yh%d" % i, [128, 4, 512], BF16) for i in range(2)]
                ot = [sbt(st, "ot%d" % i, [128, 4, 512], BF16) for i in range(2)]
                gt = [sbt(st, "gt%d" % i, [128, 16, 512], BF16) for i in range(2)]
                mT = sbt(st, "mT", [128, 8, 512], BF16)
                m1 = sbt(st, "m1", [128, 512], F32); m2 = sbt(st, "m2", [128, 512], F32)
                lnt = [(sbt(st, "rsF%d" % i, [128, D], F32), sbt(st, "rF%d" % i, [128, D], F32),
                        sbt(st, "statsF%d" % i, [128, 2, 6], F32), sbt(st, "mvF%d" % i, [128, 2], F32),
                        sbt(st, "rstdF%d" % i, [128, 1], F32), sbt(st, "tmpF%d" % i, [128, D], F32)) for i in range(2)]
                x1t = [sbt(st, "x1t0", [128, D], F32)] * 2
                x1bs = [sbt(st, "x1b%d" % i, [128, D], BF16) for i in range(2)]
                pA = [pst(st, "pA%d" % i, [128, 512]) for i in range(2)]
                pB2 = [pst(st, "pB2%d" % i, [128, 512]) for i in range(2)]
                pYl = [pst(st, "pYl%d" % i, [128, 512]) for i in range(2)]
                pT2 = pst(st, "pT2", [128, 8, 128], BF16)

                def load_T(T):
                    i = T % 2
                    sl = slice(T * 512, (T + 1) * 512)
                    P.ld("sp", yh[i][:], yhT_s.ap()[:, :, sl].rearrange("c p t -> p c t"))
                    P.ld("act", ot[i][:], oT_s.ap()[:, :, sl].rearrange("c p t -> p c t"))
                    P.ld("sp", gt[i][:], gat_s.ap()[:, :, sl].rearrange("g p t -> p g t"))

                def transposes(x1b, tok0):
                    for kc in range(8):
                        P.tr(pT2[:, kc, :], x1b[:, kc * 128:(kc + 1) * 128], ident[:], sig=(kc == 7))
                    for kc in range(8):
                        P.act(x1mT[:, kc, tok0:tok0 + 128], pT2[:, kc, :], AF.Identity,
                              bias=modT[:, 24 + kc, 0:1], scale=onep[:, 8 + kc, 0:1])

                nt = 0
                prev = None
                load_T(0)
                for T in range(8):
                    i = T % 2
                    if T + 1 < 8:
                        load_T(T + 1)
                    for fc in range(8):
                        a = pA[fc % 2]; b = pB2[fc % 2]
                        for cc in range(4):
                            P.mm(a[:], who[:, cc, fc * 128:(fc + 1) * 128], yh[i][:, cc, :], start=(cc == 0), stop=(cc == 3),
                                 sig=(cc == 3))
                        for cc in range(4):
                            P.mm(b[:], wao[:, cc, fc * 128:(fc + 1) * 128], ot[i][:, cc, :], start=(cc == 0), stop=(cc == 3),
                                 sig=(cc == 3))
                        P.tt("dve", m1[:], a[:], gt[i][:, fc, :], ALU.mult)
                        P.tt("dve", m2[:], b[:], gt[i][:, 8 + fc, :], ALU.mult)
                        P.tt("pool", mT[:, fc, :], m1[:], m2[:], ALU.add)
                    for tb in range(4):
                        tok0 = T * 512 + tb * 128
                        for hf in range(2):
                            for kc in range(8):
                                P.mm(pYl[hf][:], mT[:, kc, tb * 128:(tb + 1) * 128], wo[:, kc, hf * 512:(hf + 1) * 512],
                                     start=(kc == 0), stop=(kc == 7), sig=(kc == 7))
                        if prev is not None:
                            transposes(*prev)
                        xo = x1t[nt % 2]
                        x1b = x1bs[nt % 2]
                        lnt_ = lnt[nt % 2]
                        nt += 1
                        residual_ln("F", lnt_, pYl[0], pYl[1], g1bc,
                                    xln_s.ap()[tok0:tok0 + 128, :], l1g, l1b, xo)
                        P.ld("pool", x1_s.ap()[tok0:tok0 + 128, :], xo[:])
                        P.cp("act", x1b[:], xo[:])
                        prev = (x1b, tok0)
                transposes(*prev)
                P.barrier()

            with contextlib.ExitStack() as st:
                wuf = [sbt(st, "wuf%d" % i, [128, 8, 128], F32) for i in range(2)]
                wub = [sbt(st, "wub%d" % i, [128, 8, 128], BF16) for i in range(4)]
                g_items = []
                for fc_ in range(NFC):
                    for part_ in range(2):
                        g_items.append(part_ * NFC + fc_)
                gctr = {"issued": 0}

                def g_issue():
                    i = gctr["issued"]
                    if i >= len(g_items):
                        return
                    gctr["issued"] += 1
                    ch_ = g_items[i]
                    P.ld("sp", wuf[i % 2][:], wup_d.ap()[:, ch_ * 128:(ch_ + 1) * 128].rearrange("(kc p) c -> p kc c", p=128))
                    P.cp("dve", wub[i % 4][:], wuf[i % 2][:])
                zr2 = [sbt(st, "zr2_%d" % i, [128, L + 2], F32) for i in range(2)]
                tgs = [sbt(st, "tg%d" % i, [128, L], F32) for i in range(2)]
                tas = [sbt(st, "ta_%d" % i, [128, L], F32) for i in range(2)]
                hrow = [sbt(st, "hrow%d" % i, [128, L], BF16) for i in range(2)]
                fcw = sbt(st, "fcw", [128, 44, 3], F32); fcb = sbt(st, "fcb", [128, 44], F32)
                pG = [pst(st, "pG%d" % i, [128, 512]) for i in range(4)]
                P.ld("sp", fcw[:], fcw_d.ap()); P.ld("sp", fcb[:], fcb_d.ap())
                for i in range(2):
                    P.memset("pool", zr2[i][:, 0:1], 0.0)
                    P.memset("pool", zr2[i][:, L + 1:L + 2], 0.0)
                nw = 0
                npg = 0
                for fc in range(NFC):
                    tg = tgs[fc % 2]
                    ta_ = tas[fc % 2]
                    for part in range(2):
                        ch = part * NFC + fc
                        while gctr["issued"] <= min(nw + 2, len(g_items) - 1):
                            g_issue()
                        wbl = wub[nw % 4]
                        nw += 1
                        z = zr2[part]
                        for T in range(8):
                            ps = pG[npg % 4]
                            npg += 1
                            for kc in range(8):
                                P.mm(ps[:], wbl[:, kc, :], x1mT[:, kc, T * 512:(T + 1) * 512], start=(kc == 0), stop=(kc == 7),
                                     sig=(kc == 7))
                            P.cp("act", z[:, 1 + T * 512:1 + (T + 1) * 512], ps[:])
                        dst = ta_ if part == 0 else tg
                        P.ts("pool", dst[:], z[:, 0:L], fcw[:, ch, 0:1], fcb[:, ch:ch + 1], ALU.mult, ALU.add)
                        P.stt("dve", dst[:], z[:, 1:L + 1], fcw[:, ch, 1:2], dst[:], ALU.mult, ALU.add)
                        P.stt("dve", dst[:], z[:, 2:L + 2], fcw[:, ch, 2:3], dst[:], ALU.mult, ALU.add)
                    P.act(tg[:], tg[:], AF.Silu)
                    hr = hrow[fc % 2]
                    P.tt("dve", hr[:], tg[:], ta_[:], ALU.mult)
                    if fc > 0:
                        P.ld("sp", hT_s.ap()[fc - 1], hrow[(fc - 1) % 2][:])
                P.ld("sp", hT_s.ap()[NFC - 1], hrow[(NFC - 1) % 2][:])
                P.barrier()

        with contextlib.ExitStack() as st:
            wfs = [sbt(st, "wfsH%d" % i, [128, 1024], F32) for i in range(2)]
            wd = sbt(st, "wd", [128, NFC, D], BF16)
            stream_weight_bf16(wd, wdn_d, NFC, D, wfs)
            l2g = sbt(st, "l2g", [128, D], F32); l2b = sbt(st, "l2b", [128, D], F32)
            P.ld("sp", l2g[:], l2g_d.ap().partition_broadcast(128))
            P.ld("sp", l2b[:], l2b_d.ap().partition_broadcast(128))
            ht = [sbt(st, "ht%d" % i, [128, NFC, 512], BF16) for i in range(2)]
            lnt = [(sbt(st, "rsH%d" % i, [128, D], F32), sbt(st, "rH%d" % i, [128, D], F32),
                    sbt(st, "statsH%d" % i, [128, 2, 6], F32), sbt(st, "mvH%d" % i, [128, 2], F32),
                    sbt(st, "rstdH%d" % i, [128, 1], F32), sbt(st, "tmpH%d" % i, [128, D], F32)) for i in range(2)]
            xo = [sbt(st, "xoH%d" % i, [128, D], F32) for i in range(2)]
            pD = [[pst(st, "pD%d_%d" % (s, i), [128, 512]) for i in range(2)] for s in range(2)]
            nt = 0

            def load_ht(T):
                P.ld("sp" if T % 2 == 0 else "act", ht[T % 2][:],
                     hT_s.ap()[:, :, T * 512:(T + 1) * 512].rearrange("f p t -> p f t"))

            load_ht(0)
            for T in range(8):
                i = T % 2
                if T + 1 < 8:
                    load_ht(T + 1)
                for tb in range(4):
                    tok0 = T * 512 + tb * 128
                    pp = pD[nt % 2]
                    for hf in range(2):
                        for fc in range(NFC):
                            P.mm(pp[hf][:], ht[i][:, fc, tb * 128:(tb + 1) * 128], wd[:, fc, hf * 512:(hf + 1) * 512],
                                 start=(fc == 0), stop=(fc == NFC - 1), sig=(fc == NFC - 1))
                    o_ = xo[nt % 2]
                    lnt_ = lnt[nt % 2]
                    nt += 1
                    residual_ln("H", lnt_, pp[0], pp[1], g2bc,
                                x1_s.ap()[tok0:tok0 + 128, :], l2g, l2b, o_)
                    P.ld("pool", out_d.ap()[tok0:tok0 + 128, :], o_[:])
            P.barrier()

        P.run()
    return nc


def host_constants():
    f32 = np.float32
    rows = L // 64
    row = np.repeat(np.arange(rows, dtype=f32), 64)
    col = np.tile(np.arange(64, dtype=f32), rows)
    inv = (10000.0 ** (-np.arange(0, 32, 2, dtype=f32) / 32.0)).astype(f32)
    cosT = np.zeros((128, L), f32)
    sinT = np.zeros((128, L), f32)
    for p in range(128):
        d = p % 64
        a = d // 32
        i = d % 32
        f = i % 16
        ang = (row if a == 0 else col) * inv[f]
        cosT[p] = np.cos(ang)
        sinT[p] = (-np.sin(ang)) if i < 16 else np.sin(ang)
    t = np.linspace(0.0, 1.0, L, dtype=f32)[:, None]
    w = (f32(2.0 * math.pi / L) * np.arange(L, dtype=f32))[:, None]
    f = np.linspace(1e-4, 15, 16, dtype=f32)[None, :]
    z = np.concatenate([t, np.cos(f * w), -np.sin(f * w)], -1).astype(f32)
    min_decay = math.log(1e-2) / 1.5
    max_decay = math.log(1e-2) / 0.3
    deltas = np.linspace(min_decay, max_decay, HYW, dtype=f32)
    decay = np.exp(-t * np.abs(deltas)[None, :]).astype(f32)
    return dict(
        rope_cos=cosT, rope_sin=sinT,
        zposT_f=np.ascontiguousarray(z.T), zposT_r=np.ascontiguousarray(z[::-1].T),
        decayT_f=np.ascontiguousarray(decay.T), decayT_r=np.ascontiguousarray(decay[::-1].T),
    )


def per_part(v, nch):
    return np.ascontiguousarray(np.asarray(v, np.float32).reshape(nch, 128).T)


def make_in_maps(inp):
    f32 = np.float32
    g = {k: np.asarray(v, f32) for k, v in inp.items()}
    const = host_constants()
    w_in = g["w_in"][0]
    perm = np.zeros(1024, np.int64)
    for cidx in range(1024):
        base = 1536 + cidx
        d = cidx % 64
        i = d % 32
        perm[cidx] = base + 16 if i < 16 else base - 16
    w_qkp = np.ascontiguousarray(w_in[:, perm])
    cw = g["hy_conv_w"][0]
    hy_cw = np.ascontiguousarray(np.stack([per_part(cw[k], 12) for k in range(3)], -1))
    fw = g["ffn_conv_w"][0]
    ffn_cw = np.ascontiguousarray(np.stack([per_part(fw[k], 44) for k in range(3)], -1))
    shared = dict(
        ln_in_g=g["ln_in_g"], ln_in_b=g["ln_in_b"],
        w_ada=g["w_ada"][0], b_ada=g["b_ada"][0], b_adaT=per_part(g["b_ada"][0], 48),
        w_in=w_in, w_qkp=w_qkp,
        hy_cw=hy_cw, hy_cb=per_part(g["hy_conv_b"][0], 12),
        f_w1=g["hy_f_w1"][0], f_w2=g["hy_f_w2"][0], f_w3=g["hy_f_w3"][0], f_wout=g["hy_f_wout"][0],
        f_freqT=np.ascontiguousarray(g["hy_f_freq"][0].T),
        f_bT=np.ascontiguousarray(np.stack([g["hy_f_b1"][0], g["hy_f_b2"][0], g["hy_f_b3"][0]], -1)),
        hy_bias=g["hy_bias"][0],
        lamv=np.ascontiguousarray(np.stack([g["lam_q1"][0], g["lam_k1"][0], g["lam_q2"][0], g["lam_k2"][0]], 0)),
        subln_g=np.ascontiguousarray(g["at_subln_g"][0].reshape(128, 1)),
        w_hy_o=g["w_hy_o"][0], w_at_o=g["w_at_o"][0], w_out=g["w_out"][0],
        ln1_g=g["ln1_g"][0], ln1_b=g["ln1_b"][0], ln2_g=g["ln2_g"][0], ln2_b=g["ln2_b"][0],
        ffn_w_up=g["ffn_w_up"][0], ffn_cw=ffn_cw, ffn_cb=per_part(g["ffn_conv_b"][0], 44),
        ffn_w_down=g["ffn_w_down"][0],
    )
    shared.update(const)
    maps = []
    cc = per_part(g["c_ctx"], 8)
    for b in range(8):
        m = dict(shared)
        m["x"] = np.ascontiguousarray(g["x"][b])
        m["ctx"] = np.ascontiguousarray(g["ctx"][b])
        m["cT"] = np.ascontiguousarray(np.stack([per_part(g["c"][b], 8), cc], -1))
        maps.append(m)
    return maps


_NC = None


def kernel(**inputs):
    global _NC
    if _NC is None:
        _NC = build_program()
    maps = make_in_maps(inputs)
    res = run_bass_kernel_spmd(_NC, maps, core_ids=list(range(8)))
    out = np.stack([np.asarray(r["out"], np.float32) for r in res.results], 0)
    return out
```

```python
import math
import contextlib
import numpy as np
import concourse.bass as bass
import concourse.mybir as mybir
from concourse.bass_utils import run_bass_kernel_spmd

F32 = mybir.dt.float32
BF16 = mybir.dt.bfloat16
ALU = mybir.AluOpType
AF = mybir.ActivationFunctionType
AX = mybir.AxisListType

ENGS = ("pe", "act", "dve", "pool", "sp")

D = 1024
L = 4096
CTX = 256
NTT = 34
LK = CTX + L
HYW = 512
DFF = 2816
NFC = DFF // 128
ALPHA = 2.0 ** 0.25
LAM_INIT = 0.2
PI = math.pi
DEBUG = False


class Prog:
    NDMA = 6

    def __init__(self, nc):
        self.nc = nc
        self.ops = {e: [] for e in ENGS}
        self.cnt = {e: 0 for e in ENGS}
        self.waited = {e: {} for e in ENGS}
        self.last_w = {}
        self.reads = {}
        self.dma_uses = {}
        self.dma_rr = {e: 0 for e in ENGS}
        self.semkeys = set()
        self.sems = {}

    def _deps(self, eng, reads, writes):
        deps = []
        for k in reads:
            if k in self.last_w:
                deps.append(self.last_w[k])
        for k in writes:
            if k in self.last_w:
                deps.append(self.last_w[k])
            for ev in self.reads.get(k, ()):
                if ev[0] == eng:
                    continue
                deps.append(ev)
        best = {}
        for sk, v in deps:
            if eng == "pe" and sk == "pe":
                continue
            if v > best.get(sk, 0):
                best[sk] = v
        out = []
        w = self.waited[eng]
        for sk, v in best.items():
            if w.get(sk, 0) >= v:
                continue
            w[sk] = v
            out.append((sk, v))
        return out

    def _commit(self, ev, reads, writes):
        for k in reads:
            self.reads.setdefault(k, []).append(ev)
        for k in writes:
            self.last_w[k] = ev
            self.reads[k] = []

    def op(self, eng, fn, reads=(), writes=(), sig=True):
        waits = self._deps(eng, reads, writes)
        if sig:
            self.cnt[eng] += 1
            ev = (eng, self.cnt[eng])
            self.semkeys.add(eng)
            self.ops[eng].append(("op", fn, waits, eng))
        else:
            ev = (eng, self.cnt[eng] + 1)
            self.ops[eng].append(("op", fn, waits, None))
        self._commit(ev, reads, writes)
        return ev

    def dma(self, q, fn, reads=(), writes=()):
        slot = self.dma_rr[q] % self.NDMA
        self.dma_rr[q] += 1
        sk = ("dma", q, slot)
        self.semkeys.add(sk)
        uses = self.dma_uses.get(sk, 0)
        waits = self._deps(q, reads, writes)
        if uses > 0 and self.waited[q].get(sk, 0) < 16 * uses:
            self.waited[q][sk] = 16 * uses
            waits.append((sk, 16 * uses))
        uses += 1
        self.dma_uses[sk] = uses
        ev = (sk, 16 * uses)
        self.ops[q].append(("dma", fn, waits, sk))
        self._commit(ev, reads, writes)
        return ev

    def barrier(self):
        evs = []
        for e in ENGS:
            if self.cnt[e] > 0:
                evs.append((e, self.cnt[e]))
        for sk, uses in self.dma_uses.items():
            evs.append((sk, 16 * uses))
        for e in ENGS:
            waits = []
            for sk, v in evs:
                if sk == e:
                    continue
                if self.waited[e].get(sk, 0) >= v:
                    continue
                self.waited[e][sk] = v
                waits.append((sk, v))
            if waits:
                self.ops[e].append(("wait", None, waits, None))
        self.last_w = {}
        self.reads = {}

    def run(self):
        nc = self.nc
        with contextlib.ExitStack() as st:
            for sk in sorted(self.semkeys, key=str):
                name = sk if isinstance(sk, str) else "d_%s_%d" % (sk[1], sk[2])
                self.sems[sk] = st.enter_context(nc.semaphore("s_" + name))
            block = st.enter_context(nc.Block())

            def mk(ename):
                def body(e):
                    for kind, fn, waits, sk in self.ops[ename]:
                        for wk, wv in waits:
                            e.wait_ge(self.sems[wk], wv)
                        if kind == "wait":
                            continue
                        ins = fn(e)
                        if sk is not None:
                            ins.then_inc(self.sems[sk], 16 if kind == "dma" else 1)
                return body

            block.tensor(mk("pe"))
            block.scalar(mk("act"))
            block.vector(mk("dve"))
            block.gpsimd(mk("pool"))
            block.sync(mk("sp"))

    @staticmethod
    def _k(*aps):
        out = []
        for a in aps:
            if a is None or isinstance(a, (int, float)):
                continue
            out.append(a.name)
        return out

    def mm(self, out, lhsT, rhs, start=True, stop=True, sig=True):
        return self.op("pe", lambda e: e.matmul(out, lhsT, rhs, start=start, stop=stop),
                       reads=self._k(lhsT, rhs), writes=self._k(out), sig=sig)

    def tr(self, out, in_, ident, sig=True):
        return self.op("pe", lambda e: e.transpose(out, in_, ident),
                       reads=self._k(in_, ident), writes=self._k(out), sig=sig)

    def act(self, out, in_, func, bias=None, scale=None, accum_out=None):
        kw = {}
        if bias is not None:
            kw["bias"] = bias
        if scale is not None:
            kw["scale"] = scale
        if accum_out is not None:
            kw["accum_out"] = accum_out
        return self.op("act", lambda e: e.activation(out=out, in_=in_, func=func, **kw),
                       reads=self._k(in_, bias, scale), writes=self._k(out, accum_out))

    def tt(self, eng, out, in0, in1, op):
        return self.op(eng, lambda e: e.tensor_tensor(out=out, in0=in0, in1=in1, op=op),
                       reads=self._k(in0, in1), writes=self._k(out))

    def ts(self, eng, out, in0, s1, s2, op0, op1=None):
        if op1 is None:
            return self.op(eng, lambda e: e.tensor_scalar(out=out, in0=in0, scalar1=s1, scalar2=None, op0=op0),
                           reads=self._k(in0, s1), writes=self._k(out))
        return self.op(eng, lambda e: e.tensor_scalar(out=out, in0=in0, scalar1=s1, scalar2=s2, op0=op0, op1=op1),
                       reads=self._k(in0, s1, s2), writes=self._k(out))

    def stt(self, eng, out, in0, scalar, in1, op0, op1):
        return self.op(eng, lambda e: e.scalar_tensor_tensor(out=out, in0=in0, scalar=scalar, in1=in1, op0=op0, op1=op1),
                       reads=self._k(in0, scalar, in1), writes=self._k(out))

    def cp(self, eng, out, in_):
        if eng == "act":
            return self.act(out, in_, AF.Copy)
        return self.op(eng, lambda e: e.tensor_copy(out=out, in_=in_), reads=self._k(in_), writes=self._k(out))

    def memset(self, eng, ap, val):
        return self.op(eng, lambda e: e.memset(ap, val), writes=self._k(ap))

    def ld(self, q, out, in_):
        return self.dma(q, lambda e: e.dma_start(out=out, in_=in_), reads=self._k(in_), writes=self._k(out))


def build_program():
    nc = bass.Bass("TRN2", target_bir_lowering=False)
    P = Prog(nc)

    def din(name, shape, dt=F32):
        return nc.dram_tensor(name, list(shape), dt, kind="ExternalInput")

    def dscr(name, shape, dt, dbg=False):
        if DEBUG and dbg:
            return nc.dram_tensor(name, list(shape), dt, kind="ExternalOutput")
        return nc.dram_tensor(name, list(shape), dt)

    x_d = din("x", [L, D]); ctx_d = din("ctx", [CTX, D])
    cT_d = din("cT", [128, 8, 2])
    lng_d = din("ln_in_g", [D]); lnb_d = din("ln_in_b", [D])
    wada_d = din("w_ada", [D, 6 * D]); bada_d = din("b_ada", [6 * D]); badaT_d = din("b_adaT", [128, 48])
    win_d = din("w_in", [D, 5120]); wqkp_d = din("w_qkp", [D, 1024])
    cos_d = din("rope_cos", [128, L]); sin_d = din("rope_sin", [128, L])
    hcw_d = din("hy_cw", [128, 12, 3]); hcb_d = din("hy_cb", [128, 12])
    fw1_d = din("f_w1", [33, 64]); fw2_d = din("f_w2", [64, 64]); fw3_d = din("f_w3", [64, 64])
    fwo_d = din("f_wout", [64, 2048]); ffr_d = din("f_freqT", [64, 3]); fb_d = din("f_bT", [64, 3])
    zpf_d = din("zposT_f", [33, L]); zpr_d = din("zposT_r", [33, L])
    dcf_d = din("decayT_f", [HYW, L]); dcr_d = din("decayT_r", [HYW, L])
    hyb_d = din("hy_bias", [2, HYW])
    lam_d = din("lamv", [4, 64])
    sub_d = din("subln_g", [128, 1])
    who_d = din("w_hy_o", [HYW, D]); wao_d = din("w_at_o", [512, D]); wout_d = din("w_out", [D, D])
    l1g_d = din("ln1_g", [D]); l1b_d = din("ln1_b", [D]); l2g_d = din("ln2_g", [D]); l2b_d = din("ln2_b", [D])
    wup_d = din("ffn_w_up", [D, 2 * DFF]); fcw_d = din("ffn_cw", [128, 44, 3]); fcb_d = din("ffn_cb", [128, 44])
    wdn_d = din("ffn_w_down", [DFF, D])
    out_d = nc.dram_tensor("out", [L, D], F32, kind="ExternalOutput")

    xln_s = dscr("xln_s", [L, D], F32, True)
    qT_s = dscr("qT_s", [4, 128, L], BF16, True)
    kT_s = dscr("kT_s", [4, 128, LK], BF16, True)
    V_s = dscr("V_s", [NTT, 128, 512], BF16, True)
    hyu_s = dscr("hyu_s", [128, 1536 * 32], BF16, True)
    gat_s = dscr("gat_s", [16, 128, L], BF16, True)
    A_s = dscr("A_s", [2 * HYW, 8192], BF16, True)
    oT_s = dscr("oT_s", [4, 128, L], BF16, True)
    yhT_s = dscr("yhT_s", [4, 128, L], BF16, True)
    x1_s = dscr("x1_s", [L, D], F32, True)
    hT_s = dscr("hT_s", [NFC, 128, L], BF16, True)

    top = contextlib.ExitStack()

    def sbt(st, name, shape, dt):
        return st.enter_context(nc.sbuf_tensor("sb_" + name, list(shape), dt))

    def pst(st, name, shape, dt=F32):
        return st.enter_context(nc.psum_tensor("ps_" + name, list(shape), dt))

    with top:
        ident = sbt(top, "ident", [128, 128], BF16)
        jrev = sbt(top, "jrev", [128, 128], BF16)
        identf = sbt(top, "identf", [128, 128], F32)
        epsb = sbt(top, "epsb", [128, 1], F32)
        npib = sbt(top, "npib", [128, 1], F32)
        modT = sbt(top, "modT", [128, 48, 2], F32)
        onep = sbt(top, "onep", [128, 16, 2], F32)
        g1bc = sbt(top, "g1bc", [128, D], F32)
        g2bc = sbt(top, "g2bc", [128, D], F32)
        neglam = sbt(top, "neglam", [128, 1], F32)
        subg = sbt(top, "subg", [128, 1], F32)

        P.memset("pool", identf[:], 0.0)
        P.op("pool", lambda e: e.affine_select(out=identf[:], in_=identf[:], pattern=[[-1, 128]],
                                               compare_op=ALU.not_equal, fill=1.0, base=0, channel_multiplier=1),
             reads=[identf.name], writes=[identf.name])
        P.cp("dve", ident[:], identf[:])
        P.memset("pool", identf[:], 0.0)
        P.op("pool", lambda e: e.affine_select(out=identf[:], in_=identf[:], pattern=[[1, 128]],
                                               compare_op=ALU.not_equal, fill=1.0, base=-127, channel_multiplier=1),
             reads=[identf.name], writes=[identf.name])
        P.cp("dve", jrev[:], identf[:])
        P.memset("pool", epsb[:], 1e-5)
        P.memset("pool", npib[:], -PI)

        with contextlib.ExitStack() as st:
            cT = sbt(st, "cT", [128, 8, 2], F32)
            sc = sbt(st, "sc", [128, 8, 2], F32)
            scb = sbt(st, "scb", [128, 8, 128], F32)
            wa = [sbt(st, "wa%d" % i, [128, 8, 512], F32) for i in range(4)]
            wab = [sbt(st, "wab%d" % i, [128, 8, 512], BF16) for i in range(2)]
            sc_b = sbt(st, "sc_b", [128, 8, 2], BF16)
            scb_b = sbt(st, "scb_b", [128, 8, 128], BF16)
            badaT = sbt(st, "badaT", [128, 48], F32)
            bg = sbt(st, "bg", [128, D], F32)
            lamv = sbt(st, "lamv_t", [128, 4, 64], F32)
            lamp = sbt(st, "lamp", [128, 2, 64], F32)
            lams = sbt(st, "lams", [128, 2], F32)
            pm = pst(st, "pm", [128, 48, 2])
            pb = pst(st, "pb", [128, 512])
            P.ld("sp", cT[:], cT_d.ap())
            P.ld("sp", badaT[:], badaT_d.ap())
            P.ld("sp", subg[:], sub_d.ap())
            P.ld("sp", lamv[:].rearrange("p a b -> p (a b)"),
                 lam_d.ap().rearrange("a b -> (a b)").partition_broadcast(128))
            P.act(sc[:], cT[:], AF.Silu)
            P.cp("dve", scb[:], sc[:, :, 0:1].to_broadcast([128, 8, 128]))
            P.cp("dve", sc_b[:], sc[:])
            P.cp("dve", scb_b[:], scb[:])
            P.ts("dve", subg[:], subg[:], 1.0 - LAM_INIT, None, ALU.mult)
            P.tt("dve", lamp[:, 0, :], lamv[:, 0, :], lamv[:, 1, :], ALU.mult)
            P.tt("dve", lamp[:, 1, :], lamv[:, 2, :], lamv[:, 3, :], ALU.mult)
            P.op("dve", lambda e: e.reduce_sum(out=lams[:], in_=lamp[:], axis=AX.X), reads=[lamp.name], writes=[lams.name])
            P.act(lams[:], lams[:], AF.Exp)
            P.tt("dve", neglam[:], lams[:, 1:2], lams[:, 0:1], ALU.subtract)
            P.ts("dve", neglam[:], neglam[:], -LAM_INIT, None, ALU.add)
            for g in range(12):
                w = wa[g % 4]
                P.ld("sp" if g % 2 == 0 else "act", w[:],
                     wada_d.ap()[:, g * 512:(g + 1) * 512].rearrange("(kc p) c -> p kc c", p=128))
                wq = wab[g % 2]
                P.cp("dve" if g % 2 == 0 else "act", wq[:], w[:])
                for jj in range(4):
                    j = g * 4 + jj
                    for kc in range(8):
                        P.mm(pm[:, j, :], wq[:, kc, jj * 128:(jj + 1) * 128], sc_b[:, kc, :],
                             start=(kc == 0), stop=(kc == 7), sig=(kc == 7))
                if g in (4, 5, 10, 11):
                    for kc in range(8):
                        P.mm(pb[:], scb_b[:, kc, :], wq[:, kc, :], start=(kc == 0), stop=(kc == 7), sig=(kc == 7))
                    dst = g1bc if g in (4, 5) else g2bc
                    half = g % 2
                    P.ld("sp", bg[:, 0:512], bada_d.ap()[g * 512:(g + 1) * 512].partition_broadcast(128))
                    P.tt("dve", dst[:, half * 512:(half + 1) * 512], pb[:], bg[:, 0:512], ALU.add)
            P.tt("dve", modT[:], pm[:], badaT[:].unsqueeze(2).to_broadcast([128, 48, 2]), ALU.add)
            P.ts("dve", onep[:, 0:8, :], modT[:, 8:16, :], 1.0, None, ALU.add)
            P.ts("dve", onep[:, 8:16, :], modT[:, 32:40, :], 1.0, None, ALU.add)
            P.barrier()

        def layer_norm_tile(pfx, src, gbc, bbc, dst, stats, mv, rstd, tmp, add_eng="pool"):
            for h in range(2):
                P.op("dve", (lambda e, h=h: e.bn_stats(out=stats[:, h, :], in_=src[:, h * 512:(h + 1) * 512])),
                     reads=[src.name], writes=[stats.name])
            P.op("dve", lambda e: e.bn_aggr(out=mv[:], in_=stats[:].rearrange("p a b -> p (a b)")),
                 reads=[stats.name], writes=[mv.name])
            P.act(rstd[:], mv[:, 1:2], AF.Sqrt, bias=epsb[:], scale=1.0)
            P.op("dve", lambda e: e.reciprocal(out=rstd[:], in_=rstd[:]), reads=[rstd.name], writes=[rstd.name])
            P.ts("dve", tmp[:], src[:], mv[:, 0:1], rstd[:], ALU.subtract, ALU.mult)
            P.tt("pool", tmp[:], tmp[:], gbc[:], ALU.mult)
            P.tt(add_eng, dst[:], tmp[:], bbc[:], ALU.add)

        with contextlib.ExitStack() as stA:
            xmT = sbt(stA, "xmT", [128, 8, L], BF16)
            xcT = sbt(stA, "xcT", [128, 8, CTX], BF16)
            with contextlib.ExitStack() as st:
                gbc = sbt(st, "gbc", [128, D], F32)
                bbc = sbt(st, "bbc", [128, D], F32)
                xt = [sbt(st, "xt%d" % i, [128, D], F32) for i in range(3)]
                xh = [sbt(st, "xh%d" % i, [128, D], F32) for i in range(3)]
                xl = [sbt(st, "xl%d" % i, [128, D], F32) for i in range(3)]
                xb = [sbt(st, "xb%d" % i, [128, D], BF16) for i in range(2)]
                stats = [sbt(st, "stats%d" % i, [128, 2, 6], F32) for i in range(3)]
                mv = [sbt(st, "mv%d" % i, [128, 2], F32) for i in range(3)]
                rstd = [sbt(st, "rstd%d" % i, [128, 1], F32) for i in range(3)]
                pT = [pst(st, "pT%d" % i, [128, 8, 128], BF16) for i in range(2)]
                P.ld("sp", gbc[:], lng_d.ap().partition_broadcast(128))
                P.ld("sp", bbc[:], lnb_d.ap().partition_broadcast(128))
                def phase2(t):
                    i = t % 3
                    is_ctx = t < 2
                    if not is_ctx:
                        P.ld("pool", xln_s.ap()[(t - 2) * 128:(t - 1) * 128, :], xl[i][:])
                    P.cp("act", xb[t % 2][:], xl[i][:])
                    for kc in range(8):
                        P.tr(pT[t % 2][:, kc, :], xb[t % 2][:, kc * 128:(kc + 1) * 128], ident[:], sig=(kc == 7))
                    w = 1 if is_ctx else 0
                    for kc in range(8):
                        dst = xcT[:, kc, t * 128:(t + 1) * 128] if is_ctx else xmT[:, kc, (t - 2) * 128:(t - 1) * 128]
                        P.act(dst, pT[t % 2][:, kc, :], AF.Identity, bias=modT[:, kc, w:w + 1], scale=onep[:, kc, w:w + 1])

                for t in range(NTT):
                    i = t % 3
                    is_ctx = t < 2
                    src = ctx_d.ap()[t * 128:(t + 1) * 128, :] if is_ctx else x_d.ap()[(t - 2) * 128:(t - 1) * 128, :]
                    P.ld("sp", xt[i][:], src)
                    layer_norm_tile("A", xt[i], gbc, bbc, xl[i], stats[i], mv[i], rstd[i], xh[i])
                    if t > 0:
                        phase2(t - 1)
                phase2(NTT - 1)
                P.barrier()

            wf = [sbt(stA, "wf%d" % i, [128, 8, 128], F32) for i in range(2)]
            wb = [sbt(stA, "wb%d" % i, [128, 8, 128], BF16) for i in range(4)]
            psB = [pst(stA, "psB%d" % i, [128, 512]) for i in range(4)]
            ctr = {"w": 0, "ps": 0}

            w_items = []
            for which_ in range(2):
                for h_ in range(4):
                    w_items.append((win_d, 1536 + which_ * 512 + h_ * 128))
                    w_items.append((wqkp_d, which_ * 512 + h_ * 128))
            for cc_ in range(12):
                w_items.append((win_d, cc_ * 128))
            for gc_ in range(16):
                w_items.append((win_d, 3072 + gc_ * 128))
            for c4_ in range(4):
                w_items.append((win_d, 2560 + c4_ * 128))
            ctr["issued"] = 0

            def _issue_w():
                i = ctr["issued"]
                if i >= len(w_items):
                    return
                ctr["issued"] += 1
                src_d, col0 = w_items[i]
                P.ld("sp", wf[i % 2][:], src_d.ap()[:, col0:col0 + 128].rearrange("(kc p) c -> p kc c", p=128))
                P.cp("dve", wb[i % 4][:], wf[i % 2][:])

            def load_w(src_d, col0):
                i = ctr["w"]
                ctr["w"] += 1
                assert w_items[i][1] == col0 and w_items[i][0] is src_d
                while ctr["issued"] <= min(i + 2, len(w_items) - 1):
                    _issue_w()
                return wb[i % 4]

            def proj_fm(wt, T):
                ps = psB[ctr["ps"] % 4]
                ctr["ps"] += 1
                for kc in range(8):
                    P.mm(ps[:], wt[:, kc, :], xmT[:, kc, T * 512:(T + 1) * 512], start=(kc == 0), stop=(kc == 7),
                         sig=(kc == 7))
                return ps

            with contextlib.ExitStack() as st:
                cosT = sbt(st, "cosT", [128, L], F32)
                sinT = sbt(st, "sinT", [128, L], F32)
                ra = [sbt(st, "ra%d" % i, [128, 512], F32) for i in range(2)]
                rb = [sbt(st, "rb%d" % i, [128, 512], F32) for i in range(2)]
                qrow = [sbt(st, "qrow%d" % i, [128, LK], BF16) for i in range(2)]
                P.ld("sp", cosT[:], cos_d.ap())
                P.ld("sp", sinT[:], sin_d.ap())
                n = 0
                for which in range(2):
                    for h in range(4):
                        col = 1536 + which * 512 + h * 128
                        w_main = load_w(win_d, col)
                        w_perm = load_w(wqkp_d, which * 512 + h * 128)
                        row = qrow[n % 2]
                        off = CTX if which == 1 else 0
                        if which == 1:
                            ps = psB[ctr["ps"] % 4]
                            ctr["ps"] += 1
                            for kc in range(8):
                                P.mm(ps[:, 0:CTX], w_main[:, kc, :], xcT[:, kc, :], start=(kc == 0), stop=(kc == 7),
                                     sig=(kc == 7))
                            P.cp("act", row[:, 0:CTX], ps[:, 0:CTX])
                        for T in range(8):
                            pa = proj_fm(w_main, T)
                            pb_ = proj_fm(w_perm, T)
                            P.tt("dve", ra[T % 2][:], pa[:], cosT[:, T * 512:(T + 1) * 512], ALU.mult)
                            P.tt("dve", rb[T % 2][:], pb_[:], sinT[:, T * 512:(T + 1) * 512], ALU.mult)
                            P.tt("pool", row[:, off + T * 512:off + (T + 1) * 512], ra[T % 2][:], rb[T % 2][:], ALU.add)
                        if which == 0:
                            P.ld("sp", qT_s.ap()[h], row[:, 0:L])
                        else:
                            P.ld("sp", kT_s.ap()[h], row[:, 0:LK])
                        n += 1
                P.barrier()

            with contextlib.ExitStack() as st:
                zrows = [sbt(st, "zrow%d" % i, [128, L + 2], F32) for i in range(2)]
                t1 = sbt(st, "t1", [128, L], F32)
                urows = [sbt(st, "urow%d" % i, [128, L], BF16) for i in range(2)]
                ucc = [sbt(st, "ucc%d" % i, [128, 128, 32], BF16) for i in range(2)]
                hcw = sbt(st, "hcw", [128, 12, 3], F32)
                hcb = sbt(st, "hcb", [128, 12], F32)
                pU = [pst(st, "pU%d" % i, [128, 4, 128], BF16) for i in range(2)]
                P.ld("sp", hcw[:], hcw_d.ap())
                P.ld("sp", hcb[:], hcb_d.ap())
                for zrow in zrows:
                    P.memset("pool", zrow[:, 0:1], 0.0)
                    P.memset("pool", zrow[:, L + 1:L + 2], 0.0)
                def b2_phase2(cc):
                    urow = urows[cc % 2]
                    u = ucc[cc % 2]
                    for jq in range(8):
                        pp = pU[jq % 2]
                        for jj in range(4):
                            j = jq * 4 + jj
                            P.tr(pp[:, jj, :], urow[:, j * 128:(j + 1) * 128], ident[:], sig=(jj == 3))
                        P.cp("act", u[:, :, jq * 4:(jq + 1) * 4], pp[:].rearrange("p j c -> p c j"))
                    P.ld("sp", hyu_s.ap()[:, cc * 4096:(cc + 1) * 4096], u[:].rearrange("p c j -> p (c j)"))

                for cc in range(12):
                    zrow = zrows[cc % 2]
                    urow = urows[cc % 2]
                    wt = load_w(win_d, cc * 128)
                    for T in range(8):
                        ps = proj_fm(wt, T)
                        P.cp("act", zrow[:, 1 + T * 512:1 + (T + 1) * 512], ps[:])
                    P.ts("pool", t1[:], zrow[:, 0:L], hcw[:, cc, 0:1], hcb[:, cc:cc + 1], ALU.mult, ALU.add)
                    P.stt("dve", t1[:], zrow[:, 1:L + 1], hcw[:, cc, 1:2], t1[:], ALU.mult, ALU.add)
                    P.stt("dve", urow[:], zrow[:, 2:L + 2], hcw[:, cc, 2:3], t1[:], ALU.mult, ALU.add)
                    if cc > 0:
                        b2_phase2(cc - 1)
                    if cc == 11:
                        b2_phase2(cc)
                P.barrier()

            with contextlib.ExitStack() as st:
                grow = [sbt(st, "grow%d" % i, [128, L], BF16) for i in range(2)]
                wv = sbt(st, "wv", [128, 8, 512], BF16)
                vrow = [sbt(st, "vrow%d" % i, [128, 512], BF16) for i in range(2)]
                for gc in range(16):
                    wt = load_w(win_d, 3072 + gc * 128)
                    g = grow[gc % 2]
                    for T in range(8):
                        ps = proj_fm(wt, T)
                        P.act(g[:, T * 512:(T + 1) * 512], ps[:], AF.Sigmoid)
                    P.ld("sp", gat_s.ap()[gc], g[:])
                for c4 in range(4):
                    wt = load_w(win_d, 2560 + c4 * 128)
                    P.cp("dve", wv[:, :, c4 * 128:(c4 + 1) * 128], wt[:])
                for t in range(NTT):
                    ps = psB[ctr["ps"] % 4]
                    ctr["ps"] += 1
                    for kc in range(8):
                        lhs = xcT[:, kc, t * 128:(t + 1) * 128] if t < 2 else xmT[:, kc, (t - 2) * 128:(t - 1) * 128]
                        P.mm(ps[:], lhs, wv[:, kc, :], start=(kc == 0), stop=(kc == 7), sig=(kc == 7))
                    v = vrow[t % 2]
                    P.cp("act", v[:], ps[:])
                    P.ld("sp", V_s.ap()[t], v[:])
                P.barrier()

        with contextlib.ExitStack() as st:
            fw1 = sbt(st, "fw1", [33, 64], F32); fw2 = sbt(st, "fw2", [64, 64], F32); fw3 = sbt(st, "fw3", [64, 64], F32)
            fwo = sbt(st, "fwo", [64, 2048], F32)
            ffr = sbt(st, "ffr", [64, 3], F32); fbt = sbt(st, "fbt", [64, 3], F32); ffb = sbt(st, "ffb", [64, 3], F32)
            zp = sbt(st, "zp", [33, L], F32)
            hA = sbt(st, "hA", [64, L], F32); hB = sbt(st, "hB", [64, L], F32)
            targ = [sbt(st, "targ%d" % i, [64, 2048], F32) for i in range(2)]
            targm = [sbt(st, "targm%d" % i, [64, 2048], F32) for i in range(2)]
            dct = [sbt(st, "dct%d" % i, [128, L], F32) for i in range(2)]
            arow = [sbt(st, "arow%d" % i, [128, L], BF16) for i in range(2)]
            pf = [pst(st, "pf%d" % i, [128, 2048]) for i in range(2)]
            fwob = sbt(st, "fwob", [64, 2048], BF16)
            hAb = sbt(st, "hAb", [64, L], BF16)
            P.ld("sp", fw1[:], fw1_d.ap()); P.ld("sp", fw2[:], fw2_d.ap()); P.ld("sp", fw3[:], fw3_d.ap())
            P.ld("act", fwo[:], fwo_d.ap()); P.ld("sp", ffr[:], ffr_d.ap()); P.ld("sp", fbt[:], fb_d.ap())
            P.tt("dve", ffb[:], ffr[:], fbt[:], ALU.mult)
            P.cp("act", fwob[:], fwo[:])
            npf = 0
            nrow = 0
            for ev in range(2):
                P.ld("sp", zp[:], (zpf_d if ev == 0 else zpr_d).ap())
                srcs = [(zp, fw1, 33), (hA, fw2, 64), (hB, fw3, 64)]
                dsts = [hA, hB, hA]
                for li in range(3):
                    src, wt, kk = srcs[li]
                    dst = dsts[li]
                    for hf in range(2):
                        ps = pf[npf % 2]
                        ta = targ[npf % 2]
                        tm = targm[npf % 2]
                        npf += 1
                        for q4 in range(4):
                            col = hf * 2048 + q4 * 512
                            P.mm(ps[0:64, q4 * 512:(q4 + 1) * 512], wt[0:kk, :], src[0:kk, col:col + 512], sig=(q4 == 3))
                        P.ts("dve", ta[:], ps[0:64, :], ffr[:, li:li + 1], ffb[:, li:li + 1], ALU.mult, ALU.add)
                        P.ts("dve", tm[:], ta[:], PI, -2.0 * PI, ALU.is_gt, ALU.mult)
                        P.tt("dve", ta[:], ta[:], tm[:], ALU.add)
                        P.ts("dve", tm[:], ta[:], -PI, 2.0 * PI, ALU.is_lt, ALU.mult)
                        P.tt("dve", ta[:], ta[:], tm[:], ALU.add)
                        P.ts("dve", ta[:], ta[:], PI, -PI, ALU.min, ALU.max)
                        P.act(dst[:, hf * 2048:(hf + 1) * 2048], ta[:], AF.Sin)
                P.cp("act", hAb[:, 0:2048], hA[:, 0:2048])
                P.cp("dve", hAb[:, 2048:4096], hA[:, 2048:4096])
                for o in range(2):
                    for cc in range(4):
                        dc = dct[nrow % 2]
                        ar = arow[nrow % 2]
                        nrow += 1
                        P.ld("sp" if nrow % 2 == 0 else "act", dc[:],
                             (dcf_d if ev == 0 else dcr_d).ap()[cc * 128:(cc + 1) * 128, :])
                        col = o * 1024 + ev * 512 + cc * 128
                        for hf in range(2):
                            ps = pf[npf % 2]
                            npf += 1
                            for q4 in range(4):
                                c_ = hf * 2048 + q4 * 512
                                P.mm(ps[:, q4 * 512:(q4 + 1) * 512], fwob[:, col:col + 128], hAb[:, c_:c_ + 512], sig=(q4 == 3))
                            P.tt("dve", ar[:, hf * 2048:(hf + 1) * 2048], ps[:], dc[:, hf * 2048:(hf + 1) * 2048], ALU.mult)
                        rows = A_s.ap()[o * HYW + cc * 128:o * HYW + (cc + 1) * 128, :]
                        if ev == 0:
                            P.ld("sp", rows[:, 4095:8191], ar[:])
                        else:
                            P.ld("sp", rows[:, 0:4095], ar[:, 0:4095])
            P.barrier()

        G = 16
        S = 4
        KB = 128 // S
        TW = 8192 - KB
        NG = G // S
        with contextlib.ExitStack() as stY:
            yall = sbt(stY, "yall", [128, HYW, 32], BF16)
            with contextlib.ExitStack() as st:
                qh = sbt(st, "qh", [128, L], BF16)
                kh = sbt(st, "kh", [128, LK], BF16)
                Vh = sbt(st, "Vh", [128, NTT, 128], BF16)
                ones_f = sbt(st, "ones_f", [128, 128], F32)
                ones_b = sbt(st, "ones_b", [128, 128], BF16)
                NR = 4
                PT = [sbt(st, "PT%d" % i, [128, 512], BF16) for i in range(NR)]
                rc = [sbt(st, "rc%d" % i, [128, 512], F32) for i in range(2)]
                acc = [sbt(st, "acc%d" % i, [128, 512], F32) for i in range(2)]
                on = [sbt(st, "on%d" % i, [128, 512], F32) for i in range(4)]
                oc = [sbt(st, "oc%d" % i, [128, 512], F32) for i in range(2)]
                osq = [sbt(st, "osq%d" % i, [128, 512], F32) for i in range(2)]
                rst = [sbt(st, "rst%d" % i, [128, 512], F32) for i in range(2)]
                orow = [sbt(st, "orow%d" % i, [128, 512], BF16) for i in range(2)]
                pS = [pst(st, "pS%d" % i, [128, 512]) for i in range(2)]
                pO = [pst(st, "pO%d" % i, [128, 512]) for i in range(2)]
                pSm = [pst(st, "pSm%d" % i, [128, 512]) for i in range(2)]
                NT = 4
                tsk = [sbt(st, "tsk%d" % i, [128, TW], BF16) for i in range(NT)]
                rself = sbt(st, "rself", [128, KB], F32)
                fsel = sbt(st, "fsel", [128, S, S, 128], BF16)
                b0 = sbt(st, "b0", [128, HYW], F32); b1 = sbt(st, "b1", [128, HYW], F32)
                vg = [sbt(st, "vg%d" % i, [128, G, 32], BF16) for i in range(2)]
                x1g = [sbt(st, "x1g%d" % i, [128, G, 32], BF16) for i in range(2)]
                x2g = [sbt(st, "x2g%d" % i, [128, G, 32], BF16) for i in range(2)]
                vr = sbt(st, "vr", [128, NG, S, 32, S], BF16)
                vb = sbt(st, "vb", [128, G, 32], F32)
                tmp = sbt(st, "tmpE", [128, G, 32], F32)
                zz = sbt(st, "zz", [128, G, 32], BF16)
                zr = sbt(st, "zr", [128, NG, S, 32, S], BF16)
                zb = sbt(st, "zb", [128, G, 32], F32)
                pc = [pst(st, "pc%d" % o, [128, NG, 32, S]) for o in range(2)]
                pj = pc[1]

                P.memset("pool", ones_f[:], 1.0)
                P.memset("pool", ones_b[:], 1.0)
                P.ld("sp", b0[:], hyb_d.ap()[0].partition_broadcast(128))
                P.ld("sp", b1[:], hyb_d.ap()[1].partition_broadcast(128))
                hy3 = hyu_s.ap().rearrange("p (c j) -> p c j", j=32)
                P.memset("pool", fsel[:], 0.0)
                for hi_ in range(S):
                    P.memset("pool", rself[:], 0.0)
                    P.op("pool", (lambda e, b_=-(KB * hi_ + KB - 1): e.affine_select(
                        out=rself[:], in_=rself[:], pattern=[[1, KB]], compare_op=ALU.not_equal, fill=1.0,
                        base=b_, channel_multiplier=1)), reads=[rself.name], writes=[rself.name])
                    for sl_ in range(S):
                        P.cp("dve", fsel[:, hi_, sl_, KB * sl_:KB * sl_ + KB], rself[:])

                steps = [(h, Q, c, kb) for h in range(4) for Q in range(8) for c in range(2) for kb in range(NTT)]
                NS = len(steps)

                def gen_C():
                    pending = []
                    state = {"head": -1, "qk": -1}

                    def load_head(h):
                        P.ld("sp", qh[:], qT_s.ap()[h])
                        P.ld("sp", kh[:], kT_s.ap()[h])
                        for t0_ in (0, 17):
                            P.ld("sp", Vh[:, t0_:t0_ + 17, :],
                                 V_s.ap()[t0_:t0_ + 17, :, h * 128:(h + 1) * 128].rearrange("t p v -> p t v"))

                    def ensure_qk(m):
                        if m <= state["qk"] or m >= NS:
                            return
                        h, Q, c, kb = steps[m]
                        if h != state["head"]:
                            load_head(h)
                            state["head"] = h
                        P.mm(pS[m % 2][:], kh[64 * c:64 * c + 64, kb * 128:(kb + 1) * 128],
                             qh[64 * c:64 * c + 64, Q * 512:(Q + 1) * 512])
                        P.act(PT[m % NR][:], pS[m % 2][:], AF.Exp, scale=0.125)
                        state["qk"] = m

                    def fin1(h, Q, c, s_):
                        g = (h * 8 + Q) % 2
                        P.mm(pSm[s_][:], ones_f[:], acc[s_][:])
                        P.op("dve", (lambda e, a=rc[c][:], b=pSm[s_][:]: e.reciprocal(out=a, in_=b)),
                             reads=[pSm[s_].name], writes=[rc[c].name])
                        P.tt("dve", on[2 * g + c][:], pO[s_][:], rc[c][:], ALU.mult)
                        if c == 1:
                            P.stt("dve", oc[g][:], on[2 * g + 1][:], neglam[:], on[2 * g][:], ALU.mult, ALU.add)
                            P.tt("pool", osq[g][:], oc[g][:], oc[g][:], ALU.mult)

                    def fin2(h, Q, s_):
                        g = (h * 8 + Q) % 2
                        P.mm(pSm[s_][:], ones_f[:], osq[g][:])
                        P.act(rst[g][:], pSm[s_][:], AF.Sqrt, bias=epsb[:], scale=1.0 / 128.0)
                        P.op("dve", (lambda e, a=rst[g][:]: e.reciprocal(out=a, in_=a)),
                             reads=[rst[g].name], writes=[rst[g].name])
                        P.stt("dve", orow[g][:], oc[g][:], subg[:], rst[g][:], ALU.mult, ALU.mult)
                        P.ld("pool", oT_s.ap()[h][:, Q * 512:(Q + 1) * 512], orow[g][:])

                    for n in range(NS):
                        h, Q, c, kb = steps[n]
                        s_ = (n // NTT) % 2
                        ensure_qk(n)
                        if n + 1 < NS and steps[n + 1][0] == h:
                            ensure_qk(n + 1)
                        P.mm(pO[s_][:], Vh[:, kb, :], PT[n % NR][:], start=(kb == 0), stop=(kb == NTT - 1),
                             sig=(kb == NTT - 1))
                        if kb == 0:
                            P.cp("dve", acc[s_][:], PT[n % NR][:])
                        else:
                            P.tt("dve", acc[s_][:], acc[s_][:], PT[n % NR][:], ALU.add)
                        for item in list(pending):
                            if item[0] <= n:
                                item[1]()
                                pending.remove(item)
                        if kb == NTT - 1:
                            pending.append((n + 3, (lambda h=h, Q=Q, c=c, s_=s_: fin1(h, Q, c, s_))))
                            if c == 1:
                                pending.append((n + 10, (lambda h=h, Q=Q, s_=s_: fin2(h, Q, s_))))
                            if n + 1 < NS and steps[n + 1][0] != h:
                                for item in pending:
                                    item[1]()
                                pending = []
                        yield
                    for item in pending:
                        item[1]()

                elist = [0] + [e for e in range(-(32 * S - 1), 31 * S + 1) if e != 0]
                ectr = {"tsk": 0}

                def conv_grp(o, c, rhs_t, grp, pb):
                    tk = tsk[ectr["tsk"] % NT]
                    ectr["tsk"] += 1
                    keys = []
                    for sl in range(S):
                        q = "sp"
                        key = tk.name + "_s%d" % sl
                        keys.append(key)
                        P.dma(q, (lambda e, o_=tk[KB * sl:KB * sl + KB, :],
                                         i_=bass.AP(A_s, (o * HYW + c + sl) * 8192, [[1, KB], [1, TW]]):
                                     e.dma_start(out=o_, in_=i_)),
                              reads=[], writes=[key])
                    last = len(elist) - 1
                    for n_, e in enumerate(elist):
                        i_lo = max(0, -((-e) // S))
                        i_hi = min(31, (32 * S - 1 + e) // S)
                        nn = i_hi - i_lo + 1
                        hi = (-e) % S
                        j0 = i_lo - (e + hi) // S
                        assert nn > 0 and (e + hi) % S == 0 and 0 <= j0 and j0 + nn <= 32
                        ce = KB * e + 4096 - KB
                        assert 0 <= ce and ce + 128 <= TW
                        P.op("pe", (lambda en, o_=pb[:, grp, i_lo:i_hi + 1, :].rearrange("p i s -> p (i s)"),
                                           l_=tk[:, ce:ce + 128],
                                           r_=rhs_t[:, grp, hi, j0:j0 + nn, :].rearrange("p j s -> p (j s)"),
                                           s0=(n_ == 0), s1=(n_ == last): en.matmul(o_, l_, r_, start=s0, stop=s1)),
                             reads=keys + [rhs_t.name], writes=[pb.name], sig=(n_ == last))
                        if n_ % 28 == 27:
                            yield

                def reverse(dst, src):
                    sv = src[:].rearrange("p (g s) j -> p g s j", s=S)
                    pjf = pj[:].rearrange("p g i s -> p (g i s)")
                    for hi in range(S):
                        for sl in range(S):
                            P.mm(pjf[:, sl * NG * 32:(sl + 1) * NG * 32], fsel[:, hi, sl, :], sv[:, :, sl, :],
                                 sig=(sl == S - 1))
                        P.cp("act", dst[:, :, hi, :, :],
                             pjf.rearrange("p (s g j) -> p g j s", s=S, g=NG))

                def gen_E():
                    for g in range(HYW // G):
                        i = g % 2
                        c0 = g * G
                        P.ld("sp", vg[i][:], hy3[:, c0:c0 + G, :])
                        P.ld("sp", x1g[i][:], hy3[:, HYW + c0:HYW + c0 + G, :])
                        P.ld("sp", x2g[i][:], hy3[:, 2 * HYW + c0:2 * HYW + c0 + G, :])
                        reverse(vr, vg[i])
                        P.tt("pool", vb[:], vg[i][:], b0[:, c0:c0 + G].unsqueeze(2).to_broadcast([128, G, 32]), ALU.mult)
                        for grp in range(NG):
                            yield from conv_grp(0, c0 + S * grp, vr, grp, pc[0])
                        P.tt("dve", tmp[:].rearrange("p (q h) i -> p q h i", h=S), pc[0][:].rearrange("p q i h -> p q h i"),
                             vb[:].rearrange("p (q h) i -> p q h i", h=S), ALU.add)
                        P.tt("dve", zz[:], tmp[:], x1g[i][:], ALU.mult)
                        reverse(zr, zz)
                        P.tt("pool", zb[:], zz[:], b1[:, c0:c0 + G].unsqueeze(2).to_broadcast([128, G, 32]), ALU.mult)
                        for grp in range(NG):
                            yield from conv_grp(1, c0 + S * grp, zr, grp, pc[1])
                        P.tt("dve", tmp[:].rearrange("p (q h) i -> p q h i", h=S), pc[1][:].rearrange("p q i h -> p q h i"),
                             zb[:].rearrange("p (q h) i -> p q h i", h=S), ALU.add)
                        P.tt("dve", yall[:, c0:c0 + G, :], tmp[:], x2g[i][:], ALU.mult)
                        yield

                gC = gen_C()
                gE = gen_E()
                doneC = doneE = False
                while not (doneC and doneE):
                    if not doneE:
                        try:
                            next(gE)
                        except StopIteration:
                            doneE = True
                    if not doneC:
                        try:
                            next(gC)
                        except StopIteration:
                            doneC = True
                P.barrier()

            with contextlib.ExitStack() as st:
                yrow = [sbt(st, "yrow%d" % i, [128, 512], BF16) for i in range(2)]
                pY = [pst(st, "pY%d" % i, [128, 4, 128], BF16) for i in range(2)]
                n = 0
                for cc in range(4):
                    for iq in range(8):
                        for ii in range(4):
                            P.tr(pY[n % 2][:, ii, :], yall[:, cc * 128:(cc + 1) * 128, iq * 4 + ii], ident[:], sig=(ii == 3))
                        r = yrow[n % 2]
                        P.cp("act", r[:], pY[n % 2][:].rearrange("p a b -> p (a b)"))
                        P.ld("sp", yhT_s.ap()[cc][:, iq * 512:(iq + 1) * 512], r[:])
                        n += 1
                P.barrier()

        def stream_weight_bf16(wt, src_d, nk, ncol, wfs):
            n = 0
            for kc in range(nk):
                for c0 in range(0, ncol, 1024):
                    s_ = wfs[n % 2]
                    n += 1
                    P.ld("sp" if n % 2 == 0 else "act", s_[:], src_d.ap()[kc * 128:(kc + 1) * 128, c0:c0 + 1024])
                    P.cp("dve" if n % 2 == 0 else "act", wt[:, kc, c0:c0 + 1024], s_[:])
            return wt

        def residual_ln(pfx, st_tiles, ps_lo, ps_hi, gbc_mod, res_src_ap, lg, lb, dst):
            rs, r, stats, mv, rstd, tmp = st_tiles
            P.ld("sp", rs[:], res_src_ap)
            P.tt("dve", r[:, 0:512], ps_lo[:], gbc_mod[:, 0:512], ALU.mult)
            P.tt("dve", r[:, 512:1024], ps_hi[:], gbc_mod[:, 512:1024], ALU.mult)
            P.stt("dve", r[:], rs[:], ALPHA, r[:], ALU.mult, ALU.add)
            layer_norm_tile(pfx, r, lg, lb, dst, stats, mv, rstd, tmp, add_eng=("dve" if pfx == "F" else "pool"))

        with contextlib.ExitStack() as stF:
            x1mT = sbt(stF, "x1mT", [128, 8, L], BF16)
            with contextlib.ExitStack() as st:
                who = sbt(st, "who", [128, 4, D], BF16)
                wao = sbt(st, "wao", [128, 4, D], BF16)
                wo = sbt(st, "wo", [128, 8, D], BF16)
                with contextlib.ExitStack() as stw:
                    wfs = [sbt(stw, "wfs%d" % i, [128, 1024], F32) for i in range(2)]
                    stream_weight_bf16(who, who_d, 4, D, wfs)
                    stream_weight_bf16(wao, wao_d, 4, D, wfs)
                    stream_weight_bf16(wo, wout_d, 8, D, wfs)
                    P.barrier()
                l1g = sbt(st, "l1g", [128, D], F32); l1b = sbt(st, "l1b", [128, D], F32)
                P.ld("sp", l1g[:], l1g_d.ap().partition_broadcast(128))
                P.ld("sp", l1b[:], l1b_d.ap().partition_broadcast(128))
                yh = [sbt(st, "yh%d" % i, [128, 4, 512], BF16) for i in range(2)]
                ot = [sbt(st, "ot%d" % i, [128, 4, 512], BF16) for i in range(2)]
                gt = [sbt(st, "gt%d" % i, [128, 16, 512], BF16) for i in range(2)]
                mT = sbt(st, "mT", [128, 8, 512], BF16)
                m1 = sbt(st, "m1", [128, 512], F32); m2 = sbt(st, "m2", [128, 512], F32)
                lnt = [(sbt(st, "rsF%d" % i, [128, D], F32), sbt(st, "rF%d" % i, [128, D], F32),
                        sbt(st, "statsF%d" % i, [128, 2, 6], F32), sbt(st, "mvF%d" % i, [128, 2], F32),
                        sbt(st, "rstdF%d" % i, [128, 1], F32), sbt(st, "tmpF%d" % i, [128, D], F32)) for i in range(2)]
                x1t = [sbt(st, "x1t0", [128, D], F32)] * 2
                x1bs = [sbt(st, "x1b%d" % i, [128, D], BF16) for i in range(2)]
                pA = [pst(st, "pA%d" % i, [128, 512]) for i in range(2)]
                pB2 = [pst(st, "pB2%d" % i, [128, 512]) for i in range(2)]
                pYl = [pst(st, "pYl%d" % i, [128, 512]) for i in range(2)]
                pT2 = pst(st, "pT2", [128, 8, 128], BF16)

                def load_T(T):
                    i = T % 2
                    sl = slice(T * 512, (T + 1) * 512)
                    P.ld("sp", yh[i][:], yhT_s.ap()[:, :, sl].rearrange("c p t -> p c t"))
                    P.ld("act", ot[i][:], oT_s.ap()[:, :, sl].rearrange("c p t -> p c t"))
                    P.ld("sp", gt[i][:], gat_s.ap()[:, :, sl].rearrange("g p t -> p g t"))

                def transposes(x1b, tok0):
                    for kc in range(8):
                        P.tr(pT2[:, kc, :], x1b[:, kc * 128:(kc + 1) * 128], ident[:], sig=(kc == 7))
                    for kc in range(8):
                        P.act(x1mT[:, kc, tok0:tok0 + 128], pT2[:, kc, :], AF.Identity,
                              bias=modT[:, 24 + kc, 0:1], scale=onep[:, 8 + kc, 0:1])

                nt = 0
                prev = None
                load_T(0)
                for T in range(8):
                    i = T % 2
                    if T + 1 < 8:
                        load_T(T + 1)
                    for fc in range(8):
                        a = pA[fc % 2]; b = pB2[fc % 2]
                        for cc in range(4):
                            P.mm(a[:], who[:, cc, fc * 128:(fc + 1) * 128], yh[i][:, cc, :], start=(cc == 0), stop=(cc == 3),
                                 sig=(cc == 3))
                        for cc in range(4):
                            P.mm(b[:], wao[:, cc, fc * 128:(fc + 1) * 128], ot[i][:, cc, :], start=(cc == 0), stop=(cc == 3),
                                 sig=(cc == 3))
                        P.tt("dve", m1[:], a[:], gt[i][:, fc, :], ALU.mult)
                        P.tt("dve", m2[:], b[:], gt[i][:, 8 + fc, :], ALU.mult)
                        P.tt("pool", mT[:, fc, :], m1[:], m2[:], ALU.add)
                    for tb in range(4):
                        tok0 = T * 512 + tb * 128
                        for hf in range(2):
                            for kc in range(8):
                                P.mm(pYl[hf][:], mT[:, kc, tb * 128:(tb + 1) * 128], wo[:, kc, hf * 512:(hf + 1) * 512],
                                     start=(kc == 0), stop=(kc == 7), sig=(kc == 7))
                        if prev is not None:
                            transposes(*prev)
                        xo = x1t[nt % 2]
                        x1b = x1bs[nt % 2]
                        lnt_ = lnt[nt % 2]
                        nt += 1
                        residual_ln("F", lnt_, pYl[0], pYl[1], g1bc,
                                    xln_s.ap()[tok0:tok0 + 128, :], l1g, l1b, xo)
                        P.ld("pool", x1_s.ap()[tok0:tok0 + 128, :], xo[:])
                        P.cp("act", x1b[:], xo[:])
                        prev = (x1b, tok0)
                transposes(*prev)
                P.barrier()

            with contextlib.ExitStack() as st:
                wuf = [sbt(st, "wuf%d" % i, [128, 8, 128], F32) for i in range(2)]
                wub = [sbt(st, "wub%d" % i, [128, 8, 128], BF16) for i in range(4)]
                g_items = []
                for fc_ in range(NFC):
                    for part_ in range(2):
                        g_items.append(part_ * NFC + fc_)
                gctr = {"issued": 0}

                def g_issue():
                    i = gctr["issued"]
                    if i >= len(g_items):
                        return
                    gctr["issued"] += 1
                    ch_ = g_items[i]
                    P.ld("sp", wuf[i % 2][:], wup_d.ap()[:, ch_ * 128:(ch_ + 1) * 128].rearrange("(kc p) c -> p kc c", p=128))
                    P.cp("dve", wub[i % 4][:], wuf[i % 2][:])
                zr2 = [sbt(st, "zr2_%d" % i, [128, L + 2], F32) for i in range(2)]
                tgs = [sbt(st, "tg%d" % i, [128, L], F32) for i in range(2)]
                tas = [sbt(st, "ta_%d" % i, [128, L], F32) for i in range(2)]
                hrow = [sbt(st, "hrow%d" % i, [128, L], BF16) for i in range(2)]
                fcw = sbt(st, "fcw", [128, 44, 3], F32); fcb = sbt(st, "fcb", [128, 44], F32)
                pG = [pst(st, "pG%d" % i, [128, 512]) for i in range(4)]
                P.ld("sp", fcw[:], fcw_d.ap()); P.ld("sp", fcb[:], fcb_d.ap())
                for i in range(2):
                    P.memset("pool", zr2[i][:, 0:1], 0.0)
                    P.memset("pool", zr2[i][:, L + 1:L + 2], 0.0)
                nw = 0
                npg = 0
                for fc in range(NFC):
                    tg = tgs[fc % 2]
                    ta_ = tas[fc % 2]
                    for part in range(2):
                        ch = part * NFC + fc
                        while gctr["issued"] <= min(nw + 2, len(g_items) - 1):
                            g_issue()
                        wbl = wub[nw % 4]
                        nw += 1
                        z = zr2[part]
                        for T in range(8):
                            ps = pG[npg % 4]
                            npg += 1
                            for kc in range(8):
                                P.mm(ps[:], wbl[:, kc, :], x1mT[:, kc, T * 512:(T + 1) * 512], start=(kc == 0), stop=(kc == 7),
                                     sig=(kc == 7))
                            P.cp("act", z[:, 1 + T * 512:1 + (T + 1) * 512], ps[:])
                        dst = ta_ if part == 0 else tg
                        P.ts("pool", dst[:], z[:, 0:L], fcw[:, ch, 0:1], fcb[:, ch:ch + 1], ALU.mult, ALU.add)
                        P.stt("dve", dst[:], z[:, 1:L + 1], fcw[:, ch, 1:2], dst[:], ALU.mult, ALU.add)
                        P.stt("dve", dst[:], z[:, 2:L + 2], fcw[:, ch, 2:3], dst[:], ALU.mult, ALU.add)
                    P.act(tg[:], tg[:], AF.Silu)
                    hr = hrow[fc % 2]
                    P.tt("dve", hr[:], tg[:], ta_[:], ALU.mult)
                    if fc > 0:
                        P.ld("sp", hT_s.ap()[fc - 1], hrow[(fc - 1) % 2][:])
                P.ld("sp", hT_s.ap()[NFC - 1], hrow[(NFC - 1) % 2][:])
                P.barrier()

        with contextlib.ExitStack() as st:
            wfs = [sbt(st, "wfsH%d" % i, [128, 1024], F32) for i in range(2)]
            wd = sbt(st, "wd", [128, NFC, D], BF16)
            stream_weight_bf16(wd, wdn_d, NFC, D, wfs)
            l2g = sbt(st, "l2g", [128, D], F32); l2b = sbt(st, "l2b", [128, D], F32)
            P.ld("sp", l2g[:], l2g_d.ap().partition_broadcast(128))
            P.ld("sp", l2b[:], l2b_d.ap().partition_broadcast(128))
            ht = [sbt(st, "ht%d" % i, [128, NFC, 512], BF16) for i in range(2)]
            lnt = [(sbt(st, "rsH%d" % i, [128, D], F32), sbt(st, "rH%d" % i, [128, D], F32),
                    sbt(st, "statsH%d" % i, [128, 2, 6], F32), sbt(st, "mvH%d" % i, [128, 2], F32),
                    sbt(st, "rstdH%d" % i, [128, 1], F32), sbt(st, "tmpH%d" % i, [128, D], F32)) for i in range(2)]
            xo = [sbt(st, "xoH%d" % i, [128, D], F32) for i in range(2)]
            pD = [[pst(st, "pD%d_%d" % (s, i), [128, 512]) for i in range(2)] for s in range(2)]
            nt = 0

            def load_ht(T):
                P.ld("sp" if T % 2 == 0 else "act", ht[T % 2][:],
                     hT_s.ap()[:, :, T * 512:(T + 1) * 512].rearrange("f p t -> p f t"))

            load_ht(0)
            for T in range(8):
                i = T % 2
                if T + 1 < 8:
                    load_ht(T + 1)
                for tb in range(4):
                    tok0 = T * 512 + tb * 128
                    pp = pD[nt % 2]
                    for hf in range(2):
                        for fc in range(NFC):
                            P.mm(pp[hf][:], ht[i][:, fc, tb * 128:(tb + 1) * 128], wd[:, fc, hf * 512:(hf + 1) * 512],
                                 start=(fc == 0), stop=(fc == NFC - 1), sig=(fc == NFC - 1))
                    o_ = xo[nt % 2]
                    lnt_ = lnt[nt % 2]
                    nt += 1
                    residual_ln("H", lnt_, pp[0], pp[1], g2bc,
                                x1_s.ap()[tok0:tok0 + 128, :], l2g, l2b, o_)
                    P.ld("pool", out_d.ap()[tok0:tok0 + 128, :], o_[:])
            P.barrier()

        P.run()
    return nc


def host_constants():
    f32 = np.float32
    rows = L // 64
    row = np.repeat(np.arange(rows, dtype=f32), 64)
    col = np.tile(np.arange(64, dtype=f32), rows)
    inv = (10000.0 ** (-np.arange(0, 32, 2, dtype=f32) / 32.0)).astype(f32)
    cosT = np.zeros((128, L), f32)
    sinT = np.zeros((128, L), f32)
    for p in range(128):
        d = p % 64
        a = d // 32
        i = d % 32
        f = i % 16
        ang = (row if a == 0 else col) * inv[f]
        cosT[p] = np.cos(ang)
        sinT[p] = (-np.sin(ang)) if i < 16 else np.sin(ang)
    t = np.linspace(0.0, 1.0, L, dtype=f32)[:, None]
    w = (f32(2.0 * math.pi / L) * np.arange(L, dtype=f32))[:, None]
    f = np.linspace(1e-4, 15, 16, dtype=f32)[None, :]
    z = np.concatenate([t, np.cos(f * w), -np.sin(f * w)], -1).astype(f32)
    min_decay = math.log(1e-2) / 1.5
    max_decay = math.log(1e-2) / 0.3
    deltas = np.linspace(min_decay, max_decay, HYW, dtype=f32)
    decay = np.exp(-t * np.abs(deltas)[None, :]).astype(f32)
    return dict(
        rope_cos=cosT, rope_sin=sinT,
        zposT_f=np.ascontiguousarray(z.T), zposT_r=np.ascontiguousarray(z[::-1].T),
        decayT_f=np.ascontiguousarray(decay.T), decayT_r=np.ascontiguousarray(decay[::-1].T),
    )


def per_part(v, nch):
    return np.ascontiguousarray(np.asarray(v, np.float32).reshape(nch, 128).T)


def make_in_maps(inp):
    f32 = np.float32
    g = {k: np.asarray(v, f32) for k, v in inp.items()}
    const = host_constants()
    w_in = g["w_in"][0]
    perm = np.zeros(1024, np.int64)
    for cidx in range(1024):
        base = 1536 + cidx
        d = cidx % 64
        i = d % 32
        perm[cidx] = base + 16 if i < 16 else base - 16
    w_qkp = np.ascontiguousarray(w_in[:, perm])
    cw = g["hy_conv_w"][0]
    hy_cw = np.ascontiguousarray(np.stack([per_part(cw[k], 12) for k in range(3)], -1))
    fw = g["ffn_conv_w"][0]
    ffn_cw = np.ascontiguousarray(np.stack([per_part(fw[k], 44) for k in range(3)], -1))
    shared = dict(
        ln_in_g=g["ln_in_g"], ln_in_b=g["ln_in_b"],
        w_ada=g["w_ada"][0], b_ada=g["b_ada"][0], b_adaT=per_part(g["b_ada"][0], 48),
        w_in=w_in, w_qkp=w_qkp,
        hy_cw=hy_cw, hy_cb=per_part(g["hy_conv_b"][0], 12),
        f_w1=g["hy_f_w1"][0], f_w2=g["hy_f_w2"][0], f_w3=g["hy_f_w3"][0], f_wout=g["hy_f_wout"][0],
        f_freqT=np.ascontiguousarray(g["hy_f_freq"][0].T),
        f_bT=np.ascontiguousarray(np.stack([g["hy_f_b1"][0], g["hy_f_b2"][0], g["hy_f_b3"][0]], -1)),
        hy_bias=g["hy_bias"][0],
        lamv=np.ascontiguousarray(np.stack([g["lam_q1"][0], g["lam_k1"][0], g["lam_q2"][0], g["lam_k2"][0]], 0)),
        subln_g=np.ascontiguousarray(g["at_subln_g"][0].reshape(128, 1)),
        w_hy_o=g["w_hy_o"][0], w_at_o=g["w_at_o"][0], w_out=g["w_out"][0],
        ln1_g=g["ln1_g"][0], ln1_b=g["ln1_b"][0], ln2_g=g["ln2_g"][0], ln2_b=g["ln2_b"][0],
        ffn_w_up=g["ffn_w_up"][0], ffn_cw=ffn_cw, ffn_cb=per_part(g["ffn_conv_b"][0], 44),
        ffn_w_down=g["ffn_w_down"][0],
    )
    shared.update(const)
    maps = []
    cc = per_part(g["c_ctx"], 8)
    for b in range(8):
        m = dict(shared)
        m["x"] = np.ascontiguousarray(g["x"][b])
        m["ctx"] = np.ascontiguousarray(g["ctx"][b])
        m["cT"] = np.ascontiguousarray(np.stack([per_part(g["c"][b], 8), cc], -1))
        maps.append(m)
    return maps


_NC = None


def kernel(**inputs):
    global _NC
    if _NC is None:
        _NC = build_program()
    maps = make_in_maps(inputs)
    res = run_bass_kernel_spmd(_NC, maps, core_ids=list(range(8)))
    out = np.stack([np.asarray(r["out"], np.float32) for r in res.results], 0)
    return out
```
